# Optimizing a Trainium2 kernel written in Bass

```python
import math
import jax, jax.numpy as jnp
from jax import lax
import numpy as np

D_MODEL = 1024
BATCH = 8
SEQ = 2048
DEPTH = 4

N_MIXERS = 4
N_A = (DEPTH + 3) // 4
N_B = (DEPTH + 2) // 4
N_C = (DEPTH + 1) // 4
N_D = DEPTH // 4
EPS = 1e-6
NEG_INF = -1e30

SSM_WIDTH = D_MODEL
SSM_GROUP = 16
SSM_GROUPS = SSM_WIDTH // SSM_GROUP
SSM_STATE = 64
DT_MIN = 1e-3
DT_MAX = 1e-1

HEAD_DIM = 64
SWA_HEADS = D_MODEL // HEAD_DIM
SWA_KV_HEADS = SWA_HEADS // 8
SWA_WIDTH = SWA_HEADS * HEAD_DIM
WINDOW = 128

REL_BUCKETS = 32
REL_MAX_DIST = 128

MLA_HEADS = 16
MLA_NOPE = 64
MLA_ROPE = 32
MLA_V = 64
MLA_KV_RANK = 256
MLA_Q_RANK = 768
MLA_WIDTH = MLA_HEADS * MLA_V
ROPE_BASE = 10000.0
Q_BLOCK = 128

SGU_WIDTH = D_MODEL
SGU_CHUNK = 128
SGU_GROUPS = 16
SGU_GROUP_DIM = SGU_WIDTH // SGU_GROUPS

kernel_name = 'hybrid_interleaved_s5_swa_mla_sgu'


def rmsnorm(x, g):
    xf = x.astype(jnp.float32)
    y = xf * lax.rsqrt(jnp.mean(xf * xf, axis=-1, keepdims=True) + EPS)
    return (y * g.astype(jnp.float32)).astype(x.dtype)


def layernorm(x, g, b):
    xf = x.astype(jnp.float32)
    mu = jnp.mean(xf, axis=-1, keepdims=True)
    var = jnp.mean(jnp.square(xf - mu), axis=-1, keepdims=True)
    y = (xf - mu) * lax.rsqrt(var + EPS) * g.astype(jnp.float32) + b.astype(jnp.float32)
    return y.astype(x.dtype)


def _ssm_combine(left, right):
    a1r, a1i, b1r, b1i = left
    a2r, a2i, b2r, b2i = right
    return (a2r * a1r - a2i * a1i,
            a2r * a1i + a2i * a1r,
            a2r * b1r - a2i * b1i + b2r,
            a2r * b1i + a2i * b1r + b2i)


def s5_mixer(u, lam_re, lam_im, log_dt, b_re, b_im, c_re, c_im, d_skip, w_glu, b_glu):
    f32 = jnp.float32
    bsz, L, _ = u.shape
    ug = u.astype(f32).reshape(bsz, L, SSM_GROUPS, SSM_GROUP)
    lr = lam_re.astype(f32)
    li = lam_im.astype(f32)
    dt = jnp.exp(log_dt.astype(f32))[:, None]
    mag = jnp.exp(lr * dt)
    ab_re = mag * jnp.cos(li * dt)
    ab_im = mag * jnp.sin(li * dt)
    den = lr * lr + li * li
    nr = ab_re - 1.0
    f_re = (nr * lr + ab_im * li) / den
    f_im = (ab_im * lr - nr * li) / den
    br = b_re.astype(f32)
    bi = b_im.astype(f32)
    bb_re = f_re[..., None] * br - f_im[..., None] * bi
    bb_im = f_re[..., None] * bi + f_im[..., None] * br
    bu_re = jnp.einsum('blgh,gph->blgp', ug, bb_re)
    bu_im = jnp.einsum('blgh,gph->blgp', ug, bb_im)
    a_re = jnp.broadcast_to(ab_re, (1, L) + ab_re.shape)
    a_im = jnp.broadcast_to(ab_im, (1, L) + ab_im.shape)
    _, _, s_re, s_im = lax.associative_scan(_ssm_combine, (a_re, a_im, bu_re, bu_im), axis=1)
    y = (jnp.einsum('blgp,ghp->blgh', s_re, c_re.astype(f32))
         - jnp.einsum('blgp,ghp->blgh', s_im, c_im.astype(f32)))
    y = y.reshape(bsz, L, SSM_WIDTH) + d_skip.astype(f32) * u.astype(f32)
    y = jax.nn.gelu(y).astype(u.dtype)
    return y * jax.nn.sigmoid(y @ w_glu + b_glu)


def s5_branch(h, w_in, lam_re, lam_im, log_dt, b_re, b_im, c_re, c_im, d_skip, w_glu, b_glu, w_out):
    u, z = jnp.split(h @ w_in, [SSM_WIDTH], axis=-1)
    y = s5_mixer(u, lam_re, lam_im, log_dt, b_re, b_im, c_re, c_im, d_skip, w_glu, b_glu)
    return (y * jax.nn.silu(z)) @ w_out


def t5_bucket(dist):
    max_exact = REL_BUCKETS // 2
    dist_f = jnp.maximum(dist, 1).astype(jnp.float32)
    large = max_exact + (jnp.log(dist_f / max_exact) / math.log(REL_MAX_DIST / max_exact)
                         * (REL_BUCKETS - max_exact)).astype(jnp.int32)
    large = jnp.minimum(large, REL_BUCKETS - 1)
    return jnp.where(dist < max_exact, dist, large)


def sliding_window_attention(q, k, v, sinks, rel_bias):
    bsz, L = q.shape[0], q.shape[1]
    nb = L // WINDOW
    grp = SWA_HEADS // SWA_KV_HEADS
    qb = q.reshape(bsz, nb, WINDOW, SWA_KV_HEADS, grp, HEAD_DIM)

    def band(t):
        prev = jnp.pad(t, ((0, 0), (WINDOW, 0), (0, 0), (0, 0)))[:, :L]
        shp = (bsz, nb, WINDOW, SWA_KV_HEADS, HEAD_DIM)
        return jnp.concatenate([prev.reshape(shp), t.reshape(shp)], axis=2)

    kb = band(k)
    vb = band(v)
    s = jnp.einsum('bnqhgd,bnkhd->bnhgqk', qb, kb).astype(jnp.float32) * (HEAD_DIM ** -0.5)
    qi = jnp.arange(WINDOW)[:, None]
    kj = jnp.arange(2 * WINDOW)[None, :]
    dist = qi + WINDOW - kj
    blk = jnp.arange(nb)[:, None, None]
    valid = (dist >= 0) & (dist < WINDOW) & (blk * WINDOW + kj - WINDOW >= 0)
    bias = rel_bias[t5_bucket(jnp.maximum(dist, 0))]
    bias = jnp.transpose(bias, (2, 0, 1)).reshape(SWA_KV_HEADS, grp, WINDOW, 2 * WINDOW).astype(jnp.float32)
    s = jnp.where(valid[None, :, None, None], s + bias, NEG_INF)
    sink = jnp.broadcast_to(sinks.astype(jnp.float32).reshape(SWA_KV_HEADS, grp, 1, 1), s.shape[:-1] + (1,))
    p = jax.nn.softmax(jnp.concatenate([s, sink], axis=-1), axis=-1)[..., :-1]
    o = jnp.einsum('bnhgqk,bnkhd->bnqhgd', p.astype(v.dtype), vb)
    return o.reshape(bsz, L, SWA_WIDTH)


def swa_branch(h, w_in, sinks, w_out, rel_bias):
    bsz, L, _ = h.shape
    kv_w = SWA_KV_HEADS * HEAD_DIM
    q, k, v, z = jnp.split(h @ w_in, [SWA_WIDTH, SWA_WIDTH + kv_w, SWA_WIDTH + 2 * kv_w], axis=-1)
    q = q.reshape(bsz, L, SWA_HEADS, HEAD_DIM)
    k = k.reshape(bsz, L, SWA_KV_HEADS, HEAD_DIM)
    v = v.reshape(bsz, L, SWA_KV_HEADS, HEAD_DIM)
    o = sliding_window_attention(q, k, v, sinks, rel_bias)
    return (o * jax.nn.silu(z)) @ w_out


def rope_tables(L):
    inv = ROPE_BASE ** (-jnp.arange(0, MLA_ROPE, 2, dtype=jnp.float32) / MLA_ROPE)
    ang = jnp.arange(L, dtype=jnp.float32)[:, None] * inv[None, :]
    return jnp.cos(ang), jnp.sin(ang)


def apply_rope(x, cos, sin):
    xf = x.astype(jnp.float32)
    x1, x2 = jnp.split(xf, 2, axis=-1)
    return jnp.concatenate([x1 * cos - x2 * sin, x2 * cos + x1 * sin], axis=-1).astype(x.dtype)


def causal_block_attention(q, k, v):
    bsz, L, H, dk = q.shape
    nb = L // Q_BLOCK
    scale = dk ** -0.5
    qb = q.reshape(bsz, nb, Q_BLOCK, H, dk).transpose(1, 0, 2, 3, 4)
    kpos = jnp.arange(L)

    def one_block(args):
        qi, n = args
        s = jnp.einsum('bqhd,bkhd->bhqk', qi, k).astype(jnp.float32) * scale
        qpos = n * Q_BLOCK + jnp.arange(Q_BLOCK)
        s = jnp.where(kpos[None, :] <= qpos[:, None], s, NEG_INF)
        p = jax.nn.softmax(s, axis=-1).astype(v.dtype)
        return jnp.einsum('bhqk,bkhd->bqhd', p, v)

    o = lax.map(one_block, (qb, jnp.arange(nb)))
    return o.transpose(1, 0, 2, 3, 4).reshape(bsz, L, H, v.shape[-1])


def mla_branch(h, w_in, q_norm, kv_norm, w_uq, w_ukv, w_out):
    bsz, L, _ = h.shape
    c_q, c_kv, k_rope, z = jnp.split(
        h @ w_in, [MLA_Q_RANK, MLA_Q_RANK + MLA_KV_RANK, MLA_Q_RANK + MLA_KV_RANK + MLA_ROPE], axis=-1)
    q = (rmsnorm(c_q, q_norm) @ w_uq).reshape(bsz, L, MLA_HEADS, MLA_NOPE + MLA_ROPE)
    kv = (rmsnorm(c_kv, kv_norm) @ w_ukv).reshape(bsz, L, MLA_HEADS, MLA_NOPE + MLA_V)
    cos, sin = rope_tables(L)
    q = jnp.concatenate([q[..., :MLA_NOPE], apply_rope(q[..., MLA_NOPE:], cos[:, None], sin[:, None])], axis=-1)
    k_rope = apply_rope(k_rope, cos, sin)
    k = jnp.concatenate([kv[..., :MLA_NOPE],
                         jnp.broadcast_to(k_rope[:, :, None, :], (bsz, L, MLA_HEADS, MLA_ROPE))], axis=-1)
    o = causal_block_attention(q, k, kv[..., MLA_NOPE:])
    return (o.reshape(bsz, L, MLA_WIDTH) * jax.nn.silu(z)) @ w_out


def sgu_branch(h, w_in, ln_g, ln_b, w_s, b_s, w_out):
    bsz, L, _ = h.shape
    uv, z = jnp.split(h @ w_in, [2 * SGU_WIDTH], axis=-1)
    u, v = jnp.split(jax.nn.gelu(uv), 2, axis=-1)
    v = layernorm(v, ln_g, ln_b).reshape(bsz, L // SGU_CHUNK, SGU_CHUNK, SGU_GROUPS, SGU_GROUP_DIM)
    tril = jnp.tril(jnp.ones((SGU_CHUNK, SGU_CHUNK), dtype=bool))
    w = jnp.where(tril[None], w_s, 0.0)
    s = jnp.einsum('gts,bnsgc->bntgc', w, v) + b_s.T[:, :, None]
    s = s.reshape(bsz, L, SGU_WIDTH)
    return (u * s * jax.nn.silu(z)) @ w_out


def setup_inputs(seed: int = 0) -> dict:
    key = jax.random.key(seed)
    ks = iter(jax.random.split(key, 40))
    f32 = jnp.float32

    def nrm(shape, scale):
        return jax.random.normal(next(ks), shape, f32) * scale

    x = nrm((BATCH, SEQ, D_MODEL), 1.0)
    pre_norm = 1.0 + nrm((DEPTH, D_MODEL), 0.05)
    post_norm = 1.0 + nrm((DEPTH, D_MODEL), 0.05)
    rel_bias = nrm((REL_BUCKETS, SWA_HEADS), 0.5)
    a_w_in = nrm((N_A, D_MODEL, 2 * SSM_WIDTH), D_MODEL ** -0.5)
    n_idx = jnp.arange(SSM_STATE, dtype=f32)
    a_lam_re = -0.5 + nrm((N_A, SSM_GROUPS, SSM_STATE), 0.01)
    a_lam_im = jnp.pi * n_idx + nrm((N_A, SSM_GROUPS, SSM_STATE), 0.01)
    a_log_dt = jax.random.uniform(next(ks), (N_A, SSM_GROUPS), f32, math.log(DT_MIN), math.log(DT_MAX))
    a_b_re = nrm((N_A, SSM_GROUPS, SSM_STATE, SSM_GROUP), (2 * SSM_GROUP) ** -0.5)
    a_b_im = nrm((N_A, SSM_GROUPS, SSM_STATE, SSM_GROUP), (2 * SSM_GROUP) ** -0.5)
    a_c_re = nrm((N_A, SSM_GROUPS, SSM_GROUP, SSM_STATE), SSM_STATE ** -0.5)
    a_c_im = nrm((N_A, SSM_GROUPS, SSM_GROUP, SSM_STATE), SSM_STATE ** -0.5)
    a_d = nrm((N_A, SSM_WIDTH), 1.0)
    a_w_glu = nrm((N_A, SSM_WIDTH, SSM_WIDTH), SSM_WIDTH ** -0.5)
    a_b_glu = nrm((N_A, SSM_WIDTH), 0.02)
    a_w_out = nrm((N_A, SSM_WIDTH, D_MODEL), SSM_WIDTH ** -0.5)
    b_w_in = nrm((N_B, D_MODEL, 2 * SWA_WIDTH + 2 * SWA_KV_HEADS * HEAD_DIM), D_MODEL ** -0.5)
    b_sinks = nrm((N_B, SWA_HEADS), 1.0)
    b_w_out = nrm((N_B, SWA_WIDTH, D_MODEL), SWA_WIDTH ** -0.5)
    c_w_in = nrm((N_C, D_MODEL, MLA_Q_RANK + MLA_KV_RANK + MLA_ROPE + MLA_WIDTH), D_MODEL ** -0.5)
    c_q_norm = 1.0 + nrm((N_C, MLA_Q_RANK), 0.05)
    c_kv_norm = 1.0 + nrm((N_C, MLA_KV_RANK), 0.05)
    c_w_uq = nrm((N_C, MLA_Q_RANK, MLA_HEADS * (MLA_NOPE + MLA_ROPE)), MLA_Q_RANK ** -0.5)
    c_w_ukv = nrm((N_C, MLA_KV_RANK, MLA_HEADS * (MLA_NOPE + MLA_V)), MLA_KV_RANK ** -0.5)
    c_w_out = nrm((N_C, MLA_WIDTH, D_MODEL), MLA_WIDTH ** -0.5)
    d_w_in = nrm((N_D, D_MODEL, 3 * SGU_WIDTH), D_MODEL ** -0.5)
    d_ln_g = 1.0 + nrm((N_D, SGU_WIDTH), 0.05)
    d_ln_b = nrm((N_D, SGU_WIDTH), 0.02)
    d_w_s = nrm((N_D, SGU_GROUPS, SGU_CHUNK, SGU_CHUNK), 0.5 * SGU_CHUNK ** -0.5)
    d_b_s = 1.0 + nrm((N_D, SGU_GROUPS, SGU_CHUNK), 0.1)
    d_w_out = nrm((N_D, SGU_WIDTH, D_MODEL), SGU_WIDTH ** -0.5)
    return {'x': x, 'pre_norm': pre_norm, 'post_norm': post_norm, 'rel_bias': rel_bias,
            'a_w_in': a_w_in, 'a_lam_re': a_lam_re, 'a_lam_im': a_lam_im, 'a_log_dt': a_log_dt,
            'a_b_re': a_b_re, 'a_b_im': a_b_im, 'a_c_re': a_c_re, 'a_c_im': a_c_im, 'a_d': a_d,
            'a_w_glu': a_w_glu, 'a_b_glu': a_b_glu, 'a_w_out': a_w_out,
            'b_w_in': b_w_in, 'b_sinks': b_sinks, 'b_w_out': b_w_out,
            'c_w_in': c_w_in, 'c_q_norm': c_q_norm, 'c_kv_norm': c_kv_norm, 'c_w_uq': c_w_uq,
            'c_w_ukv': c_w_ukv, 'c_w_out': c_w_out,
            'd_w_in': d_w_in, 'd_ln_g': d_ln_g, 'd_ln_b': d_ln_b, 'd_w_s': d_w_s, 'd_b_s': d_b_s,
            'd_w_out': d_w_out}


def reference(x, pre_norm, post_norm, rel_bias,
              a_w_in, a_lam_re, a_lam_im, a_log_dt, a_b_re, a_b_im, a_c_re, a_c_im, a_d,
              a_w_glu, a_b_glu, a_w_out,
              b_w_in, b_sinks, b_w_out,
              c_w_in, c_q_norm, c_kv_norm, c_w_uq, c_w_ukv, c_w_out,
              d_w_in, d_ln_g, d_ln_b, d_w_s, d_b_s, d_w_out):
    for i in range(DEPTH):
        kind = i % N_MIXERS
        j = i // N_MIXERS
        h = rmsnorm(x, pre_norm[i])
        if kind == 0:
            y = s5_branch(h, a_w_in[j], a_lam_re[j], a_lam_im[j], a_log_dt[j], a_b_re[j], a_b_im[j],
                          a_c_re[j], a_c_im[j], a_d[j], a_w_glu[j], a_b_glu[j], a_w_out[j])
        elif kind == 1:
            y = swa_branch(h, b_w_in[j], b_sinks[j], b_w_out[j], rel_bias)
        elif kind == 2:
            y = mla_branch(h, c_w_in[j], c_q_norm[j], c_kv_norm[j], c_w_uq[j], c_w_ukv[j], c_w_out[j])
        else:
            y = sgu_branch(h, d_w_in[j], d_ln_g[j], d_ln_b[j], d_w_s[j], d_b_s[j], d_w_out[j])
        x = x + rmsnorm(y, post_norm[i])
    return x
```

```python
import math
import numpy as np
from contextlib import ExitStack
import concourse.bass as bass
import concourse.mybir as mybir
from concourse.bass_utils import run_bass_kernel_spmd

F32 = mybir.dt.float32
BF16 = mybir.dt.bfloat16
ALU = mybir.AluOpType
AF = mybir.ActivationFunctionType

P = 128
L = 2048
D = 1024
NBLK = 4
BW = 512
EPS = 1e-6
SELF_SYNC = True
NDS = 12


class Buf:
    __slots__ = ("w", "r")

    def __init__(self):
        self.w = None
        self.r = {}


def bufs(*shape):
    if len(shape) == 1:
        return [Buf() for _ in range(shape[0])]
    return [bufs(*shape[1:]) for _ in range(shape[0])]


class KB:
    def __init__(self, nc, es):
        self.nc = nc
        self.E = dict(pe=nc.tensor, act=nc.scalar, dve=nc.vector, pool=nc.gpsimd, sp=nc.sync)
        self.sem = {e: es.enter_context(nc.semaphore("s_" + e)) for e in ("pe", "act", "dve", "pool")}
        self.cnt = {e: 0 for e in self.sem}
        self.pend = {e: False for e in self.sem}
        self.dsem = {q: [[es.enter_context(nc.semaphore("d_%s%d" % (q, i))), 0] for i in range(NDS)]
                     for q in ("sp", "pool")}
        self.dcnt = {"sp": 0, "pool": 0}
        self.seen = {e: {} for e in self.E}
        self.ps = []
        self.psB = []
        self.psrot = 0
        self.nrot = 6

    def _semh(self, key):
        if isinstance(key, str):
            return self.sem[key]
        return self.dsem[key[0]][key[1]][0]

    def _wait(self, e, toks):
        need = {}
        for key, v in toks:
            if need.get(key, 0) < v:
                need[key] = v
        for key, v in need.items():
            if key == e and (e == "pe" or not SELF_SYNC):
                continue
            if self.seen[e].get(key, 0) >= v:
                continue
            self.E[e].wait_ge(self._semh(key), v)
            self.seen[e][key] = v

    def _deps(self, reads, writes):
        toks = []
        for b in reads:
            if b.w is not None:
                toks.append(b.w)
        for b in writes:
            if b.w is not None:
                toks.append(b.w)
            toks.extend(b.r.items())
        return toks

    def _mark(self, tok, reads, writes):
        key, v = tok
        for b in reads:
            if b.r.get(key, 0) < v:
                b.r[key] = v
        for b in writes:
            b.w = tok
            b.r = {}

    def op(self, e, reads, writes, fn, inc=True):
        self._wait(e, self._deps(reads, writes))
        ins = fn(self.E[e])
        if inc:
            self.cnt[e] += 1
            ins.then_inc(self.sem[e], 1)
            self.pend[e] = False
            tok = (e, self.cnt[e])
        else:
            self.pend[e] = True
            tok = (e, self.cnt[e] + 1)
        self._mark(tok, reads, writes)
        return ins

    def dma(self, q, out, in_, reads, writes):
        self._wait(q, self._deps(reads, writes))
        i = self.dcnt[q] % NDS
        self.dcnt[q] += 1
        ent = self.dsem[q][i]
        key = (q, i)
        if ent[1] > 0:
            self._wait(q, [(key, 16 * ent[1])])
        ins = self.E[q].dma_start(out=out, in_=in_)
        ins.then_inc(ent[0], 16)
        ent[1] += 1
        tok = (key, 16 * ent[1])
        self._mark(tok, reads, writes)
        return tok

    def all_tokens(self):
        toks = [(e, c) for e, c in self.cnt.items() if c > 0]
        for q in self.dsem:
            for i, ent in enumerate(self.dsem[q]):
                if ent[1] > 0:
                    toks.append(((q, i), 16 * ent[1]))
        return toks

    def barrier(self):
        for e in self.sem:
            assert not self.pend[e], e
        toks = self.all_tokens()
        for e in self.E:
            self._wait(e, [t for t in toks if t[0] != e])

    def nextps(self):
        b = self.psrot
        self.psrot = (self.psrot + 1) % self.nrot
        return b


def build(nc, layers):
    es = ExitStack()
    with es:
        _build(nc, es, layers)
    return nc


def _build(nc, es, layers):
    k = KB(nc, es)
    op, dma = k.op, k.dma

    def dram_in(name, shape, dt=F32):
        return nc.dram_tensor(name, list(shape), dt, kind="ExternalInput").ap()

    uid = [0]

    def sb(st, name, shape, dt):
        uid[0] += 1
        return st.enter_context(nc.sbuf_tensor("%s_%d" % (name, uid[0]), list(shape), dt))

    xT_d = dram_in("xT", [D, L])
    outT_d = nc.dram_tensor("outT", [D, L], F32, kind="ExternalOutput").ap()
    gpre_d = dram_in("gpre", [P, 32])
    gpost_d = dram_in("gpost", [P, 32])
    ident_d = dram_in("ident", [P, P])
    dw = {}
    if 3 in layers:
        dw["d_w_in"] = dram_in("d_w_in", [D, 3072])
        dw["d_w_out"] = dram_in("d_w_out", [D, D])
        dw["d_ws"] = dram_in("d_ws", [P, 16, P])
        dw["d_tril"] = dram_in("d_tril", [P, P])
        dw["d_bs"] = dram_in("d_bs", [16, P])
        dw["d_lng"] = dram_in("d_lng", [1, D])
        dw["d_lnb"] = dram_in("d_lnb", [1, D])

    if 1 in layers:
        dw["b_w_in"] = dram_in("b_w_in", [D, 2432])
        dw["b_w_out"] = dram_in("b_w_out", [D, D])
        dw["b_biasT"] = dram_in("b_biasT", [P, 16, 256])
        dw["b_sinks"] = dram_in("b_sinks", [1, 16])

    if 2 in layers:
        dw["c_w1"] = dram_in("c_w1", [D, 1088])
        dw["c_wz"] = dram_in("c_wz", [D, D])
        dw["c_wuq"] = dram_in("c_wuq", [768, 2048])
        dw["c_wukv"] = dram_in("c_wukv", [256, 2048])
        dw["c_w_out"] = dram_in("c_w_out", [D, D])
        dw["c_rope"] = dram_in("c_rope", [64, L])
        dw["c_gqkv"] = dram_in("c_gqkv", [P, 8])
        dw["c_maskT"] = dram_in("c_maskT", [P, P])

    if 0 in layers:
        dw["a_wu"] = dram_in("a_wu", [D, D])
        dw["a_wz"] = dram_in("a_wz", [D, D])
        dw["a_wglu"] = dram_in("a_wglu", [D, D])
        dw["a_w_out"] = dram_in("a_w_out", [D, D])
        dw["a_dg"] = dram_in("a_dg", [P, 16])
        dw["a_lam"] = dram_in("a_lam", [P, 3, 32])
        for nm in ("a_brl", "a_bil", "a_crp", "a_cip"):
            dw[nm] = dram_in(nm, [P, 32, P])

    X = sb(es, "X", [P, 8, L], F32)
    xB = bufs(8, NBLK)
    ones = sb(es, "ones", [P, P], BF16)
    onesB = Buf()
    ident = sb(es, "identb", [P, P], BF16)
    identB = Buf()
    gpre = sb(es, "gpre_s", [P, 32], F32)
    gpost = sb(es, "gpost_s", [P, 32], F32)
    gB = Buf()
    rstd = sb(es, "rstd", [P, BW], F32)
    rstdB = Buf()
    for i in range(8):
        k.ps.append(es.enter_context(nc.psum_tensor("ps%d" % i, [P, BW], F32)))
        k.psB.append(Buf())
    ps, psB = k.ps, k.psB

    for c in range(8):
        for tb in range(NBLK):
            dma("sp", X[:, c, tb * BW:(tb + 1) * BW], xT_d[c * P:(c + 1) * P, tb * BW:(tb + 1) * BW], [], [xB[c][tb]])
    dma("sp", gpre[:], gpre_d, [], [gB])
    dma("sp", gpost[:], gpost_d, [], [gB])
    dma("pool", ident[:], ident_d, [], [identB])
    op("dve", [], [onesB], lambda e: e.memset(ones[:], 1.0))
    epsc = sb(es, "epsc", [P, 1], F32)
    op("dve", [], [onesB], lambda e: e.memset(epsc[:], EPS))

    def load_w(st, name, d_ap, ncols, q="pool"):
        K = d_ap.shape[0]
        kc_n = K // P
        t = sb(st, name, [P, kc_n, ncols], BF16)
        B = Buf()
        for kc in range(kc_n):
            for c0 in range(0, ncols, 2048):
                c1 = min(ncols, c0 + 2048)
                dma(q, t[:, kc, c0:c1], d_ap[kc * P:(kc + 1) * P, c0:c1], [], [B])
        return t, B

    def rstd_from_sq(sq, sqB, n, scale, c0=0):
        b = k.nextps()
        for c in range(n):
            op("pe", [sqB[c0 + c], onesB], [psB[b]],
               lambda e, c=c: e.matmul(ps[b][:, :], lhsT=ones[:, :], rhs=sq[:, c0 + c, :], start=(c == 0), stop=(c == n - 1)),
               inc=(c == n - 1))
        op("act", [psB[b]], [rstdB],
           lambda e: e.activation(out=rstd[:], in_=ps[b][:, :], func=AF.Sqrt, scale=scale, bias=epsc[:, 0:1]))
        op("dve", [rstdB], [rstdB], lambda e: e.reciprocal(out=rstd[:], in_=rstd[:]))

    def prenorm(li, tb, xn, xnB, tmpA, tmpAB):
        blk = slice(tb * BW, (tb + 1) * BW)
        for c in range(8):
            op("act", [xB[c][tb]], [tmpAB[c]],
               lambda e, c=c: e.activation(out=tmpA[:, c, :], in_=X[:, c, blk], func=AF.Square))
        rstd_from_sq(tmpA, tmpAB, 8, 1.0 / D)
        for c in range(8):
            op("dve", [xB[c][tb], rstdB, gB], [xnB[c]],
               lambda e, c=c: e.scalar_tensor_tensor(out=xn[:, c, :], in0=X[:, c, blk],
                                                     scalar=gpre[:, li * 8 + c:li * 8 + c + 1], in1=rstd[:],
                                                     op0=ALU.mult, op1=ALU.mult))

    def prenorm_ap(li, tb, xn_ap, xnB, tmpA, tmpAB):
        blk = slice(tb * BW, (tb + 1) * BW)
        for c in range(8):
            op("act", [xB[c][tb]], [tmpAB[c]],
               lambda e, c=c: e.activation(out=tmpA[:, c, :], in_=X[:, c, blk], func=AF.Square))
        rstd_from_sq(tmpA, tmpAB, 8, 1.0 / D)
        for c in range(8):
            op("dve", [xB[c][tb], rstdB, gB], [xnB[c]],
               lambda e, c=c: e.scalar_tensor_tensor(out=xn_ap(c), in0=X[:, c, blk],
                                                     scalar=gpre[:, li * 8 + c:li * 8 + c + 1], in1=rstd[:],
                                                     op0=ALU.mult, op1=ALU.mult))

    def proj_ap(w, wB, col0, M, rhs_ap, rhsB, nk):
        b = k.nextps()
        for kc in range(nk):
            op("pe", [wB, rhsB[kc]], [psB[b]],
               lambda e, kc=kc: e.matmul(ps[b][0:M, :], lhsT=w[:, kc, col0:col0 + M], rhs=rhs_ap(kc),
                                         start=(kc == 0), stop=(kc == nk - 1)),
               inc=(kc == nk - 1))
        return b

    def proj_fm(w, wB, col0, rhs, rhsB, nk, M=P):
        b = k.nextps()
        for kc in range(nk):
            op("pe", [wB, rhsB[kc]], [psB[b]],
               lambda e, kc=kc: e.matmul(ps[b][0:M, :], lhsT=w[:, kc, col0:col0 + M], rhs=rhs[:, kc, :],
                                         start=(kc == 0), stop=(kc == nk - 1)),
               inc=(kc == nk - 1))
        return b

    def outproj(li, tb, G, GB, wout, woutB, ybuf, ybufB, tmpA, tmpAB):
        blk = slice(tb * BW, (tb + 1) * BW)
        for m in range(8):
            b = proj_fm(wout, woutB, m * P, G, GB, 8)
            op("act", [psB[b]], [ybufB[m]], lambda e, m=m, b=b: e.activation(out=ybuf[:, m, :], in_=ps[b][:, :], func=AF.Copy))
            op("act", [psB[b]], [tmpAB[m]], lambda e, m=m, b=b: e.activation(out=tmpA[:, m, :], in_=ps[b][:, :], func=AF.Square))
        rstd_from_sq(tmpA, tmpAB, 8, 1.0 / D)
        for m in range(8):
            op("dve", [ybufB[m], rstdB, gB], [ybufB[m]],
               lambda e, m=m: e.scalar_tensor_tensor(out=ybuf[:, m, :], in0=ybuf[:, m, :],
                                                     scalar=gpost[:, li * 8 + m:li * 8 + m + 1], in1=rstd[:],
                                                     op0=ALU.mult, op1=ALU.mult))
            op("pool", [ybufB[m], xB[m][tb]], [xB[m][tb]],
               lambda e, m=m: e.tensor_tensor(out=X[:, m, blk], in0=X[:, m, blk], in1=ybuf[:, m, :], op=ALU.add))

    def layer_sgu(li):
        st = ExitStack()
        with st:
            w_in, w_inB = load_w(st, "d_win", dw["d_w_in"], 3072)
            w_out, w_outB = load_w(st, "d_wout", dw["d_w_out"], D)
            wsf = sb(st, "wsf", [P, 16, P], F32)
            wsfB = Buf()
            tril = sb(st, "tril", [P, P], F32)
            trilB = Buf()
            wsm = sb(st, "wsm", [P, 16, P], BF16)
            wsmB = Buf()
            wsT = sb(st, "wsT", [P, 16, P], BF16)
            wsTB = bufs(16)
            bsT = sb(st, "bsT", [P, 8, P], F32)
            bsTB = Buf()
            lng = sb(st, "lng", [P, D], F32)
            lnb = sb(st, "lnb", [P, D], F32)
            lnB = Buf()
            R1 = sb(st, "R1", [P, 8, BW], F32)
            R1B = bufs(8)
            R1bf = R1[:].bitcast(BF16)
            tmpA = sb(st, "tmpA", [P, 8, BW], BF16)
            tmpAB = bufs(8)
            gu = sb(st, "gu", [P, 8, BW], BF16)
            guB = bufs(8)
            sz = sb(st, "sz", [P, 8, BW], BF16)
            szB = bufs(8)
            vtmp = sb(st, "vtmp", [P, D], F32)
            vtmpB = Buf()
            stt = sb(st, "stt", [P, 2, 6], F32)
            mv = sb(st, "mv", [P, 2], F32)
            rs1 = sb(st, "rs1", [P, 1], F32)
            sttB = Buf()
            tmpS = sb(st, "tmpS", [P, BW], F32)
            tmpSB = Buf()

            def xn_ap(c):
                return R1bf[:, c // 2, (c % 2) * BW:(c % 2) * BW + BW]

            def vln_ap(ch, c0, c1):
                return R1bf[:, 4 + ch, c0:c1]

            xnB = [R1B[c // 2] for c in range(8)]

            dma("sp", wsf[:], dw["d_ws"], [], [wsfB])
            dma("sp", tril[:], dw["d_tril"], [], [trilB])
            for h in range(2):
                src = dw["d_bs"].rearrange("(gp h) t -> h gp t", h=2)[h]
                dma("sp", bsT[h * 64:(h + 1) * 64, :, :], src.unsqueeze(0).to_broadcast([64, 8, P]), [], [bsTB])
            dma("sp", lng[:], dw["d_lng"].to_broadcast([P, D]), [], [lnB])
            dma("sp", lnb[:], dw["d_lnb"].to_broadcast([P, D]), [], [lnB])
            op("dve", [wsfB, trilB], [wsmB],
               lambda e: e.tensor_tensor(out=wsm[:], in0=wsf[:], in1=tril[:].unsqueeze(1).to_broadcast([P, 16, P]), op=ALU.mult))
            for g in range(16):
                b = k.nextps()
                op("pe", [wsmB, identB], [psB[b]],
                   lambda e, g=g, b=b: e.matmul(ps[b][:, 0:P], lhsT=wsm[:, g, :], rhs=ident[:, :], start=True, stop=True))
                op("act", [psB[b]], [wsTB[g]], lambda e, g=g, b=b: e.activation(out=wsT[:, g, :], in_=ps[b][:, 0:P], func=AF.Copy))

            for tb in range(NBLK):
                blk = slice(tb * BW, (tb + 1) * BW)
                for c in range(8):
                    op("act", [xB[c][tb]], [tmpAB[c]],
                       lambda e, c=c: e.activation(out=tmpA[:, c, :], in_=X[:, c, blk], func=AF.Square))
                rstd_from_sq(tmpA, tmpAB, 8, 1.0 / D)
                for c in range(8):
                    op("dve", [xB[c][tb], rstdB, gB], [xnB[c]],
                       lambda e, c=c: e.scalar_tensor_tensor(out=xn_ap(c), in0=X[:, c, blk],
                                                             scalar=gpre[:, li * 8 + c:li * 8 + c + 1], in1=rstd[:],
                                                             op0=ALU.mult, op1=ALU.mult))
                for ch in range(4):
                    for half in range(2):
                        b = k.nextps()
                        for kc in range(8):
                            op("pe", [w_inB, xnB[kc]], [psB[b]],
                               lambda e, kc=kc, b=b: e.matmul(ps[b][:, :], lhsT=xn_ap(kc)[:, ch * P:(ch + 1) * P],
                                                              rhs=w_in[:, kc, D + half * BW:D + (half + 1) * BW],
                                                              start=(kc == 0), stop=(kc == 7)),
                               inc=(kc == 7))
                        op("act", [psB[b]], [vtmpB],
                           lambda e, b=b, half=half: e.activation(out=vtmp[:, half * BW:(half + 1) * BW], in_=ps[b][:, :],
                                                                  func=AF.Gelu_apprx_tanh))
                    for half in range(2):
                        op("dve", [vtmpB], [sttB], lambda e, half=half: e.bn_stats(out=stt[:, half, :], in_=vtmp[:, half * BW:(half + 1) * BW]))
                    op("dve", [sttB], [sttB], lambda e: e.bn_aggr(out=mv[:], in_=stt[:].rearrange("p a b -> p (a b)")))
                    op("act", [sttB], [sttB],
                       lambda e: e.activation(out=rs1[:], in_=mv[:, 1:2], func=AF.Sqrt, scale=1.0, bias=epsc[:, 0:1]))
                    op("dve", [sttB], [sttB], lambda e: e.reciprocal(out=rs1[:], in_=rs1[:]))
                    op("dve", [vtmpB, sttB], [vtmpB],
                       lambda e: e.tensor_scalar(out=vtmp[:], in0=vtmp[:], scalar1=mv[:, 0:1], scalar2=rs1[:, 0:1],
                                                 op0=ALU.subtract, op1=ALU.mult))
                    op("pool", [vtmpB, lnB], [vtmpB], lambda e: e.tensor_tensor(out=vtmp[:], in0=vtmp[:], in1=lng[:], op=ALU.mult))
                    op("pool", [vtmpB, lnB], [R1B[4 + ch]],
                       lambda e, ch=ch: e.tensor_tensor(out=vln_ap(ch, 0, D), in0=vtmp[:], in1=lnb[:], op=ALU.add))
                for m in range(8):
                    b = k.nextps()
                    for kc in range(8):
                        op("pe", [w_inB, xnB[kc]], [psB[b]],
                           lambda e, kc=kc, b=b, m=m: e.matmul(ps[b][:, :], lhsT=w_in[:, kc, m * P:(m + 1) * P], rhs=xn_ap(kc),
                                                               start=(kc == 0), stop=(kc == 7)), inc=(kc == 7))
                    op("act", [psB[b]], [guB[m]], lambda e, b=b, m=m: e.activation(out=gu[:, m, :], in_=ps[b][:, :], func=AF.Gelu_apprx_tanh))
                for m in range(8):
                    b = k.nextps()
                    for kc in range(8):
                        op("pe", [w_inB, xnB[kc]], [psB[b]],
                           lambda e, kc=kc, b=b, m=m: e.matmul(ps[b][:, :], lhsT=w_in[:, kc, 2 * D + m * P:2 * D + (m + 1) * P], rhs=xn_ap(kc),
                                                               start=(kc == 0), stop=(kc == 7)), inc=(kc == 7))
                    op("act", [psB[b]], [szB[m]], lambda e, b=b, m=m: e.activation(out=sz[:, m, :], in_=ps[b][:, :], func=AF.Silu))
                for m in range(8):
                    op("pool", [guB[m], szB[m]], [guB[m]], lambda e, m=m: e.tensor_tensor(out=gu[:, m, :], in0=gu[:, m, :], in1=sz[:, m, :], op=ALU.mult))
                for gp in range(8):
                    b = k.nextps()
                    n = 0
                    for ch in range(4):
                        for h in range(2):
                            g = 2 * gp + h
                            n += 1
                            op("pe", [R1B[4 + ch], wsTB[g]], [psB[b]],
                               lambda e, ch=ch, h=h, g=g, b=b: e.matmul(ps[b][h * 64:(h + 1) * 64, ch * P:(ch + 1) * P],
                                                                        lhsT=vln_ap(ch, g * 64, (g + 1) * 64), rhs=wsT[:, g, :],
                                                                        start=True, stop=True),
                               inc=(n == 8))
                    op("dve", [psB[b], bsTB], [tmpSB],
                       lambda e, b=b, gp=gp: e.tensor_tensor(out=tmpS[:].rearrange("p (a t) -> p a t", a=4),
                                                             in0=ps[b][:, :].rearrange("p (a t) -> p a t", a=4),
                                                             in1=bsT[:, gp, :].unsqueeze(1).to_broadcast([P, 4, P]), op=ALU.add))
                    op("dve", [tmpSB, guB[gp]], [guB[gp]],
                       lambda e, gp=gp: e.tensor_tensor(out=gu[:, gp, :], in0=tmpS[:], in1=gu[:, gp, :], op=ALU.mult))
                outproj(li, tb, gu, guB, w_out, w_outB, R1, R1B, tmpA, tmpAB)
            k.barrier()

    def layer_swa(li):
        st = ExitStack()
        with st:
            NC_IN = 2432
            w_in, w_inB = load_w(st, "b_win", dw["b_w_in"], NC_IN)
            w_out, w_outB = load_w(st, "b_wout", dw["b_w_out"], D)
            KT2 = sb(st, "KT2", [P, 2, L], BF16)
            KTB = bufs(2, NBLK)
            VA = [[sb(st, "VA%d%d" % (kv, par), [P, 16, P], BF16) for par in range(2)] for kv in range(2)]
            VAB = bufs(2, 2, 16)
            biasT = sb(st, "biasT", [P, 16, 256], F32)
            biasB = Buf()
            esink = sb(st, "esink", [P, 16], F32)
            esinkB = Buf()
            R1 = sb(st, "R1", [P, 8, BW], F32)
            R1B = bufs(8)
            R1bf = R1[:].bitcast(BF16)
            tmpA = sb(st, "tmpA", [P, 8, BW], BF16)
            tmpAB = bufs(8)
            sz = sb(st, "sz", [P, 8, BW], BF16)
            szB = bufs(8)
            tmpP = sb(st, "tmpP", [P, 2, 256], F32)
            tmpPB = bufs(2)
            PT = sb(st, "PT", [P, 2, 256], BF16)
            PTB = bufs(2)
            rsb = sb(st, "rsb", [P, BW], F32)
            rsbB = bufs(2)
            tG = sb(st, "tG", [P, BW], F32)
            tGB = bufs(2)

            def xn_ap(c):
                return R1bf[:, c // 2, (c % 2) * BW:(c % 2) * BW + BW]
            xnB = [R1B[c // 2] for c in range(8)]

            def qt_ap(j, p0, p1, c0, c1):
                return R1bf[p0:p1, 4 + j // 2, (j % 2) * BW + c0:(j % 2) * BW + c1]
            qtB = [R1B[4 + j // 2] for j in range(8)]

            import os
            SK = os.environ.get("SKIP", "")
            if "a" not in SK:
                dma("sp", biasT[:], dw["b_biasT"], [], [biasB])
            if "b" not in SK:
                dma("sp", esink[:], dw["b_sinks"].to_broadcast([P, 16]), [], [esinkB])
                op("act", [esinkB], [esinkB], lambda e: e.activation(out=esink[:], in_=esink[:], func=AF.Exp))
            for kv in range(2):
                for par in range(2):
                    c0 = 64 if par == 0 else 0
                    if "c" not in SK:
                        op("pool", [], sum([[VAB[kv][par][t]] for t in range(16)], []),
                           lambda e, kv=kv, par=par, c0=c0: e.memset(VA[kv][par][:, :, c0:c0 + 64], 1.0))

            for tb in range(NBLK):
                blk = slice(tb * BW, (tb + 1) * BW)
                prenorm_ap(li, tb, xn_ap, xnB, tmpA, tmpAB)
                STG = int(os.environ.get("STG", "9"))
                for j in range(8 if STG >= 1 else 0):
                    b = proj_ap(w_in, w_inB, j * P, P, xn_ap, xnB, 8)
                    op("act", [psB[b]], [qtB[j]], lambda e, b=b, j=j: e.activation(out=qt_ap(j, 0, P, 0, BW), in_=ps[b][:, :], func=AF.Copy))
                for kv in range(2 if STG >= 2 else 0):
                    b = proj_ap(w_in, w_inB, D + kv * P, P, xn_ap, xnB, 8)
                    op("act", [psB[b]], [KTB[kv][tb]], lambda e, b=b, kv=kv: e.activation(out=KT2[:, kv, blk], in_=ps[b][:, :], func=AF.Copy))
                for ch in range(4 if STG >= 3 else 0):
                    tt = tb * 4 + ch
                    b = k.nextps()
                    for kc in range(8):
                        op("pe", [w_inB, xnB[kc]], [psB[b]],
                           lambda e, kc=kc, b=b, ch=ch: e.matmul(ps[b][:, 0:P], lhsT=xn_ap(kc)[:, ch * P:(ch + 1) * P],
                                                                 rhs=w_in[:, kc, D + 256:D + 384], start=(kc == 0), stop=(kc == 7)),
                           inc=(kc == 7))
                    for kv in range(2):
                        op("act", [psB[b]], [VAB[kv][0][tt]],
                           lambda e, b=b, kv=kv, tt=tt: e.activation(out=VA[kv][0][:, tt, 0:64], in_=ps[b][:, kv * 64:(kv + 1) * 64], func=AF.Copy))
                        op("act", [psB[b]], [VAB[kv][1][tt]],
                           lambda e, b=b, kv=kv, tt=tt: e.activation(out=VA[kv][1][:, tt, 64:128], in_=ps[b][:, kv * 64:(kv + 1) * 64], func=AF.Copy))
                for m in range(8):
                    b = proj_ap(w_in, w_inB, D + 384 + m * P, P, xn_ap, xnB, 8)
                    op("act", [psB[b]], [szB[m]], lambda e, b=b, m=m: e.activation(out=sz[:, m, :], in_=ps[b][:, :], func=AF.Silu))
                import os
                for h in range(int(os.environ.get('SWA_NH', '16'))):
                    kv, par, j = h // 8, h % 2, h // 2
                    base = par * 64
                    oth = 64 - base
                    bo = 6 + (h % 2)
                    for nbl in range(4):
                        nb = tb * 4 + nbl
                        b = k.nextps()
                        pi = nb % 2
                        c_lo = 0 if nb > 0 else P
                        rhs_q = qt_ap(j, base, base + 64, nbl * P, (nbl + 1) * P)
                        if nb > 0:
                            tbp = (nb - 1) // 4
                            op("pe", [KTB[kv][tbp], qtB[j]], [psB[b]],
                               lambda e, b=b, kv=kv, nb=nb, rhs_q=rhs_q, base=base: e.matmul(
                                   ps[b][:, 0:P], lhsT=KT2[base:base + 64, kv, (nb - 1) * P:nb * P], rhs=rhs_q, start=True, stop=True),
                               inc=False)
                        op("pe", [KTB[kv][tb], qtB[j]], [psB[b]],
                           lambda e, b=b, kv=kv, nb=nb, rhs_q=rhs_q, base=base: e.matmul(
                               ps[b][:, P:2 * P], lhsT=KT2[base:base + 64, kv, nb * P:(nb + 1) * P], rhs=rhs_q, start=True, stop=True))
                        op("dve", [psB[b], biasB], [tmpPB[pi]],
                           lambda e, b=b, h=h, pi=pi, c_lo=c_lo: e.scalar_tensor_tensor(
                               out=tmpP[:, pi, c_lo:256], in0=ps[b][:, c_lo:256], scalar=0.125, in1=biasT[:, h, c_lo:256],
                               op0=ALU.mult, op1=ALU.add))
                        op("act", [tmpPB[pi]], [PTB[pi]],
                           lambda e, pi=pi, c_lo=c_lo: e.activation(out=PT[:, pi, c_lo:256], in_=tmpP[:, pi, c_lo:256], func=AF.Exp))
                        if nb > 0:
                            op("pe", [PTB[pi], VAB[kv][par][nb - 1]], [psB[bo]],
                               lambda e, bo=bo, kv=kv, par=par, nb=nb, pi=pi, nbl=nbl: e.matmul(
                                   ps[bo][:, nbl * P:(nbl + 1) * P], lhsT=VA[kv][par][:, nb - 1, :], rhs=PT[:, pi, 0:P], start=True, stop=False),
                               inc=False)
                        op("pe", [PTB[pi], VAB[kv][par][nb]], [psB[bo]],
                           lambda e, bo=bo, kv=kv, par=par, nb=nb, pi=pi, nbl=nbl: e.matmul(
                               ps[bo][:, nbl * P:(nbl + 1) * P], lhsT=VA[kv][par][:, nb, :], rhs=PT[:, pi, P:2 * P], start=(nb == 0), stop=True))
                    op("dve", [psB[bo], esinkB], [rsbB[par]],
                       lambda e, bo=bo, h=h, base=base, oth=oth: e.tensor_scalar(
                           out=rsb[base:base + 64, :], in0=ps[bo][oth:oth + 64, :], scalar1=esink[oth:oth + 64, h:h + 1], scalar2=None, op0=ALU.add))
                    op("dve", [rsbB[par]], [rsbB[par]], lambda e, base=base: e.reciprocal(out=rsb[base:base + 64, :], in_=rsb[base:base + 64, :]))
                    op("dve", [psB[bo], rsbB[par]], [tGB[par]],
                       lambda e, bo=bo, base=base: e.tensor_tensor(out=tG[base:base + 64, :], in0=ps[bo][base:base + 64, :], in1=rsb[base:base + 64, :], op=ALU.mult))
                    op("pool", [tGB[par], szB[j]], [szB[j]],
                       lambda e, base=base, j=j: e.tensor_tensor(out=sz[base:base + 64, j, :], in0=tG[base:base + 64, :], in1=sz[base:base + 64, j, :], op=ALU.mult))
                outproj(li, tb, sz, szB, w_out, w_outB, R1, R1B, tmpA, tmpAB)
            k.barrier()

    def layer_mla(li):
        SC = 96.0 ** -0.5
        st = ExitStack()
        with st:
            GT = sb(st, "GT", [P, 8, L], BF16)
            GTB = bufs(8, NBLK)
            sA = ExitStack()
            sA.__enter__()
            CQN = sb(sA, "CQN", [P, 6, L], BF16)
            CQNB = bufs(6, NBLK)
            CKVN = sb(sA, "CKVN", [P, 2, L], BF16)
            CKVNB = bufs(2, NBLK)
            KR = sb(sA, "KR", [32, L], BF16)
            KRB = bufs(NBLK)
            ROPE = sb(sA, "ROPE", [64, L], F32)
            ropeB = Buf()
            gq = sb(sA, "gq", [P, 8], F32)
            gqB = Buf()
            dma("sp", ROPE[:], dw["c_rope"], [], [ropeB])
            dma("sp", gq[:], dw["c_gqkv"], [], [gqB])
            tR = sb(sA, "tR", [32, 2, BW], F32)
            tRB = bufs(2)
            s1 = ExitStack()
            with s1:
                w1, w1B = load_w(s1, "s_c_w1", dw["c_w1"], 1088)
                R1 = sb(s1, "R1", [P, 8, BW], F32)
                R1B = bufs(8)
                xnt = sb(s1, "xnt", [P, 8, BW], BF16)
                xnB = bufs(8)
                tmpA = sb(s1, "tmpA", [P, 8, BW], BF16)
                tmpAB = bufs(8)
                xn_ap = lambda c: xnt[:, c, :]
                for tb in range(NBLK):
                    blk = slice(tb * BW, (tb + 1) * BW)
                    prenorm_ap(li, tb, xn_ap, xnB, tmpA, tmpAB)
                    for m in range(8):
                        b = proj_ap(w1, w1B, m * P, P, xn_ap, xnB, 8)
                        op("act", [psB[b]], [R1B[m]], lambda e, b=b, m=m: e.activation(out=R1[:, m, :], in_=ps[b][:, :], func=AF.Copy))
                        op("act", [psB[b]], [tmpAB[m]], lambda e, b=b, m=m: e.activation(out=tmpA[:, m, :], in_=ps[b][:, :], func=AF.Square))
                    rstd_from_sq(tmpA, tmpAB, 6, 1.0 / 768, 0)
                    for m in range(6):
                        op("dve", [R1B[m], rstdB, gqB], [CQNB[m][tb]],
                           lambda e, m=m: e.scalar_tensor_tensor(out=CQN[:, m, blk], in0=R1[:, m, :], scalar=gq[:, m:m + 1], in1=rstd[:],
                                                                 op0=ALU.mult, op1=ALU.mult))
                    rstd_from_sq(tmpA, tmpAB, 2, 1.0 / 256, 6)
                    for m in range(2):
                        op("dve", [R1B[6 + m], rstdB, gqB], [CKVNB[m][tb]],
                           lambda e, m=m: e.scalar_tensor_tensor(out=CKVN[:, m, blk], in0=R1[:, 6 + m, :], scalar=gq[:, 6 + m:7 + m], in1=rstd[:],
                                                                 op0=ALU.mult, op1=ALU.mult))
                    b = proj_ap(w1, w1B, D, 64, xn_ap, xnB, 8)
                    op("dve", [psB[b], ropeB], [tRB[0]],
                       lambda e, b=b: e.tensor_tensor(out=tR[:, 0, :], in0=ps[b][32:64, :], in1=ROPE[32:64, blk], op=ALU.mult))
                    op("dve", [psB[b], ropeB], [tRB[1]],
                       lambda e, b=b: e.tensor_tensor(out=tR[:, 1, :], in0=ps[b][0:32, :], in1=ROPE[0:32, blk], op=ALU.mult))
                    op("pool", [tRB[0], tRB[1]], [KRB[tb]],
                       lambda e: e.tensor_tensor(out=KR[:, blk], in0=tR[:, 0, :], in1=tR[:, 1, :], op=ALU.add))
                k.barrier()
            s2 = ExitStack()
            with s2:
                wuq, wuqB = load_w(s2, "s_c_wuq", dw["c_wuq"], 2048)
                wukv, wukvB = load_w(s2, "s_c_wukv", dw["c_wukv"], 2048)
                KTh = [sb(s2, "KTh%d" % i, [P, L], BF16) for i in range(2)]
                KThB = bufs(2, NBLK)
                VAh = [sb(s2, "VAh%d" % i, [P, 16, P], BF16) for i in range(2)]
                VAhB = bufs(2, 4)
                QT = [sb(s2, "QT%d" % i, [P, BW], BF16) for i in range(2)]
                QTB = bufs(2)
                PT = [sb(s2, "PT%d" % i, [P, BW], BF16) for i in range(3)]
                PTB = bufs(3)
                QF = sb(s2, "QF", [64, BW], F32)
                QFB = Buf()
                maskT = sb(s2, "maskT", [P, P], BF16)
                maskB = Buf()
                rsb = sb(s2, "rsb", [P, BW], F32)
                rsbB = bufs(2)
                dma("pool", maskT[:], dw["c_maskT"], [], [maskB])
                for i in range(2):
                    op("pool", [], KThB[i], lambda e, i=i: e.memset(KTh[i][32:64, :], 0.0))
                    c0 = 64 if i == 0 else 0
                    op("pool", [], VAhB[i], lambda e, i=i, c0=c0: e.memset(VAh[i][:, :, c0:c0 + 64], 1.0))
                pti = 0
                qti = 0
                boi = 0
                import os
                for h in range(int(os.environ.get("MLA_NH", "16"))):
                    par, j = h % 2, h // 2
                    base = par * 64
                    oth = 64 - base
                    vc0 = base
                    for tb in range(NBLK):
                        blk = slice(tb * BW, (tb + 1) * BW)
                        b = k.nextps()
                        for kc in range(2):
                            op("pe", [wukvB, CKVNB[kc][tb]], [psB[b]],
                               lambda e, kc=kc, b=b, blk=blk: e.matmul(ps[b][64:128, :], lhsT=wukv[:, kc, h * P:h * P + 64], rhs=CKVN[:, kc, blk],
                                                                      start=(kc == 0), stop=(kc == 1)), inc=(kc == 1))
                        op("act", [psB[b]], [KThB[par][tb]],
                           lambda e, b=b, blk=blk: e.activation(out=KTh[par][64:128, blk], in_=ps[b][64:128, :], func=AF.Copy))
                        op("pool", [KRB[tb]], [KThB[par][tb]], lambda e, blk=blk: e.tensor_scalar(out=KTh[par][0:32, blk], in0=KR[:, blk], scalar1=1.0, scalar2=None, op0=ALU.mult))
                        b = k.nextps()
                        for i4 in range(4):
                            tt = tb * 4 + i4
                            for kc in range(2):
                                op("pe", [wukvB, CKVNB[kc][tb]], [psB[b]],
                                   lambda e, kc=kc, b=b, tt=tt, i4=i4: e.matmul(ps[b][:, i4 * 64:(i4 + 1) * 64], lhsT=CKVN[:, kc, tt * P:(tt + 1) * P],
                                                                                rhs=wukv[:, kc, h * P + 64:h * P + 128], start=(kc == 0), stop=(kc == 1)),
                                   inc=(kc == 1 and i4 == 3))
                        op("act", [psB[b]], [VAhB[par][tb]],
                           lambda e, b=b, tb=tb: e.activation(out=VAh[par][:, tb * 4:(tb + 1) * 4, vc0:vc0 + 64],
                                                              in_=ps[b][:, 0:256].rearrange("p (a c) -> p a c", a=4), func=AF.Copy))
                    MSTG = int(os.environ.get("MLA_STG", "9"))
                    for qb in range(NBLK if MSTG >= 2 else 0):
                        qblk = slice(qb * BW, (qb + 1) * BW)
                        qi = qti % 2
                        qti += 1
                        b = k.nextps()
                        for kc in range(6):
                            op("pe", [wuqB, CQNB[kc][qb]], [psB[b]],
                               lambda e, kc=kc, b=b, qblk=qblk: e.matmul(ps[b][:, :], lhsT=wuq[:, kc, h * P:(h + 1) * P], rhs=CQN[:, kc, qblk],
                                                                         start=(kc == 0), stop=(kc == 5)), inc=(kc == 5))
                        op("act", [psB[b]], [QTB[qi]], lambda e, b=b, qi=qi: e.activation(out=QT[qi][:, :], in_=ps[b][:, :], func=AF.Copy))
                        op("act", [psB[b]], [QFB], lambda e, b=b: e.activation(out=QF[:, :], in_=ps[b][0:64, :], func=AF.Copy))
                        op("dve", [QFB, ropeB], [tRB[0]],
                           lambda e, qblk=qblk: e.tensor_tensor(out=tR[:, 0, :], in0=QF[32:64, :], in1=ROPE[32:64, qblk], op=ALU.mult))
                        op("dve", [QFB, ropeB], [tRB[1]],
                           lambda e, qblk=qblk: e.tensor_tensor(out=tR[:, 1, :], in0=QF[0:32, :], in1=ROPE[0:32, qblk], op=ALU.mult))
                        op("pool", [tRB[0], tRB[1]], [QTB[qi]],
                           lambda e, qi=qi: e.tensor_tensor(out=QT[qi][0:32, :], in0=tR[:, 0, :], in1=tR[:, 1, :], op=ALU.add))
                        bo = 6 + (boi % 2)
                        boi += 1
                        nkc = 4 * qb + 4
                        if MSTG < 3:
                            continue
                        for kc in range(nkc):
                            q_lo = max(0, kc - 4 * qb) * P
                            pi = pti % 3
                            pti += 1
                            b = k.nextps()
                            op("pe", [KThB[par][kc // 4], QTB[qi]], [psB[b]],
                               lambda e, b=b, kc=kc, q_lo=q_lo, qi=qi: e.matmul(ps[b][:, q_lo:BW], lhsT=KTh[par][:, kc * P:(kc + 1) * P],
                                                                                rhs=QT[qi][:, q_lo:BW], start=True, stop=True))
                            op("act", [psB[b]], [PTB[pi]],
                               lambda e, b=b, pi=pi, q_lo=q_lo: e.activation(out=PT[pi][:, q_lo:BW], in_=ps[b][:, q_lo:BW], func=AF.Exp, scale=SC))
                            if kc >= 4 * qb:
                                op("pool", [PTB[pi], maskB], [PTB[pi]],
                                   lambda e, pi=pi, q_lo=q_lo: e.tensor_tensor(out=PT[pi][:, q_lo:q_lo + P], in0=PT[pi][:, q_lo:q_lo + P],
                                                                               in1=maskT[:, :], op=ALU.mult))
                            op("pe", [PTB[pi], VAhB[par][kc // 4]], [psB[bo]],
                               lambda e, bo=bo, kc=kc, pi=pi, q_lo=q_lo: e.matmul(ps[bo][:, q_lo:BW], lhsT=VAh[par][:, kc, :], rhs=PT[pi][:, q_lo:BW],
                                                                                  start=(kc == 0), stop=(kc == nkc - 1)))
                        op("dve", [psB[bo]], [rsbB[par]],
                           lambda e, bo=bo: e.reciprocal(out=rsb[base:base + 64, :], in_=ps[bo][oth:oth + 64, :]))
                        op("dve", [psB[bo], rsbB[par]], [GTB[j][qb]],
                           lambda e, bo=bo, qblk=qblk: e.tensor_tensor(out=GT[base:base + 64, j, qblk], in0=ps[bo][base:base + 64, :],
                                                                       in1=rsb[base:base + 64, :], op=ALU.mult))
                k.barrier()
            sA.close()
            s3 = ExitStack()
            with s3:
                wz, wzB = load_w(s3, "s_c_wz", dw["c_wz"], D)
                w_out, w_outB = load_w(s3, "s_c_wout", dw["c_w_out"], D)
                R1 = sb(s3, "R1", [P, 8, BW], F32)
                R1B = bufs(8)
                xnt = sb(s3, "xnt", [P, 8, BW], BF16)
                xnB = bufs(8)
                tmpA = sb(s3, "tmpA", [P, 8, BW], BF16)
                tmpAB = bufs(8)
                sz = sb(s3, "sz", [P, 8, BW], BF16)
                szB = bufs(8)
                xn_ap = lambda c: xnt[:, c, :]
                for tb in range(NBLK):
                    blk = slice(tb * BW, (tb + 1) * BW)
                    prenorm_ap(li, tb, xn_ap, xnB, tmpA, tmpAB)
                    for m in range(8):
                        b = proj_ap(wz, wzB, m * P, P, xn_ap, xnB, 8)
                        op("act", [psB[b]], [szB[m]], lambda e, b=b, m=m: e.activation(out=sz[:, m, :], in_=ps[b][:, :], func=AF.Silu))
                        op("pool", [szB[m], GTB[m][tb]], [szB[m]],
                           lambda e, m=m, blk=blk: e.tensor_tensor(out=sz[:, m, :], in0=sz[:, m, :], in1=GT[:, m, blk], op=ALU.mult))
                    outproj(li, tb, sz, szB, w_out, w_outB, R1, R1B, tmpA, tmpAB)
                k.barrier()

    def layer_s5(li):
        TT = ALU
        st = ExitStack()
        with st:
            UY = sb(st, "UY", [P, 8, L], BF16)
            UYB = bufs(8, NBLK)
            dT = sb(st, "dT", [P, 16], F32)
            dTB = Buf()
            dma("sp", dT[:], dw["a_dg"], [], [dTB])
            s1 = ExitStack()
            with s1:
                wu, wuB = load_w(s1, "a_wu", dw["a_wu"], D)
                xnt = sb(s1, "xnt", [P, 8, BW], BF16)
                xnB = bufs(8)
                tmpA = sb(s1, "tmpA", [P, 8, BW], BF16)
                tmpAB = bufs(8)
                xn_ap = lambda c: xnt[:, c, :]
                for tb in range(NBLK):
                    blk = slice(tb * BW, (tb + 1) * BW)
                    prenorm_ap(li, tb, xn_ap, xnB, tmpA, tmpAB)
                    for m in range(8):
                        b = proj_ap(wu, wuB, m * P, P, xn_ap, xnB, 8)
                        op("act", [psB[b]], [UYB[m][tb]], lambda e, b=b, m=m, blk=blk: e.activation(out=UY[:, m, blk], in_=ps[b][:, :], func=AF.Copy))
                k.barrier()
            s2 = ExitStack()
            with s2:
                NJ = 32
                BrL, BrLB = sb(s2, "BrL", [P, NJ, P], BF16), Buf()
                BiL, BiLB = sb(s2, "BiL", [P, NJ, P], BF16), Buf()
                CrP, CrPB = sb(s2, "CrP", [P, NJ, P], BF16), Buf()
                CiP, CiPB = sb(s2, "CiP", [P, NJ, P], BF16), Buf()
                for t_, B_, nm in ((BrL, BrLB, "a_brl"), (BiL, BiLB, "a_bil"), (CrP, CrPB, "a_crp"), (CiP, CiPB, "a_cip")):
                    for q4 in range(4):
                        dma("pool", t_[:, q4 * 8:(q4 + 1) * 8, :], dw[nm][:, q4 * 8:(q4 + 1) * 8, :], [], [B_])
                lam = sb(s2, "lam", [P, 3, NJ], F32)
                prepB = Buf()
                dma("sp", lam[:], dw["a_lam"], [], [prepB])
                W = {}
                for nm in ("dt", "lrdt", "th", "mag", "t", "t2", "c", "s", "q", "c2", "s2", "cs", "ar", "ai", "den", "nr", "fr", "fi",
                           "u1", "u2", "ir", "ii", "pr", "pi", "A128r", "nA128i", "A128i"):
                    W[nm] = sb(s2, "w_" + nm, [P, NJ], F32)
                lr, li_, ldt = lam[:, 0, :], lam[:, 1, :], lam[:, 2, :]

                def tt(o, a, b_, o_):
                    op("dve", [prepB], [prepB], lambda e: e.tensor_tensor(out=o, in0=a, in1=b_, op=o_))

                def ts(o, a, s1_, s2_, o1, o2=None):
                    if o2 is None:
                        op("dve", [prepB], [prepB], lambda e: e.tensor_scalar(out=o, in0=a, scalar1=s1_, scalar2=None, op0=o1))
                    else:
                        op("dve", [prepB], [prepB], lambda e: e.tensor_scalar(out=o, in0=a, scalar1=s1_, scalar2=s2_, op0=o1, op1=o2))

                def stt(o, a, sc, b_, o1, o2):
                    op("dve", [prepB], [prepB], lambda e: e.scalar_tensor_tensor(out=o, in0=a, scalar=sc, in1=b_, op0=o1, op1=o2))

                def csq(cr, ci):
                    tt(W["c2"][:], cr, cr, TT.mult)
                    tt(W["s2"][:], ci, ci, TT.mult)
                    tt(W["cs"][:], cr, ci, TT.mult)
                    tt(cr, W["c2"][:], W["s2"][:], TT.subtract)
                    ts(ci, W["cs"][:], 2.0, None, TT.mult)

                op("act", [prepB], [prepB], lambda e: e.activation(out=W["dt"][:], in_=ldt, func=AF.Exp))
                tt(W["lrdt"][:], lr, W["dt"][:], TT.mult)
                tt(W["th"][:], li_, W["dt"][:], TT.mult)
                op("act", [prepB], [prepB], lambda e: e.activation(out=W["mag"][:], in_=W["lrdt"][:], func=AF.Exp))
                ts(W["t"][:], W["th"][:], 1.0 / 64, None, TT.mult)
                tt(W["t2"][:], W["t"][:], W["t"][:], TT.mult)
                ts(W["q"][:], W["t2"][:], -1.0 / 720, None, TT.mult)
                stt(W["q"][:], W["q"][:], 1.0 / 24, W["t2"][:], TT.add, TT.mult)
                stt(W["q"][:], W["q"][:], -0.5, W["t2"][:], TT.add, TT.mult)
                ts(W["c"][:], W["q"][:], 1.0, None, TT.add)
                ts(W["q"][:], W["t2"][:], -1.0 / 5040, None, TT.mult)
                stt(W["q"][:], W["q"][:], 1.0 / 120, W["t2"][:], TT.add, TT.mult)
                stt(W["q"][:], W["q"][:], -1.0 / 6, W["t2"][:], TT.add, TT.mult)
                stt(W["s"][:], W["q"][:], 1.0, W["t"][:], TT.add, TT.mult)
                for _ in range(6):
                    csq(W["c"][:], W["s"][:])
                tt(W["ar"][:], W["mag"][:], W["c"][:], TT.mult)
                tt(W["ai"][:], W["mag"][:], W["s"][:], TT.mult)
                tt(W["den"][:], lr, lr, TT.mult)
                tt(W["u1"][:], li_, li_, TT.mult)
                tt(W["den"][:], W["den"][:], W["u1"][:], TT.add)
                op("dve", [prepB], [prepB], lambda e: e.reciprocal(out=W["den"][:], in_=W["den"][:]))
                ts(W["nr"][:], W["ar"][:], -1.0, None, TT.add)
                tt(W["u1"][:], W["nr"][:], lr, TT.mult)
                tt(W["u2"][:], W["ai"][:], li_, TT.mult)
                tt(W["u1"][:], W["u1"][:], W["u2"][:], TT.add)
                tt(W["fr"][:], W["u1"][:], W["den"][:], TT.mult)
                tt(W["u1"][:], W["ai"][:], lr, TT.mult)
                tt(W["u2"][:], W["nr"][:], li_, TT.mult)
                tt(W["u1"][:], W["u1"][:], W["u2"][:], TT.subtract)
                tt(W["fi"][:], W["u1"][:], W["den"][:], TT.mult)
                tt(W["u1"][:], W["ar"][:], W["ar"][:], TT.mult)
                tt(W["u2"][:], W["ai"][:], W["ai"][:], TT.mult)
                tt(W["u1"][:], W["u1"][:], W["u2"][:], TT.add)
                op("dve", [prepB], [prepB], lambda e: e.reciprocal(out=W["u1"][:], in_=W["u1"][:]))
                tt(W["ir"][:], W["ar"][:], W["u1"][:], TT.mult)
                tt(W["ii"][:], W["ai"][:], W["u1"][:], TT.mult)
                ts(W["ii"][:], W["ii"][:], -1.0, None, TT.mult)
                TPr = sb(s2, "TPr", [P, NJ, P], BF16)
                TPi = sb(s2, "TPi", [P, NJ, P], BF16)
                TNr = sb(s2, "TNr", [P, NJ, P], BF16)
                TNi = sb(s2, "TNi", [P, NJ, P], BF16)
                sT = ExitStack()
                sT.__enter__()
                m1 = sb(sT, "m1t", [P, NJ, 64], F32)
                m2 = sb(sT, "m2t", [P, NJ, 64], F32)
                op("dve", [prepB], [prepB], lambda e: e.memset(TPr[:, :, 0:1], 1.0))
                op("dve", [prepB], [prepB], lambda e: e.memset(TPi[:, :, 0:1], 0.0))
                ts(TNr[:, :, 0:1], W["fr"][:].unsqueeze(2), 1.0, None, TT.mult)
                ts(TNi[:, :, 0:1], W["fi"][:].unsqueeze(2), 1.0, None, TT.mult)
                for (Tr_, Ti_, pr0, pi0) in ((TPr, TPi, "ar", "ai"), (TNr, TNi, "ir", "ii")):
                    ts(W["pr"][:], W[pr0][:], 1.0, None, TT.mult)
                    ts(W["pi"][:], W[pi0][:], 1.0, None, TT.mult)
                    for kk in range(7):
                        n = 1 << kk
                        pr_b = W["pr"][:].unsqueeze(2).to_broadcast([P, NJ, n])
                        pi_b = W["pi"][:].unsqueeze(2).to_broadcast([P, NJ, n])
                        lo_r, lo_i = Tr_[:, :, 0:n], Ti_[:, :, 0:n]
                        tt(m1[:, :, 0:n], lo_r, pr_b, TT.mult)
                        tt(m2[:, :, 0:n], lo_i, pi_b, TT.mult)
                        tt(Tr_[:, :, n:2 * n], m1[:, :, 0:n], m2[:, :, 0:n], TT.subtract)
                        tt(m1[:, :, 0:n], lo_r, pi_b, TT.mult)
                        tt(m2[:, :, 0:n], lo_i, pr_b, TT.mult)
                        tt(Ti_[:, :, n:2 * n], m1[:, :, 0:n], m2[:, :, 0:n], TT.add)
                        csq(W["pr"][:], W["pi"][:])
                    if pr0 == "ar":
                        ts(W["A128r"][:], W["pr"][:], 1.0, None, TT.mult)
                        ts(W["nA128i"][:], W["pi"][:], -1.0, None, TT.mult)
                        ts(W["A128i"][:], W["pi"][:], 1.0, None, TT.mult)
                k.barrier()
                sT.close()
                car = sb(s2, "car", [P, 2, NJ], F32)
                carB = bufs(NJ)
                op("dve", [prepB], carB, lambda e: e.memset(car[:], 0.0))
                onesf = sb(s2, "onesf", [P, P], F32)
                op("dve", [prepB], [prepB], lambda e: e.memset(onesf[:], 1.0))
                vr = sb(s2, "vr", [P, BW], F32)
                vi = sb(s2, "vi", [P, BW], F32)
                vB = Buf()
                kr_ = sb(s2, "kr", [P, BW], F32)
                ki_ = sb(s2, "ki", [P, BW], F32)
                kB = Buf()
                n1 = sb(s2, "n1", [P, BW], F32)
                n2 = sb(s2, "n2", [P, BW], F32)
                nB = Buf()
                Xr = sb(s2, "Xr", [P, BW], F32)
                Xi = sb(s2, "Xi", [P, BW], F32)
                XB = Buf()
                xr_b = sb(s2, "xrb", [P, BW], BF16)
                xi_b = sb(s2, "xib", [P, BW], BF16)
                xbB = Buf()
                e1 = sb(s2, "e1", [P, 1], F32)
                ytmp = sb(s2, "ytmp", [P, BW], F32)
                yB = Buf()

                def v4(ap):
                    return ap.rearrange("p (a t) -> p a t", a=4)

                def tb4(T_, j):
                    return T_[:, j, :].unsqueeze(1).to_broadcast([P, 4, P])

                for chc in range(8):
                    for qt in range(NBLK):
                        blk = slice(qt * BW, (qt + 1) * BW)
                        bo = 6 + ((chc * NBLK + qt) % 2)
                        for jl in range(4):
                            j = chc * 4 + jl
                            ba = k.nextps()
                            op("pe", [BrLB, UYB[chc][qt]], [psB[ba]],
                               lambda e, ba=ba, j=j, blk=blk: e.matmul(ps[ba][:, :], lhsT=BrL[:, j, :], rhs=UY[:, chc, blk], start=True, stop=True))
                            bb = k.nextps()
                            op("pe", [BiLB, UYB[chc][qt]], [psB[bb]],
                               lambda e, bb=bb, j=j, blk=blk: e.matmul(ps[bb][:, :], lhsT=BiL[:, j, :], rhs=UY[:, chc, blk], start=True, stop=True))
                            op("act", [psB[ba]], [vB], lambda e, ba=ba: e.activation(out=vr[:], in_=ps[ba][:, :], func=AF.Copy))
                            op("act", [psB[bb]], [vB], lambda e, bb=bb: e.activation(out=vi[:], in_=ps[bb][:, :], func=AF.Copy))
                            op("dve", [vB, prepB], [nB], lambda e, j=j: e.tensor_tensor(out=v4(n1[:]), in0=v4(vr[:]), in1=tb4(TNr, j), op=TT.mult))
                            op("pool", [vB, prepB], [nB], lambda e, j=j: e.tensor_tensor(out=v4(n2[:]), in0=v4(vi[:]), in1=tb4(TNi, j), op=TT.mult))
                            op("dve", [nB], [kB], lambda e: e.tensor_tensor(out=kr_[:], in0=n1[:], in1=n2[:], op=TT.subtract))
                            op("dve", [vB, prepB, kB], [nB], lambda e, j=j: e.tensor_tensor(out=v4(n1[:]), in0=v4(vi[:]), in1=tb4(TNr, j), op=TT.mult))
                            op("pool", [vB, prepB, kB], [nB], lambda e, j=j: e.tensor_tensor(out=v4(n2[:]), in0=v4(vr[:]), in1=tb4(TNi, j), op=TT.mult))
                            op("dve", [nB], [kB], lambda e: e.tensor_tensor(out=ki_[:], in0=n1[:], in1=n2[:], op=TT.add))
                            for c4 in range(4):
                                cs_ = slice(c4 * P, (c4 + 1) * P)
                                op("dve", [kB, carB[j]], [XB],
                                   lambda e, cs_=cs_, j=j: e.tensor_tensor_scan(out=Xr[:, cs_], data0=onesf[:, :], data1=kr_[:, cs_],
                                                                                initial=car[:, 0, j:j + 1], op0=TT.mult, op1=TT.add))
                                op("dve", [kB, carB[j]], [XB],
                                   lambda e, cs_=cs_, j=j: e.tensor_tensor_scan(out=Xi[:, cs_], data0=onesf[:, :], data1=ki_[:, cs_],
                                                                                initial=car[:, 1, j:j + 1], op0=TT.mult, op1=TT.add))
                                last = c4 * P + P - 1
                                op("dve", [XB, prepB], [carB[j]],
                                   lambda e, last=last, j=j: e.tensor_scalar(out=e1[:], in0=Xr[:, last:last + 1], scalar1=W["A128r"][:, j:j + 1], scalar2=None, op0=TT.mult))
                                op("dve", [XB, prepB], [carB[j]],
                                   lambda e, last=last, j=j: e.scalar_tensor_tensor(out=car[:, 0, j:j + 1], in0=Xi[:, last:last + 1], scalar=W["nA128i"][:, j:j + 1],
                                                                                    in1=e1[:], op0=TT.mult, op1=TT.add))
                                op("dve", [XB, prepB], [carB[j]],
                                   lambda e, last=last, j=j: e.tensor_scalar(out=e1[:], in0=Xi[:, last:last + 1], scalar1=W["A128r"][:, j:j + 1], scalar2=None, op0=TT.mult))
                                op("dve", [XB, prepB], [carB[j]],
                                   lambda e, last=last, j=j: e.scalar_tensor_tensor(out=car[:, 1, j:j + 1], in0=Xr[:, last:last + 1], scalar=W["A128i"][:, j:j + 1],
                                                                                    in1=e1[:], op0=TT.mult, op1=TT.add))
                            op("dve", [XB, prepB], [nB], lambda e, j=j: e.tensor_tensor(out=v4(n1[:]), in0=v4(Xr[:]), in1=tb4(TPr, j), op=TT.mult))
                            op("pool", [XB, prepB], [nB], lambda e, j=j: e.tensor_tensor(out=v4(n2[:]), in0=v4(Xi[:]), in1=tb4(TPi, j), op=TT.mult))
                            op("dve", [nB], [xbB], lambda e: e.tensor_tensor(out=xr_b[:], in0=n1[:], in1=n2[:], op=TT.subtract))
                            op("dve", [XB, prepB, xbB], [nB], lambda e, j=j: e.tensor_tensor(out=v4(n1[:]), in0=v4(Xr[:]), in1=tb4(TPi, j), op=TT.mult))
                            op("pool", [XB, prepB, xbB], [nB], lambda e, j=j: e.tensor_tensor(out=v4(n2[:]), in0=v4(Xi[:]), in1=tb4(TPr, j), op=TT.mult))
                            op("dve", [nB], [xbB], lambda e: e.scalar_tensor_tensor(out=xi_b[:], in0=n1[:], scalar=-1.0, in1=n2[:], op0=TT.mult, op1=TT.subtract))
                            op("pe", [xbB, CrPB], [psB[bo]],
                               lambda e, bo=bo, j=j, jl=jl: e.matmul(ps[bo][:, :], lhsT=CrP[:, j, :], rhs=xr_b[:], start=(jl == 0), stop=False), inc=False)
                            op("pe", [xbB, CiPB], [psB[bo]],
                               lambda e, bo=bo, j=j, jl=jl: e.matmul(ps[bo][:, :], lhsT=CiP[:, j, :], rhs=xi_b[:], start=False, stop=(jl == 3)))
                        op("act", [psB[bo]], [yB], lambda e, bo=bo: e.activation(out=ytmp[:], in_=ps[bo][:, :], func=AF.Copy))
                        op("dve", [yB, UYB[chc][qt], dTB], [yB],
                           lambda e, blk=blk: e.scalar_tensor_tensor(out=ytmp[:], in0=UY[:, chc, blk], scalar=dT[:, chc:chc + 1], in1=ytmp[:],
                                                                     op0=TT.mult, op1=TT.add))
                        op("act", [yB], [UYB[chc][qt]], lambda e, blk=blk: e.activation(out=UY[:, chc, blk], in_=ytmp[:], func=AF.Gelu_apprx_tanh))
                k.barrier()
            s3 = ExitStack()
            with s3:
                wz, wzB = load_w(s3, "a_wzs", dw["a_wz"], D)
                wg, wgB = load_w(s3, "a_wgs", dw["a_wglu"], D)
                w_out, w_outB = load_w(s3, "a_wouts", dw["a_w_out"], D)
                R1 = sb(s3, "R1", [P, 8, BW], F32)
                R1B = bufs(8)
                xnt = sb(s3, "xnt", [P, 8, BW], BF16)
                xnB = bufs(8)
                tmpA = sb(s3, "tmpA", [P, 8, BW], BF16)
                tmpAB = bufs(8)
                sz = sb(s3, "sz", [P, 8, BW], BF16)
                szB = bufs(8)
                sg = sb(s3, "sg", [P, 2, BW], BF16)
                sgB = bufs(2)
                xn_ap = lambda c: xnt[:, c, :]
                for tb in range(NBLK):
                    blk = slice(tb * BW, (tb + 1) * BW)
                    prenorm_ap(li, tb, xn_ap, xnB, tmpA, tmpAB)
                    uy_ap = lambda c, blk=blk: UY[:, c, blk]
                    uyB_t = [UYB[c][tb] for c in range(8)]
                    for m in range(8):
                        b = proj_ap(wz, wzB, m * P, P, xn_ap, xnB, 8)
                        op("act", [psB[b]], [szB[m]], lambda e, b=b, m=m: e.activation(out=sz[:, m, :], in_=ps[b][:, :], func=AF.Silu))
                        b = proj_ap(wg, wgB, m * P, P, uy_ap, uyB_t, 8)
                        op("act", [psB[b], dTB], [sgB[m % 2]],
                           lambda e, b=b, m=m: e.activation(out=sg[:, m % 2, :], in_=ps[b][:, :], func=AF.Sigmoid, bias=dT[:, 8 + m:9 + m], scale=1.0))
                        op("pool", [szB[m], uyB_t[m]], [szB[m]],
                           lambda e, m=m, blk=blk: e.tensor_tensor(out=sz[:, m, :], in0=sz[:, m, :], in1=UY[:, m, blk], op=ALU.mult))
                        op("pool", [szB[m], sgB[m % 2]], [szB[m]],
                           lambda e, m=m: e.tensor_tensor(out=sz[:, m, :], in0=sz[:, m, :], in1=sg[:, m % 2, :], op=ALU.mult))
                    outproj(li, tb, sz, szB, w_out, w_outB, R1, R1B, tmpA, tmpAB)
                k.barrier()

    for li in layers:
        if li == 3:
            layer_sgu(li)
        elif li == 1:
            layer_swa(li)
        elif li == 2:
            layer_mla(li)
        elif li == 0:
            layer_s5(li)

    toks = []
    for c in range(8):
        for tb in range(NBLK):
            toks.append(dma("sp", outT_d[c * P:(c + 1) * P, tb * BW:(tb + 1) * BW], X[:, c, tb * BW:(tb + 1) * BW], [xB[c][tb]], []))
    k._wait("sp", toks)
    k.barrier()


def host_inputs(inp, layers):
    f = lambda a: np.ascontiguousarray(np.asarray(a, dtype=np.float32))
    common = {}
    common["gpre"] = f(np.asarray(inp["pre_norm"]).reshape(4, 8, P).transpose(2, 0, 1).reshape(P, 32))
    common["gpost"] = f(np.asarray(inp["post_norm"]).reshape(4, 8, P).transpose(2, 0, 1).reshape(P, 32))
    common["ident"] = np.eye(P, dtype=np.float32)
    if 3 in layers:
        common["d_w_in"] = f(inp["d_w_in"][0])
        common["d_w_out"] = f(inp["d_w_out"][0])
        common["d_ws"] = f(np.asarray(inp["d_w_s"][0]).transpose(1, 0, 2))
        common["d_tril"] = np.tril(np.ones((P, P), dtype=np.float32))
        common["d_bs"] = f(inp["d_b_s"][0])
        common["d_lng"] = f(inp["d_ln_g"])
        common["d_lnb"] = f(inp["d_ln_b"])
    if 1 in layers:
        w = np.asarray(inp["b_w_in"][0], dtype=np.float32)
        q, kk, v, z = w[:, :1024], w[:, 1024:1152], w[:, 1152:1280], w[:, 1280:]
        common["b_w_in"] = f(np.concatenate([q, kk[:, :64], kk[:, :64], kk[:, 64:], kk[:, 64:], v, z], axis=1))
        common["b_w_out"] = f(inp["b_w_out"][0])
        common["b_sinks"] = f(inp["b_sinks"])
        def bucket(d):
            if d < 16:
                return d
            v_ = 16 + int(math.log(max(d, 1) / 16.0) / math.log(128 / 16.0) * 16)
            return min(v_, 31)
        rb = np.asarray(inp["rel_bias"], dtype=np.float32)
        bt = np.full((P, 16, 256), -1e30, dtype=np.float32)
        for kj in range(P):
            for qi in range(P):
                d_prev = qi + P - kj
                if d_prev < P:
                    bt[kj, :, qi] = rb[bucket(d_prev), :]
                d_cur = qi - kj
                if d_cur >= 0:
                    bt[kj, :, P + qi] = rb[bucket(d_cur), :]
        common["b_biasT"] = bt
    if 2 in layers:
        w = np.asarray(inp["c_w_in"][0], dtype=np.float32)
        kr = w[:, 1024:1056]
        common["c_w1"] = f(np.concatenate([w[:, :1024], kr, kr[:, 16:], kr[:, :16]], axis=1))
        common["c_wz"] = f(w[:, 1056:])
        uq = np.asarray(inp["c_w_uq"][0], dtype=np.float32).reshape(768, 16, 96)
        nope, rp = uq[:, :, :64], uq[:, :, 64:]
        common["c_wuq"] = f(np.concatenate([rp, rp[:, :, 16:], rp[:, :, :16], nope], axis=2).reshape(768, 2048))
        common["c_wukv"] = f(inp["c_w_ukv"][0])
        common["c_w_out"] = f(inp["c_w_out"][0])
        inv = (np.float32(10000.0) ** (-np.arange(0, 32, 2, dtype=np.float32) / np.float32(32))).astype(np.float32)
        ang = (np.arange(L, dtype=np.float32)[:, None] * inv[None, :]).astype(np.float32)
        cos, sin = np.cos(ang).astype(np.float32).T, np.sin(ang).astype(np.float32).T
        common["c_rope"] = f(np.concatenate([cos, cos, -sin, sin], axis=0))
        g = np.concatenate([np.asarray(inp["c_q_norm"][0]), np.asarray(inp["c_kv_norm"][0])]).astype(np.float32)
        common["c_gqkv"] = f(g.reshape(8, P).T)
        common["c_maskT"] = np.triu(np.ones((P, P), dtype=np.float32))
    if 0 in layers:
        w = np.asarray(inp["a_w_in"][0], dtype=np.float32)
        common["a_wu"] = f(w[:, :1024])
        common["a_wz"] = f(w[:, 1024:])
        common["a_wglu"] = f(inp["a_w_glu"][0])
        common["a_w_out"] = f(inp["a_w_out"][0])
        dg = np.concatenate([np.asarray(inp["a_d"][0]).reshape(8, P).T, np.asarray(inp["a_b_glu"][0]).reshape(8, P).T], axis=1)
        common["a_dg"] = f(dg)
        lam = np.stack([np.asarray(inp["a_lam_re"][0]).reshape(32, P).T, np.asarray(inp["a_lam_im"][0]).reshape(32, P).T,
                        np.repeat(np.asarray(inp["a_log_dt"][0]), 64).reshape(32, P).T], axis=1)
        common["a_lam"] = f(lam)
        brl = np.zeros((P, 32, P), np.float32); bil = np.zeros((P, 32, P), np.float32)
        crp = np.zeros((P, 32, P), np.float32); cip = np.zeros((P, 32, P), np.float32)
        b_re, b_im = np.asarray(inp["a_b_re"][0]), np.asarray(inp["a_b_im"][0])
        c_re, c_im = np.asarray(inp["a_c_re"][0]), np.asarray(inp["a_c_im"][0])
        for j in range(32):
            for gl in range(2):
                g = 2 * j + gl
                r0 = 32 * (j % 4) + gl * 16
                brl[r0:r0 + 16, j, gl * 64:(gl + 1) * 64] = b_re[g].T
                bil[r0:r0 + 16, j, gl * 64:(gl + 1) * 64] = b_im[g].T
                crp[gl * 64:(gl + 1) * 64, j, r0:r0 + 16] = c_re[g].T
                cip[gl * 64:(gl + 1) * 64, j, r0:r0 + 16] = c_im[g].T
        common["a_brl"], common["a_bil"], common["a_crp"], common["a_cip"] = brl, bil, crp, cip
    return common


def run(inp, layers=(0, 1, 2, 3), cores=8, trace=False):
    nc = bass.Bass("TRN2", target_bir_lowering=False)
    build(nc, list(layers))
    common = host_inputs(inp, list(layers))
    x = np.asarray(inp["x"], dtype=np.float32)
    in_maps = []
    for b in range(cores):
        m = dict(common)
        m["xT"] = np.ascontiguousarray(x[b].T)
        in_maps.append(m)
    res = run_bass_kernel_spmd(nc, in_maps, core_ids=list(range(cores)), trace=trace)
    out = np.stack([np.ascontiguousarray(r["outT"].T) for r in res.results], axis=0)
    return out.astype(np.float32), res


def kernel(**inputs):
    out, _ = run(inputs)
    return out
```

```python
import math
import numpy as np
from contextlib import ExitStack
import concourse.bass as bass
import concourse.mybir as mybir
from concourse.bass_utils import run_bass_kernel_spmd

F32 = mybir.dt.float32
BF16 = mybir.dt.bfloat16
ALU = mybir.AluOpType
AF = mybir.ActivationFunctionType

P = 128
L = 2048
D = 1024
NBLK = 4
BW = 512
EPS = 1e-6
SELF_SYNC = True
NDS = 12


class Buf:
    __slots__ = ("w", "r")

    def __init__(self):
        self.w = None
        self.r = {}


def bufs(*shape):
    if len(shape) == 1:
        return [Buf() for _ in range(shape[0])]
    return [bufs(*shape[1:]) for _ in range(shape[0])]


class KB:
    def __init__(self, nc, es):
        self.nc = nc
        self.E = dict(pe=nc.tensor, act=nc.scalar, dve=nc.vector, pool=nc.gpsimd, sp=nc.sync)
        self.sem = {e: es.enter_context(nc.semaphore("s_" + e)) for e in ("pe", "act", "dve", "pool")}
        self.cnt = {e: 0 for e in self.sem}
        self.pend = {e: False for e in self.sem}
        self.dsem = {q: [[es.enter_context(nc.semaphore("d_%s%d" % (q, i))), 0] for i in range(NDS)]
                     for q in ("sp", "pool")}
        self.dcnt = {"sp": 0, "pool": 0}
        self.seen = {e: {} for e in self.E}
        self.ps = []
        self.psB = []
        self.psrot = 0
        self.nrot = 6

    def _semh(self, key):
        if isinstance(key, str):
            return self.sem[key]
        return self.dsem[key[0]][key[1]][0]

    def _wait(self, e, toks):
        need = {}
        for key, v in toks:
            if need.get(key, 0) < v:
                need[key] = v
        for key, v in need.items():
            if key == e and (e == "pe" or not SELF_SYNC):
                continue
            if self.seen[e].get(key, 0) >= v:
                continue
            self.E[e].wait_ge(self._semh(key), v)
            self.seen[e][key] = v

    def _deps(self, reads, writes):
        toks = []
        for b in reads:
            if b.w is not None:
                toks.append(b.w)
        for b in writes:
            if b.w is not None:
                toks.append(b.w)
            toks.extend(b.r.items())
        return toks

    def _mark(self, tok, reads, writes):
        key, v = tok
        for b in reads:
            if b.r.get(key, 0) < v:
                b.r[key] = v
        for b in writes:
            b.w = tok
            b.r = {}

    def op(self, e, reads, writes, fn, inc=True):
        self._wait(e, self._deps(reads, writes))
        ins = fn(self.E[e])
        if inc:
            self.cnt[e] += 1
            ins.then_inc(self.sem[e], 1)
            self.pend[e] = False
            tok = (e, self.cnt[e])
        else:
            self.pend[e] = True
            tok = (e, self.cnt[e] + 1)
        self._mark(tok, reads, writes)
        return ins

    def dma(self, q, out, in_, reads, writes):
        self._wait(q, self._deps(reads, writes))
        i = self.dcnt[q] % NDS
        self.dcnt[q] += 1
        ent = self.dsem[q][i]
        key = (q, i)
        if ent[1] > 0:
            self._wait(q, [(key, 16 * ent[1])])
        ins = self.E[q].dma_start(out=out, in_=in_)
        ins.then_inc(ent[0], 16)
        ent[1] += 1
        tok = (key, 16 * ent[1])
        self._mark(tok, reads, writes)
        return tok

    def all_tokens(self):
        toks = [(e, c) for e, c in self.cnt.items() if c > 0]
        for q in self.dsem:
            for i, ent in enumerate(self.dsem[q]):
                if ent[1] > 0:
                    toks.append(((q, i), 16 * ent[1]))
        return toks

    def barrier(self):
        for e in self.sem:
            assert not self.pend[e], e
        toks = self.all_tokens()
        for e in self.E:
            self._wait(e, [t for t in toks if t[0] != e])

    def nextps(self):
        b = self.psrot
        self.psrot = (self.psrot + 1) % self.nrot
        return b


def build(nc, layers):
    es = ExitStack()
    with es:
        _build(nc, es, layers)
    return nc


def _build(nc, es, layers):
    k = KB(nc, es)
    op, dma = k.op, k.dma

    def dram_in(name, shape, dt=F32):
        return nc.dram_tensor(name, list(shape), dt, kind="ExternalInput").ap()

    uid = [0]

    def sb(st, name, shape, dt):
        uid[0] += 1
        return st.enter_context(nc.sbuf_tensor("%s_%d" % (name, uid[0]), list(shape), dt))

    xT_d = dram_in("xT", [D, L])
    outT_d = nc.dram_tensor("outT", [D, L], F32, kind="ExternalOutput").ap()
    gpre_d = dram_in("gpre", [P, 32])
    gpost_d = dram_in("gpost", [P, 32])
    ident_d = dram_in("ident", [P, P])
    dw = {}
    if 3 in layers:
        dw["d_w_in"] = dram_in("d_w_in", [D, 3072])
        dw["d_w_out"] = dram_in("d_w_out", [D, D])
        dw["d_ws"] = dram_in("d_ws", [P, 16, P])
        dw["d_tril"] = dram_in("d_tril", [P, P])
        dw["d_bs"] = dram_in("d_bs", [16, P])
        dw["d_lng"] = dram_in("d_lng", [1, D])
        dw["d_lnb"] = dram_in("d_lnb", [1, D])

    if 1 in layers:
        dw["b_w_in"] = dram_in("b_w_in", [D, 2432])
        dw["b_w_out"] = dram_in("b_w_out", [D, D])
        dw["b_biasT"] = dram_in("b_biasT", [P, 16, 256])
        dw["b_sinks"] = dram_in("b_sinks", [1, 16])

    if 2 in layers:
        dw["c_w1"] = dram_in("c_w1", [D, 1088])
        dw["c_wz"] = dram_in("c_wz", [D, D])
        dw["c_wuq"] = dram_in("c_wuq", [768, 2048])
        dw["c_wukv"] = dram_in("c_wukv", [256, 2048])
        dw["c_w_out"] = dram_in("c_w_out", [D, D])
        dw["c_rope"] = dram_in("c_rope", [64, L])
        dw["c_gqkv"] = dram_in("c_gqkv", [P, 8])
        dw["c_maskT"] = dram_in("c_maskT", [P, P])

    if 0 in layers:
        dw["a_wu"] = dram_in("a_wu", [D, D])
        dw["a_wz"] = dram_in("a_wz", [D, D])
        dw["a_wglu"] = dram_in("a_wglu", [D, D])
        dw["a_w_out"] = dram_in("a_w_out", [D, D])
        dw["a_dg"] = dram_in("a_dg", [P, 16])
        dw["a_lam"] = dram_in("a_lam", [P, 3, 32])
        for nm in ("a_brl", "a_bil", "a_crp", "a_cip"):
            dw[nm] = dram_in(nm, [P, 32, P])

    X = sb(es, "X", [P, 8, L], F32)
    xB = bufs(8, NBLK)
    ones = sb(es, "ones", [P, P], BF16)
    onesB = Buf()
    ident = sb(es, "identb", [P, P], BF16)
    identB = Buf()
    gpre = sb(es, "gpre_s", [P, 32], F32)
    gpost = sb(es, "gpost_s", [P, 32], F32)
    gB = Buf()
    rstd = sb(es, "rstd", [P, BW], F32)
    rstdB = Buf()
    for i in range(8):
        k.ps.append(es.enter_context(nc.psum_tensor("ps%d" % i, [P, BW], F32)))
        k.psB.append(Buf())
    ps, psB = k.ps, k.psB

    for c in range(8):
        for tb in range(NBLK):
            dma("sp", X[:, c, tb * BW:(tb + 1) * BW], xT_d[c * P:(c + 1) * P, tb * BW:(tb + 1) * BW], [], [xB[c][tb]])
    dma("sp", gpre[:], gpre_d, [], [gB])
    dma("sp", gpost[:], gpost_d, [], [gB])
    dma("pool", ident[:], ident_d, [], [identB])
    op("dve", [], [onesB], lambda e: e.memset(ones[:], 1.0))
    epsc = sb(es, "epsc", [P, 1], F32)
    op("dve", [], [onesB], lambda e: e.memset(epsc[:], EPS))

    def load_w(st, name, d_ap, ncols, q="pool"):
        K = d_ap.shape[0]
        kc_n = K // P
        t = sb(st, name, [P, kc_n, ncols], BF16)
        B = Buf()
        for kc in range(kc_n):
            for c0 in range(0, ncols, 2048):
                c1 = min(ncols, c0 + 2048)
                dma(q, t[:, kc, c0:c1], d_ap[kc * P:(kc + 1) * P, c0:c1], [], [B])
        return t, B

    def rstd_from_sq(sq, sqB, n, scale, c0=0):
        b = k.nextps()
        for c in range(n):
            op("pe", [sqB[c0 + c], onesB], [psB[b]],
               lambda e, c=c: e.matmul(ps[b][:, :], lhsT=ones[:, :], rhs=sq[:, c0 + c, :], start=(c == 0), stop=(c == n - 1)),
               inc=(c == n - 1))
        op("act", [psB[b]], [rstdB],
           lambda e: e.activation(out=rstd[:], in_=ps[b][:, :], func=AF.Sqrt, scale=scale, bias=epsc[:, 0:1]))
        op("dve", [rstdB], [rstdB], lambda e: e.reciprocal(out=rstd[:], in_=rstd[:]))

    def prenorm(li, tb, xn, xnB, tmpA, tmpAB):
        blk = slice(tb * BW, (tb + 1) * BW)
        for c in range(8):
            op("act", [xB[c][tb]], [tmpAB[c]],
               lambda e, c=c: e.activation(out=tmpA[:, c, :], in_=X[:, c, blk], func=AF.Square))
        rstd_from_sq(tmpA, tmpAB, 8, 1.0 / D)
        for c in range(8):
            op("dve", [xB[c][tb], rstdB, gB], [xnB[c]],
               lambda e, c=c: e.scalar_tensor_tensor(out=xn[:, c, :], in0=X[:, c, blk],
                                                     scalar=gpre[:, li * 8 + c:li * 8 + c + 1], in1=rstd[:],
                                                     op0=ALU.mult, op1=ALU.mult))

    def prenorm_ap(li, tb, xn_ap, xnB, tmpA, tmpAB):
        blk = slice(tb * BW, (tb + 1) * BW)
        for c in range(8):
            op("act", [xB[c][tb]], [tmpAB[c]],
               lambda e, c=c: e.activation(out=tmpA[:, c, :], in_=X[:, c, blk], func=AF.Square))
        rstd_from_sq(tmpA, tmpAB, 8, 1.0 / D)
        for c in range(8):
            op("dve", [xB[c][tb], rstdB, gB], [xnB[c]],
               lambda e, c=c: e.scalar_tensor_tensor(out=xn_ap(c), in0=X[:, c, blk],
                                                     scalar=gpre[:, li * 8 + c:li * 8 + c + 1], in1=rstd[:],
                                                     op0=ALU.mult, op1=ALU.mult))

    def proj_ap(w, wB, col0, M, rhs_ap, rhsB, nk):
        b = k.nextps()
        for kc in range(nk):
            op("pe", [wB, rhsB[kc]], [psB[b]],
               lambda e, kc=kc: e.matmul(ps[b][0:M, :], lhsT=w[:, kc, col0:col0 + M], rhs=rhs_ap(kc),
                                         start=(kc == 0), stop=(kc == nk - 1)),
               inc=(kc == nk - 1))
        return b

    def proj_fm(w, wB, col0, rhs, rhsB, nk, M=P):
        b = k.nextps()
        for kc in range(nk):
            op("pe", [wB, rhsB[kc]], [psB[b]],
               lambda e, kc=kc: e.matmul(ps[b][0:M, :], lhsT=w[:, kc, col0:col0 + M], rhs=rhs[:, kc, :],
                                         start=(kc == 0), stop=(kc == nk - 1)),
               inc=(kc == nk - 1))
        return b

    def outproj(li, tb, G, GB, wout, woutB, ybuf, ybufB, tmpA, tmpAB):
        blk = slice(tb * BW, (tb + 1) * BW)
        for m in range(8):
            b = proj_fm(wout, woutB, m * P, G, GB, 8)
            op("act", [psB[b]], [ybufB[m]], lambda e, m=m, b=b: e.activation(out=ybuf[:, m, :], in_=ps[b][:, :], func=AF.Copy))
            op("act", [psB[b]], [tmpAB[m]], lambda e, m=m, b=b: e.activation(out=tmpA[:, m, :], in_=ps[b][:, :], func=AF.Square))
        rstd_from_sq(tmpA, tmpAB, 8, 1.0 / D)
        for m in range(8):
            op("dve", [ybufB[m], rstdB, gB], [ybufB[m]],
               lambda e, m=m: e.scalar_tensor_tensor(out=ybuf[:, m, :], in0=ybuf[:, m, :],
                                                     scalar=gpost[:, li * 8 + m:li * 8 + m + 1], in1=rstd[:],
                                                     op0=ALU.mult, op1=ALU.mult))
            op("pool", [ybufB[m], xB[m][tb]], [xB[m][tb]],
               lambda e, m=m: e.tensor_tensor(out=X[:, m, blk], in0=X[:, m, blk], in1=ybuf[:, m, :], op=ALU.add))

    def layer_sgu(li):
        st = ExitStack()
        with st:
            w_in, w_inB = load_w(st, "d_win", dw["d_w_in"], 3072)
            w_out, w_outB = load_w(st, "d_wout", dw["d_w_out"], D)
            wsf = sb(st, "wsf", [P, 16, P], F32)
            wsfB = Buf()
            tril = sb(st, "tril", [P, P], F32)
            trilB = Buf()
            wsm = sb(st, "wsm", [P, 16, P], BF16)
            wsmB = Buf()
            wsT = sb(st, "wsT", [P, 16, P], BF16)
            wsTB = bufs(16)
            bsT = sb(st, "bsT", [P, 8, P], F32)
            bsTB = Buf()
            lng = sb(st, "lng", [P, D], F32)
            lnb = sb(st, "lnb", [P, D], F32)
            lnB = Buf()
            R1 = sb(st, "R1", [P, 8, BW], F32)
            R1B = bufs(8)
            R1bf = R1[:].bitcast(BF16)
            tmpA = sb(st, "tmpA", [P, 8, BW], BF16)
            tmpAB = bufs(8)
            gu = sb(st, "gu", [P, 8, BW], BF16)
            guB = bufs(8)
            sz = sb(st, "sz", [P, 8, BW], BF16)
            szB = bufs(8)
            vtmp = sb(st, "vtmp", [P, D], F32)
            vtmpB = Buf()
            stt = sb(st, "stt", [P, 2, 6], F32)
            mv = sb(st, "mv", [P, 2], F32)
            rs1 = sb(st, "rs1", [P, 1], F32)
            sttB = Buf()
            tmpS = sb(st, "tmpS", [P, BW], F32)
            tmpSB = Buf()

            def xn_ap(c):
                return R1bf[:, c // 2, (c % 2) * BW:(c % 2) * BW + BW]

            def vln_ap(ch, c0, c1):
                return R1bf[:, 4 + ch, c0:c1]

            xnB = [R1B[c // 2] for c in range(8)]

            dma("sp", wsf[:], dw["d_ws"], [], [wsfB])
            dma("sp", tril[:], dw["d_tril"], [], [trilB])
            for h in range(2):
                src = dw["d_bs"].rearrange("(gp h) t -> h gp t", h=2)[h]
                dma("sp", bsT[h * 64:(h + 1) * 64, :, :], src.unsqueeze(0).to_broadcast([64, 8, P]), [], [bsTB])
            dma("sp", lng[:], dw["d_lng"].to_broadcast([P, D]), [], [lnB])
            dma("sp", lnb[:], dw["d_lnb"].to_broadcast([P, D]), [], [lnB])
            op("dve", [wsfB, trilB], [wsmB],
               lambda e: e.tensor_tensor(out=wsm[:], in0=wsf[:], in1=tril[:].unsqueeze(1).to_broadcast([P, 16, P]), op=ALU.mult))
            for g in range(16):
                b = k.nextps()
                op("pe", [wsmB, identB], [psB[b]],
                   lambda e, g=g, b=b: e.matmul(ps[b][:, 0:P], lhsT=wsm[:, g, :], rhs=ident[:, :], start=True, stop=True))
                op("act", [psB[b]], [wsTB[g]], lambda e, g=g, b=b: e.activation(out=wsT[:, g, :], in_=ps[b][:, 0:P], func=AF.Copy))

            for tb in range(NBLK):
                blk = slice(tb * BW, (tb + 1) * BW)
                for c in range(8):
                    op("act", [xB[c][tb]], [tmpAB[c]],
                       lambda e, c=c: e.activation(out=tmpA[:, c, :], in_=X[:, c, blk], func=AF.Square))
                rstd_from_sq(tmpA, tmpAB, 8, 1.0 / D)
                for c in range(8):
                    op("dve", [xB[c][tb], rstdB, gB], [xnB[c]],
                       lambda e, c=c: e.scalar_tensor_tensor(out=xn_ap(c), in0=X[:, c, blk],
                                                             scalar=gpre[:, li * 8 + c:li * 8 + c + 1], in1=rstd[:],
                                                             op0=ALU.mult, op1=ALU.mult))
                for ch in range(4):
                    for half in range(2):
                        b = k.nextps()
                        for kc in range(8):
                            op("pe", [w_inB, xnB[kc]], [psB[b]],
                               lambda e, kc=kc, b=b: e.matmul(ps[b][:, :], lhsT=xn_ap(kc)[:, ch * P:(ch + 1) * P],
                                                              rhs=w_in[:, kc, D + half * BW:D + (half + 1) * BW],
                                                              start=(kc == 0), stop=(kc == 7)),
                               inc=(kc == 7))
                        op("act", [psB[b]], [vtmpB],
                           lambda e, b=b, half=half: e.activation(out=vtmp[:, half * BW:(half + 1) * BW], in_=ps[b][:, :],
                                                                  func=AF.Gelu_apprx_tanh))
                    for half in range(2):
                        op("dve", [vtmpB], [sttB], lambda e, half=half: e.bn_stats(out=stt[:, half, :], in_=vtmp[:, half * BW:(half + 1) * BW]))
                    op("dve", [sttB], [sttB], lambda e: e.bn_aggr(out=mv[:], in_=stt[:].rearrange("p a b -> p (a b)")))
                    op("act", [sttB], [sttB],
                       lambda e: e.activation(out=rs1[:], in_=mv[:, 1:2], func=AF.Sqrt, scale=1.0, bias=epsc[:, 0:1]))
                    op("dve", [sttB], [sttB], lambda e: e.reciprocal(out=rs1[:], in_=rs1[:]))
                    op("dve", [vtmpB, sttB], [vtmpB],
                       lambda e: e.tensor_scalar(out=vtmp[:], in0=vtmp[:], scalar1=mv[:, 0:1], scalar2=rs1[:, 0:1],
                                                 op0=ALU.subtract, op1=ALU.mult))
                    op("pool", [vtmpB, lnB], [vtmpB], lambda e: e.tensor_tensor(out=vtmp[:], in0=vtmp[:], in1=lng[:], op=ALU.mult))
                    op("pool", [vtmpB, lnB], [R1B[4 + ch]],
                       lambda e, ch=ch: e.tensor_tensor(out=vln_ap(ch, 0, D), in0=vtmp[:], in1=lnb[:], op=ALU.add))
                for m in range(8):
                    b = k.nextps()
                    for kc in range(8):
                        op("pe", [w_inB, xnB[kc]], [psB[b]],
                           lambda e, kc=kc, b=b, m=m: e.matmul(ps[b][:, :], lhsT=w_in[:, kc, m * P:(m + 1) * P], rhs=xn_ap(kc),
                                                               start=(kc == 0), stop=(kc == 7)), inc=(kc == 7))
                    op("act", [psB[b]], [guB[m]], lambda e, b=b, m=m: e.activation(out=gu[:, m, :], in_=ps[b][:, :], func=AF.Gelu_apprx_tanh))
                for m in range(8):
                    b = k.nextps()
                    for kc in range(8):
                        op("pe", [w_inB, xnB[kc]], [psB[b]],
                           lambda e, kc=kc, b=b, m=m: e.matmul(ps[b][:, :], lhsT=w_in[:, kc, 2 * D + m * P:2 * D + (m + 1) * P], rhs=xn_ap(kc),
                                                               start=(kc == 0), stop=(kc == 7)), inc=(kc == 7))
                    op("act", [psB[b]], [szB[m]], lambda e, b=b, m=m: e.activation(out=sz[:, m, :], in_=ps[b][:, :], func=AF.Silu))
                for m in range(8):
                    op("pool", [guB[m], szB[m]], [guB[m]], lambda e, m=m: e.tensor_tensor(out=gu[:, m, :], in0=gu[:, m, :], in1=sz[:, m, :], op=ALU.mult))
                for gp in range(8):
                    b = k.nextps()
                    n = 0
                    for ch in range(4):
                        for h in range(2):
                            g = 2 * gp + h
                            n += 1
                            op("pe", [R1B[4 + ch], wsTB[g]], [psB[b]],
                               lambda e, ch=ch, h=h, g=g, b=b: e.matmul(ps[b][h * 64:(h + 1) * 64, ch * P:(ch + 1) * P],
                                                                        lhsT=vln_ap(ch, g * 64, (g + 1) * 64), rhs=wsT[:, g, :],
                                                                        start=True, stop=True),
                               inc=(n == 8))
                    op("dve", [psB[b], bsTB], [tmpSB],
                       lambda e, b=b, gp=gp: e.tensor_tensor(out=tmpS[:].rearrange("p (a t) -> p a t", a=4),
                                                             in0=ps[b][:, :].rearrange("p (a t) -> p a t", a=4),
                                                             in1=bsT[:, gp, :].unsqueeze(1).to_broadcast([P, 4, P]), op=ALU.add))
                    op("dve", [tmpSB, guB[gp]], [guB[gp]],
                       lambda e, gp=gp: e.tensor_tensor(out=gu[:, gp, :], in0=tmpS[:], in1=gu[:, gp, :], op=ALU.mult))
                outproj(li, tb, gu, guB, w_out, w_outB, R1, R1B, tmpA, tmpAB)
            k.barrier()

    def layer_swa(li):
        st = ExitStack()
        with st:
            NC_IN = 2432
            w_in, w_inB = load_w(st, "b_win", dw["b_w_in"], NC_IN)
            w_out, w_outB = load_w(st, "b_wout", dw["b_w_out"], D)
            KT2 = sb(st, "KT2", [P, 2, L], BF16)
            KTB = bufs(2, NBLK)
            VA = [[sb(st, "VA%d%d" % (kv, par), [P, 16, P], BF16) for par in range(2)] for kv in range(2)]
            VAB = bufs(2, 2, 16)
            biasT = sb(st, "biasT", [P, 16, 256], F32)
            biasB = Buf()
            esink = sb(st, "esink", [P, 16], F32)
            esinkB = Buf()
            R1 = sb(st, "R1", [P, 8, BW], F32)
            R1B = bufs(8)
            R1bf = R1[:].bitcast(BF16)
            tmpA = sb(st, "tmpA", [P, 8, BW], BF16)
            tmpAB = bufs(8)
            sz = sb(st, "sz", [P, 8, BW], BF16)
            szB = bufs(8)
            NPB = 3
            apc = [0]
            tmpP = sb(st, "tmpP", [P, NPB, 256], F32)
            tmpPB = bufs(NPB)
            PT = sb(st, "PT", [P, NPB, 256], BF16)
            PTB = bufs(NPB)
            rsb = sb(st, "rsb", [P, BW], F32)
            rsbB = bufs(2)
            tG = sb(st, "tG", [P, BW], F32)
            tGB = bufs(2)

            def xn_ap(c):
                return R1bf[:, c // 2, (c % 2) * BW:(c % 2) * BW + BW]
            xnB = [R1B[c // 2] for c in range(8)]

            def qt_ap(j, p0, p1, c0, c1):
                return R1bf[p0:p1, 4 + j // 2, (j % 2) * BW + c0:(j % 2) * BW + c1]
            qtB = [R1B[4 + j // 2] for j in range(8)]

            import os
            SK = os.environ.get("SKIP", "")
            if "a" not in SK:
                dma("sp", biasT[:], dw["b_biasT"], [], [biasB])
            if "b" not in SK:
                dma("sp", esink[:], dw["b_sinks"].to_broadcast([P, 16]), [], [esinkB])
                op("act", [esinkB], [esinkB], lambda e: e.activation(out=esink[:], in_=esink[:], func=AF.Exp))
            for kv in range(2):
                for par in range(2):
                    c0 = 64 if par == 0 else 0
                    if "c" not in SK:
                        op("pool", [], sum([[VAB[kv][par][t]] for t in range(16)], []),
                           lambda e, kv=kv, par=par, c0=c0: e.memset(VA[kv][par][:, :, c0:c0 + 64], 1.0))

            for tb in range(NBLK):
                blk = slice(tb * BW, (tb + 1) * BW)
                prenorm_ap(li, tb, xn_ap, xnB, tmpA, tmpAB)
                STG = int(os.environ.get("STG", "9"))
                for j in range(8 if STG >= 1 else 0):
                    b = proj_ap(w_in, w_inB, j * P, P, xn_ap, xnB, 8)
                    op("act", [psB[b]], [qtB[j]], lambda e, b=b, j=j: e.activation(out=qt_ap(j, 0, P, 0, BW), in_=ps[b][:, :], func=AF.Copy))
                for kv in range(2 if STG >= 2 else 0):
                    b = proj_ap(w_in, w_inB, D + kv * P, P, xn_ap, xnB, 8)
                    op("act", [psB[b]], [KTB[kv][tb]], lambda e, b=b, kv=kv: e.activation(out=KT2[:, kv, blk], in_=ps[b][:, :], func=AF.Copy))
                for ch in range(4 if STG >= 3 else 0):
                    tt = tb * 4 + ch
                    b = k.nextps()
                    for kc in range(8):
                        op("pe", [w_inB, xnB[kc]], [psB[b]],
                           lambda e, kc=kc, b=b, ch=ch: e.matmul(ps[b][:, 0:P], lhsT=xn_ap(kc)[:, ch * P:(ch + 1) * P],
                                                                 rhs=w_in[:, kc, D + 256:D + 384], start=(kc == 0), stop=(kc == 7)),
                           inc=(kc == 7))
                    for kv in range(2):
                        op("act", [psB[b]], [VAB[kv][0][tt]],
                           lambda e, b=b, kv=kv, tt=tt: e.activation(out=VA[kv][0][:, tt, 0:64], in_=ps[b][:, kv * 64:(kv + 1) * 64], func=AF.Copy))
                        op("act", [psB[b]], [VAB[kv][1][tt]],
                           lambda e, b=b, kv=kv, tt=tt: e.activation(out=VA[kv][1][:, tt, 64:128], in_=ps[b][:, kv * 64:(kv + 1) * 64], func=AF.Copy))
                for m in range(8):
                    b = proj_ap(w_in, w_inB, D + 384 + m * P, P, xn_ap, xnB, 8)
                    op("act", [psB[b]], [szB[m]], lambda e, b=b, m=m: e.activation(out=sz[:, m, :], in_=ps[b][:, :], func=AF.Silu))
                def scores(h, nbl, pi):
                    kv, par, j = h // 8, h % 2, h // 2
                    base = par * 64
                    nb = tb * 4 + nbl
                    b = k.nextps()
                    c_lo = 0 if nb > 0 else P
                    rhs_q = qt_ap(j, base, base + 64, nbl * P, (nbl + 1) * P)
                    if nb > 0:
                        tbp = (nb - 1) // 4
                        op("pe", [KTB[kv][tbp], qtB[j]], [psB[b]],
                           lambda e: e.matmul(ps[b][:, 0:P], lhsT=KT2[base:base + 64, kv, (nb - 1) * P:nb * P], rhs=rhs_q, start=True, stop=True),
                           inc=False)
                    op("pe", [KTB[kv][tb], qtB[j]], [psB[b]],
                       lambda e: e.matmul(ps[b][:, P:2 * P], lhsT=KT2[base:base + 64, kv, nb * P:(nb + 1) * P], rhs=rhs_q, start=True, stop=True))
                    op("dve", [psB[b], biasB], [tmpPB[pi]],
                       lambda e: e.scalar_tensor_tensor(out=tmpP[:, pi, c_lo:256], in0=ps[b][:, c_lo:256], scalar=0.125, in1=biasT[:, h, c_lo:256],
                                                        op0=ALU.mult, op1=ALU.add))
                    op("act", [tmpPB[pi]], [PTB[pi]],
                       lambda e: e.activation(out=PT[:, pi, c_lo:256], in_=tmpP[:, pi, c_lo:256], func=AF.Exp))

                def pv_epi(h, nbl, pi):
                    kv, par, j = h // 8, h % 2, h // 2
                    base = par * 64
                    oth = 64 - base
                    bo = 6 + (h % 2)
                    nb = tb * 4 + nbl
                    if nb > 0:
                        op("pe", [PTB[pi], VAB[kv][par][nb - 1]], [psB[bo]],
                           lambda e: e.matmul(ps[bo][:, nbl * P:(nbl + 1) * P], lhsT=VA[kv][par][:, nb - 1, :], rhs=PT[:, pi, 0:P], start=True, stop=False),
                           inc=False)
                    op("pe", [PTB[pi], VAB[kv][par][nb]], [psB[bo]],
                       lambda e: e.matmul(ps[bo][:, nbl * P:(nbl + 1) * P], lhsT=VA[kv][par][:, nb, :], rhs=PT[:, pi, P:2 * P], start=(nb == 0), stop=True))
                    if nbl == 3:
                        op("dve", [psB[bo], esinkB], [rsbB[par]],
                           lambda e: e.tensor_scalar(out=rsb[base:base + 64, :], in0=ps[bo][oth:oth + 64, :], scalar1=esink[oth:oth + 64, h:h + 1],
                                                     scalar2=None, op0=ALU.add))
                        op("dve", [rsbB[par]], [rsbB[par]], lambda e: e.reciprocal(out=rsb[base:base + 64, :], in_=rsb[base:base + 64, :]))
                        op("dve", [psB[bo], rsbB[par]], [tGB[par]],
                           lambda e: e.tensor_tensor(out=tG[base:base + 64, :], in0=ps[bo][base:base + 64, :], in1=rsb[base:base + 64, :], op=ALU.mult))
                        op("pool", [tGB[par], szB[j]], [szB[j]],
                           lambda e: e.tensor_tensor(out=sz[base:base + 64, j, :], in0=tG[base:base + 64, :], in1=sz[base:base + 64, j, :], op=ALU.mult))

                import os
                tasks = [(h, nbl) for h in range(int(os.environ.get('SWA_NH', '16'))) for nbl in range(4)]
                if tasks:
                    scores(tasks[0][0], tasks[0][1], apc[0] % NPB)
                for i, (h, nbl) in enumerate(tasks):
                    pi = apc[0] % NPB
                    apc[0] += 1
                    if i + 1 < len(tasks):
                        scores(tasks[i + 1][0], tasks[i + 1][1], apc[0] % NPB)
                    pv_epi(h, nbl, pi)
                outproj(li, tb, sz, szB, w_out, w_outB, R1, R1B, tmpA, tmpAB)
            k.barrier()

    def layer_mla(li):
        SC = 96.0 ** -0.5
        st = ExitStack()
        with st:
            GT = sb(st, "GT", [P, 8, L], BF16)
            GTB = bufs(8, NBLK)
            sA = ExitStack()
            sA.__enter__()
            CQN = sb(sA, "CQN", [P, 6, L], BF16)
            CQNB = bufs(6, NBLK)
            CKVN = sb(sA, "CKVN", [P, 2, L], BF16)
            CKVNB = bufs(2, NBLK)
            KR = sb(sA, "KR", [32, L], BF16)
            KRB = bufs(NBLK)
            ROPE = sb(sA, "ROPE", [64, L], F32)
            ropeB = Buf()
            gq = sb(sA, "gq", [P, 8], F32)
            gqB = Buf()
            dma("sp", ROPE[:], dw["c_rope"], [], [ropeB])
            dma("sp", gq[:], dw["c_gqkv"], [], [gqB])
            tR = sb(sA, "tR", [32, 2, BW], F32)
            tRB = bufs(2)
            s1 = ExitStack()
            with s1:
                w1, w1B = load_w(s1, "s_c_w1", dw["c_w1"], 1088)
                R1 = sb(s1, "R1", [P, 8, BW], F32)
                R1B = bufs(8)
                xnt = sb(s1, "xnt", [P, 8, BW], BF16)
                xnB = bufs(8)
                tmpA = sb(s1, "tmpA", [P, 8, BW], BF16)
                tmpAB = bufs(8)
                xn_ap = lambda c: xnt[:, c, :]
                for tb in range(NBLK):
                    blk = slice(tb * BW, (tb + 1) * BW)
                    prenorm_ap(li, tb, xn_ap, xnB, tmpA, tmpAB)
                    for m in range(8):
                        b = proj_ap(w1, w1B, m * P, P, xn_ap, xnB, 8)
                        op("act", [psB[b]], [R1B[m]], lambda e, b=b, m=m: e.activation(out=R1[:, m, :], in_=ps[b][:, :], func=AF.Copy))
                        op("act", [psB[b]], [tmpAB[m]], lambda e, b=b, m=m: e.activation(out=tmpA[:, m, :], in_=ps[b][:, :], func=AF.Square))
                    rstd_from_sq(tmpA, tmpAB, 6, 1.0 / 768, 0)
                    for m in range(6):
                        op("dve", [R1B[m], rstdB, gqB], [CQNB[m][tb]],
                           lambda e, m=m: e.scalar_tensor_tensor(out=CQN[:, m, blk], in0=R1[:, m, :], scalar=gq[:, m:m + 1], in1=rstd[:],
                                                                 op0=ALU.mult, op1=ALU.mult))
                    rstd_from_sq(tmpA, tmpAB, 2, 1.0 / 256, 6)
                    for m in range(2):
                        op("dve", [R1B[6 + m], rstdB, gqB], [CKVNB[m][tb]],
                           lambda e, m=m: e.scalar_tensor_tensor(out=CKVN[:, m, blk], in0=R1[:, 6 + m, :], scalar=gq[:, 6 + m:7 + m], in1=rstd[:],
                                                                 op0=ALU.mult, op1=ALU.mult))
                    b = proj_ap(w1, w1B, D, 64, xn_ap, xnB, 8)
                    op("dve", [psB[b], ropeB], [tRB[0]],
                       lambda e, b=b: e.tensor_tensor(out=tR[:, 0, :], in0=ps[b][32:64, :], in1=ROPE[32:64, blk], op=ALU.mult))
                    op("dve", [psB[b], ropeB], [tRB[1]],
                       lambda e, b=b: e.tensor_tensor(out=tR[:, 1, :], in0=ps[b][0:32, :], in1=ROPE[0:32, blk], op=ALU.mult))
                    op("pool", [tRB[0], tRB[1]], [KRB[tb]],
                       lambda e: e.tensor_tensor(out=KR[:, blk], in0=tR[:, 0, :], in1=tR[:, 1, :], op=ALU.add))
                k.barrier()
            s2 = ExitStack()
            with s2:
                wuq, wuqB = load_w(s2, "s_c_wuq", dw["c_wuq"], 2048)
                wukv, wukvB = load_w(s2, "s_c_wukv", dw["c_wukv"], 2048)
                KTh = [sb(s2, "KTh%d" % i, [P, L], BF16) for i in range(2)]
                KThB = bufs(2, NBLK)
                VAh = [sb(s2, "VAh%d" % i, [P, 16, P], BF16) for i in range(2)]
                VAhB = bufs(2, 4)
                QT = [sb(s2, "QT%d" % i, [P, BW], BF16) for i in range(2)]
                QTB = bufs(2)
                NPT = 4
                PT = [sb(s2, "PT%d" % i, [P, BW], BF16) for i in range(NPT)]
                PTB = bufs(NPT)
                QF = sb(s2, "QF", [64, BW], F32)
                QFB = Buf()
                maskT = sb(s2, "maskT", [P, P], BF16)
                maskB = Buf()
                rsb = sb(s2, "rsb", [P, BW], F32)
                rsbB = bufs(2)
                dma("pool", maskT[:], dw["c_maskT"], [], [maskB])
                for i in range(2):
                    op("pool", [], KThB[i], lambda e, i=i: e.memset(KTh[i][32:64, :], 0.0))
                    c0 = 64 if i == 0 else 0
                    op("pool", [], VAhB[i], lambda e, i=i, c0=c0: e.memset(VAh[i][:, :, c0:c0 + 64], 1.0))
                def kv_build(h):
                    par = h % 2
                    vc0 = par * 64
                    for tb in range(NBLK):
                        blk = slice(tb * BW, (tb + 1) * BW)
                        b = k.nextps()
                        for kc in range(2):
                            op("pe", [wukvB, CKVNB[kc][tb]], [psB[b]],
                               lambda e, kc=kc, b=b, blk=blk: e.matmul(ps[b][64:128, :], lhsT=wukv[:, kc, h * P:h * P + 64], rhs=CKVN[:, kc, blk],
                                                                      start=(kc == 0), stop=(kc == 1)), inc=(kc == 1))
                        op("act", [psB[b]], [KThB[par][tb]],
                           lambda e, b=b, blk=blk: e.activation(out=KTh[par][64:128, blk], in_=ps[b][64:128, :], func=AF.Copy))
                        op("dve", [KRB[tb]], [KThB[par][tb]],
                           lambda e, blk=blk: e.tensor_scalar(out=KTh[par][0:32, blk], in0=KR[:, blk], scalar1=1.0, scalar2=None, op0=ALU.mult))
                        b = k.nextps()
                        for i4 in range(4):
                            tt = tb * 4 + i4
                            for kc in range(2):
                                op("pe", [wukvB, CKVNB[kc][tb]], [psB[b]],
                                   lambda e, kc=kc, b=b, tt=tt, i4=i4: e.matmul(ps[b][:, i4 * 64:(i4 + 1) * 64], lhsT=CKVN[:, kc, tt * P:(tt + 1) * P],
                                                                                rhs=wukv[:, kc, h * P + 64:h * P + 128], start=(kc == 0), stop=(kc == 1)),
                                   inc=(kc == 1 and i4 == 3))
                        op("act", [psB[b]], [VAhB[par][tb]],
                           lambda e, b=b, tb=tb: e.activation(out=VAh[par][:, tb * 4:(tb + 1) * 4, vc0:vc0 + 64],
                                                              in_=ps[b][:, 0:256].rearrange("p (a c) -> p a c", a=4), func=AF.Copy))

                def q_prep(h, qb, qi):
                    qblk = slice(qb * BW, (qb + 1) * BW)
                    b = k.nextps()
                    for kc in range(6):
                        op("pe", [wuqB, CQNB[kc][qb]], [psB[b]],
                           lambda e, kc=kc, b=b: e.matmul(ps[b][:, :], lhsT=wuq[:, kc, h * P:(h + 1) * P], rhs=CQN[:, kc, qblk],
                                                          start=(kc == 0), stop=(kc == 5)), inc=(kc == 5))
                    op("act", [psB[b]], [QTB[qi]], lambda e, b=b: e.activation(out=QT[qi][:, :], in_=ps[b][:, :], func=AF.Copy))
                    op("act", [psB[b]], [QFB], lambda e, b=b: e.activation(out=QF[:, :], in_=ps[b][0:64, :], func=AF.Copy))
                    op("dve", [QFB, ropeB], [tRB[0]],
                       lambda e: e.tensor_tensor(out=tR[:, 0, :], in0=QF[32:64, :], in1=ROPE[32:64, qblk], op=ALU.mult))
                    op("dve", [QFB, ropeB], [tRB[1]],
                       lambda e: e.tensor_tensor(out=tR[:, 1, :], in0=QF[0:32, :], in1=ROPE[0:32, qblk], op=ALU.mult))
                    op("dve", [tRB[0], tRB[1]], [QTB[qi]],
                       lambda e: e.tensor_tensor(out=QT[qi][0:32, :], in0=tR[:, 0, :], in1=tR[:, 1, :], op=ALU.add))

                pti = [0]

                def attend(h, qb, qi, bo):
                    par, j = h % 2, h // 2
                    base = par * 64
                    oth = 64 - base
                    qblk = slice(qb * BW, (qb + 1) * BW)
                    nkc = 4 * qb + 4

                    def pv(kc, pi, q_lo):
                        op("pe", [PTB[pi], VAhB[par][kc // 4]], [psB[bo]],
                           lambda e: e.matmul(ps[bo][:, q_lo:BW], lhsT=VAh[par][:, kc, :], rhs=PT[pi][:, q_lo:BW],
                                              start=(kc == 0), stop=(kc == nkc - 1)))
                    prev = None
                    for kc in range(nkc):
                        q_lo = max(0, kc - 4 * qb) * P
                        pi = pti[0] % NPT
                        pti[0] += 1
                        b = k.nextps()
                        op("pe", [KThB[par][kc // 4], QTB[qi]], [psB[b]],
                           lambda e, b=b, kc=kc, q_lo=q_lo: e.matmul(ps[b][:, q_lo:BW], lhsT=KTh[par][:, kc * P:(kc + 1) * P],
                                                                     rhs=QT[qi][:, q_lo:BW], start=True, stop=True))
                        op("act", [psB[b]], [PTB[pi]],
                           lambda e, b=b, pi=pi, q_lo=q_lo: e.activation(out=PT[pi][:, q_lo:BW], in_=ps[b][:, q_lo:BW], func=AF.Exp, scale=SC))
                        if kc >= 4 * qb:
                            op("dve", [PTB[pi], maskB], [PTB[pi]],
                               lambda e, pi=pi, q_lo=q_lo: e.tensor_tensor(out=PT[pi][:, q_lo:q_lo + P], in0=PT[pi][:, q_lo:q_lo + P],
                                                                           in1=maskT[:, :], op=ALU.mult))
                        if prev is not None:
                            pv(*prev)
                        prev = (kc, pi, q_lo)
                    pv(*prev)
                    op("dve", [psB[bo]], [rsbB[par]],
                       lambda e: e.reciprocal(out=rsb[base:base + 64, :], in_=ps[bo][oth:oth + 64, :]))
                    op("dve", [psB[bo], rsbB[par]], [GTB[j][qb]],
                       lambda e: e.tensor_tensor(out=GT[base:base + 64, j, qblk], in0=ps[bo][base:base + 64, :],
                                                 in1=rsb[base:base + 64, :], op=ALU.mult))

                import os
                NH = int(os.environ.get("MLA_NH", "16"))
                tasks = [(h, qb) for h in range(NH) for qb in range(NBLK)]
                if tasks:
                    kv_build(0)
                    q_prep(0, 0, 0)
                for i, (h, qb) in enumerate(tasks):
                    if i + 1 < len(tasks):
                        h2, qb2 = tasks[i + 1]
                        if h2 != h:
                            kv_build(h2)
                        q_prep(h2, qb2, (i + 1) % 2)
                    attend(h, qb, i % 2, 6 + (i % 2))
                k.barrier()
            sA.close()
            s3 = ExitStack()
            with s3:
                wz, wzB = load_w(s3, "s_c_wz", dw["c_wz"], D)
                w_out, w_outB = load_w(s3, "s_c_wout", dw["c_w_out"], D)
                R1 = sb(s3, "R1", [P, 8, BW], F32)
                R1B = bufs(8)
                xnt = sb(s3, "xnt", [P, 8, BW], BF16)
                xnB = bufs(8)
                tmpA = sb(s3, "tmpA", [P, 8, BW], BF16)
                tmpAB = bufs(8)
                sz = sb(s3, "sz", [P, 8, BW], BF16)
                szB = bufs(8)
                xn_ap = lambda c: xnt[:, c, :]
                for tb in range(NBLK):
                    blk = slice(tb * BW, (tb + 1) * BW)
                    prenorm_ap(li, tb, xn_ap, xnB, tmpA, tmpAB)
                    for m in range(8):
                        b = proj_ap(wz, wzB, m * P, P, xn_ap, xnB, 8)
                        op("act", [psB[b]], [szB[m]], lambda e, b=b, m=m: e.activation(out=sz[:, m, :], in_=ps[b][:, :], func=AF.Silu))
                        op("pool", [szB[m], GTB[m][tb]], [szB[m]],
                           lambda e, m=m, blk=blk: e.tensor_tensor(out=sz[:, m, :], in0=sz[:, m, :], in1=GT[:, m, blk], op=ALU.mult))
                    outproj(li, tb, sz, szB, w_out, w_outB, R1, R1B, tmpA, tmpAB)
                k.barrier()

    def layer_s5(li):
        TT = ALU
        st = ExitStack()
        with st:
            UY = sb(st, "UY", [P, 8, L], BF16)
            UYB = bufs(8, NBLK)
            dT = sb(st, "dT", [P, 16], F32)
            dTB = Buf()
            dma("sp", dT[:], dw["a_dg"], [], [dTB])
            s1 = ExitStack()
            with s1:
                wu, wuB = load_w(s1, "a_wu", dw["a_wu"], D)
                xnt = sb(s1, "xnt", [P, 8, BW], BF16)
                xnB = bufs(8)
                tmpA = sb(s1, "tmpA", [P, 8, BW], BF16)
                tmpAB = bufs(8)
                xn_ap = lambda c: xnt[:, c, :]
                for tb in range(NBLK):
                    blk = slice(tb * BW, (tb + 1) * BW)
                    prenorm_ap(li, tb, xn_ap, xnB, tmpA, tmpAB)
                    for m in range(8):
                        b = proj_ap(wu, wuB, m * P, P, xn_ap, xnB, 8)
                        op("act", [psB[b]], [UYB[m][tb]], lambda e, b=b, m=m, blk=blk: e.activation(out=UY[:, m, blk], in_=ps[b][:, :], func=AF.Copy))
                k.barrier()
            s2 = ExitStack()
            with s2:
                NJ = 32
                BrL, BrLB = sb(s2, "BrL", [P, NJ, P], BF16), Buf()
                BiL, BiLB = sb(s2, "BiL", [P, NJ, P], BF16), Buf()
                CrP, CrPB = sb(s2, "CrP", [P, NJ, P], BF16), Buf()
                CiP, CiPB = sb(s2, "CiP", [P, NJ, P], BF16), Buf()
                for t_, B_, nm in ((BrL, BrLB, "a_brl"), (BiL, BiLB, "a_bil"), (CrP, CrPB, "a_crp"), (CiP, CiPB, "a_cip")):
                    for q4 in range(4):
                        dma("pool", t_[:, q4 * 8:(q4 + 1) * 8, :], dw[nm][:, q4 * 8:(q4 + 1) * 8, :], [], [B_])
                lam = sb(s2, "lam", [P, 3, NJ], F32)
                prepB = Buf()
                dma("sp", lam[:], dw["a_lam"], [], [prepB])
                W = {}
                for nm in ("dt", "lrdt", "th", "mag", "t", "t2", "c", "s", "q", "c2", "s2", "cs", "ar", "ai", "den", "nr", "fr", "fi",
                           "u1", "u2", "ir", "ii", "pr", "pi", "A128r", "nA128i", "A128i"):
                    W[nm] = sb(s2, "w_" + nm, [P, NJ], F32)
                lr, li_, ldt = lam[:, 0, :], lam[:, 1, :], lam[:, 2, :]

                def tt(o, a, b_, o_):
                    op("dve", [prepB], [prepB], lambda e: e.tensor_tensor(out=o, in0=a, in1=b_, op=o_))

                def ts(o, a, s1_, s2_, o1, o2=None):
                    if o2 is None:
                        op("dve", [prepB], [prepB], lambda e: e.tensor_scalar(out=o, in0=a, scalar1=s1_, scalar2=None, op0=o1))
                    else:
                        op("dve", [prepB], [prepB], lambda e: e.tensor_scalar(out=o, in0=a, scalar1=s1_, scalar2=s2_, op0=o1, op1=o2))

                def stt(o, a, sc, b_, o1, o2):
                    op("dve", [prepB], [prepB], lambda e: e.scalar_tensor_tensor(out=o, in0=a, scalar=sc, in1=b_, op0=o1, op1=o2))

                def csq(cr, ci):
                    tt(W["c2"][:], cr, cr, TT.mult)
                    tt(W["s2"][:], ci, ci, TT.mult)
                    tt(W["cs"][:], cr, ci, TT.mult)
                    tt(cr, W["c2"][:], W["s2"][:], TT.subtract)
                    ts(ci, W["cs"][:], 2.0, None, TT.mult)

                op("act", [prepB], [prepB], lambda e: e.activation(out=W["dt"][:], in_=ldt, func=AF.Exp))
                tt(W["lrdt"][:], lr, W["dt"][:], TT.mult)
                tt(W["th"][:], li_, W["dt"][:], TT.mult)
                op("act", [prepB], [prepB], lambda e: e.activation(out=W["mag"][:], in_=W["lrdt"][:], func=AF.Exp))
                ts(W["t"][:], W["th"][:], 1.0 / 64, None, TT.mult)
                tt(W["t2"][:], W["t"][:], W["t"][:], TT.mult)
                ts(W["q"][:], W["t2"][:], -1.0 / 720, None, TT.mult)
                stt(W["q"][:], W["q"][:], 1.0 / 24, W["t2"][:], TT.add, TT.mult)
                stt(W["q"][:], W["q"][:], -0.5, W["t2"][:], TT.add, TT.mult)
                ts(W["c"][:], W["q"][:], 1.0, None, TT.add)
                ts(W["q"][:], W["t2"][:], -1.0 / 5040, None, TT.mult)
                stt(W["q"][:], W["q"][:], 1.0 / 120, W["t2"][:], TT.add, TT.mult)
                stt(W["q"][:], W["q"][:], -1.0 / 6, W["t2"][:], TT.add, TT.mult)
                stt(W["s"][:], W["q"][:], 1.0, W["t"][:], TT.add, TT.mult)
                for _ in range(6):
                    csq(W["c"][:], W["s"][:])
                tt(W["ar"][:], W["mag"][:], W["c"][:], TT.mult)
                tt(W["ai"][:], W["mag"][:], W["s"][:], TT.mult)
                tt(W["den"][:], lr, lr, TT.mult)
                tt(W["u1"][:], li_, li_, TT.mult)
                tt(W["den"][:], W["den"][:], W["u1"][:], TT.add)
                op("dve", [prepB], [prepB], lambda e: e.reciprocal(out=W["den"][:], in_=W["den"][:]))
                ts(W["nr"][:], W["ar"][:], -1.0, None, TT.add)
                tt(W["u1"][:], W["nr"][:], lr, TT.mult)
                tt(W["u2"][:], W["ai"][:], li_, TT.mult)
                tt(W["u1"][:], W["u1"][:], W["u2"][:], TT.add)
                tt(W["fr"][:], W["u1"][:], W["den"][:], TT.mult)
                tt(W["u1"][:], W["ai"][:], lr, TT.mult)
                tt(W["u2"][:], W["nr"][:], li_, TT.mult)
                tt(W["u1"][:], W["u1"][:], W["u2"][:], TT.subtract)
                tt(W["fi"][:], W["u1"][:], W["den"][:], TT.mult)
                tt(W["u1"][:], W["ar"][:], W["ar"][:], TT.mult)
                tt(W["u2"][:], W["ai"][:], W["ai"][:], TT.mult)
                tt(W["u1"][:], W["u1"][:], W["u2"][:], TT.add)
                op("dve", [prepB], [prepB], lambda e: e.reciprocal(out=W["u1"][:], in_=W["u1"][:]))
                tt(W["ir"][:], W["ar"][:], W["u1"][:], TT.mult)
                tt(W["ii"][:], W["ai"][:], W["u1"][:], TT.mult)
                ts(W["ii"][:], W["ii"][:], -1.0, None, TT.mult)
                TPr = sb(s2, "TPr", [P, NJ, P], BF16)
                TPi = sb(s2, "TPi", [P, NJ, P], BF16)
                TNr = sb(s2, "TNr", [P, NJ, P], BF16)
                TNi = sb(s2, "TNi", [P, NJ, P], BF16)
                sT = ExitStack()
                sT.__enter__()
                m1 = sb(sT, "m1t", [P, NJ, 64], F32)
                m2 = sb(sT, "m2t", [P, NJ, 64], F32)
                op("dve", [prepB], [prepB], lambda e: e.memset(TPr[:, :, 0:1], 1.0))
                op("dve", [prepB], [prepB], lambda e: e.memset(TPi[:, :, 0:1], 0.0))
                ts(TNr[:, :, 0:1], W["fr"][:].unsqueeze(2), 1.0, None, TT.mult)
                ts(TNi[:, :, 0:1], W["fi"][:].unsqueeze(2), 1.0, None, TT.mult)
                for (Tr_, Ti_, pr0, pi0) in ((TPr, TPi, "ar", "ai"), (TNr, TNi, "ir", "ii")):
                    ts(W["pr"][:], W[pr0][:], 1.0, None, TT.mult)
                    ts(W["pi"][:], W[pi0][:], 1.0, None, TT.mult)
                    for kk in range(7):
                        n = 1 << kk
                        pr_b = W["pr"][:].unsqueeze(2).to_broadcast([P, NJ, n])
                        pi_b = W["pi"][:].unsqueeze(2).to_broadcast([P, NJ, n])
                        lo_r, lo_i = Tr_[:, :, 0:n], Ti_[:, :, 0:n]
                        tt(m1[:, :, 0:n], lo_r, pr_b, TT.mult)
                        tt(m2[:, :, 0:n], lo_i, pi_b, TT.mult)
                        tt(Tr_[:, :, n:2 * n], m1[:, :, 0:n], m2[:, :, 0:n], TT.subtract)
                        tt(m1[:, :, 0:n], lo_r, pi_b, TT.mult)
                        tt(m2[:, :, 0:n], lo_i, pr_b, TT.mult)
                        tt(Ti_[:, :, n:2 * n], m1[:, :, 0:n], m2[:, :, 0:n], TT.add)
                        csq(W["pr"][:], W["pi"][:])
                    if pr0 == "ar":
                        ts(W["A128r"][:], W["pr"][:], 1.0, None, TT.mult)
                        ts(W["nA128i"][:], W["pi"][:], -1.0, None, TT.mult)
                        ts(W["A128i"][:], W["pi"][:], 1.0, None, TT.mult)
                k.barrier()
                sT.close()
                car = sb(s2, "car", [P, 2, NJ], F32)
                carB = bufs(NJ)
                op("dve", [prepB], carB, lambda e: e.memset(car[:], 0.0))
                onesf = sb(s2, "onesf", [P, P], F32)
                op("dve", [prepB], [prepB], lambda e: e.memset(onesf[:], 1.0))
                vr = sb(s2, "vr", [P, BW], F32)
                vi = sb(s2, "vi", [P, BW], F32)
                vB = Buf()
                kr_ = sb(s2, "kr", [P, BW], F32)
                ki_ = sb(s2, "ki", [P, BW], F32)
                kB = Buf()
                n1 = sb(s2, "n1", [P, BW], F32)
                n2 = sb(s2, "n2", [P, BW], F32)
                nB = Buf()
                Xr = sb(s2, "Xr", [P, BW], F32)
                Xi = sb(s2, "Xi", [P, BW], F32)
                XB = Buf()
                xr_b = sb(s2, "xrb", [P, BW], BF16)
                xi_b = sb(s2, "xib", [P, BW], BF16)
                xbB = Buf()
                e1 = sb(s2, "e1", [P, 1], F32)
                ytmp = sb(s2, "ytmp", [P, BW], F32)
                yB = Buf()

                def v4(ap):
                    return ap.rearrange("p (a t) -> p a t", a=4)

                def tb4(T_, j):
                    return T_[:, j, :].unsqueeze(1).to_broadcast([P, 4, P])

                for chc in range(8):
                    for qt in range(NBLK):
                        blk = slice(qt * BW, (qt + 1) * BW)
                        bo = 6 + ((chc * NBLK + qt) % 2)
                        for jl in range(4):
                            j = chc * 4 + jl
                            ba = k.nextps()
                            op("pe", [BrLB, UYB[chc][qt]], [psB[ba]],
                               lambda e, ba=ba, j=j, blk=blk: e.matmul(ps[ba][:, :], lhsT=BrL[:, j, :], rhs=UY[:, chc, blk], start=True, stop=True))
                            bb = k.nextps()
                            op("pe", [BiLB, UYB[chc][qt]], [psB[bb]],
                               lambda e, bb=bb, j=j, blk=blk: e.matmul(ps[bb][:, :], lhsT=BiL[:, j, :], rhs=UY[:, chc, blk], start=True, stop=True))
                            op("act", [psB[ba]], [vB], lambda e, ba=ba: e.activation(out=vr[:], in_=ps[ba][:, :], func=AF.Copy))
                            op("act", [psB[bb]], [vB], lambda e, bb=bb: e.activation(out=vi[:], in_=ps[bb][:, :], func=AF.Copy))
                            op("dve", [vB, prepB], [nB], lambda e, j=j: e.tensor_tensor(out=v4(n1[:]), in0=v4(vr[:]), in1=tb4(TNr, j), op=TT.mult))
                            op("pool", [vB, prepB], [nB], lambda e, j=j: e.tensor_tensor(out=v4(n2[:]), in0=v4(vi[:]), in1=tb4(TNi, j), op=TT.mult))
                            op("dve", [nB], [kB], lambda e: e.tensor_tensor(out=kr_[:], in0=n1[:], in1=n2[:], op=TT.subtract))
                            op("dve", [vB, prepB, kB], [nB], lambda e, j=j: e.tensor_tensor(out=v4(n1[:]), in0=v4(vi[:]), in1=tb4(TNr, j), op=TT.mult))
                            op("pool", [vB, prepB, kB], [nB], lambda e, j=j: e.tensor_tensor(out=v4(n2[:]), in0=v4(vr[:]), in1=tb4(TNi, j), op=TT.mult))
                            op("dve", [nB], [kB], lambda e: e.tensor_tensor(out=ki_[:], in0=n1[:], in1=n2[:], op=TT.add))
                            for c4 in range(4):
                                cs_ = slice(c4 * P, (c4 + 1) * P)
                                op("dve", [kB, carB[j]], [XB],
                                   lambda e, cs_=cs_, j=j: e.tensor_tensor_scan(out=Xr[:, cs_], data0=onesf[:, :], data1=kr_[:, cs_],
                                                                                initial=car[:, 0, j:j + 1], op0=TT.mult, op1=TT.add))
                                op("dve", [kB, carB[j]], [XB],
                                   lambda e, cs_=cs_, j=j: e.tensor_tensor_scan(out=Xi[:, cs_], data0=onesf[:, :], data1=ki_[:, cs_],
                                                                                initial=car[:, 1, j:j + 1], op0=TT.mult, op1=TT.add))
                                last = c4 * P + P - 1
                                op("dve", [XB, prepB], [carB[j]],
                                   lambda e, last=last, j=j: e.tensor_scalar(out=e1[:], in0=Xr[:, last:last + 1], scalar1=W["A128r"][:, j:j + 1], scalar2=None, op0=TT.mult))
                                op("dve", [XB, prepB], [carB[j]],
                                   lambda e, last=last, j=j: e.scalar_tensor_tensor(out=car[:, 0, j:j + 1], in0=Xi[:, last:last + 1], scalar=W["nA128i"][:, j:j + 1],
                                                                                    in1=e1[:], op0=TT.mult, op1=TT.add))
                                op("dve", [XB, prepB], [carB[j]],
                                   lambda e, last=last, j=j: e.tensor_scalar(out=e1[:], in0=Xi[:, last:last + 1], scalar1=W["A128r"][:, j:j + 1], scalar2=None, op0=TT.mult))
                                op("dve", [XB, prepB], [carB[j]],
                                   lambda e, last=last, j=j: e.scalar_tensor_tensor(out=car[:, 1, j:j + 1], in0=Xr[:, last:last + 1], scalar=W["A128i"][:, j:j + 1],
                                                                                    in1=e1[:], op0=TT.mult, op1=TT.add))
                            op("dve", [XB, prepB], [nB], lambda e, j=j: e.tensor_tensor(out=v4(n1[:]), in0=v4(Xr[:]), in1=tb4(TPr, j), op=TT.mult))
                            op("pool", [XB, prepB], [nB], lambda e, j=j: e.tensor_tensor(out=v4(n2[:]), in0=v4(Xi[:]), in1=tb4(TPi, j), op=TT.mult))
                            op("dve", [nB], [xbB], lambda e: e.tensor_tensor(out=xr_b[:], in0=n1[:], in1=n2[:], op=TT.subtract))
                            op("dve", [XB, prepB, xbB], [nB], lambda e, j=j: e.tensor_tensor(out=v4(n1[:]), in0=v4(Xr[:]), in1=tb4(TPi, j), op=TT.mult))
                            op("pool", [XB, prepB, xbB], [nB], lambda e, j=j: e.tensor_tensor(out=v4(n2[:]), in0=v4(Xi[:]), in1=tb4(TPr, j), op=TT.mult))
                            op("dve", [nB], [xbB], lambda e: e.scalar_tensor_tensor(out=xi_b[:], in0=n1[:], scalar=-1.0, in1=n2[:], op0=TT.mult, op1=TT.subtract))
                            op("pe", [xbB, CrPB], [psB[bo]],
                               lambda e, bo=bo, j=j, jl=jl: e.matmul(ps[bo][:, :], lhsT=CrP[:, j, :], rhs=xr_b[:], start=(jl == 0), stop=False), inc=False)
                            op("pe", [xbB, CiPB], [psB[bo]],
                               lambda e, bo=bo, j=j, jl=jl: e.matmul(ps[bo][:, :], lhsT=CiP[:, j, :], rhs=xi_b[:], start=False, stop=(jl == 3)))
                        op("act", [psB[bo]], [yB], lambda e, bo=bo: e.activation(out=ytmp[:], in_=ps[bo][:, :], func=AF.Copy))
                        op("dve", [yB, UYB[chc][qt], dTB], [yB],
                           lambda e, blk=blk: e.scalar_tensor_tensor(out=ytmp[:], in0=UY[:, chc, blk], scalar=dT[:, chc:chc + 1], in1=ytmp[:],
                                                                     op0=TT.mult, op1=TT.add))
                        op("act", [yB], [UYB[chc][qt]], lambda e, blk=blk: e.activation(out=UY[:, chc, blk], in_=ytmp[:], func=AF.Gelu_apprx_tanh))
                k.barrier()
            s3 = ExitStack()
            with s3:
                wz, wzB = load_w(s3, "a_wzs", dw["a_wz"], D)
                wg, wgB = load_w(s3, "a_wgs", dw["a_wglu"], D)
                w_out, w_outB = load_w(s3, "a_wouts", dw["a_w_out"], D)
                R1 = sb(s3, "R1", [P, 8, BW], F32)
                R1B = bufs(8)
                xnt = sb(s3, "xnt", [P, 8, BW], BF16)
                xnB = bufs(8)
                tmpA = sb(s3, "tmpA", [P, 8, BW], BF16)
                tmpAB = bufs(8)
                sz = sb(s3, "sz", [P, 8, BW], BF16)
                szB = bufs(8)
                sg = sb(s3, "sg", [P, 2, BW], BF16)
                sgB = bufs(2)
                xn_ap = lambda c: xnt[:, c, :]
                for tb in range(NBLK):
                    blk = slice(tb * BW, (tb + 1) * BW)
                    prenorm_ap(li, tb, xn_ap, xnB, tmpA, tmpAB)
                    uy_ap = lambda c, blk=blk: UY[:, c, blk]
                    uyB_t = [UYB[c][tb] for c in range(8)]
                    for m in range(8):
                        b = proj_ap(wz, wzB, m * P, P, xn_ap, xnB, 8)
                        op("act", [psB[b]], [szB[m]], lambda e, b=b, m=m: e.activation(out=sz[:, m, :], in_=ps[b][:, :], func=AF.Silu))
                        b = proj_ap(wg, wgB, m * P, P, uy_ap, uyB_t, 8)
                        op("act", [psB[b], dTB], [sgB[m % 2]],
                           lambda e, b=b, m=m: e.activation(out=sg[:, m % 2, :], in_=ps[b][:, :], func=AF.Sigmoid, bias=dT[:, 8 + m:9 + m], scale=1.0))
                        op("pool", [szB[m], uyB_t[m]], [szB[m]],
                           lambda e, m=m, blk=blk: e.tensor_tensor(out=sz[:, m, :], in0=sz[:, m, :], in1=UY[:, m, blk], op=ALU.mult))
                        op("pool", [szB[m], sgB[m % 2]], [szB[m]],
                           lambda e, m=m: e.tensor_tensor(out=sz[:, m, :], in0=sz[:, m, :], in1=sg[:, m % 2, :], op=ALU.mult))
                    outproj(li, tb, sz, szB, w_out, w_outB, R1, R1B, tmpA, tmpAB)
                k.barrier()

    for li in layers:
        if li == 3:
            layer_sgu(li)
        elif li == 1:
            layer_swa(li)
        elif li == 2:
            layer_mla(li)
        elif li == 0:
            layer_s5(li)

    toks = []
    for c in range(8):
        for tb in range(NBLK):
            toks.append(dma("sp", outT_d[c * P:(c + 1) * P, tb * BW:(tb + 1) * BW], X[:, c, tb * BW:(tb + 1) * BW], [xB[c][tb]], []))
    k._wait("sp", toks)
    k.barrier()


def host_inputs(inp, layers):
    f = lambda a: np.ascontiguousarray(np.asarray(a, dtype=np.float32))
    common = {}
    common["gpre"] = f(np.asarray(inp["pre_norm"]).reshape(4, 8, P).transpose(2, 0, 1).reshape(P, 32))
    common["gpost"] = f(np.asarray(inp["post_norm"]).reshape(4, 8, P).transpose(2, 0, 1).reshape(P, 32))
    common["ident"] = np.eye(P, dtype=np.float32)
    if 3 in layers:
        common["d_w_in"] = f(inp["d_w_in"][0])
        common["d_w_out"] = f(inp["d_w_out"][0])
        common["d_ws"] = f(np.asarray(inp["d_w_s"][0]).transpose(1, 0, 2))
        common["d_tril"] = np.tril(np.ones((P, P), dtype=np.float32))
        common["d_bs"] = f(inp["d_b_s"][0])
        common["d_lng"] = f(inp["d_ln_g"])
        common["d_lnb"] = f(inp["d_ln_b"])
    if 1 in layers:
        w = np.asarray(inp["b_w_in"][0], dtype=np.float32)
        q, kk, v, z = w[:, :1024], w[:, 1024:1152], w[:, 1152:1280], w[:, 1280:]
        common["b_w_in"] = f(np.concatenate([q, kk[:, :64], kk[:, :64], kk[:, 64:], kk[:, 64:], v, z], axis=1))
        common["b_w_out"] = f(inp["b_w_out"][0])
        common["b_sinks"] = f(inp["b_sinks"])
        def bucket(d):
            if d < 16:
                return d
            v_ = 16 + int(math.log(max(d, 1) / 16.0) / math.log(128 / 16.0) * 16)
            return min(v_, 31)
        rb = np.asarray(inp["rel_bias"], dtype=np.float32)
        bt = np.full((P, 16, 256), -1e30, dtype=np.float32)
        for kj in range(P):
            for qi in range(P):
                d_prev = qi + P - kj
                if d_prev < P:
                    bt[kj, :, qi] = rb[bucket(d_prev), :]
                d_cur = qi - kj
                if d_cur >= 0:
                    bt[kj, :, P + qi] = rb[bucket(d_cur), :]
        common["b_biasT"] = bt
    if 2 in layers:
        w = np.asarray(inp["c_w_in"][0], dtype=np.float32)
        kr = w[:, 1024:1056]
        common["c_w1"] = f(np.concatenate([w[:, :1024], kr, kr[:, 16:], kr[:, :16]], axis=1))
        common["c_wz"] = f(w[:, 1056:])
        uq = np.asarray(inp["c_w_uq"][0], dtype=np.float32).reshape(768, 16, 96)
        nope, rp = uq[:, :, :64], uq[:, :, 64:]
        common["c_wuq"] = f(np.concatenate([rp, rp[:, :, 16:], rp[:, :, :16], nope], axis=2).reshape(768, 2048))
        common["c_wukv"] = f(inp["c_w_ukv"][0])
        common["c_w_out"] = f(inp["c_w_out"][0])
        inv = (np.float32(10000.0) ** (-np.arange(0, 32, 2, dtype=np.float32) / np.float32(32))).astype(np.float32)
        ang = (np.arange(L, dtype=np.float32)[:, None] * inv[None, :]).astype(np.float32)
        cos, sin = np.cos(ang).astype(np.float32).T, np.sin(ang).astype(np.float32).T
        common["c_rope"] = f(np.concatenate([cos, cos, -sin, sin], axis=0))
        g = np.concatenate([np.asarray(inp["c_q_norm"][0]), np.asarray(inp["c_kv_norm"][0])]).astype(np.float32)
        common["c_gqkv"] = f(g.reshape(8, P).T)
        common["c_maskT"] = np.triu(np.ones((P, P), dtype=np.float32))
    if 0 in layers:
        w = np.asarray(inp["a_w_in"][0], dtype=np.float32)
        common["a_wu"] = f(w[:, :1024])
        common["a_wz"] = f(w[:, 1024:])
        common["a_wglu"] = f(inp["a_w_glu"][0])
        common["a_w_out"] = f(inp["a_w_out"][0])
        dg = np.concatenate([np.asarray(inp["a_d"][0]).reshape(8, P).T, np.asarray(inp["a_b_glu"][0]).reshape(8, P).T], axis=1)
        common["a_dg"] = f(dg)
        lam = np.stack([np.asarray(inp["a_lam_re"][0]).reshape(32, P).T, np.asarray(inp["a_lam_im"][0]).reshape(32, P).T,
                        np.repeat(np.asarray(inp["a_log_dt"][0]), 64).reshape(32, P).T], axis=1)
        common["a_lam"] = f(lam)
        brl = np.zeros((P, 32, P), np.float32); bil = np.zeros((P, 32, P), np.float32)
        crp = np.zeros((P, 32, P), np.float32); cip = np.zeros((P, 32, P), np.float32)
        b_re, b_im = np.asarray(inp["a_b_re"][0]), np.asarray(inp["a_b_im"][0])
        c_re, c_im = np.asarray(inp["a_c_re"][0]), np.asarray(inp["a_c_im"][0])
        for j in range(32):
            for gl in range(2):
                g = 2 * j + gl
                r0 = 32 * (j % 4) + gl * 16
                brl[r0:r0 + 16, j, gl * 64:(gl + 1) * 64] = b_re[g].T
                bil[r0:r0 + 16, j, gl * 64:(gl + 1) * 64] = b_im[g].T
                crp[gl * 64:(gl + 1) * 64, j, r0:r0 + 16] = c_re[g].T
                cip[gl * 64:(gl + 1) * 64, j, r0:r0 + 16] = c_im[g].T
        common["a_brl"], common["a_bil"], common["a_crp"], common["a_cip"] = brl, bil, crp, cip
    return common


def run(inp, layers=(0, 1, 2, 3), cores=8, trace=False):
    nc = bass.Bass("TRN2", target_bir_lowering=False)
    build(nc, list(layers))
    common = host_inputs(inp, list(layers))
    x = np.asarray(inp["x"], dtype=np.float32)
    in_maps = []
    for b in range(cores):
        m = dict(common)
        m["xT"] = np.ascontiguousarray(x[b].T)
        in_maps.append(m)
    res = run_bass_kernel_spmd(nc, in_maps, core_ids=list(range(cores)), trace=trace)
    out = np.stack([np.ascontiguousarray(r["outT"].T) for r in res.results], axis=0)
    return out.astype(np.float32), res


def kernel(**inputs):
    out, _ = run(inputs)
    return out
```

```python
import math
import numpy as np
from contextlib import ExitStack
import concourse.bass as bass
import concourse.mybir as mybir
from concourse.bass_utils import run_bass_kernel_spmd

F32 = mybir.dt.float32
BF16 = mybir.dt.bfloat16
ALU = mybir.AluOpType
AF = mybir.ActivationFunctionType

P = 128
L = 2048
D = 1024
NBLK = 4
BW = 512
EPS = 1e-6
SELF_SYNC = True
NDS = 12


class Buf:
    __slots__ = ("w", "r")

    def __init__(self):
        self.w = None
        self.r = {}


def bufs(*shape):
    if len(shape) == 1:
        return [Buf() for _ in range(shape[0])]
    return [bufs(*shape[1:]) for _ in range(shape[0])]


class KB:
    def __init__(self, nc, es):
        self.nc = nc
        self.E = dict(pe=nc.tensor, act=nc.scalar, dve=nc.vector, pool=nc.gpsimd, sp=nc.sync)
        self.sem = {e: es.enter_context(nc.semaphore("s_" + e)) for e in ("pe", "act", "dve", "pool")}
        self.cnt = {e: 0 for e in self.sem}
        self.pend = {e: False for e in self.sem}
        self.dsem = {q: [[es.enter_context(nc.semaphore("d_%s%d" % (q, i))), 0] for i in range(NDS)]
                     for q in ("sp", "pool")}
        self.dcnt = {"sp": 0, "pool": 0}
        self.seen = {e: {} for e in self.E}
        self.ps = []
        self.psB = []
        self.psrot = 0
        self.nrot = 6

    def _semh(self, key):
        if isinstance(key, str):
            return self.sem[key]
        return self.dsem[key[0]][key[1]][0]

    def _wait(self, e, toks):
        need = {}
        for key, v in toks:
            if need.get(key, 0) < v:
                need[key] = v
        for key, v in need.items():
            if key == e and (e == "pe" or not SELF_SYNC):
                continue
            if self.seen[e].get(key, 0) >= v:
                continue
            self.E[e].wait_ge(self._semh(key), v)
            self.seen[e][key] = v

    def _deps(self, reads, writes):
        toks = []
        for b in reads:
            if b.w is not None:
                toks.append(b.w)
        for b in writes:
            if b.w is not None:
                toks.append(b.w)
            toks.extend(b.r.items())
        return toks

    def _mark(self, tok, reads, writes):
        key, v = tok
        for b in reads:
            if b.r.get(key, 0) < v:
                b.r[key] = v
        for b in writes:
            b.w = tok
            b.r = {}

    def op(self, e, reads, writes, fn, inc=True):
        self._wait(e, self._deps(reads, writes))
        ins = fn(self.E[e])
        if inc:
            self.cnt[e] += 1
            ins.then_inc(self.sem[e], 1)
            self.pend[e] = False
            tok = (e, self.cnt[e])
        else:
            self.pend[e] = True
            tok = (e, self.cnt[e] + 1)
        self._mark(tok, reads, writes)
        return ins

    def dma(self, q, out, in_, reads, writes):
        self._wait(q, self._deps(reads, writes))
        i = self.dcnt[q] % NDS
        self.dcnt[q] += 1
        ent = self.dsem[q][i]
        key = (q, i)
        if ent[1] > 0:
            self._wait(q, [(key, 16 * ent[1])])
        ins = self.E[q].dma_start(out=out, in_=in_)
        ins.then_inc(ent[0], 16)
        ent[1] += 1
        tok = (key, 16 * ent[1])
        self._mark(tok, reads, writes)
        return tok

    def all_tokens(self):
        toks = [(e, c) for e, c in self.cnt.items() if c > 0]
        for q in self.dsem:
            for i, ent in enumerate(self.dsem[q]):
                if ent[1] > 0:
                    toks.append(((q, i), 16 * ent[1]))
        return toks

    def barrier(self):
        for e in self.sem:
            assert not self.pend[e], e
        toks = self.all_tokens()
        for e in self.E:
            self._wait(e, [t for t in toks if t[0] != e])

    def nextps(self):
        b = self.psrot
        self.psrot = (self.psrot + 1) % self.nrot
        return b


def build(nc, layers):
    es = ExitStack()
    with es:
        _build(nc, es, layers)
    return nc


def _build(nc, es, layers):
    k = KB(nc, es)
    op, dma = k.op, k.dma

    def dram_in(name, shape, dt=F32):
        return nc.dram_tensor(name, list(shape), dt, kind="ExternalInput").ap()

    uid = [0]

    def sb(st, name, shape, dt):
        uid[0] += 1
        return st.enter_context(nc.sbuf_tensor("%s_%d" % (name, uid[0]), list(shape), dt))

    xT_d = dram_in("xT", [D, L])
    outT_d = nc.dram_tensor("outT", [D, L], F32, kind="ExternalOutput").ap()
    gpre_d = dram_in("gpre", [P, 32])
    gpost_d = dram_in("gpost", [P, 32])
    ident_d = dram_in("ident", [P, P])
    dw = {}
    if 3 in layers:
        dw["d_w_in"] = dram_in("d_w_in", [D, 3072])
        dw["d_w_out"] = dram_in("d_w_out", [D, D])
        dw["d_ws"] = dram_in("d_ws", [P, 16, P])
        dw["d_tril"] = dram_in("d_tril", [P, P])
        dw["d_bs"] = dram_in("d_bs", [16, P])
        dw["d_lng"] = dram_in("d_lng", [1, D])
        dw["d_lnb"] = dram_in("d_lnb", [1, D])

    if 1 in layers:
        dw["b_w_in"] = dram_in("b_w_in", [D, 2432])
        dw["b_w_out"] = dram_in("b_w_out", [D, D])
        dw["b_biasT"] = dram_in("b_biasT", [P, 16, 256])
        dw["b_sinks"] = dram_in("b_sinks", [1, 16])

    if 2 in layers:
        dw["c_w1"] = dram_in("c_w1", [D, 1088])
        dw["c_wz"] = dram_in("c_wz", [D, D])
        dw["c_wuq"] = dram_in("c_wuq", [768, 2048])
        dw["c_wukv"] = dram_in("c_wukv", [256, 2048])
        dw["c_w_out"] = dram_in("c_w_out", [D, D])
        dw["c_rope"] = dram_in("c_rope", [64, L])
        dw["c_gqkv"] = dram_in("c_gqkv", [P, 8])
        dw["c_maskT"] = dram_in("c_maskT", [P, P])

    if 0 in layers:
        dw["a_wu"] = dram_in("a_wu", [D, D])
        dw["a_wz"] = dram_in("a_wz", [D, D])
        dw["a_wglu"] = dram_in("a_wglu", [D, D])
        dw["a_w_out"] = dram_in("a_w_out", [D, D])
        dw["a_dg"] = dram_in("a_dg", [P, 16])
        dw["a_lam"] = dram_in("a_lam", [P, 3, 32])
        for nm in ("a_brl", "a_bil", "a_crp", "a_cip"):
            dw[nm] = dram_in(nm, [P, 32, P])

    X = sb(es, "X", [P, 8, L], F32)
    xB = bufs(8, NBLK)
    ones = sb(es, "ones", [P, P], BF16)
    onesB = Buf()
    ident = sb(es, "identb", [P, P], BF16)
    identB = Buf()
    gpre = sb(es, "gpre_s", [P, 32], F32)
    gpost = sb(es, "gpost_s", [P, 32], F32)
    gB = Buf()
    rstd = sb(es, "rstd", [P, BW], F32)
    rstdB = Buf()
    for i in range(8):
        k.ps.append(es.enter_context(nc.psum_tensor("ps%d" % i, [P, BW], F32)))
        k.psB.append(Buf())
    ps, psB = k.ps, k.psB

    for c in range(8):
        for tb in range(NBLK):
            dma("sp", X[:, c, tb * BW:(tb + 1) * BW], xT_d[c * P:(c + 1) * P, tb * BW:(tb + 1) * BW], [], [xB[c][tb]])
    dma("sp", gpre[:], gpre_d, [], [gB])
    dma("sp", gpost[:], gpost_d, [], [gB])
    dma("pool", ident[:], ident_d, [], [identB])
    op("dve", [], [onesB], lambda e: e.memset(ones[:], 1.0))
    epsc = sb(es, "epsc", [P, 1], F32)
    op("dve", [], [onesB], lambda e: e.memset(epsc[:], EPS))

    def load_w(st, name, d_ap, ncols, q="pool"):
        K = d_ap.shape[0]
        kc_n = K // P
        t = sb(st, name, [P, kc_n, ncols], BF16)
        B = Buf()
        for kc in range(kc_n):
            for c0 in range(0, ncols, 2048):
                c1 = min(ncols, c0 + 2048)
                dma(q, t[:, kc, c0:c1], d_ap[kc * P:(kc + 1) * P, c0:c1], [], [B])
        return t, B

    def rstd_from_sq(sq, sqB, n, scale, c0=0):
        b = k.nextps()
        for c in range(n):
            op("pe", [sqB[c0 + c], onesB], [psB[b]],
               lambda e, c=c: e.matmul(ps[b][:, :], lhsT=ones[:, :], rhs=sq[:, c0 + c, :], start=(c == 0), stop=(c == n - 1)),
               inc=(c == n - 1))
        op("act", [psB[b]], [rstdB],
           lambda e: e.activation(out=rstd[:], in_=ps[b][:, :], func=AF.Sqrt, scale=scale, bias=epsc[:, 0:1]))
        op("dve", [rstdB], [rstdB], lambda e: e.reciprocal(out=rstd[:], in_=rstd[:]))

    def prenorm(li, tb, xn, xnB, tmpA, tmpAB):
        blk = slice(tb * BW, (tb + 1) * BW)
        for c in range(8):
            op("act", [xB[c][tb]], [tmpAB[c]],
               lambda e, c=c: e.activation(out=tmpA[:, c, :], in_=X[:, c, blk], func=AF.Square))
        rstd_from_sq(tmpA, tmpAB, 8, 1.0 / D)
        for c in range(8):
            op("dve", [xB[c][tb], rstdB, gB], [xnB[c]],
               lambda e, c=c: e.scalar_tensor_tensor(out=xn[:, c, :], in0=X[:, c, blk],
                                                     scalar=gpre[:, li * 8 + c:li * 8 + c + 1], in1=rstd[:],
                                                     op0=ALU.mult, op1=ALU.mult))

    def prenorm_ap(li, tb, xn_ap, xnB, tmpA, tmpAB):
        blk = slice(tb * BW, (tb + 1) * BW)
        for c in range(8):
            op("act", [xB[c][tb]], [tmpAB[c]],
               lambda e, c=c: e.activation(out=tmpA[:, c, :], in_=X[:, c, blk], func=AF.Square))
        rstd_from_sq(tmpA, tmpAB, 8, 1.0 / D)
        for c in range(8):
            op("dve", [xB[c][tb], rstdB, gB], [xnB[c]],
               lambda e, c=c: e.scalar_tensor_tensor(out=xn_ap(c), in0=X[:, c, blk],
                                                     scalar=gpre[:, li * 8 + c:li * 8 + c + 1], in1=rstd[:],
                                                     op0=ALU.mult, op1=ALU.mult))

    def proj_ap(w, wB, col0, M, rhs_ap, rhsB, nk):
        b = k.nextps()
        for kc in range(nk):
            op("pe", [wB, rhsB[kc]], [psB[b]],
               lambda e, kc=kc: e.matmul(ps[b][0:M, :], lhsT=w[:, kc, col0:col0 + M], rhs=rhs_ap(kc),
                                         start=(kc == 0), stop=(kc == nk - 1)),
               inc=(kc == nk - 1))
        return b

    def proj_fm(w, wB, col0, rhs, rhsB, nk, M=P):
        b = k.nextps()
        for kc in range(nk):
            op("pe", [wB, rhsB[kc]], [psB[b]],
               lambda e, kc=kc: e.matmul(ps[b][0:M, :], lhsT=w[:, kc, col0:col0 + M], rhs=rhs[:, kc, :],
                                         start=(kc == 0), stop=(kc == nk - 1)),
               inc=(kc == nk - 1))
        return b

    def outproj(li, tb, G, GB, wout, woutB, ybuf, ybufB, tmpA, tmpAB):
        blk = slice(tb * BW, (tb + 1) * BW)
        for m in range(8):
            b = proj_fm(wout, woutB, m * P, G, GB, 8)
            op("act", [psB[b]], [ybufB[m]], lambda e, m=m, b=b: e.activation(out=ybuf[:, m, :], in_=ps[b][:, :], func=AF.Copy))
            op("act", [psB[b]], [tmpAB[m]], lambda e, m=m, b=b: e.activation(out=tmpA[:, m, :], in_=ps[b][:, :], func=AF.Square))
        rstd_from_sq(tmpA, tmpAB, 8, 1.0 / D)
        for m in range(8):
            op("dve", [ybufB[m], rstdB, gB], [ybufB[m]],
               lambda e, m=m: e.scalar_tensor_tensor(out=ybuf[:, m, :], in0=ybuf[:, m, :],
                                                     scalar=gpost[:, li * 8 + m:li * 8 + m + 1], in1=rstd[:],
                                                     op0=ALU.mult, op1=ALU.mult))
            op("pool", [ybufB[m], xB[m][tb]], [xB[m][tb]],
               lambda e, m=m: e.tensor_tensor(out=X[:, m, blk], in0=X[:, m, blk], in1=ybuf[:, m, :], op=ALU.add))

    def layer_sgu(li):
        st = ExitStack()
        with st:
            w_in, w_inB = load_w(st, "d_win", dw["d_w_in"], 3072)
            w_out, w_outB = load_w(st, "d_wout", dw["d_w_out"], D)
            wsf = sb(st, "wsf", [P, 16, P], F32)
            wsfB = Buf()
            tril = sb(st, "tril", [P, P], F32)
            trilB = Buf()
            wsm = sb(st, "wsm", [P, 16, P], BF16)
            wsmB = Buf()
            wsT = sb(st, "wsT", [P, 16, P], BF16)
            wsTB = bufs(16)
            bsT = sb(st, "bsT", [P, 8, P], F32)
            bsTB = Buf()
            lng = sb(st, "lng", [P, D], F32)
            lnb = sb(st, "lnb", [P, D], F32)
            lnB = Buf()
            R1 = sb(st, "R1", [P, 8, BW], F32)
            R1B = bufs(8)
            R1bf = R1[:].bitcast(BF16)
            tmpA = sb(st, "tmpA", [P, 8, BW], BF16)
            tmpAB = bufs(8)
            gu = sb(st, "gu", [P, 8, BW], BF16)
            guB = bufs(8)
            sz = sb(st, "sz", [P, 8, BW], BF16)
            szB = bufs(8)
            vtmp = sb(st, "vtmp", [P, D], F32)
            vtmpB = Buf()
            stt = sb(st, "stt", [P, 2, 6], F32)
            mv = sb(st, "mv", [P, 2], F32)
            rs1 = sb(st, "rs1", [P, 1], F32)
            sttB = Buf()
            tmpS = sb(st, "tmpS", [P, BW], F32)
            tmpSB = Buf()

            def xn_ap(c):
                return R1bf[:, c // 2, (c % 2) * BW:(c % 2) * BW + BW]

            def vln_ap(ch, c0, c1):
                return R1bf[:, 4 + ch, c0:c1]

            xnB = [R1B[c // 2] for c in range(8)]

            dma("sp", wsf[:], dw["d_ws"], [], [wsfB])
            dma("sp", tril[:], dw["d_tril"], [], [trilB])
            for h in range(2):
                src = dw["d_bs"].rearrange("(gp h) t -> h gp t", h=2)[h]
                dma("sp", bsT[h * 64:(h + 1) * 64, :, :], src.unsqueeze(0).to_broadcast([64, 8, P]), [], [bsTB])
            dma("sp", lng[:], dw["d_lng"].to_broadcast([P, D]), [], [lnB])
            dma("sp", lnb[:], dw["d_lnb"].to_broadcast([P, D]), [], [lnB])
            op("dve", [wsfB, trilB], [wsmB],
               lambda e: e.tensor_tensor(out=wsm[:], in0=wsf[:], in1=tril[:].unsqueeze(1).to_broadcast([P, 16, P]), op=ALU.mult))
            for g in range(16):
                b = k.nextps()
                op("pe", [wsmB, identB], [psB[b]],
                   lambda e, g=g, b=b: e.matmul(ps[b][:, 0:P], lhsT=wsm[:, g, :], rhs=ident[:, :], start=True, stop=True))
                op("act", [psB[b]], [wsTB[g]], lambda e, g=g, b=b: e.activation(out=wsT[:, g, :], in_=ps[b][:, 0:P], func=AF.Copy))

            for tb in range(NBLK):
                blk = slice(tb * BW, (tb + 1) * BW)
                for c in range(8):
                    op("act", [xB[c][tb]], [tmpAB[c]],
                       lambda e, c=c: e.activation(out=tmpA[:, c, :], in_=X[:, c, blk], func=AF.Square))
                rstd_from_sq(tmpA, tmpAB, 8, 1.0 / D)
                for c in range(8):
                    op("dve", [xB[c][tb], rstdB, gB], [xnB[c]],
                       lambda e, c=c: e.scalar_tensor_tensor(out=xn_ap(c), in0=X[:, c, blk],
                                                             scalar=gpre[:, li * 8 + c:li * 8 + c + 1], in1=rstd[:],
                                                             op0=ALU.mult, op1=ALU.mult))
                for ch in range(4):
                    for half in range(2):
                        b = k.nextps()
                        for kc in range(8):
                            op("pe", [w_inB, xnB[kc]], [psB[b]],
                               lambda e, kc=kc, b=b: e.matmul(ps[b][:, :], lhsT=xn_ap(kc)[:, ch * P:(ch + 1) * P],
                                                              rhs=w_in[:, kc, D + half * BW:D + (half + 1) * BW],
                                                              start=(kc == 0), stop=(kc == 7)),
                               inc=(kc == 7))
                        op("act", [psB[b]], [vtmpB],
                           lambda e, b=b, half=half: e.activation(out=vtmp[:, half * BW:(half + 1) * BW], in_=ps[b][:, :],
                                                                  func=AF.Gelu_apprx_tanh))
                    for half in range(2):
                        op("dve", [vtmpB], [sttB], lambda e, half=half: e.bn_stats(out=stt[:, half, :], in_=vtmp[:, half * BW:(half + 1) * BW]))
                    op("dve", [sttB], [sttB], lambda e: e.bn_aggr(out=mv[:], in_=stt[:].rearrange("p a b -> p (a b)")))
                    op("act", [sttB], [sttB],
                       lambda e: e.activation(out=rs1[:], in_=mv[:, 1:2], func=AF.Sqrt, scale=1.0, bias=epsc[:, 0:1]))
                    op("dve", [sttB], [sttB], lambda e: e.reciprocal(out=rs1[:], in_=rs1[:]))
                    op("dve", [vtmpB, sttB], [vtmpB],
                       lambda e: e.tensor_scalar(out=vtmp[:], in0=vtmp[:], scalar1=mv[:, 0:1], scalar2=rs1[:, 0:1],
                                                 op0=ALU.subtract, op1=ALU.mult))
                    op("pool", [vtmpB, lnB], [vtmpB], lambda e: e.tensor_tensor(out=vtmp[:], in0=vtmp[:], in1=lng[:], op=ALU.mult))
                    op("pool", [vtmpB, lnB], [R1B[4 + ch]],
                       lambda e, ch=ch: e.tensor_tensor(out=vln_ap(ch, 0, D), in0=vtmp[:], in1=lnb[:], op=ALU.add))
                for m in range(8):
                    b = k.nextps()
                    for kc in range(8):
                        op("pe", [w_inB, xnB[kc]], [psB[b]],
                           lambda e, kc=kc, b=b, m=m: e.matmul(ps[b][:, :], lhsT=w_in[:, kc, m * P:(m + 1) * P], rhs=xn_ap(kc),
                                                               start=(kc == 0), stop=(kc == 7)), inc=(kc == 7))
                    op("act", [psB[b]], [guB[m]], lambda e, b=b, m=m: e.activation(out=gu[:, m, :], in_=ps[b][:, :], func=AF.Gelu_apprx_tanh))
                for m in range(8):
                    b = k.nextps()
                    for kc in range(8):
                        op("pe", [w_inB, xnB[kc]], [psB[b]],
                           lambda e, kc=kc, b=b, m=m: e.matmul(ps[b][:, :], lhsT=w_in[:, kc, 2 * D + m * P:2 * D + (m + 1) * P], rhs=xn_ap(kc),
                                                               start=(kc == 0), stop=(kc == 7)), inc=(kc == 7))
                    op("act", [psB[b]], [szB[m]], lambda e, b=b, m=m: e.activation(out=sz[:, m, :], in_=ps[b][:, :], func=AF.Silu))
                for m in range(8):
                    op("pool", [guB[m], szB[m]], [guB[m]], lambda e, m=m: e.tensor_tensor(out=gu[:, m, :], in0=gu[:, m, :], in1=sz[:, m, :], op=ALU.mult))
                for gp in range(8):
                    b = k.nextps()
                    n = 0
                    for ch in range(4):
                        for h in range(2):
                            g = 2 * gp + h
                            n += 1
                            op("pe", [R1B[4 + ch], wsTB[g]], [psB[b]],
                               lambda e, ch=ch, h=h, g=g, b=b: e.matmul(ps[b][h * 64:(h + 1) * 64, ch * P:(ch + 1) * P],
                                                                        lhsT=vln_ap(ch, g * 64, (g + 1) * 64), rhs=wsT[:, g, :],
                                                                        start=True, stop=True),
                               inc=(n == 8))
                    op("dve", [psB[b], bsTB], [tmpSB],
                       lambda e, b=b, gp=gp: e.tensor_tensor(out=tmpS[:].rearrange("p (a t) -> p a t", a=4),
                                                             in0=ps[b][:, :].rearrange("p (a t) -> p a t", a=4),
                                                             in1=bsT[:, gp, :].unsqueeze(1).to_broadcast([P, 4, P]), op=ALU.add))
                    op("dve", [tmpSB, guB[gp]], [guB[gp]],
                       lambda e, gp=gp: e.tensor_tensor(out=gu[:, gp, :], in0=tmpS[:], in1=gu[:, gp, :], op=ALU.mult))
                outproj(li, tb, gu, guB, w_out, w_outB, R1, R1B, tmpA, tmpAB)
            k.barrier()

    def layer_swa(li):
        st = ExitStack()
        with st:
            NC_IN = 2432
            w_in, w_inB = load_w(st, "b_win", dw["b_w_in"], NC_IN)
            w_out, w_outB = load_w(st, "b_wout", dw["b_w_out"], D)
            KT2 = sb(st, "KT2", [P, 2, L], BF16)
            KTB = bufs(2, NBLK)
            VA = [[sb(st, "VA%d%d" % (kv, par), [P, 16, P], BF16) for par in range(2)] for kv in range(2)]
            VAB = bufs(2, 2, 16)
            biasT = sb(st, "biasT", [P, 16, 256], F32)
            biasB = Buf()
            esink = sb(st, "esink", [P, 16], F32)
            esinkB = Buf()
            R1 = sb(st, "R1", [P, 8, BW], F32)
            R1B = bufs(8)
            R1bf = R1[:].bitcast(BF16)
            tmpA = sb(st, "tmpA", [P, 8, BW], BF16)
            tmpAB = bufs(8)
            sz = sb(st, "sz", [P, 8, BW], BF16)
            szB = bufs(8)
            NPB = 3
            apc = [0]
            tmpP = sb(st, "tmpP", [P, NPB, 256], F32)
            tmpPB = bufs(NPB)
            PT = sb(st, "PT", [P, NPB, 256], BF16)
            PTB = bufs(NPB)
            rsb = sb(st, "rsb", [P, BW], F32)
            rsbB = bufs(2)
            tG = sb(st, "tG", [P, BW], F32)
            tGB = bufs(2)

            def xn_ap(c):
                return R1bf[:, c // 2, (c % 2) * BW:(c % 2) * BW + BW]
            xnB = [R1B[c // 2] for c in range(8)]

            def qt_ap(j, p0, p1, c0, c1):
                return R1bf[p0:p1, 4 + j // 2, (j % 2) * BW + c0:(j % 2) * BW + c1]
            qtB = [R1B[4 + j // 2] for j in range(8)]

            import os
            SK = os.environ.get("SKIP", "")
            if "a" not in SK:
                dma("sp", biasT[:], dw["b_biasT"], [], [biasB])
            if "b" not in SK:
                dma("sp", esink[:], dw["b_sinks"].to_broadcast([P, 16]), [], [esinkB])
                op("act", [esinkB], [esinkB], lambda e: e.activation(out=esink[:], in_=esink[:], func=AF.Exp))
            for kv in range(2):
                for par in range(2):
                    c0 = 64 if par == 0 else 0
                    if "c" not in SK:
                        op("pool", [], sum([[VAB[kv][par][t]] for t in range(16)], []),
                           lambda e, kv=kv, par=par, c0=c0: e.memset(VA[kv][par][:, :, c0:c0 + 64], 1.0))

            for tb in range(NBLK):
                blk = slice(tb * BW, (tb + 1) * BW)
                prenorm_ap(li, tb, xn_ap, xnB, tmpA, tmpAB)
                STG = int(os.environ.get("STG", "9"))
                for j in range(8 if STG >= 1 else 0):
                    b = proj_ap(w_in, w_inB, j * P, P, xn_ap, xnB, 8)
                    op("act", [psB[b]], [qtB[j]], lambda e, b=b, j=j: e.activation(out=qt_ap(j, 0, P, 0, BW), in_=ps[b][:, :], func=AF.Copy))
                for kv in range(2 if STG >= 2 else 0):
                    b = proj_ap(w_in, w_inB, D + kv * P, P, xn_ap, xnB, 8)
                    op("act", [psB[b]], [KTB[kv][tb]], lambda e, b=b, kv=kv: e.activation(out=KT2[:, kv, blk], in_=ps[b][:, :], func=AF.Copy))
                for ch in range(4 if STG >= 3 else 0):
                    tt = tb * 4 + ch
                    b = k.nextps()
                    for kc in range(8):
                        op("pe", [w_inB, xnB[kc]], [psB[b]],
                           lambda e, kc=kc, b=b, ch=ch: e.matmul(ps[b][:, 0:P], lhsT=xn_ap(kc)[:, ch * P:(ch + 1) * P],
                                                                 rhs=w_in[:, kc, D + 256:D + 384], start=(kc == 0), stop=(kc == 7)),
                           inc=(kc == 7))
                    for kv in range(2):
                        op("act", [psB[b]], [VAB[kv][0][tt]],
                           lambda e, b=b, kv=kv, tt=tt: e.activation(out=VA[kv][0][:, tt, 0:64], in_=ps[b][:, kv * 64:(kv + 1) * 64], func=AF.Copy))
                        op("act", [psB[b]], [VAB[kv][1][tt]],
                           lambda e, b=b, kv=kv, tt=tt: e.activation(out=VA[kv][1][:, tt, 64:128], in_=ps[b][:, kv * 64:(kv + 1) * 64], func=AF.Copy))
                for m in range(8):
                    b = proj_ap(w_in, w_inB, D + 384 + m * P, P, xn_ap, xnB, 8)
                    op("act", [psB[b]], [szB[m]], lambda e, b=b, m=m: e.activation(out=sz[:, m, :], in_=ps[b][:, :], func=AF.Silu))
                def scores(h, nbl, pi):
                    kv, par, j = h // 8, h % 2, h // 2
                    base = par * 64
                    nb = tb * 4 + nbl
                    b = k.nextps()
                    c_lo = 0 if nb > 0 else P
                    rhs_q = qt_ap(j, base, base + 64, nbl * P, (nbl + 1) * P)
                    if nb > 0:
                        tbp = (nb - 1) // 4
                        op("pe", [KTB[kv][tbp], qtB[j]], [psB[b]],
                           lambda e: e.matmul(ps[b][:, 0:P], lhsT=KT2[base:base + 64, kv, (nb - 1) * P:nb * P], rhs=rhs_q, start=True, stop=True),
                           inc=False)
                    op("pe", [KTB[kv][tb], qtB[j]], [psB[b]],
                       lambda e: e.matmul(ps[b][:, P:2 * P], lhsT=KT2[base:base + 64, kv, nb * P:(nb + 1) * P], rhs=rhs_q, start=True, stop=True))
                    op("dve", [psB[b], biasB], [tmpPB[pi]],
                       lambda e: e.scalar_tensor_tensor(out=tmpP[:, pi, c_lo:256], in0=ps[b][:, c_lo:256], scalar=0.125, in1=biasT[:, h, c_lo:256],
                                                        op0=ALU.mult, op1=ALU.add))
                    op("act", [tmpPB[pi]], [PTB[pi]],
                       lambda e: e.activation(out=PT[:, pi, c_lo:256], in_=tmpP[:, pi, c_lo:256], func=AF.Exp))

                def pv_epi(h, nbl, pi):
                    kv, par, j = h // 8, h % 2, h // 2
                    base = par * 64
                    oth = 64 - base
                    bo = 6 + (h % 2)
                    nb = tb * 4 + nbl
                    if nb > 0:
                        op("pe", [PTB[pi], VAB[kv][par][nb - 1]], [psB[bo]],
                           lambda e: e.matmul(ps[bo][:, nbl * P:(nbl + 1) * P], lhsT=VA[kv][par][:, nb - 1, :], rhs=PT[:, pi, 0:P], start=True, stop=False),
                           inc=False)
                    op("pe", [PTB[pi], VAB[kv][par][nb]], [psB[bo]],
                       lambda e: e.matmul(ps[bo][:, nbl * P:(nbl + 1) * P], lhsT=VA[kv][par][:, nb, :], rhs=PT[:, pi, P:2 * P], start=(nb == 0), stop=True))
                    if nbl == 3:
                        op("dve", [psB[bo], esinkB], [rsbB[par]],
                           lambda e: e.tensor_scalar(out=rsb[base:base + 64, :], in0=ps[bo][oth:oth + 64, :], scalar1=esink[oth:oth + 64, h:h + 1],
                                                     scalar2=None, op0=ALU.add))
                        op("dve", [rsbB[par]], [rsbB[par]], lambda e: e.reciprocal(out=rsb[base:base + 64, :], in_=rsb[base:base + 64, :]))
                        op("dve", [psB[bo], rsbB[par]], [tGB[par]],
                           lambda e: e.tensor_tensor(out=tG[base:base + 64, :], in0=ps[bo][base:base + 64, :], in1=rsb[base:base + 64, :], op=ALU.mult))
                        op("pool", [tGB[par], szB[j]], [szB[j]],
                           lambda e: e.tensor_tensor(out=sz[base:base + 64, j, :], in0=tG[base:base + 64, :], in1=sz[base:base + 64, j, :], op=ALU.mult))

                import os
                tasks = [(h, nbl) for h in range(int(os.environ.get('SWA_NH', '16'))) for nbl in range(4)]
                if tasks:
                    scores(tasks[0][0], tasks[0][1], apc[0] % NPB)
                for i, (h, nbl) in enumerate(tasks):
                    pi = apc[0] % NPB
                    apc[0] += 1
                    if i + 1 < len(tasks):
                        scores(tasks[i + 1][0], tasks[i + 1][1], apc[0] % NPB)
                    pv_epi(h, nbl, pi)
                outproj(li, tb, sz, szB, w_out, w_outB, R1, R1B, tmpA, tmpAB)
            k.barrier()

    def layer_mla(li):
        SC = 96.0 ** -0.5
        st = ExitStack()
        with st:
            GT = sb(st, "GT", [P, 8, L], BF16)
            GTB = bufs(8, NBLK)
            sA = ExitStack()
            sA.__enter__()
            CQN = sb(sA, "CQN", [P, 6, L], BF16)
            CQNB = bufs(6, NBLK)
            CKVN = sb(sA, "CKVN", [P, 2, L], BF16)
            CKVNB = bufs(2, NBLK)
            KR = sb(sA, "KR", [32, L], BF16)
            KRB = bufs(NBLK)
            ROPE = sb(sA, "ROPE", [64, L], F32)
            ropeB = Buf()
            gq = sb(sA, "gq", [P, 8], F32)
            gqB = Buf()
            dma("sp", ROPE[:], dw["c_rope"], [], [ropeB])
            dma("sp", gq[:], dw["c_gqkv"], [], [gqB])
            tR = sb(sA, "tR", [32, 2, BW], F32)
            tRB = bufs(2)
            s1 = ExitStack()
            with s1:
                w1, w1B = load_w(s1, "s_c_w1", dw["c_w1"], 1088)
                R1 = sb(s1, "R1", [P, 8, BW], F32)
                R1B = bufs(8)
                xnt = sb(s1, "xnt", [P, 8, BW], BF16)
                xnB = bufs(8)
                tmpA = sb(s1, "tmpA", [P, 8, BW], BF16)
                tmpAB = bufs(8)
                xn_ap = lambda c: xnt[:, c, :]
                for tb in range(NBLK):
                    blk = slice(tb * BW, (tb + 1) * BW)
                    prenorm_ap(li, tb, xn_ap, xnB, tmpA, tmpAB)
                    for m in range(8):
                        b = proj_ap(w1, w1B, m * P, P, xn_ap, xnB, 8)
                        op("act", [psB[b]], [R1B[m]], lambda e, b=b, m=m: e.activation(out=R1[:, m, :], in_=ps[b][:, :], func=AF.Copy))
                        op("act", [psB[b]], [tmpAB[m]], lambda e, b=b, m=m: e.activation(out=tmpA[:, m, :], in_=ps[b][:, :], func=AF.Square))
                    rstd_from_sq(tmpA, tmpAB, 6, 1.0 / 768, 0)
                    for m in range(6):
                        op("dve", [R1B[m], rstdB, gqB], [CQNB[m][tb]],
                           lambda e, m=m: e.scalar_tensor_tensor(out=CQN[:, m, blk], in0=R1[:, m, :], scalar=gq[:, m:m + 1], in1=rstd[:],
                                                                 op0=ALU.mult, op1=ALU.mult))
                    rstd_from_sq(tmpA, tmpAB, 2, 1.0 / 256, 6)
                    for m in range(2):
                        op("dve", [R1B[6 + m], rstdB, gqB], [CKVNB[m][tb]],
                           lambda e, m=m: e.scalar_tensor_tensor(out=CKVN[:, m, blk], in0=R1[:, 6 + m, :], scalar=gq[:, 6 + m:7 + m], in1=rstd[:],
                                                                 op0=ALU.mult, op1=ALU.mult))
                    b = proj_ap(w1, w1B, D, 64, xn_ap, xnB, 8)
                    op("dve", [psB[b], ropeB], [tRB[0]],
                       lambda e, b=b: e.tensor_tensor(out=tR[:, 0, :], in0=ps[b][32:64, :], in1=ROPE[32:64, blk], op=ALU.mult))
                    op("dve", [psB[b], ropeB], [tRB[1]],
                       lambda e, b=b: e.tensor_tensor(out=tR[:, 1, :], in0=ps[b][0:32, :], in1=ROPE[0:32, blk], op=ALU.mult))
                    op("pool", [tRB[0], tRB[1]], [KRB[tb]],
                       lambda e: e.tensor_tensor(out=KR[:, blk], in0=tR[:, 0, :], in1=tR[:, 1, :], op=ALU.add))
                k.barrier()
            s2 = ExitStack()
            with s2:
                wuq, wuqB = load_w(s2, "s_c_wuq", dw["c_wuq"], 2048)
                wukv, wukvB = load_w(s2, "s_c_wukv", dw["c_wukv"], 2048)
                KTh = [sb(s2, "KTh%d" % i, [P, L], BF16) for i in range(2)]
                KThB = bufs(2, NBLK)
                VAh = [sb(s2, "VAh%d" % i, [P, 16, P], BF16) for i in range(2)]
                VAhB = bufs(2, 4)
                QT = [sb(s2, "QT%d" % i, [P, BW], BF16) for i in range(2)]
                QTB = bufs(2)
                NPT = 4
                PT = [sb(s2, "PT%d" % i, [P, BW], BF16) for i in range(NPT)]
                PTB = bufs(NPT)
                QF = sb(s2, "QF", [64, BW], F32)
                QFB = Buf()
                maskT = sb(s2, "maskT", [P, P], BF16)
                maskB = Buf()
                rsb = sb(s2, "rsb", [P, BW], F32)
                rsbB = bufs(2)
                dma("pool", maskT[:], dw["c_maskT"], [], [maskB])
                for i in range(2):
                    op("pool", [], KThB[i], lambda e, i=i: e.memset(KTh[i][32:64, :], 0.0))
                    c0 = 64 if i == 0 else 0
                    op("pool", [], VAhB[i], lambda e, i=i, c0=c0: e.memset(VAh[i][:, :, c0:c0 + 64], 1.0))
                def kv_build(h):
                    par = h % 2
                    vc0 = par * 64
                    for tb in range(NBLK):
                        blk = slice(tb * BW, (tb + 1) * BW)
                        b = k.nextps()
                        for kc in range(2):
                            op("pe", [wukvB, CKVNB[kc][tb]], [psB[b]],
                               lambda e, kc=kc, b=b, blk=blk: e.matmul(ps[b][64:128, :], lhsT=wukv[:, kc, h * P:h * P + 64], rhs=CKVN[:, kc, blk],
                                                                      start=(kc == 0), stop=(kc == 1)), inc=(kc == 1))
                        op("act", [psB[b]], [KThB[par][tb]],
                           lambda e, b=b, blk=blk: e.activation(out=KTh[par][64:128, blk], in_=ps[b][64:128, :], func=AF.Copy))
                        op("dve", [KRB[tb]], [KThB[par][tb]],
                           lambda e, blk=blk: e.tensor_scalar(out=KTh[par][0:32, blk], in0=KR[:, blk], scalar1=1.0, scalar2=None, op0=ALU.mult))
                        b = k.nextps()
                        for i4 in range(4):
                            tt = tb * 4 + i4
                            for kc in range(2):
                                op("pe", [wukvB, CKVNB[kc][tb]], [psB[b]],
                                   lambda e, kc=kc, b=b, tt=tt, i4=i4: e.matmul(ps[b][:, i4 * 64:(i4 + 1) * 64], lhsT=CKVN[:, kc, tt * P:(tt + 1) * P],
                                                                                rhs=wukv[:, kc, h * P + 64:h * P + 128], start=(kc == 0), stop=(kc == 1)),
                                   inc=(kc == 1 and i4 == 3))
                        op("act", [psB[b]], [VAhB[par][tb]],
                           lambda e, b=b, tb=tb: e.activation(out=VAh[par][:, tb * 4:(tb + 1) * 4, vc0:vc0 + 64],
                                                              in_=ps[b][:, 0:256].rearrange("p (a c) -> p a c", a=4), func=AF.Copy))

                def q_prep(h, qb, qi):
                    qblk = slice(qb * BW, (qb + 1) * BW)
                    b = k.nextps()
                    for kc in range(6):
                        op("pe", [wuqB, CQNB[kc][qb]], [psB[b]],
                           lambda e, kc=kc, b=b: e.matmul(ps[b][:, :], lhsT=wuq[:, kc, h * P:(h + 1) * P], rhs=CQN[:, kc, qblk],
                                                          start=(kc == 0), stop=(kc == 5)), inc=(kc == 5))
                    op("act", [psB[b]], [QTB[qi]], lambda e, b=b: e.activation(out=QT[qi][:, :], in_=ps[b][:, :], func=AF.Copy))
                    op("act", [psB[b]], [QFB], lambda e, b=b: e.activation(out=QF[:, :], in_=ps[b][0:64, :], func=AF.Copy))
                    op("dve", [QFB, ropeB], [tRB[0]],
                       lambda e: e.tensor_tensor(out=tR[:, 0, :], in0=QF[32:64, :], in1=ROPE[32:64, qblk], op=ALU.mult))
                    op("dve", [QFB, ropeB], [tRB[1]],
                       lambda e: e.tensor_tensor(out=tR[:, 1, :], in0=QF[0:32, :], in1=ROPE[0:32, qblk], op=ALU.mult))
                    op("dve", [tRB[0], tRB[1]], [QTB[qi]],
                       lambda e: e.tensor_tensor(out=QT[qi][0:32, :], in0=tR[:, 0, :], in1=tR[:, 1, :], op=ALU.add))

                pti = [0]

                def attend(h, qb, qi, bo):
                    par, j = h % 2, h // 2
                    base = par * 64
                    oth = 64 - base
                    qblk = slice(qb * BW, (qb + 1) * BW)
                    nkc = 4 * qb + 4

                    def pv(kc, pi, q_lo):
                        op("pe", [PTB[pi], VAhB[par][kc // 4]], [psB[bo]],
                           lambda e: e.matmul(ps[bo][:, q_lo:BW], lhsT=VAh[par][:, kc, :], rhs=PT[pi][:, q_lo:BW],
                                              start=(kc == 0), stop=(kc == nkc - 1)))
                    prev = None
                    for kc in range(nkc):
                        q_lo = max(0, kc - 4 * qb) * P
                        pi = pti[0] % NPT
                        pti[0] += 1
                        b = k.nextps()
                        op("pe", [KThB[par][kc // 4], QTB[qi]], [psB[b]],
                           lambda e, b=b, kc=kc, q_lo=q_lo: e.matmul(ps[b][:, q_lo:BW], lhsT=KTh[par][:, kc * P:(kc + 1) * P],
                                                                     rhs=QT[qi][:, q_lo:BW], start=True, stop=True))
                        op("act", [psB[b]], [PTB[pi]],
                           lambda e, b=b, pi=pi, q_lo=q_lo: e.activation(out=PT[pi][:, q_lo:BW], in_=ps[b][:, q_lo:BW], func=AF.Exp, scale=SC))
                        if kc >= 4 * qb:
                            op("dve", [PTB[pi], maskB], [PTB[pi]],
                               lambda e, pi=pi, q_lo=q_lo: e.tensor_tensor(out=PT[pi][:, q_lo:q_lo + P], in0=PT[pi][:, q_lo:q_lo + P],
                                                                           in1=maskT[:, :], op=ALU.mult))
                        if prev is not None:
                            pv(*prev)
                        prev = (kc, pi, q_lo)
                    pv(*prev)
                    op("dve", [psB[bo]], [rsbB[par]],
                       lambda e: e.reciprocal(out=rsb[base:base + 64, :], in_=ps[bo][oth:oth + 64, :]))
                    op("dve", [psB[bo], rsbB[par]], [GTB[j][qb]],
                       lambda e: e.tensor_tensor(out=GT[base:base + 64, j, qblk], in0=ps[bo][base:base + 64, :],
                                                 in1=rsb[base:base + 64, :], op=ALU.mult))

                import os
                NH = int(os.environ.get("MLA_NH", "16"))
                tasks = [(h, qb) for h in range(NH) for qb in range(NBLK)]
                if tasks:
                    kv_build(0)
                    q_prep(0, 0, 0)
                for i, (h, qb) in enumerate(tasks):
                    if i + 1 < len(tasks):
                        h2, qb2 = tasks[i + 1]
                        if h2 != h:
                            kv_build(h2)
                        q_prep(h2, qb2, (i + 1) % 2)
                    attend(h, qb, i % 2, 6 + (i % 2))
                k.barrier()
            sA.close()
            s3 = ExitStack()
            with s3:
                wz, wzB = load_w(s3, "s_c_wz", dw["c_wz"], D)
                w_out, w_outB = load_w(s3, "s_c_wout", dw["c_w_out"], D)
                R1 = sb(s3, "R1", [P, 8, BW], F32)
                R1B = bufs(8)
                xnt = sb(s3, "xnt", [P, 8, BW], BF16)
                xnB = bufs(8)
                tmpA = sb(s3, "tmpA", [P, 8, BW], BF16)
                tmpAB = bufs(8)
                sz = sb(s3, "sz", [P, 8, BW], BF16)
                szB = bufs(8)
                xn_ap = lambda c: xnt[:, c, :]
                for tb in range(NBLK):
                    blk = slice(tb * BW, (tb + 1) * BW)
                    prenorm_ap(li, tb, xn_ap, xnB, tmpA, tmpAB)
                    for m in range(8):
                        b = proj_ap(wz, wzB, m * P, P, xn_ap, xnB, 8)
                        op("act", [psB[b]], [szB[m]], lambda e, b=b, m=m: e.activation(out=sz[:, m, :], in_=ps[b][:, :], func=AF.Silu))
                        op("pool", [szB[m], GTB[m][tb]], [szB[m]],
                           lambda e, m=m, blk=blk: e.tensor_tensor(out=sz[:, m, :], in0=sz[:, m, :], in1=GT[:, m, blk], op=ALU.mult))
                    outproj(li, tb, sz, szB, w_out, w_outB, R1, R1B, tmpA, tmpAB)
                k.barrier()

    def layer_s5(li):
        TT = ALU
        st = ExitStack()
        with st:
            UY = sb(st, "UY", [P, 8, L], BF16)
            UYB = bufs(8, NBLK)
            dT = sb(st, "dT", [P, 16], F32)
            dTB = Buf()
            dma("sp", dT[:], dw["a_dg"], [], [dTB])
            s1 = ExitStack()
            with s1:
                wu, wuB = load_w(s1, "a_wu", dw["a_wu"], D)
                xnt = sb(s1, "xnt", [P, 8, BW], BF16)
                xnB = bufs(8)
                tmpA = sb(s1, "tmpA", [P, 8, BW], BF16)
                tmpAB = bufs(8)
                xn_ap = lambda c: xnt[:, c, :]
                for tb in range(NBLK):
                    blk = slice(tb * BW, (tb + 1) * BW)
                    prenorm_ap(li, tb, xn_ap, xnB, tmpA, tmpAB)
                    for m in range(8):
                        b = proj_ap(wu, wuB, m * P, P, xn_ap, xnB, 8)
                        op("act", [psB[b]], [UYB[m][tb]], lambda e, b=b, m=m, blk=blk: e.activation(out=UY[:, m, blk], in_=ps[b][:, :], func=AF.Copy))
                k.barrier()
            s2 = ExitStack()
            with s2:
                NJ = 32
                BrL, BrLB = sb(s2, "BrL", [P, NJ, P], BF16), Buf()
                BiL, BiLB = sb(s2, "BiL", [P, NJ, P], BF16), Buf()
                CrP, CrPB = sb(s2, "CrP", [P, NJ, P], BF16), Buf()
                CiP, CiPB = sb(s2, "CiP", [P, NJ, P], BF16), Buf()
                for t_, B_, nm in ((BrL, BrLB, "a_brl"), (BiL, BiLB, "a_bil"), (CrP, CrPB, "a_crp"), (CiP, CiPB, "a_cip")):
                    for q4 in range(4):
                        dma("pool", t_[:, q4 * 8:(q4 + 1) * 8, :], dw[nm][:, q4 * 8:(q4 + 1) * 8, :], [], [B_])
                lam = sb(s2, "lam", [P, 3, NJ], F32)
                prepB = Buf()
                dma("sp", lam[:], dw["a_lam"], [], [prepB])
                W = {}
                for nm in ("dt", "lrdt", "th", "mag", "t", "t2", "c", "s", "q", "c2", "s2", "cs", "ar", "ai", "den", "nr", "fr", "fi",
                           "u1", "u2", "ir", "ii", "pr", "pi", "A128r", "nA128i", "A128i"):
                    W[nm] = sb(s2, "w_" + nm, [P, NJ], F32)
                lr, li_, ldt = lam[:, 0, :], lam[:, 1, :], lam[:, 2, :]

                def tt(o, a, b_, o_):
                    op("dve", [prepB], [prepB], lambda e: e.tensor_tensor(out=o, in0=a, in1=b_, op=o_))

                def ts(o, a, s1_, s2_, o1, o2=None):
                    if o2 is None:
                        op("dve", [prepB], [prepB], lambda e: e.tensor_scalar(out=o, in0=a, scalar1=s1_, scalar2=None, op0=o1))
                    else:
                        op("dve", [prepB], [prepB], lambda e: e.tensor_scalar(out=o, in0=a, scalar1=s1_, scalar2=s2_, op0=o1, op1=o2))

                def stt(o, a, sc, b_, o1, o2):
                    op("dve", [prepB], [prepB], lambda e: e.scalar_tensor_tensor(out=o, in0=a, scalar=sc, in1=b_, op0=o1, op1=o2))

                def csq(cr, ci):
                    tt(W["c2"][:], cr, cr, TT.mult)
                    tt(W["s2"][:], ci, ci, TT.mult)
                    tt(W["cs"][:], cr, ci, TT.mult)
                    tt(cr, W["c2"][:], W["s2"][:], TT.subtract)
                    ts(ci, W["cs"][:], 2.0, None, TT.mult)

                op("act", [prepB], [prepB], lambda e: e.activation(out=W["dt"][:], in_=ldt, func=AF.Exp))
                tt(W["lrdt"][:], lr, W["dt"][:], TT.mult)
                tt(W["th"][:], li_, W["dt"][:], TT.mult)
                op("act", [prepB], [prepB], lambda e: e.activation(out=W["mag"][:], in_=W["lrdt"][:], func=AF.Exp))
                ts(W["t"][:], W["th"][:], 1.0 / 64, None, TT.mult)
                tt(W["t2"][:], W["t"][:], W["t"][:], TT.mult)
                ts(W["q"][:], W["t2"][:], -1.0 / 720, None, TT.mult)
                stt(W["q"][:], W["q"][:], 1.0 / 24, W["t2"][:], TT.add, TT.mult)
                stt(W["q"][:], W["q"][:], -0.5, W["t2"][:], TT.add, TT.mult)
                ts(W["c"][:], W["q"][:], 1.0, None, TT.add)
                ts(W["q"][:], W["t2"][:], -1.0 / 5040, None, TT.mult)
                stt(W["q"][:], W["q"][:], 1.0 / 120, W["t2"][:], TT.add, TT.mult)
                stt(W["q"][:], W["q"][:], -1.0 / 6, W["t2"][:], TT.add, TT.mult)
                stt(W["s"][:], W["q"][:], 1.0, W["t"][:], TT.add, TT.mult)
                for _ in range(6):
                    csq(W["c"][:], W["s"][:])
                tt(W["ar"][:], W["mag"][:], W["c"][:], TT.mult)
                tt(W["ai"][:], W["mag"][:], W["s"][:], TT.mult)
                tt(W["den"][:], lr, lr, TT.mult)
                tt(W["u1"][:], li_, li_, TT.mult)
                tt(W["den"][:], W["den"][:], W["u1"][:], TT.add)
                op("dve", [prepB], [prepB], lambda e: e.reciprocal(out=W["den"][:], in_=W["den"][:]))
                ts(W["nr"][:], W["ar"][:], -1.0, None, TT.add)
                tt(W["u1"][:], W["nr"][:], lr, TT.mult)
                tt(W["u2"][:], W["ai"][:], li_, TT.mult)
                tt(W["u1"][:], W["u1"][:], W["u2"][:], TT.add)
                tt(W["fr"][:], W["u1"][:], W["den"][:], TT.mult)
                tt(W["u1"][:], W["ai"][:], lr, TT.mult)
                tt(W["u2"][:], W["nr"][:], li_, TT.mult)
                tt(W["u1"][:], W["u1"][:], W["u2"][:], TT.subtract)
                tt(W["fi"][:], W["u1"][:], W["den"][:], TT.mult)
                tt(W["u1"][:], W["ar"][:], W["ar"][:], TT.mult)
                tt(W["u2"][:], W["ai"][:], W["ai"][:], TT.mult)
                tt(W["u1"][:], W["u1"][:], W["u2"][:], TT.add)
                op("dve", [prepB], [prepB], lambda e: e.reciprocal(out=W["u1"][:], in_=W["u1"][:]))
                tt(W["ir"][:], W["ar"][:], W["u1"][:], TT.mult)
                tt(W["ii"][:], W["ai"][:], W["u1"][:], TT.mult)
                ts(W["ii"][:], W["ii"][:], -1.0, None, TT.mult)
                TPr = sb(s2, "TPr", [P, NJ, P], BF16)
                TPi = sb(s2, "TPi", [P, NJ, P], BF16)
                TNr = sb(s2, "TNr", [P, NJ, P], BF16)
                TNi = sb(s2, "TNi", [P, NJ, P], BF16)
                sT = ExitStack()
                sT.__enter__()
                m1 = sb(sT, "m1t", [P, NJ, 64], F32)
                m2 = sb(sT, "m2t", [P, NJ, 64], F32)
                op("dve", [prepB], [prepB], lambda e: e.memset(TPr[:, :, 0:1], 1.0))
                op("dve", [prepB], [prepB], lambda e: e.memset(TPi[:, :, 0:1], 0.0))
                ts(TNr[:, :, 0:1], W["fr"][:].unsqueeze(2), 1.0, None, TT.mult)
                ts(TNi[:, :, 0:1], W["fi"][:].unsqueeze(2), 1.0, None, TT.mult)
                for (Tr_, Ti_, pr0, pi0) in ((TPr, TPi, "ar", "ai"), (TNr, TNi, "ir", "ii")):
                    ts(W["pr"][:], W[pr0][:], 1.0, None, TT.mult)
                    ts(W["pi"][:], W[pi0][:], 1.0, None, TT.mult)
                    for kk in range(7):
                        n = 1 << kk
                        pr_b = W["pr"][:].unsqueeze(2).to_broadcast([P, NJ, n])
                        pi_b = W["pi"][:].unsqueeze(2).to_broadcast([P, NJ, n])
                        lo_r, lo_i = Tr_[:, :, 0:n], Ti_[:, :, 0:n]
                        tt(m1[:, :, 0:n], lo_r, pr_b, TT.mult)
                        tt(m2[:, :, 0:n], lo_i, pi_b, TT.mult)
                        tt(Tr_[:, :, n:2 * n], m1[:, :, 0:n], m2[:, :, 0:n], TT.subtract)
                        tt(m1[:, :, 0:n], lo_r, pi_b, TT.mult)
                        tt(m2[:, :, 0:n], lo_i, pr_b, TT.mult)
                        tt(Ti_[:, :, n:2 * n], m1[:, :, 0:n], m2[:, :, 0:n], TT.add)
                        csq(W["pr"][:], W["pi"][:])
                    if pr0 == "ar":
                        ts(W["A128r"][:], W["pr"][:], 1.0, None, TT.mult)
                        ts(W["nA128i"][:], W["pi"][:], -1.0, None, TT.mult)
                        ts(W["A128i"][:], W["pi"][:], 1.0, None, TT.mult)
                k.barrier()
                sT.close()
                car = sb(s2, "car", [P, 2, NJ], F32)
                carB = bufs(NJ)
                op("dve", [prepB], carB, lambda e: e.memset(car[:], 0.0))
                rmask = sb(s2, "rmask", [P, 2, 4, P], F32)
                op("dve", [prepB], [prepB], lambda e: e.memset(rmask[:], 1.0))
                op("dve", [prepB], [prepB], lambda e: e.memset(rmask[:, :, :, 0:1], 0.0))
                op("dve", [prepB], [prepB], lambda e: e.tensor_scalar(out=TNi[:], in0=TNi[:], scalar1=-1.0, scalar2=None, op0=TT.mult))
                NA, NCD, NQ = 2, 4, 2
                TA = [sb(s2, "TA%d" % i, [P, 4, P], BF16) for i in range(NA)]
                TBf = [sb(s2, "TB%d" % i, [P, 4, P], BF16) for i in range(NA)]
                CD = [sb(s2, "CD%d" % i, [P, 2, 4, P], F32) for i in range(NCD)]
                T1 = sb(s2, "T1", [P, 4, P], F32)
                T2 = sb(s2, "T2", [P, 4, P], F32)
                Q = [[sb(s2, "Q%d_%d" % (i, q_), [P, 4, P], BF16) for q_ in range(4)] for i in range(NQ)]
                AB_, BB_, CDB = bufs(NA), bufs(NA), bufs(NCD)
                T1B, T2B = Buf(), Buf()
                QB = bufs(NQ, 4)
                ea = sb(s2, "ea", [P, 2, 4], F32)
                eb = sb(s2, "eb", [P, 2, 4], F32)
                eB = Buf()
                ytmp = sb(s2, "ytmp", [P, BW], F32)
                yB = Buf()

                def flat(t_):
                    return t_[:].rearrange("p a t -> p (a t)")

                units = [(chc, c) for cp in range(4) for c in range(16) for chc in (2 * cp, 2 * cp + 1)]
                NU = len(units)

                def stA(u):
                    chc, c = units[u]
                    j0, qt = chc * 4, c // 4
                    cols = slice(c * P, (c + 1) * P)
                    ai = u % NA
                    ba = k.nextps()
                    bb = k.nextps()
                    for jl in range(4):
                        op("pe", [BrLB, UYB[chc][qt]], [psB[ba]],
                           lambda e, jl=jl: e.matmul(ps[ba][:, jl * P:(jl + 1) * P], lhsT=BrL[:, j0 + jl, :], rhs=UY[:, chc, cols], start=True, stop=True),
                           inc=(jl == 3))
                    for jl in range(4):
                        op("pe", [BiLB, UYB[chc][qt]], [psB[bb]],
                           lambda e, jl=jl: e.matmul(ps[bb][:, jl * P:(jl + 1) * P], lhsT=BiL[:, j0 + jl, :], rhs=UY[:, chc, cols], start=True, stop=True),
                           inc=(jl == 3))
                    op("act", [psB[ba]], [AB_[ai]], lambda e: e.activation(out=flat(TA[ai]), in_=ps[ba][:, :], func=AF.Copy))
                    op("act", [psB[bb]], [BB_[ai]], lambda e: e.activation(out=flat(TBf[ai]), in_=ps[bb][:, :], func=AF.Copy))

                def ctx(u):
                    chc, c = units[u]
                    j0 = chc * 4
                    ai, ci, qi = u % NA, u % NCD, u % NQ
                    return chc, c, j0, ai, ci, qi

                def stB_mul(u):
                    chc, c, j0, ai, ci, qi = ctx(u)
                    A, B_ = TA[ai], TBf[ai]
                    C, Dd = CD[ci][:, 0, :, :], CD[ci][:, 1, :, :]
                    tnr, ntni = TNr[:, j0:j0 + 4, :], TNi[:, j0:j0 + 4, :]
                    op("dve", [AB_[ai], prepB], [CDB[ci]], lambda e: e.tensor_tensor(out=C, in0=A[:], in1=tnr, op=TT.mult))
                    op("dve", [BB_[ai], prepB], [T1B], lambda e: e.tensor_tensor(out=T1[:], in0=B_[:], in1=ntni, op=TT.mult))
                    op("dve", [AB_[ai], prepB], [CDB[ci]], lambda e: e.tensor_tensor(out=Dd, in0=A[:], in1=ntni, op=TT.mult))
                    op("dve", [BB_[ai], prepB], [T2B], lambda e: e.tensor_tensor(out=T2[:], in0=B_[:], in1=tnr, op=TT.mult))

                def stB_comb(u):
                    chc, c, j0, ai, ci, qi = ctx(u)
                    C, Dd = CD[ci][:, 0, :, :], CD[ci][:, 1, :, :]
                    op("dve", [CDB[ci], T1B], [CDB[ci]], lambda e: e.tensor_tensor(out=C, in0=C, in1=T1[:], op=TT.add))
                    op("dve", [CDB[ci], T2B], [CDB[ci]], lambda e: e.tensor_tensor(out=Dd, in0=Dd, in1=T2[:], op=TT.subtract))

                def stC_inject(u):
                    chc, c, j0, ai, ci, qi = ctx(u)
                    if c > 0:
                        op("dve", [CDB[ci], carB[j0]], [CDB[ci]],
                           lambda e: e.tensor_tensor(out=CD[ci][:, :, :, 0:1], in0=CD[ci][:, :, :, 0:1],
                                                     in1=car[:, :, j0:j0 + 4].unsqueeze(3), op=TT.add))

                def stC_scan(u):
                    chc, c, j0, ai, ci, qi = ctx(u)
                    op("dve", [CDB[ci], prepB], [CDB[ci]],
                       lambda e: e.tensor_tensor_scan(out=CD[ci][:].rearrange("p a b t -> p (a b t)"), data0=rmask[:].rearrange("p a b t -> p (a b t)"),
                                                      data1=CD[ci][:].rearrange("p a b t -> p (a b t)"), initial=0.0, op0=TT.mult, op1=TT.add))

                def stD_e(u):
                    chc, c, j0, ai, ci, qi = ctx(u)
                    if c < 15:
                        xl = CD[ci][:, :, :, P - 1]
                        a_r = W["A128r"][:, j0:j0 + 4].unsqueeze(1).to_broadcast([P, 2, 4])
                        a_i = W["A128i"][:, j0:j0 + 4].unsqueeze(1).to_broadcast([P, 2, 4])
                        op("dve", [CDB[ci], prepB, carB[j0]], [eB], lambda e: e.tensor_tensor(out=ea[:], in0=xl, in1=a_r, op=TT.mult))
                        op("dve", [CDB[ci], prepB, carB[j0]], [eB], lambda e: e.tensor_tensor(out=eb[:], in0=xl, in1=a_i, op=TT.mult))

                def stD_q3(u):
                    chc, c, j0, ai, ci, qi = ctx(u)
                    C = CD[ci][:, 0, :, :]
                    tpi = TPi[:, j0:j0 + 4, :]
                    op("dve", [CDB[ci], prepB], [QB[qi][2]],
                       lambda e: e.scalar_tensor_tensor(out=Q[qi][2][:], in0=C, scalar=-1.0, in1=tpi, op0=TT.mult, op1=TT.mult))

                def stD_car(u):
                    chc, c, j0, ai, ci, qi = ctx(u)
                    if c < 15:
                        op("dve", [eB], [carB[j0]], lambda e: e.tensor_tensor(out=car[:, 0, j0:j0 + 4], in0=ea[:, 0, :], in1=eb[:, 1, :], op=TT.add))
                        op("dve", [eB], [carB[j0]], lambda e: e.tensor_tensor(out=car[:, 1, j0:j0 + 4], in0=ea[:, 1, :], in1=eb[:, 0, :], op=TT.subtract))

                def stD_rest(u):
                    chc, c, j0, ai, ci, qi = ctx(u)
                    qt, cq = c // 4, c % 4
                    C, Dd = CD[ci][:, 0, :, :], CD[ci][:, 1, :, :]
                    bo = 6 + (chc % 2)
                    tpr, tpi = TPr[:, j0:j0 + 4, :], TPi[:, j0:j0 + 4, :]
                    q = Q[qi]
                    op("dve", [CDB[ci], prepB], [QB[qi][0]], lambda e: e.tensor_tensor(out=q[0][:], in0=C, in1=tpr, op=TT.mult))
                    op("dve", [CDB[ci], prepB], [QB[qi][1]], lambda e: e.tensor_tensor(out=q[1][:], in0=Dd, in1=tpi, op=TT.mult))
                    op("pool", [CDB[ci], prepB], [QB[qi][3]], lambda e: e.tensor_tensor(out=q[3][:], in0=Dd, in1=tpr, op=TT.mult))
                    n = 0
                    for jl in range(4):
                        j = j0 + jl
                        for qq in range(4):
                            wt, wtB = (CrP, CrPB) if qq < 2 else (CiP, CiPB)
                            n += 1
                            op("pe", [QB[qi][qq], wtB], [psB[bo]],
                               lambda e, j=j, jl=jl, qq=qq, wt=wt, n=n: e.matmul(ps[bo][:, cq * P:(cq + 1) * P], lhsT=wt[:, j, :], rhs=q[qq][:, jl, :],
                                                                                  start=(n == 1), stop=(n == 16)), inc=(n == 16))
                    if cq == 3:
                        blk = slice(qt * BW, (qt + 1) * BW)
                        op("act", [psB[bo]], [yB], lambda e: e.activation(out=ytmp[:], in_=ps[bo][:, :], func=AF.Copy))
                        op("dve", [yB, UYB[chc][qt], dTB], [yB],
                           lambda e: e.scalar_tensor_tensor(out=ytmp[:], in0=UY[:, chc, blk], scalar=dT[:, chc:chc + 1], in1=ytmp[:],
                                                            op0=TT.mult, op1=TT.add))
                        op("act", [yB], [UYB[chc][qt]], lambda e: e.activation(out=UY[:, chc, blk], in_=ytmp[:], func=AF.Gelu_apprx_tanh))

                def ok(u):
                    return 0 <= u < NU

                for i in range(NU + 3):
                    if ok(i):
                        stA(i)
                    if ok(i - 2):
                        stC_inject(i - 2)
                    if ok(i - 1):
                        stB_mul(i - 1)
                    if ok(i - 2):
                        stC_scan(i - 2)
                    if ok(i - 3):
                        stD_e(i - 3)
                    if ok(i - 1):
                        stB_comb(i - 1)
                    if ok(i - 3):
                        stD_q3(i - 3)
                        stD_car(i - 3)
                        stD_rest(i - 3)
                k.barrier()
            s3 = ExitStack()
            with s3:
                wz, wzB = load_w(s3, "a_wzs", dw["a_wz"], D)
                wg, wgB = load_w(s3, "a_wgs", dw["a_wglu"], D)
                w_out, w_outB = load_w(s3, "a_wouts", dw["a_w_out"], D)
                R1 = sb(s3, "R1", [P, 8, BW], F32)
                R1B = bufs(8)
                xnt = sb(s3, "xnt", [P, 8, BW], BF16)
                xnB = bufs(8)
                tmpA = sb(s3, "tmpA", [P, 8, BW], BF16)
                tmpAB = bufs(8)
                sz = sb(s3, "sz", [P, 8, BW], BF16)
                szB = bufs(8)
                sg = sb(s3, "sg", [P, 2, BW], BF16)
                sgB = bufs(2)
                xn_ap = lambda c: xnt[:, c, :]
                for tb in range(NBLK):
                    blk = slice(tb * BW, (tb + 1) * BW)
                    prenorm_ap(li, tb, xn_ap, xnB, tmpA, tmpAB)
                    uy_ap = lambda c, blk=blk: UY[:, c, blk]
                    uyB_t = [UYB[c][tb] for c in range(8)]
                    for m in range(8):
                        b = proj_ap(wz, wzB, m * P, P, xn_ap, xnB, 8)
                        op("act", [psB[b]], [szB[m]], lambda e, b=b, m=m: e.activation(out=sz[:, m, :], in_=ps[b][:, :], func=AF.Silu))
                        b = proj_ap(wg, wgB, m * P, P, uy_ap, uyB_t, 8)
                        op("act", [psB[b], dTB], [sgB[m % 2]],
                           lambda e, b=b, m=m: e.activation(out=sg[:, m % 2, :], in_=ps[b][:, :], func=AF.Sigmoid, bias=dT[:, 8 + m:9 + m], scale=1.0))
                        op("pool", [szB[m], uyB_t[m]], [szB[m]],
                           lambda e, m=m, blk=blk: e.tensor_tensor(out=sz[:, m, :], in0=sz[:, m, :], in1=UY[:, m, blk], op=ALU.mult))
                        op("pool", [szB[m], sgB[m % 2]], [szB[m]],
                           lambda e, m=m: e.tensor_tensor(out=sz[:, m, :], in0=sz[:, m, :], in1=sg[:, m % 2, :], op=ALU.mult))
                    outproj(li, tb, sz, szB, w_out, w_outB, R1, R1B, tmpA, tmpAB)
                k.barrier()

    for li in layers:
        if li == 3:
            layer_sgu(li)
        elif li == 1:
            layer_swa(li)
        elif li == 2:
            layer_mla(li)
        elif li == 0:
            layer_s5(li)

    toks = []
    for c in range(8):
        for tb in range(NBLK):
            toks.append(dma("sp", outT_d[c * P:(c + 1) * P, tb * BW:(tb + 1) * BW], X[:, c, tb * BW:(tb + 1) * BW], [xB[c][tb]], []))
    k._wait("sp", toks)
    k.barrier()


def host_inputs(inp, layers):
    f = lambda a: np.ascontiguousarray(np.asarray(a, dtype=np.float32))
    common = {}
    common["gpre"] = f(np.asarray(inp["pre_norm"]).reshape(4, 8, P).transpose(2, 0, 1).reshape(P, 32))
    common["gpost"] = f(np.asarray(inp["post_norm"]).reshape(4, 8, P).transpose(2, 0, 1).reshape(P, 32))
    common["ident"] = np.eye(P, dtype=np.float32)
    if 3 in layers:
        common["d_w_in"] = f(inp["d_w_in"][0])
        common["d_w_out"] = f(inp["d_w_out"][0])
        common["d_ws"] = f(np.asarray(inp["d_w_s"][0]).transpose(1, 0, 2))
        common["d_tril"] = np.tril(np.ones((P, P), dtype=np.float32))
        common["d_bs"] = f(inp["d_b_s"][0])
        common["d_lng"] = f(inp["d_ln_g"])
        common["d_lnb"] = f(inp["d_ln_b"])
    if 1 in layers:
        w = np.asarray(inp["b_w_in"][0], dtype=np.float32)
        q, kk, v, z = w[:, :1024], w[:, 1024:1152], w[:, 1152:1280], w[:, 1280:]
        common["b_w_in"] = f(np.concatenate([q, kk[:, :64], kk[:, :64], kk[:, 64:], kk[:, 64:], v, z], axis=1))
        common["b_w_out"] = f(inp["b_w_out"][0])
        common["b_sinks"] = f(inp["b_sinks"])
        def bucket(d):
            if d < 16:
                return d
            v_ = 16 + int(math.log(max(d, 1) / 16.0) / math.log(128 / 16.0) * 16)
            return min(v_, 31)
        rb = np.asarray(inp["rel_bias"], dtype=np.float32)
        bt = np.full((P, 16, 256), -1e30, dtype=np.float32)
        for kj in range(P):
            for qi in range(P):
                d_prev = qi + P - kj
                if d_prev < P:
                    bt[kj, :, qi] = rb[bucket(d_prev), :]
                d_cur = qi - kj
                if d_cur >= 0:
                    bt[kj, :, P + qi] = rb[bucket(d_cur), :]
        common["b_biasT"] = bt
    if 2 in layers:
        w = np.asarray(inp["c_w_in"][0], dtype=np.float32)
        kr = w[:, 1024:1056]
        common["c_w1"] = f(np.concatenate([w[:, :1024], kr, kr[:, 16:], kr[:, :16]], axis=1))
        common["c_wz"] = f(w[:, 1056:])
        uq = np.asarray(inp["c_w_uq"][0], dtype=np.float32).reshape(768, 16, 96)
        nope, rp = uq[:, :, :64], uq[:, :, 64:]
        common["c_wuq"] = f(np.concatenate([rp, rp[:, :, 16:], rp[:, :, :16], nope], axis=2).reshape(768, 2048))
        common["c_wukv"] = f(inp["c_w_ukv"][0])
        common["c_w_out"] = f(inp["c_w_out"][0])
        inv = (np.float32(10000.0) ** (-np.arange(0, 32, 2, dtype=np.float32) / np.float32(32))).astype(np.float32)
        ang = (np.arange(L, dtype=np.float32)[:, None] * inv[None, :]).astype(np.float32)
        cos, sin = np.cos(ang).astype(np.float32).T, np.sin(ang).astype(np.float32).T
        common["c_rope"] = f(np.concatenate([cos, cos, -sin, sin], axis=0))
        g = np.concatenate([np.asarray(inp["c_q_norm"][0]), np.asarray(inp["c_kv_norm"][0])]).astype(np.float32)
        common["c_gqkv"] = f(g.reshape(8, P).T)
        common["c_maskT"] = np.triu(np.ones((P, P), dtype=np.float32))
    if 0 in layers:
        w = np.asarray(inp["a_w_in"][0], dtype=np.float32)
        common["a_wu"] = f(w[:, :1024])
        common["a_wz"] = f(w[:, 1024:])
        common["a_wglu"] = f(inp["a_w_glu"][0])
        common["a_w_out"] = f(inp["a_w_out"][0])
        dg = np.concatenate([np.asarray(inp["a_d"][0]).reshape(8, P).T, np.asarray(inp["a_b_glu"][0]).reshape(8, P).T], axis=1)
        common["a_dg"] = f(dg)
        lam = np.stack([np.asarray(inp["a_lam_re"][0]).reshape(32, P).T, np.asarray(inp["a_lam_im"][0]).reshape(32, P).T,
                        np.repeat(np.asarray(inp["a_log_dt"][0]), 64).reshape(32, P).T], axis=1)
        common["a_lam"] = f(lam)
        brl = np.zeros((P, 32, P), np.float32); bil = np.zeros((P, 32, P), np.float32)
        crp = np.zeros((P, 32, P), np.float32); cip = np.zeros((P, 32, P), np.float32)
        b_re, b_im = np.asarray(inp["a_b_re"][0]), np.asarray(inp["a_b_im"][0])
        c_re, c_im = np.asarray(inp["a_c_re"][0]), np.asarray(inp["a_c_im"][0])
        for j in range(32):
            for gl in range(2):
                g = 2 * j + gl
                r0 = 32 * (j % 4) + gl * 16
                brl[r0:r0 + 16, j, gl * 64:(gl + 1) * 64] = b_re[g].T
                bil[r0:r0 + 16, j, gl * 64:(gl + 1) * 64] = b_im[g].T
                crp[gl * 64:(gl + 1) * 64, j, r0:r0 + 16] = c_re[g].T
                cip[gl * 64:(gl + 1) * 64, j, r0:r0 + 16] = c_im[g].T
        common["a_brl"], common["a_bil"], common["a_crp"], common["a_cip"] = brl, bil, crp, cip
    return common


def run(inp, layers=(0, 1, 2, 3), cores=8, trace=False):
    nc = bass.Bass("TRN2", target_bir_lowering=False)
    build(nc, list(layers))
    common = host_inputs(inp, list(layers))
    x = np.asarray(inp["x"], dtype=np.float32)
    in_maps = []
    for b in range(cores):
        m = dict(common)
        m["xT"] = np.ascontiguousarray(x[b].T)
        in_maps.append(m)
    res = run_bass_kernel_spmd(nc, in_maps, core_ids=list(range(cores)), trace=trace)
    out = np.stack([np.ascontiguousarray(r["outT"].T) for r in res.results], axis=0)
    return out.astype(np.float32), res


def kernel(**inputs):
    out, _ = run(inputs)
    return out
```

```python
import math
import numpy as np
from contextlib import ExitStack
import concourse.bass as bass
import concourse.mybir as mybir
from concourse.bass_utils import run_bass_kernel_spmd

F32 = mybir.dt.float32
BF16 = mybir.dt.bfloat16
ALU = mybir.AluOpType
AF = mybir.ActivationFunctionType

P = 128
L = 2048
D = 1024
NBLK = 4
BW = 512
EPS = 1e-6
SELF_SYNC = True
NDS = 12


class Buf:
    __slots__ = ("w", "r")

    def __init__(self):
        self.w = None
        self.r = {}


def bufs(*shape):
    if len(shape) == 1:
        return [Buf() for _ in range(shape[0])]
    return [bufs(*shape[1:]) for _ in range(shape[0])]


class KB:
    def __init__(self, nc, es):
        self.nc = nc
        self.E = dict(pe=nc.tensor, act=nc.scalar, dve=nc.vector, pool=nc.gpsimd, sp=nc.sync)
        self.sem = {e: es.enter_context(nc.semaphore("s_" + e)) for e in ("pe", "act", "dve", "pool")}
        self.cnt = {e: 0 for e in self.sem}
        self.pend = {e: False for e in self.sem}
        self.dsem = {q: [[es.enter_context(nc.semaphore("d_%s%d" % (q, i))), 0] for i in range(NDS)]
                     for q in ("sp", "pool")}
        self.dcnt = {"sp": 0, "pool": 0}
        self.seen = {e: {} for e in self.E}
        self.ps = []
        self.psB = []
        self.psrot = 0
        self.nrot = 6

    def _semh(self, key):
        if isinstance(key, str):
            return self.sem[key]
        return self.dsem[key[0]][key[1]][0]

    def _wait(self, e, toks):
        need = {}
        for key, v in toks:
            if need.get(key, 0) < v:
                need[key] = v
        for key, v in need.items():
            if key == e and (e == "pe" or not SELF_SYNC):
                continue
            if self.seen[e].get(key, 0) >= v:
                continue
            self.E[e].wait_ge(self._semh(key), v)
            self.seen[e][key] = v

    def _deps(self, reads, writes):
        toks = []
        for b in reads:
            if b.w is not None:
                toks.append(b.w)
        for b in writes:
            if b.w is not None:
                toks.append(b.w)
            toks.extend(b.r.items())
        return toks

    def _mark(self, tok, reads, writes):
        key, v = tok
        for b in reads:
            if b.r.get(key, 0) < v:
                b.r[key] = v
        for b in writes:
            b.w = tok
            b.r = {}

    def op(self, e, reads, writes, fn, inc=True):
        self._wait(e, self._deps(reads, writes))
        ins = fn(self.E[e])
        if inc:
            self.cnt[e] += 1
            ins.then_inc(self.sem[e], 1)
            self.pend[e] = False
            tok = (e, self.cnt[e])
        else:
            self.pend[e] = True
            tok = (e, self.cnt[e] + 1)
        self._mark(tok, reads, writes)
        return ins

    def dma(self, q, out, in_, reads, writes):
        self._wait(q, self._deps(reads, writes))
        i = self.dcnt[q] % NDS
        self.dcnt[q] += 1
        ent = self.dsem[q][i]
        key = (q, i)
        if ent[1] > 0:
            self._wait(q, [(key, 16 * ent[1])])
        ins = self.E[q].dma_start(out=out, in_=in_)
        ins.then_inc(ent[0], 16)
        ent[1] += 1
        tok = (key, 16 * ent[1])
        self._mark(tok, reads, writes)
        return tok

    def all_tokens(self):
        toks = [(e, c) for e, c in self.cnt.items() if c > 0]
        for q in self.dsem:
            for i, ent in enumerate(self.dsem[q]):
                if ent[1] > 0:
                    toks.append(((q, i), 16 * ent[1]))
        return toks

    def barrier(self):
        for e in self.sem:
            assert not self.pend[e], e
        toks = self.all_tokens()
        for e in self.E:
            self._wait(e, [t for t in toks if t[0] != e])

    def nextps(self):
        b = self.psrot
        self.psrot = (self.psrot + 1) % self.nrot
        return b


def build(nc, layers):
    es = ExitStack()
    with es:
        _build(nc, es, layers)
    return nc


def _build(nc, es, layers):
    k = KB(nc, es)
    op, dma = k.op, k.dma

    def dram_in(name, shape, dt=F32):
        return nc.dram_tensor(name, list(shape), dt, kind="ExternalInput").ap()

    uid = [0]

    def sb(st, name, shape, dt):
        uid[0] += 1
        return st.enter_context(nc.sbuf_tensor("%s_%d" % (name, uid[0]), list(shape), dt))

    xT_d = dram_in("xT", [D, L])
    outT_d = nc.dram_tensor("outT", [D, L], F32, kind="ExternalOutput").ap()
    gpre_d = dram_in("gpre", [P, 32])
    gpost_d = dram_in("gpost", [P, 32])
    ident_d = dram_in("ident", [P, P])
    dw = {}
    if 3 in layers:
        dw["d_w_in"] = dram_in("d_w_in", [D, 3072])
        dw["d_w_out"] = dram_in("d_w_out", [D, D])
        dw["d_ws"] = dram_in("d_ws", [P, 16, P])
        dw["d_tril"] = dram_in("d_tril", [P, P])
        dw["d_bs"] = dram_in("d_bs", [16, P])
        dw["d_lng"] = dram_in("d_lng", [1, D])
        dw["d_lnb"] = dram_in("d_lnb", [1, D])

    if 1 in layers:
        dw["b_w_in"] = dram_in("b_w_in", [D, 2432])
        dw["b_w_out"] = dram_in("b_w_out", [D, D])
        dw["b_biasT"] = dram_in("b_biasT", [P, 16, 256])
        dw["b_sinks"] = dram_in("b_sinks", [1, 16])

    if 2 in layers:
        dw["c_w1"] = dram_in("c_w1", [D, 1088])
        dw["c_wz"] = dram_in("c_wz", [D, D])
        dw["c_wuq"] = dram_in("c_wuq", [768, 2048])
        dw["c_wukv"] = dram_in("c_wukv", [256, 2048])
        dw["c_w_out"] = dram_in("c_w_out", [D, D])
        dw["c_rope"] = dram_in("c_rope", [64, L])
        dw["c_gqkv"] = dram_in("c_gqkv", [P, 8])
        dw["c_maskT"] = dram_in("c_maskT", [P, P])

    if 0 in layers:
        dw["a_wu"] = dram_in("a_wu", [D, D])
        dw["a_wz"] = dram_in("a_wz", [D, D])
        dw["a_wglu"] = dram_in("a_wglu", [D, D])
        dw["a_w_out"] = dram_in("a_w_out", [D, D])
        dw["a_dg"] = dram_in("a_dg", [P, 16])
        dw["a_lam"] = dram_in("a_lam", [P, 3, 32])
        for nm in ("a_brl", "a_bil", "a_crp", "a_cip"):
            dw[nm] = dram_in(nm, [P, 32, P])

    X = sb(es, "X", [P, 8, L], F32)
    xB = bufs(8, NBLK)
    ones = sb(es, "ones", [P, P], BF16)
    onesB = Buf()
    ident = sb(es, "identb", [P, P], BF16)
    identB = Buf()
    gpre = sb(es, "gpre_s", [P, 32], F32)
    gpost = sb(es, "gpost_s", [P, 32], F32)
    gB = Buf()
    rstd = sb(es, "rstd", [P, BW], F32)
    rstdB = Buf()
    for i in range(8):
        k.ps.append(es.enter_context(nc.psum_tensor("ps%d" % i, [P, BW], F32)))
        k.psB.append(Buf())
    ps, psB = k.ps, k.psB

    for c in range(8):
        for tb in range(NBLK):
            dma("sp", X[:, c, tb * BW:(tb + 1) * BW], xT_d[c * P:(c + 1) * P, tb * BW:(tb + 1) * BW], [], [xB[c][tb]])
    dma("sp", gpre[:], gpre_d, [], [gB])
    dma("sp", gpost[:], gpost_d, [], [gB])
    dma("pool", ident[:], ident_d, [], [identB])
    op("dve", [], [onesB], lambda e: e.memset(ones[:], 1.0))
    epsc = sb(es, "epsc", [P, 1], F32)
    op("dve", [], [onesB], lambda e: e.memset(epsc[:], EPS))

    def load_w(st, name, d_ap, ncols, q="pool"):
        K = d_ap.shape[0]
        kc_n = K // P
        t = sb(st, name, [P, kc_n, ncols], BF16)
        B = Buf()
        for kc in range(kc_n):
            for c0 in range(0, ncols, 2048):
                c1 = min(ncols, c0 + 2048)
                dma(q, t[:, kc, c0:c1], d_ap[kc * P:(kc + 1) * P, c0:c1], [], [B])
        return t, B

    def rstd_from_sq(sq, sqB, n, scale, c0=0):
        b = k.nextps()
        for c in range(n):
            op("pe", [sqB[c0 + c], onesB], [psB[b]],
               lambda e, c=c: e.matmul(ps[b][:, :], lhsT=ones[:, :], rhs=sq[:, c0 + c, :], start=(c == 0), stop=(c == n - 1)),
               inc=(c == n - 1))
        op("act", [psB[b]], [rstdB],
           lambda e: e.activation(out=rstd[:], in_=ps[b][:, :], func=AF.Ln, scale=scale, bias=epsc[:, 0:1]))
        op("act", [rstdB], [rstdB], lambda e: e.activation(out=rstd[:], in_=rstd[:], func=AF.Exp, scale=-0.5))

    def prenorm(li, tb, xn, xnB, tmpA, tmpAB):
        blk = slice(tb * BW, (tb + 1) * BW)
        for c in range(8):
            op("act", [xB[c][tb]], [tmpAB[c]],
               lambda e, c=c: e.activation(out=tmpA[:, c, :], in_=X[:, c, blk], func=AF.Square))
        rstd_from_sq(tmpA, tmpAB, 8, 1.0 / D)
        for c in range(8):
            op("dve", [xB[c][tb], rstdB, gB], [xnB[c]],
               lambda e, c=c: e.scalar_tensor_tensor(out=xn[:, c, :], in0=X[:, c, blk],
                                                     scalar=gpre[:, li * 8 + c:li * 8 + c + 1], in1=rstd[:],
                                                     op0=ALU.mult, op1=ALU.mult))

    def prenorm_ap(li, tb, xn_ap, xnB, tmpA, tmpAB):
        blk = slice(tb * BW, (tb + 1) * BW)
        for c in range(8):
            op("act", [xB[c][tb]], [tmpAB[c]],
               lambda e, c=c: e.activation(out=tmpA[:, c, :], in_=X[:, c, blk], func=AF.Square))
        rstd_from_sq(tmpA, tmpAB, 8, 1.0 / D)
        for c in range(8):
            op("dve", [xB[c][tb], rstdB, gB], [xnB[c]],
               lambda e, c=c: e.scalar_tensor_tensor(out=xn_ap(c), in0=X[:, c, blk],
                                                     scalar=gpre[:, li * 8 + c:li * 8 + c + 1], in1=rstd[:],
                                                     op0=ALU.mult, op1=ALU.mult))

    def proj_ap(w, wB, col0, M, rhs_ap, rhsB, nk):
        b = k.nextps()
        for kc in range(nk):
            op("pe", [wB, rhsB[kc]], [psB[b]],
               lambda e, kc=kc: e.matmul(ps[b][0:M, :], lhsT=w[:, kc, col0:col0 + M], rhs=rhs_ap(kc),
                                         start=(kc == 0), stop=(kc == nk - 1)),
               inc=(kc == nk - 1))
        return b

    def proj_fm(w, wB, col0, rhs, rhsB, nk, M=P):
        b = k.nextps()
        for kc in range(nk):
            op("pe", [wB, rhsB[kc]], [psB[b]],
               lambda e, kc=kc: e.matmul(ps[b][0:M, :], lhsT=w[:, kc, col0:col0 + M], rhs=rhs[:, kc, :],
                                         start=(kc == 0), stop=(kc == nk - 1)),
               inc=(kc == nk - 1))
        return b

    def outproj(li, tb, G, GB, wout, woutB, ybuf, ybufB, tmpA, tmpAB):
        blk = slice(tb * BW, (tb + 1) * BW)
        for m in range(8):
            b = proj_fm(wout, woutB, m * P, G, GB, 8)
            op("act", [psB[b]], [ybufB[m]], lambda e, m=m, b=b: e.activation(out=ybuf[:, m, :], in_=ps[b][:, :], func=AF.Copy))
            op("act", [psB[b]], [tmpAB[m]], lambda e, m=m, b=b: e.activation(out=tmpA[:, m, :], in_=ps[b][:, :], func=AF.Square))
        rstd_from_sq(tmpA, tmpAB, 8, 1.0 / D)
        for m in range(8):
            op("dve", [ybufB[m], rstdB, gB], [ybufB[m]],
               lambda e, m=m: e.scalar_tensor_tensor(out=ybuf[:, m, :], in0=ybuf[:, m, :],
                                                     scalar=gpost[:, li * 8 + m:li * 8 + m + 1], in1=rstd[:],
                                                     op0=ALU.mult, op1=ALU.mult))
            op("pool", [ybufB[m], xB[m][tb]], [xB[m][tb]],
               lambda e, m=m: e.tensor_tensor(out=X[:, m, blk], in0=X[:, m, blk], in1=ybuf[:, m, :], op=ALU.add))

    def layer_sgu(li):
        st = ExitStack()
        with st:
            w_in, w_inB = load_w(st, "d_win", dw["d_w_in"], 3072)
            w_out, w_outB = load_w(st, "d_wout", dw["d_w_out"], D)
            wsf = sb(st, "wsf", [P, 16, P], F32)
            wsfB = Buf()
            tril = sb(st, "tril", [P, P], F32)
            trilB = Buf()
            wsm = sb(st, "wsm", [P, 16, P], BF16)
            wsmB = Buf()
            wsT = sb(st, "wsT", [P, 16, P], BF16)
            wsTB = bufs(16)
            bsT = sb(st, "bsT", [P, 8, P], F32)
            bsTB = Buf()
            lng = sb(st, "lng", [P, D], F32)
            lnb = sb(st, "lnb", [P, D], F32)
            lnB = Buf()
            R1 = sb(st, "R1", [P, 8, BW], F32)
            R1B = bufs(8)
            R1bf = R1[:].bitcast(BF16)
            tmpA = sb(st, "tmpA", [P, 8, BW], BF16)
            tmpAB = bufs(8)
            gu = sb(st, "gu", [P, 8, BW], BF16)
            guB = bufs(8)
            sz = sb(st, "sz", [P, 8, BW], BF16)
            szB = bufs(8)
            vtmp = sb(st, "vtmp", [P, D], F32)
            vtmpB = Buf()
            stt = sb(st, "stt", [P, 2, 6], F32)
            mv = sb(st, "mv", [P, 2], F32)
            rs1 = sb(st, "rs1", [P, 1], F32)
            sttB = Buf()
            tmpS = sb(st, "tmpS", [P, BW], F32)
            tmpSB = Buf()

            def xn_ap(c):
                return R1bf[:, c // 2, (c % 2) * BW:(c % 2) * BW + BW]

            def vln_ap(ch, c0, c1):
                return R1bf[:, 4 + ch, c0:c1]

            xnB = [R1B[c // 2] for c in range(8)]

            dma("sp", wsf[:], dw["d_ws"], [], [wsfB])
            dma("sp", tril[:], dw["d_tril"], [], [trilB])
            for h in range(2):
                src = dw["d_bs"].rearrange("(gp h) t -> h gp t", h=2)[h]
                dma("sp", bsT[h * 64:(h + 1) * 64, :, :], src.unsqueeze(0).to_broadcast([64, 8, P]), [], [bsTB])
            dma("sp", lng[:], dw["d_lng"].to_broadcast([P, D]), [], [lnB])
            dma("sp", lnb[:], dw["d_lnb"].to_broadcast([P, D]), [], [lnB])
            op("dve", [wsfB, trilB], [wsmB],
               lambda e: e.tensor_tensor(out=wsm[:], in0=wsf[:], in1=tril[:].unsqueeze(1).to_broadcast([P, 16, P]), op=ALU.mult))
            for g in range(16):
                b = k.nextps()
                op("pe", [wsmB, identB], [psB[b]],
                   lambda e, g=g, b=b: e.matmul(ps[b][:, 0:P], lhsT=wsm[:, g, :], rhs=ident[:, :], start=True, stop=True))
                op("act", [psB[b]], [wsTB[g]], lambda e, g=g, b=b: e.activation(out=wsT[:, g, :], in_=ps[b][:, 0:P], func=AF.Copy))

            for tb in range(NBLK):
                blk = slice(tb * BW, (tb + 1) * BW)
                for c in range(8):
                    op("act", [xB[c][tb]], [tmpAB[c]],
                       lambda e, c=c: e.activation(out=tmpA[:, c, :], in_=X[:, c, blk], func=AF.Square))
                rstd_from_sq(tmpA, tmpAB, 8, 1.0 / D)
                for c in range(8):
                    op("dve", [xB[c][tb], rstdB, gB], [xnB[c]],
                       lambda e, c=c: e.scalar_tensor_tensor(out=xn_ap(c), in0=X[:, c, blk],
                                                             scalar=gpre[:, li * 8 + c:li * 8 + c + 1], in1=rstd[:],
                                                             op0=ALU.mult, op1=ALU.mult))
                for ch in range(4):
                    for half in range(2):
                        b = k.nextps()
                        for kc in range(8):
                            op("pe", [w_inB, xnB[kc]], [psB[b]],
                               lambda e, kc=kc, b=b: e.matmul(ps[b][:, :], lhsT=xn_ap(kc)[:, ch * P:(ch + 1) * P],
                                                              rhs=w_in[:, kc, D + half * BW:D + (half + 1) * BW],
                                                              start=(kc == 0), stop=(kc == 7)),
                               inc=(kc == 7))
                        op("act", [psB[b]], [vtmpB],
                           lambda e, b=b, half=half: e.activation(out=vtmp[:, half * BW:(half + 1) * BW], in_=ps[b][:, :],
                                                                  func=AF.Gelu_apprx_tanh))
                    for half in range(2):
                        op("dve", [vtmpB], [sttB], lambda e, half=half: e.bn_stats(out=stt[:, half, :], in_=vtmp[:, half * BW:(half + 1) * BW]))
                    op("dve", [sttB], [sttB], lambda e: e.bn_aggr(out=mv[:], in_=stt[:].rearrange("p a b -> p (a b)")))
                    op("act", [sttB], [sttB],
                       lambda e: e.activation(out=rs1[:], in_=mv[:, 1:2], func=AF.Sqrt, scale=1.0, bias=epsc[:, 0:1]))
                    op("dve", [sttB], [sttB], lambda e: e.reciprocal(out=rs1[:], in_=rs1[:]))
                    op("dve", [vtmpB, sttB], [vtmpB],
                       lambda e: e.tensor_scalar(out=vtmp[:], in0=vtmp[:], scalar1=mv[:, 0:1], scalar2=rs1[:, 0:1],
                                                 op0=ALU.subtract, op1=ALU.mult))
                    op("pool", [vtmpB, lnB], [vtmpB], lambda e: e.tensor_tensor(out=vtmp[:], in0=vtmp[:], in1=lng[:], op=ALU.mult))
                    op("pool", [vtmpB, lnB], [R1B[4 + ch]],
                       lambda e, ch=ch: e.tensor_tensor(out=vln_ap(ch, 0, D), in0=vtmp[:], in1=lnb[:], op=ALU.add))
                for m in range(8):
                    b = k.nextps()
                    for kc in range(8):
                        op("pe", [w_inB, xnB[kc]], [psB[b]],
                           lambda e, kc=kc, b=b, m=m: e.matmul(ps[b][:, :], lhsT=w_in[:, kc, m * P:(m + 1) * P], rhs=xn_ap(kc),
                                                               start=(kc == 0), stop=(kc == 7)), inc=(kc == 7))
                    op("act", [psB[b]], [guB[m]], lambda e, b=b, m=m: e.activation(out=gu[:, m, :], in_=ps[b][:, :], func=AF.Gelu_apprx_tanh))
                for m in range(8):
                    b = k.nextps()
                    for kc in range(8):
                        op("pe", [w_inB, xnB[kc]], [psB[b]],
                           lambda e, kc=kc, b=b, m=m: e.matmul(ps[b][:, :], lhsT=w_in[:, kc, 2 * D + m * P:2 * D + (m + 1) * P], rhs=xn_ap(kc),
                                                               start=(kc == 0), stop=(kc == 7)), inc=(kc == 7))
                    op("act", [psB[b]], [szB[m]], lambda e, b=b, m=m: e.activation(out=sz[:, m, :], in_=ps[b][:, :], func=AF.Silu))
                for m in range(8):
                    op("pool", [guB[m], szB[m]], [guB[m]], lambda e, m=m: e.tensor_tensor(out=gu[:, m, :], in0=gu[:, m, :], in1=sz[:, m, :], op=ALU.mult))
                for gp in range(8):
                    b = k.nextps()
                    n = 0
                    for ch in range(4):
                        for h in range(2):
                            g = 2 * gp + h
                            n += 1
                            op("pe", [R1B[4 + ch], wsTB[g]], [psB[b]],
                               lambda e, ch=ch, h=h, g=g, b=b: e.matmul(ps[b][h * 64:(h + 1) * 64, ch * P:(ch + 1) * P],
                                                                        lhsT=vln_ap(ch, g * 64, (g + 1) * 64), rhs=wsT[:, g, :],
                                                                        start=True, stop=True),
                               inc=(n == 8))
                    op("dve", [psB[b], bsTB], [tmpSB],
                       lambda e, b=b, gp=gp: e.tensor_tensor(out=tmpS[:].rearrange("p (a t) -> p a t", a=4),
                                                             in0=ps[b][:, :].rearrange("p (a t) -> p a t", a=4),
                                                             in1=bsT[:, gp, :].unsqueeze(1).to_broadcast([P, 4, P]), op=ALU.add))
                    op("dve", [tmpSB, guB[gp]], [guB[gp]],
                       lambda e, gp=gp: e.tensor_tensor(out=gu[:, gp, :], in0=tmpS[:], in1=gu[:, gp, :], op=ALU.mult))
                outproj(li, tb, gu, guB, w_out, w_outB, R1, R1B, tmpA, tmpAB)
            k.barrier()

    def layer_swa(li):
        st = ExitStack()
        with st:
            NC_IN = 2432
            w_in, w_inB = load_w(st, "b_win", dw["b_w_in"], NC_IN)
            w_out, w_outB = load_w(st, "b_wout", dw["b_w_out"], D)
            KT2 = sb(st, "KT2", [P, 2, L], BF16)
            KTB = bufs(2, NBLK)
            VA = [[sb(st, "VA%d%d" % (kv, par), [P, 16, P], BF16) for par in range(2)] for kv in range(2)]
            VAB = bufs(2, 2, 16)
            biasT = sb(st, "biasT", [P, 16, 256], F32)
            biasB = Buf()
            esink = sb(st, "esink", [P, 16], F32)
            esinkB = Buf()
            R1 = sb(st, "R1", [P, 8, BW], F32)
            R1B = bufs(8)
            R1bf = R1[:].bitcast(BF16)
            tmpA = sb(st, "tmpA", [P, 8, BW], BF16)
            tmpAB = bufs(8)
            sz = sb(st, "sz", [P, 8, BW], BF16)
            szB = bufs(8)
            NPB = 4
            apc = [0]
            tmpP = sb(st, "tmpP", [P, NPB, 256], F32)
            tmpPB = bufs(NPB)
            PT = sb(st, "PT", [P, NPB, 256], BF16)
            PTB = bufs(NPB)
            rsb = sb(st, "rsb", [P, BW], F32)
            rsbB = bufs(2)
            tG = sb(st, "tG", [P, BW], F32)
            tGB = bufs(2)

            def xn_ap(c):
                return R1bf[:, c // 2, (c % 2) * BW:(c % 2) * BW + BW]
            xnB = [R1B[c // 2] for c in range(8)]

            def qt_ap(j, p0, p1, c0, c1):
                return R1bf[p0:p1, 4 + j // 2, (j % 2) * BW + c0:(j % 2) * BW + c1]
            qtB = [R1B[4 + j // 2] for j in range(8)]

            import os
            SK = os.environ.get("SKIP", "")
            if "a" not in SK:
                dma("sp", biasT[:], dw["b_biasT"], [], [biasB])
            if "b" not in SK:
                dma("sp", esink[:], dw["b_sinks"].to_broadcast([P, 16]), [], [esinkB])
                op("act", [esinkB], [esinkB], lambda e: e.activation(out=esink[:], in_=esink[:], func=AF.Exp))
            for kv in range(2):
                for par in range(2):
                    c0 = 64 if par == 0 else 0
                    if "c" not in SK:
                        op("pool", [], sum([[VAB[kv][par][t]] for t in range(16)], []),
                           lambda e, kv=kv, par=par, c0=c0: e.memset(VA[kv][par][:, :, c0:c0 + 64], 1.0))

            for tb in range(NBLK):
                blk = slice(tb * BW, (tb + 1) * BW)
                prenorm_ap(li, tb, xn_ap, xnB, tmpA, tmpAB)
                STG = int(os.environ.get("STG", "9"))
                for j in range(8 if STG >= 1 else 0):
                    b = proj_ap(w_in, w_inB, j * P, P, xn_ap, xnB, 8)
                    op("act", [psB[b]], [qtB[j]], lambda e, b=b, j=j: e.activation(out=qt_ap(j, 0, P, 0, BW), in_=ps[b][:, :], func=AF.Copy))
                for kv in range(2 if STG >= 2 else 0):
                    b = proj_ap(w_in, w_inB, D + kv * P, P, xn_ap, xnB, 8)
                    op("act", [psB[b]], [KTB[kv][tb]], lambda e, b=b, kv=kv: e.activation(out=KT2[:, kv, blk], in_=ps[b][:, :], func=AF.Copy))
                for ch in range(4 if STG >= 3 else 0):
                    tt = tb * 4 + ch
                    b = k.nextps()
                    for kc in range(8):
                        op("pe", [w_inB, xnB[kc]], [psB[b]],
                           lambda e, kc=kc, b=b, ch=ch: e.matmul(ps[b][:, 0:P], lhsT=xn_ap(kc)[:, ch * P:(ch + 1) * P],
                                                                 rhs=w_in[:, kc, D + 256:D + 384], start=(kc == 0), stop=(kc == 7)),
                           inc=(kc == 7))
                    for kv in range(2):
                        op("act", [psB[b]], [VAB[kv][0][tt]],
                           lambda e, b=b, kv=kv, tt=tt: e.activation(out=VA[kv][0][:, tt, 0:64], in_=ps[b][:, kv * 64:(kv + 1) * 64], func=AF.Copy))
                        op("act", [psB[b]], [VAB[kv][1][tt]],
                           lambda e, b=b, kv=kv, tt=tt: e.activation(out=VA[kv][1][:, tt, 64:128], in_=ps[b][:, kv * 64:(kv + 1) * 64], func=AF.Copy))
                for m in range(8):
                    b = proj_ap(w_in, w_inB, D + 384 + m * P, P, xn_ap, xnB, 8)
                    op("act", [psB[b]], [szB[m]], lambda e, b=b, m=m: e.activation(out=sz[:, m, :], in_=ps[b][:, :], func=AF.Silu))
                def scores(h, nbl, pi):
                    kv, par, j = h // 8, h % 2, h // 2
                    base = par * 64
                    nb = tb * 4 + nbl
                    b = k.nextps()
                    c_lo = 0 if nb > 0 else P
                    rhs_q = qt_ap(j, base, base + 64, nbl * P, (nbl + 1) * P)
                    if nb > 0:
                        tbp = (nb - 1) // 4
                        op("pe", [KTB[kv][tbp], qtB[j]], [psB[b]],
                           lambda e: e.matmul(ps[b][:, 0:P], lhsT=KT2[base:base + 64, kv, (nb - 1) * P:nb * P], rhs=rhs_q, start=True, stop=True),
                           inc=False)
                    op("pe", [KTB[kv][tb], qtB[j]], [psB[b]],
                       lambda e: e.matmul(ps[b][:, P:2 * P], lhsT=KT2[base:base + 64, kv, nb * P:(nb + 1) * P], rhs=rhs_q, start=True, stop=True))
                    op("dve", [psB[b], biasB], [tmpPB[pi]],
                       lambda e: e.scalar_tensor_tensor(out=tmpP[:, pi, c_lo:256], in0=ps[b][:, c_lo:256], scalar=0.125, in1=biasT[:, h, c_lo:256],
                                                        op0=ALU.mult, op1=ALU.add))
                    op("act", [tmpPB[pi]], [PTB[pi]],
                       lambda e: e.activation(out=PT[:, pi, c_lo:256], in_=tmpP[:, pi, c_lo:256], func=AF.Exp))

                def pv_epi(h, nbl, pi):
                    kv, par, j = h // 8, h % 2, h // 2
                    base = par * 64
                    oth = 64 - base
                    bo = 6 + (h % 2)
                    nb = tb * 4 + nbl
                    if nb > 0:
                        op("pe", [PTB[pi], VAB[kv][par][nb - 1]], [psB[bo]],
                           lambda e: e.matmul(ps[bo][:, nbl * P:(nbl + 1) * P], lhsT=VA[kv][par][:, nb - 1, :], rhs=PT[:, pi, 0:P], start=True, stop=False),
                           inc=False)
                    op("pe", [PTB[pi], VAB[kv][par][nb]], [psB[bo]],
                       lambda e: e.matmul(ps[bo][:, nbl * P:(nbl + 1) * P], lhsT=VA[kv][par][:, nb, :], rhs=PT[:, pi, P:2 * P], start=(nb == 0), stop=True))
                    if nbl == 3:
                        op("dve", [psB[bo], esinkB], [rsbB[par]],
                           lambda e: e.tensor_scalar(out=rsb[base:base + 64, :], in0=ps[bo][oth:oth + 64, :], scalar1=esink[oth:oth + 64, h:h + 1],
                                                     scalar2=None, op0=ALU.add))
                        op("act", [rsbB[par]], [rsbB[par]], lambda e: e.activation(out=rsb[base:base + 64, :], in_=rsb[base:base + 64, :], func=AF.Ln))
                        op("act", [rsbB[par]], [rsbB[par]], lambda e: e.activation(out=rsb[base:base + 64, :], in_=rsb[base:base + 64, :], func=AF.Exp, scale=-1.0))
                        op("dve", [psB[bo], rsbB[par]], [tGB[par]],
                           lambda e: e.tensor_tensor(out=tG[base:base + 64, :], in0=ps[bo][base:base + 64, :], in1=rsb[base:base + 64, :], op=ALU.mult))
                        op("pool", [tGB[par], szB[j]], [szB[j]],
                           lambda e: e.tensor_tensor(out=sz[base:base + 64, j, :], in0=tG[base:base + 64, :], in1=sz[base:base + 64, j, :], op=ALU.mult))

                import os
                tasks = [(h, nbl) for h in range(int(os.environ.get('SWA_NH', '16'))) for nbl in range(4)]
                LAS = 2
                for i in range(min(LAS, len(tasks))):
                    scores(tasks[i][0], tasks[i][1], (apc[0] + i) % NPB)
                for i, (h, nbl) in enumerate(tasks):
                    pi = apc[0] % NPB
                    apc[0] += 1
                    if i + LAS < len(tasks):
                        scores(tasks[i + LAS][0], tasks[i + LAS][1], (apc[0] + LAS - 1) % NPB)
                    pv_epi(h, nbl, pi)
                outproj(li, tb, sz, szB, w_out, w_outB, R1, R1B, tmpA, tmpAB)
            k.barrier()

    def layer_mla(li):
        SC = 96.0 ** -0.5
        st = ExitStack()
        with st:
            GT = sb(st, "GT", [P, 8, L], BF16)
            GTB = bufs(8, NBLK)
            sA = ExitStack()
            sA.__enter__()
            CQN = sb(sA, "CQN", [P, 6, L], BF16)
            CQNB = bufs(6, NBLK)
            CKVN = sb(sA, "CKVN", [P, 2, L], BF16)
            CKVNB = bufs(2, NBLK)
            KR = sb(sA, "KR", [32, L], BF16)
            KRB = bufs(NBLK)
            ROPE = sb(sA, "ROPE", [64, L], F32)
            ropeB = Buf()
            gq = sb(sA, "gq", [P, 8], F32)
            gqB = Buf()
            dma("sp", ROPE[:], dw["c_rope"], [], [ropeB])
            dma("sp", gq[:], dw["c_gqkv"], [], [gqB])
            tR = sb(sA, "tR", [32, 2, BW], F32)
            tRB = bufs(2)
            s1 = ExitStack()
            with s1:
                w1, w1B = load_w(s1, "s_c_w1", dw["c_w1"], 1088)
                R1 = sb(s1, "R1", [P, 8, BW], F32)
                R1B = bufs(8)
                xnt = sb(s1, "xnt", [P, 8, BW], BF16)
                xnB = bufs(8)
                tmpA = sb(s1, "tmpA", [P, 8, BW], BF16)
                tmpAB = bufs(8)
                xn_ap = lambda c: xnt[:, c, :]
                for tb in range(NBLK):
                    blk = slice(tb * BW, (tb + 1) * BW)
                    prenorm_ap(li, tb, xn_ap, xnB, tmpA, tmpAB)
                    for m in range(8):
                        b = proj_ap(w1, w1B, m * P, P, xn_ap, xnB, 8)
                        op("act", [psB[b]], [R1B[m]], lambda e, b=b, m=m: e.activation(out=R1[:, m, :], in_=ps[b][:, :], func=AF.Copy))
                        op("act", [psB[b]], [tmpAB[m]], lambda e, b=b, m=m: e.activation(out=tmpA[:, m, :], in_=ps[b][:, :], func=AF.Square))
                    rstd_from_sq(tmpA, tmpAB, 6, 1.0 / 768, 0)
                    for m in range(6):
                        op("dve", [R1B[m], rstdB, gqB], [CQNB[m][tb]],
                           lambda e, m=m: e.scalar_tensor_tensor(out=CQN[:, m, blk], in0=R1[:, m, :], scalar=gq[:, m:m + 1], in1=rstd[:],
                                                                 op0=ALU.mult, op1=ALU.mult))
                    rstd_from_sq(tmpA, tmpAB, 2, 1.0 / 256, 6)
                    for m in range(2):
                        op("dve", [R1B[6 + m], rstdB, gqB], [CKVNB[m][tb]],
                           lambda e, m=m: e.scalar_tensor_tensor(out=CKVN[:, m, blk], in0=R1[:, 6 + m, :], scalar=gq[:, 6 + m:7 + m], in1=rstd[:],
                                                                 op0=ALU.mult, op1=ALU.mult))
                    b = proj_ap(w1, w1B, D, 64, xn_ap, xnB, 8)
                    op("dve", [psB[b], ropeB], [tRB[0]],
                       lambda e, b=b: e.tensor_tensor(out=tR[:, 0, :], in0=ps[b][32:64, :], in1=ROPE[32:64, blk], op=ALU.mult))
                    op("dve", [psB[b], ropeB], [tRB[1]],
                       lambda e, b=b: e.tensor_tensor(out=tR[:, 1, :], in0=ps[b][0:32, :], in1=ROPE[0:32, blk], op=ALU.mult))
                    op("pool", [tRB[0], tRB[1]], [KRB[tb]],
                       lambda e: e.tensor_tensor(out=KR[:, blk], in0=tR[:, 0, :], in1=tR[:, 1, :], op=ALU.add))
                k.barrier()
            s2 = ExitStack()
            with s2:
                wuq, wuqB = load_w(s2, "s_c_wuq", dw["c_wuq"], 2048)
                wukv, wukvB = load_w(s2, "s_c_wukv", dw["c_wukv"], 2048)
                KTh = [sb(s2, "KTh%d" % i, [P, L], BF16) for i in range(2)]
                KThB = bufs(2, NBLK)
                VAh = [sb(s2, "VAh%d" % i, [P, 16, P], BF16) for i in range(2)]
                VAhB = bufs(2, 4)
                QT = [sb(s2, "QT%d" % i, [P, BW], BF16) for i in range(2)]
                QTB = bufs(2)
                NPT = 4
                LA = 2
                PT = [sb(s2, "PT%d" % i, [P, BW], BF16) for i in range(NPT)]
                PTB = bufs(NPT)
                QF = sb(s2, "QF", [64, BW], F32)
                QFB = Buf()
                maskT = sb(s2, "maskT", [P, P], BF16)
                maskB = Buf()
                rsb = sb(s2, "rsb", [P, BW], F32)
                rsbB = bufs(2)
                dma("pool", maskT[:], dw["c_maskT"], [], [maskB])
                for i in range(2):
                    op("pool", [], KThB[i], lambda e, i=i: e.memset(KTh[i][32:64, :], 0.0))
                    c0 = 64 if i == 0 else 0
                    op("pool", [], VAhB[i], lambda e, i=i, c0=c0: e.memset(VAh[i][:, :, c0:c0 + 64], 1.0))
                def kv_build(h):
                    par = h % 2
                    vc0 = par * 64
                    for tb in range(NBLK):
                        blk = slice(tb * BW, (tb + 1) * BW)
                        b = k.nextps()
                        for kc in range(2):
                            op("pe", [wukvB, CKVNB[kc][tb]], [psB[b]],
                               lambda e, kc=kc, b=b, blk=blk: e.matmul(ps[b][64:128, :], lhsT=wukv[:, kc, h * P:h * P + 64], rhs=CKVN[:, kc, blk],
                                                                      start=(kc == 0), stop=(kc == 1)), inc=(kc == 1))
                        op("act", [psB[b]], [KThB[par][tb]],
                           lambda e, b=b, blk=blk: e.activation(out=KTh[par][64:128, blk], in_=ps[b][64:128, :], func=AF.Copy))
                        op("dve", [KRB[tb]], [KThB[par][tb]],
                           lambda e, blk=blk: e.tensor_scalar(out=KTh[par][0:32, blk], in0=KR[:, blk], scalar1=1.0, scalar2=None, op0=ALU.mult))
                        b = k.nextps()
                        for i4 in range(4):
                            tt = tb * 4 + i4
                            for kc in range(2):
                                op("pe", [wukvB, CKVNB[kc][tb]], [psB[b]],
                                   lambda e, kc=kc, b=b, tt=tt, i4=i4: e.matmul(ps[b][:, i4 * 64:(i4 + 1) * 64], lhsT=CKVN[:, kc, tt * P:(tt + 1) * P],
                                                                                rhs=wukv[:, kc, h * P + 64:h * P + 128], start=(kc == 0), stop=(kc == 1)),
                                   inc=(kc == 1 and i4 == 3))
                        op("act", [psB[b]], [VAhB[par][tb]],
                           lambda e, b=b, tb=tb: e.activation(out=VAh[par][:, tb * 4:(tb + 1) * 4, vc0:vc0 + 64],
                                                              in_=ps[b][:, 0:256].rearrange("p (a c) -> p a c", a=4), func=AF.Copy))

                def q_prep(h, qb, qi):
                    qblk = slice(qb * BW, (qb + 1) * BW)
                    b = k.nextps()
                    for kc in range(6):
                        op("pe", [wuqB, CQNB[kc][qb]], [psB[b]],
                           lambda e, kc=kc, b=b: e.matmul(ps[b][:, :], lhsT=wuq[:, kc, h * P:(h + 1) * P], rhs=CQN[:, kc, qblk],
                                                          start=(kc == 0), stop=(kc == 5)), inc=(kc == 5))
                    op("act", [psB[b]], [QTB[qi]], lambda e, b=b: e.activation(out=QT[qi][:, :], in_=ps[b][:, :], func=AF.Copy))
                    op("act", [psB[b]], [QFB], lambda e, b=b: e.activation(out=QF[:, :], in_=ps[b][0:64, :], func=AF.Copy))
                    op("dve", [QFB, ropeB], [tRB[0]],
                       lambda e: e.tensor_tensor(out=tR[:, 0, :], in0=QF[32:64, :], in1=ROPE[32:64, qblk], op=ALU.mult))
                    op("dve", [QFB, ropeB], [tRB[1]],
                       lambda e: e.tensor_tensor(out=tR[:, 1, :], in0=QF[0:32, :], in1=ROPE[0:32, qblk], op=ALU.mult))
                    op("dve", [tRB[0], tRB[1]], [QTB[qi]],
                       lambda e: e.tensor_tensor(out=QT[qi][0:32, :], in0=tR[:, 0, :], in1=tR[:, 1, :], op=ALU.add))

                pti = [0]

                def attend(h, qb, qi, bo):
                    par, j = h % 2, h // 2
                    base = par * 64
                    oth = 64 - base
                    qblk = slice(qb * BW, (qb + 1) * BW)
                    nkc = 4 * qb + 4

                    def pv(kc, pi, q_lo):
                        op("pe", [PTB[pi], VAhB[par][kc // 4]], [psB[bo]],
                           lambda e: e.matmul(ps[bo][:, q_lo:BW], lhsT=VAh[par][:, kc, :], rhs=PT[pi][:, q_lo:BW],
                                              start=(kc == 0), stop=(kc == nkc - 1)))
                    pend_pv = []
                    for kc in range(nkc):
                        q_lo = max(0, kc - 4 * qb) * P
                        pi = pti[0] % NPT
                        pti[0] += 1
                        b = k.nextps()
                        op("pe", [KThB[par][kc // 4], QTB[qi]], [psB[b]],
                           lambda e, b=b, kc=kc, q_lo=q_lo: e.matmul(ps[b][:, q_lo:BW], lhsT=KTh[par][:, kc * P:(kc + 1) * P],
                                                                     rhs=QT[qi][:, q_lo:BW], start=True, stop=True))
                        op("act", [psB[b]], [PTB[pi]],
                           lambda e, b=b, pi=pi, q_lo=q_lo: e.activation(out=PT[pi][:, q_lo:BW], in_=ps[b][:, q_lo:BW], func=AF.Exp, scale=SC))
                        if kc >= 4 * qb:
                            op("dve", [PTB[pi], maskB], [PTB[pi]],
                               lambda e, pi=pi, q_lo=q_lo: e.tensor_tensor(out=PT[pi][:, q_lo:q_lo + P], in0=PT[pi][:, q_lo:q_lo + P],
                                                                           in1=maskT[:, :], op=ALU.mult))
                        pend_pv.append((kc, pi, q_lo))
                        if len(pend_pv) > LA:
                            pv(*pend_pv.pop(0))
                    while pend_pv:
                        pv(*pend_pv.pop(0))
                    op("dve", [psB[bo]], [rsbB[par]],
                       lambda e: e.tensor_scalar(out=rsb[base:base + 64, :], in0=ps[bo][oth:oth + 64, :], scalar1=1.0, scalar2=None, op0=ALU.mult))
                    op("act", [rsbB[par]], [rsbB[par]], lambda e: e.activation(out=rsb[base:base + 64, :], in_=rsb[base:base + 64, :], func=AF.Ln))
                    op("act", [rsbB[par]], [rsbB[par]], lambda e: e.activation(out=rsb[base:base + 64, :], in_=rsb[base:base + 64, :], func=AF.Exp, scale=-1.0))
                    op("dve", [psB[bo], rsbB[par]], [GTB[j][qb]],
                       lambda e: e.tensor_tensor(out=GT[base:base + 64, j, qblk], in0=ps[bo][base:base + 64, :],
                                                 in1=rsb[base:base + 64, :], op=ALU.mult))

                import os
                NH = int(os.environ.get("MLA_NH", "16"))
                tasks = [(h, qb) for h in range(NH) for qb in range(NBLK)]
                if tasks:
                    kv_build(0)
                    q_prep(0, 0, 0)
                for i, (h, qb) in enumerate(tasks):
                    if i + 1 < len(tasks):
                        h2, qb2 = tasks[i + 1]
                        if h2 != h:
                            kv_build(h2)
                        q_prep(h2, qb2, (i + 1) % 2)
                    attend(h, qb, i % 2, 6 + (i % 2))
                k.barrier()
            sA.close()
            s3 = ExitStack()
            with s3:
                wz, wzB = load_w(s3, "s_c_wz", dw["c_wz"], D)
                w_out, w_outB = load_w(s3, "s_c_wout", dw["c_w_out"], D)
                R1 = sb(s3, "R1", [P, 8, BW], F32)
                R1B = bufs(8)
                xnt = sb(s3, "xnt", [P, 8, BW], BF16)
                xnB = bufs(8)
                tmpA = sb(s3, "tmpA", [P, 8, BW], BF16)
                tmpAB = bufs(8)
                sz = sb(s3, "sz", [P, 8, BW], BF16)
                szB = bufs(8)
                xn_ap = lambda c: xnt[:, c, :]
                for tb in range(NBLK):
                    blk = slice(tb * BW, (tb + 1) * BW)
                    prenorm_ap(li, tb, xn_ap, xnB, tmpA, tmpAB)
                    for m in range(8):
                        b = proj_ap(wz, wzB, m * P, P, xn_ap, xnB, 8)
                        op("act", [psB[b]], [szB[m]], lambda e, b=b, m=m: e.activation(out=sz[:, m, :], in_=ps[b][:, :], func=AF.Silu))
                        op("pool", [szB[m], GTB[m][tb]], [szB[m]],
                           lambda e, m=m, blk=blk: e.tensor_tensor(out=sz[:, m, :], in0=sz[:, m, :], in1=GT[:, m, blk], op=ALU.mult))
                    outproj(li, tb, sz, szB, w_out, w_outB, R1, R1B, tmpA, tmpAB)
                k.barrier()

    def layer_s5(li):
        TT = ALU
        st = ExitStack()
        with st:
            UY = sb(st, "UY", [P, 8, L], BF16)
            UYB = bufs(8, NBLK)
            dT = sb(st, "dT", [P, 16], F32)
            dTB = Buf()
            dma("sp", dT[:], dw["a_dg"], [], [dTB])
            s1 = ExitStack()
            with s1:
                wu, wuB = load_w(s1, "a_wu", dw["a_wu"], D)
                xnt = sb(s1, "xnt", [P, 8, BW], BF16)
                xnB = bufs(8)
                tmpA = sb(s1, "tmpA", [P, 8, BW], BF16)
                tmpAB = bufs(8)
                xn_ap = lambda c: xnt[:, c, :]
                for tb in range(NBLK):
                    blk = slice(tb * BW, (tb + 1) * BW)
                    prenorm_ap(li, tb, xn_ap, xnB, tmpA, tmpAB)
                    for m in range(8):
                        b = proj_ap(wu, wuB, m * P, P, xn_ap, xnB, 8)
                        op("act", [psB[b]], [UYB[m][tb]], lambda e, b=b, m=m, blk=blk: e.activation(out=UY[:, m, blk], in_=ps[b][:, :], func=AF.Copy))
                k.barrier()
            s2 = ExitStack()
            with s2:
                NJ = 32
                BrL, BrLB = sb(s2, "BrL", [P, NJ, P], BF16), Buf()
                BiL, BiLB = sb(s2, "BiL", [P, NJ, P], BF16), Buf()
                CrP, CrPB = sb(s2, "CrP", [P, NJ, P], BF16), Buf()
                CiP, CiPB = sb(s2, "CiP", [P, NJ, P], BF16), Buf()
                for t_, B_, nm in ((BrL, BrLB, "a_brl"), (BiL, BiLB, "a_bil"), (CrP, CrPB, "a_crp"), (CiP, CiPB, "a_cip")):
                    for q4 in range(4):
                        dma("pool", t_[:, q4 * 8:(q4 + 1) * 8, :], dw[nm][:, q4 * 8:(q4 + 1) * 8, :], [], [B_])
                lam = sb(s2, "lam", [P, 3, NJ], F32)
                prepB = Buf()
                dma("sp", lam[:], dw["a_lam"], [], [prepB])
                W = {}
                for nm in ("dt", "lrdt", "th", "mag", "t", "t2", "c", "s", "q", "c2", "s2", "cs", "ar", "ai", "den", "nr", "fr", "fi",
                           "u1", "u2", "ir", "ii", "pr", "pi", "A128r", "nA128i", "A128i"):
                    W[nm] = sb(s2, "w_" + nm, [P, NJ], F32)
                lr, li_, ldt = lam[:, 0, :], lam[:, 1, :], lam[:, 2, :]

                def tt(o, a, b_, o_):
                    op("dve", [prepB], [prepB], lambda e: e.tensor_tensor(out=o, in0=a, in1=b_, op=o_))

                def ts(o, a, s1_, s2_, o1, o2=None):
                    if o2 is None:
                        op("dve", [prepB], [prepB], lambda e: e.tensor_scalar(out=o, in0=a, scalar1=s1_, scalar2=None, op0=o1))
                    else:
                        op("dve", [prepB], [prepB], lambda e: e.tensor_scalar(out=o, in0=a, scalar1=s1_, scalar2=s2_, op0=o1, op1=o2))

                def stt(o, a, sc, b_, o1, o2):
                    op("dve", [prepB], [prepB], lambda e: e.scalar_tensor_tensor(out=o, in0=a, scalar=sc, in1=b_, op0=o1, op1=o2))

                def csq(cr, ci):
                    tt(W["c2"][:], cr, cr, TT.mult)
                    tt(W["s2"][:], ci, ci, TT.mult)
                    tt(W["cs"][:], cr, ci, TT.mult)
                    tt(cr, W["c2"][:], W["s2"][:], TT.subtract)
                    ts(ci, W["cs"][:], 2.0, None, TT.mult)

                op("act", [prepB], [prepB], lambda e: e.activation(out=W["dt"][:], in_=ldt, func=AF.Exp))
                tt(W["lrdt"][:], lr, W["dt"][:], TT.mult)
                tt(W["th"][:], li_, W["dt"][:], TT.mult)
                op("act", [prepB], [prepB], lambda e: e.activation(out=W["mag"][:], in_=W["lrdt"][:], func=AF.Exp))
                ts(W["t"][:], W["th"][:], 1.0 / 64, None, TT.mult)
                tt(W["t2"][:], W["t"][:], W["t"][:], TT.mult)
                ts(W["q"][:], W["t2"][:], -1.0 / 720, None, TT.mult)
                stt(W["q"][:], W["q"][:], 1.0 / 24, W["t2"][:], TT.add, TT.mult)
                stt(W["q"][:], W["q"][:], -0.5, W["t2"][:], TT.add, TT.mult)
                ts(W["c"][:], W["q"][:], 1.0, None, TT.add)
                ts(W["q"][:], W["t2"][:], -1.0 / 5040, None, TT.mult)
                stt(W["q"][:], W["q"][:], 1.0 / 120, W["t2"][:], TT.add, TT.mult)
                stt(W["q"][:], W["q"][:], -1.0 / 6, W["t2"][:], TT.add, TT.mult)
                stt(W["s"][:], W["q"][:], 1.0, W["t"][:], TT.add, TT.mult)
                for _ in range(6):
                    csq(W["c"][:], W["s"][:])
                tt(W["ar"][:], W["mag"][:], W["c"][:], TT.mult)
                tt(W["ai"][:], W["mag"][:], W["s"][:], TT.mult)
                tt(W["den"][:], lr, lr, TT.mult)
                tt(W["u1"][:], li_, li_, TT.mult)
                tt(W["den"][:], W["den"][:], W["u1"][:], TT.add)
                op("dve", [prepB], [prepB], lambda e: e.reciprocal(out=W["den"][:], in_=W["den"][:]))
                ts(W["nr"][:], W["ar"][:], -1.0, None, TT.add)
                tt(W["u1"][:], W["nr"][:], lr, TT.mult)
                tt(W["u2"][:], W["ai"][:], li_, TT.mult)
                tt(W["u1"][:], W["u1"][:], W["u2"][:], TT.add)
                tt(W["fr"][:], W["u1"][:], W["den"][:], TT.mult)
                tt(W["u1"][:], W["ai"][:], lr, TT.mult)
                tt(W["u2"][:], W["nr"][:], li_, TT.mult)
                tt(W["u1"][:], W["u1"][:], W["u2"][:], TT.subtract)
                tt(W["fi"][:], W["u1"][:], W["den"][:], TT.mult)
                tt(W["u1"][:], W["ar"][:], W["ar"][:], TT.mult)
                tt(W["u2"][:], W["ai"][:], W["ai"][:], TT.mult)
                tt(W["u1"][:], W["u1"][:], W["u2"][:], TT.add)
                op("dve", [prepB], [prepB], lambda e: e.reciprocal(out=W["u1"][:], in_=W["u1"][:]))
                tt(W["ir"][:], W["ar"][:], W["u1"][:], TT.mult)
                tt(W["ii"][:], W["ai"][:], W["u1"][:], TT.mult)
                ts(W["ii"][:], W["ii"][:], -1.0, None, TT.mult)
                TPr = sb(s2, "TPr", [P, NJ, P], BF16)
                TPi = sb(s2, "TPi", [P, NJ, P], BF16)
                TNr = sb(s2, "TNr", [P, NJ, P], BF16)
                TNi = sb(s2, "TNi", [P, NJ, P], BF16)
                sT = ExitStack()
                sT.__enter__()
                m1 = sb(sT, "m1t", [P, NJ, 64], F32)
                m2 = sb(sT, "m2t", [P, NJ, 64], F32)
                op("dve", [prepB], [prepB], lambda e: e.memset(TPr[:, :, 0:1], 1.0))
                op("dve", [prepB], [prepB], lambda e: e.memset(TPi[:, :, 0:1], 0.0))
                ts(TNr[:, :, 0:1], W["fr"][:].unsqueeze(2), 1.0, None, TT.mult)
                ts(TNi[:, :, 0:1], W["fi"][:].unsqueeze(2), 1.0, None, TT.mult)
                for (Tr_, Ti_, pr0, pi0) in ((TPr, TPi, "ar", "ai"), (TNr, TNi, "ir", "ii")):
                    ts(W["pr"][:], W[pr0][:], 1.0, None, TT.mult)
                    ts(W["pi"][:], W[pi0][:], 1.0, None, TT.mult)
                    for kk in range(7):
                        n = 1 << kk
                        pr_b = W["pr"][:].unsqueeze(2).to_broadcast([P, NJ, n])
                        pi_b = W["pi"][:].unsqueeze(2).to_broadcast([P, NJ, n])
                        lo_r, lo_i = Tr_[:, :, 0:n], Ti_[:, :, 0:n]
                        tt(m1[:, :, 0:n], lo_r, pr_b, TT.mult)
                        tt(m2[:, :, 0:n], lo_i, pi_b, TT.mult)
                        tt(Tr_[:, :, n:2 * n], m1[:, :, 0:n], m2[:, :, 0:n], TT.subtract)
                        tt(m1[:, :, 0:n], lo_r, pi_b, TT.mult)
                        tt(m2[:, :, 0:n], lo_i, pr_b, TT.mult)
                        tt(Ti_[:, :, n:2 * n], m1[:, :, 0:n], m2[:, :, 0:n], TT.add)
                        csq(W["pr"][:], W["pi"][:])
                    if pr0 == "ar":
                        ts(W["A128r"][:], W["pr"][:], 1.0, None, TT.mult)
                        ts(W["nA128i"][:], W["pi"][:], -1.0, None, TT.mult)
                        ts(W["A128i"][:], W["pi"][:], 1.0, None, TT.mult)
                k.barrier()
                sT.close()
                car = sb(s2, "car", [P, 2, NJ], F32)
                carB = bufs(NJ)
                op("dve", [prepB], carB, lambda e: e.memset(car[:], 0.0))
                rmask = sb(s2, "rmask", [P, 2, 4, P], F32)
                op("dve", [prepB], [prepB], lambda e: e.memset(rmask[:], 1.0))
                op("dve", [prepB], [prepB], lambda e: e.memset(rmask[:, :, :, 0:1], 0.0))
                op("dve", [prepB], [prepB], lambda e: e.tensor_scalar(out=TNi[:], in0=TNi[:], scalar1=-1.0, scalar2=None, op0=TT.mult))
                NA, NCD, NQ = 2, 4, 2
                TA = [sb(s2, "TA%d" % i, [P, 4, P], BF16) for i in range(NA)]
                TBf = [sb(s2, "TB%d" % i, [P, 4, P], BF16) for i in range(NA)]
                CD = [sb(s2, "CD%d" % i, [P, 2, 4, P], F32) for i in range(NCD)]
                T1 = sb(s2, "T1", [P, 4, P], F32)
                T2 = sb(s2, "T2", [P, 4, P], F32)
                Q = [[sb(s2, "Q%d_%d" % (i, q_), [P, 4, P], BF16) for q_ in range(4)] for i in range(NQ)]
                AB_, BB_, CDB = bufs(NA), bufs(NA), bufs(NCD)
                T1B, T2B = Buf(), Buf()
                QB = bufs(NQ, 4)
                ea = sb(s2, "ea", [P, 2, 4], F32)
                eb = sb(s2, "eb", [P, 2, 4], F32)
                eB = Buf()
                ytmp = sb(s2, "ytmp", [P, BW], F32)
                yB = Buf()

                def flat(t_):
                    return t_[:].rearrange("p a t -> p (a t)")

                units = [(chc, c) for cp in range(4) for c in range(16) for chc in (2 * cp, 2 * cp + 1)]
                NU = len(units)

                def stA(u):
                    chc, c = units[u]
                    j0, qt = chc * 4, c // 4
                    cols = slice(c * P, (c + 1) * P)
                    ai = u % NA
                    ba = k.nextps()
                    bb = k.nextps()
                    for jl in range(4):
                        op("pe", [BrLB, UYB[chc][qt]], [psB[ba]],
                           lambda e, jl=jl: e.matmul(ps[ba][:, jl * P:(jl + 1) * P], lhsT=BrL[:, j0 + jl, :], rhs=UY[:, chc, cols], start=True, stop=True),
                           inc=(jl == 3))
                    for jl in range(4):
                        op("pe", [BiLB, UYB[chc][qt]], [psB[bb]],
                           lambda e, jl=jl: e.matmul(ps[bb][:, jl * P:(jl + 1) * P], lhsT=BiL[:, j0 + jl, :], rhs=UY[:, chc, cols], start=True, stop=True),
                           inc=(jl == 3))
                    op("act", [psB[ba]], [AB_[ai]], lambda e: e.activation(out=flat(TA[ai]), in_=ps[ba][:, :], func=AF.Copy))
                    op("act", [psB[bb]], [BB_[ai]], lambda e: e.activation(out=flat(TBf[ai]), in_=ps[bb][:, :], func=AF.Copy))

                def ctx(u):
                    chc, c = units[u]
                    j0 = chc * 4
                    ai, ci, qi = u % NA, u % NCD, u % NQ
                    return chc, c, j0, ai, ci, qi

                def stB_mul(u):
                    chc, c, j0, ai, ci, qi = ctx(u)
                    A, B_ = TA[ai], TBf[ai]
                    C, Dd = CD[ci][:, 0, :, :], CD[ci][:, 1, :, :]
                    tnr, ntni = TNr[:, j0:j0 + 4, :], TNi[:, j0:j0 + 4, :]
                    op("dve", [AB_[ai], prepB], [CDB[ci]], lambda e: e.tensor_tensor(out=C, in0=A[:], in1=tnr, op=TT.mult))
                    op("dve", [BB_[ai], prepB], [T1B], lambda e: e.tensor_tensor(out=T1[:], in0=B_[:], in1=ntni, op=TT.mult))
                    op("dve", [AB_[ai], prepB], [CDB[ci]], lambda e: e.tensor_tensor(out=Dd, in0=A[:], in1=ntni, op=TT.mult))
                    op("dve", [BB_[ai], prepB], [T2B], lambda e: e.tensor_tensor(out=T2[:], in0=B_[:], in1=tnr, op=TT.mult))

                def stB_comb(u):
                    chc, c, j0, ai, ci, qi = ctx(u)
                    C, Dd = CD[ci][:, 0, :, :], CD[ci][:, 1, :, :]
                    op("dve", [CDB[ci], T1B], [CDB[ci]], lambda e: e.tensor_tensor(out=C, in0=C, in1=T1[:], op=TT.add))
                    op("dve", [CDB[ci], T2B], [CDB[ci]], lambda e: e.tensor_tensor(out=Dd, in0=Dd, in1=T2[:], op=TT.subtract))

                def stC_inject(u):
                    chc, c, j0, ai, ci, qi = ctx(u)
                    if c > 0:
                        op("dve", [CDB[ci], carB[j0]], [CDB[ci]],
                           lambda e: e.tensor_tensor(out=CD[ci][:, :, :, 0:1], in0=CD[ci][:, :, :, 0:1],
                                                     in1=car[:, :, j0:j0 + 4].unsqueeze(3), op=TT.add))

                def stC_scan(u):
                    chc, c, j0, ai, ci, qi = ctx(u)
                    op("dve", [CDB[ci], prepB], [CDB[ci]],
                       lambda e: e.tensor_tensor_scan(out=CD[ci][:].rearrange("p a b t -> p (a b t)"), data0=rmask[:].rearrange("p a b t -> p (a b t)"),
                                                      data1=CD[ci][:].rearrange("p a b t -> p (a b t)"), initial=0.0, op0=TT.mult, op1=TT.add))

                def stD_e(u):
                    chc, c, j0, ai, ci, qi = ctx(u)
                    if c < 15:
                        xl = CD[ci][:, :, :, P - 1]
                        a_r = W["A128r"][:, j0:j0 + 4].unsqueeze(1).to_broadcast([P, 2, 4])
                        a_i = W["A128i"][:, j0:j0 + 4].unsqueeze(1).to_broadcast([P, 2, 4])
                        op("dve", [CDB[ci], prepB, carB[j0]], [eB], lambda e: e.tensor_tensor(out=ea[:], in0=xl, in1=a_r, op=TT.mult))
                        op("dve", [CDB[ci], prepB, carB[j0]], [eB], lambda e: e.tensor_tensor(out=eb[:], in0=xl, in1=a_i, op=TT.mult))

                def stD_q3(u):
                    chc, c, j0, ai, ci, qi = ctx(u)
                    C = CD[ci][:, 0, :, :]
                    tpi = TPi[:, j0:j0 + 4, :]
                    op("dve", [CDB[ci], prepB], [QB[qi][2]],
                       lambda e: e.scalar_tensor_tensor(out=Q[qi][2][:], in0=C, scalar=-1.0, in1=tpi, op0=TT.mult, op1=TT.mult))

                def stD_car(u):
                    chc, c, j0, ai, ci, qi = ctx(u)
                    if c < 15:
                        op("dve", [eB], [carB[j0]], lambda e: e.tensor_tensor(out=car[:, 0, j0:j0 + 4], in0=ea[:, 0, :], in1=eb[:, 1, :], op=TT.add))
                        op("dve", [eB], [carB[j0]], lambda e: e.tensor_tensor(out=car[:, 1, j0:j0 + 4], in0=ea[:, 1, :], in1=eb[:, 0, :], op=TT.subtract))

                def stD_rest(u):
                    chc, c, j0, ai, ci, qi = ctx(u)
                    qt, cq = c // 4, c % 4
                    C, Dd = CD[ci][:, 0, :, :], CD[ci][:, 1, :, :]
                    bo = 6 + (chc % 2)
                    tpr, tpi = TPr[:, j0:j0 + 4, :], TPi[:, j0:j0 + 4, :]
                    q = Q[qi]
                    op("dve", [CDB[ci], prepB], [QB[qi][0]], lambda e: e.tensor_tensor(out=q[0][:], in0=C, in1=tpr, op=TT.mult))
                    op("dve", [CDB[ci], prepB], [QB[qi][1]], lambda e: e.tensor_tensor(out=q[1][:], in0=Dd, in1=tpi, op=TT.mult))
                    op("pool", [CDB[ci], prepB], [QB[qi][3]], lambda e: e.tensor_tensor(out=q[3][:], in0=Dd, in1=tpr, op=TT.mult))
                    n = 0
                    for jl in range(4):
                        j = j0 + jl
                        for qq in range(4):
                            wt, wtB = (CrP, CrPB) if qq < 2 else (CiP, CiPB)
                            n += 1
                            op("pe", [QB[qi][qq], wtB], [psB[bo]],
                               lambda e, j=j, jl=jl, qq=qq, wt=wt, n=n: e.matmul(ps[bo][:, cq * P:(cq + 1) * P], lhsT=wt[:, j, :], rhs=q[qq][:, jl, :],
                                                                                  start=(n == 1), stop=(n == 16)), inc=(n == 16))
                    if cq == 3:
                        blk = slice(qt * BW, (qt + 1) * BW)
                        op("act", [psB[bo]], [yB], lambda e: e.activation(out=ytmp[:], in_=ps[bo][:, :], func=AF.Copy))
                        op("dve", [yB, UYB[chc][qt], dTB], [yB],
                           lambda e: e.scalar_tensor_tensor(out=ytmp[:], in0=UY[:, chc, blk], scalar=dT[:, chc:chc + 1], in1=ytmp[:],
                                                            op0=TT.mult, op1=TT.add))
                        op("act", [yB], [UYB[chc][qt]], lambda e: e.activation(out=UY[:, chc, blk], in_=ytmp[:], func=AF.Gelu_apprx_tanh))

                def ok(u):
                    return 0 <= u < NU

                for i in range(NU + 3):
                    if ok(i):
                        stA(i)
                    if ok(i - 2):
                        stC_inject(i - 2)
                    if ok(i - 1):
                        stB_mul(i - 1)
                    if ok(i - 2):
                        stC_scan(i - 2)
                    if ok(i - 3):
                        stD_e(i - 3)
                    if ok(i - 1):
                        stB_comb(i - 1)
                    if ok(i - 3):
                        stD_q3(i - 3)
                        stD_car(i - 3)
                        stD_rest(i - 3)
                k.barrier()
            s3 = ExitStack()
            with s3:
                wz, wzB = load_w(s3, "a_wzs", dw["a_wz"], D)
                wg, wgB = load_w(s3, "a_wgs", dw["a_wglu"], D)
                w_out, w_outB = load_w(s3, "a_wouts", dw["a_w_out"], D)
                R1 = sb(s3, "R1", [P, 8, BW], F32)
                R1B = bufs(8)
                xnt = sb(s3, "xnt", [P, 8, BW], BF16)
                xnB = bufs(8)
                tmpA = sb(s3, "tmpA", [P, 8, BW], BF16)
                tmpAB = bufs(8)
                sz = sb(s3, "sz", [P, 8, BW], BF16)
                szB = bufs(8)
                sg = sb(s3, "sg", [P, 2, BW], BF16)
                sgB = bufs(2)
                xn_ap = lambda c: xnt[:, c, :]
                for tb in range(NBLK):
                    blk = slice(tb * BW, (tb + 1) * BW)
                    prenorm_ap(li, tb, xn_ap, xnB, tmpA, tmpAB)
                    uy_ap = lambda c, blk=blk: UY[:, c, blk]
                    uyB_t = [UYB[c][tb] for c in range(8)]
                    for m in range(8):
                        b = proj_ap(wz, wzB, m * P, P, xn_ap, xnB, 8)
                        op("act", [psB[b]], [szB[m]], lambda e, b=b, m=m: e.activation(out=sz[:, m, :], in_=ps[b][:, :], func=AF.Silu))
                        b = proj_ap(wg, wgB, m * P, P, uy_ap, uyB_t, 8)
                        op("act", [psB[b], dTB], [sgB[m % 2]],
                           lambda e, b=b, m=m: e.activation(out=sg[:, m % 2, :], in_=ps[b][:, :], func=AF.Sigmoid, bias=dT[:, 8 + m:9 + m], scale=1.0))
                        op("pool", [szB[m], uyB_t[m]], [szB[m]],
                           lambda e, m=m, blk=blk: e.tensor_tensor(out=sz[:, m, :], in0=sz[:, m, :], in1=UY[:, m, blk], op=ALU.mult))
                        op("pool", [szB[m], sgB[m % 2]], [szB[m]],
                           lambda e, m=m: e.tensor_tensor(out=sz[:, m, :], in0=sz[:, m, :], in1=sg[:, m % 2, :], op=ALU.mult))
                    outproj(li, tb, sz, szB, w_out, w_outB, R1, R1B, tmpA, tmpAB)
                k.barrier()

    for li in layers:
        if li == 3:
            layer_sgu(li)
        elif li == 1:
            layer_swa(li)
        elif li == 2:
            layer_mla(li)
        elif li == 0:
            layer_s5(li)

    toks = []
    for c in range(8):
        for tb in range(NBLK):
            toks.append(dma("sp", outT_d[c * P:(c + 1) * P, tb * BW:(tb + 1) * BW], X[:, c, tb * BW:(tb + 1) * BW], [xB[c][tb]], []))
    k._wait("sp", toks)
    k.barrier()


def host_inputs(inp, layers):
    f = lambda a: np.ascontiguousarray(np.asarray(a, dtype=np.float32))
    common = {}
    common["gpre"] = f(np.asarray(inp["pre_norm"]).reshape(4, 8, P).transpose(2, 0, 1).reshape(P, 32))
    common["gpost"] = f(np.asarray(inp["post_norm"]).reshape(4, 8, P).transpose(2, 0, 1).reshape(P, 32))
    common["ident"] = np.eye(P, dtype=np.float32)
    if 3 in layers:
        common["d_w_in"] = f(inp["d_w_in"][0])
        common["d_w_out"] = f(inp["d_w_out"][0])
        common["d_ws"] = f(np.asarray(inp["d_w_s"][0]).transpose(1, 0, 2))
        common["d_tril"] = np.tril(np.ones((P, P), dtype=np.float32))
        common["d_bs"] = f(inp["d_b_s"][0])
        common["d_lng"] = f(inp["d_ln_g"])
        common["d_lnb"] = f(inp["d_ln_b"])
    if 1 in layers:
        w = np.asarray(inp["b_w_in"][0], dtype=np.float32)
        q, kk, v, z = w[:, :1024], w[:, 1024:1152], w[:, 1152:1280], w[:, 1280:]
        common["b_w_in"] = f(np.concatenate([q, kk[:, :64], kk[:, :64], kk[:, 64:], kk[:, 64:], v, z], axis=1))
        common["b_w_out"] = f(inp["b_w_out"][0])
        common["b_sinks"] = f(inp["b_sinks"])
        def bucket(d):
            if d < 16:
                return d
            v_ = 16 + int(math.log(max(d, 1) / 16.0) / math.log(128 / 16.0) * 16)
            return min(v_, 31)
        rb = np.asarray(inp["rel_bias"], dtype=np.float32)
        bt = np.full((P, 16, 256), -1e30, dtype=np.float32)
        for kj in range(P):
            for qi in range(P):
                d_prev = qi + P - kj
                if d_prev < P:
                    bt[kj, :, qi] = rb[bucket(d_prev), :]
                d_cur = qi - kj
                if d_cur >= 0:
                    bt[kj, :, P + qi] = rb[bucket(d_cur), :]
        common["b_biasT"] = bt
    if 2 in layers:
        w = np.asarray(inp["c_w_in"][0], dtype=np.float32)
        kr = w[:, 1024:1056]
        common["c_w1"] = f(np.concatenate([w[:, :1024], kr, kr[:, 16:], kr[:, :16]], axis=1))
        common["c_wz"] = f(w[:, 1056:])
        uq = np.asarray(inp["c_w_uq"][0], dtype=np.float32).reshape(768, 16, 96)
        nope, rp = uq[:, :, :64], uq[:, :, 64:]
        common["c_wuq"] = f(np.concatenate([rp, rp[:, :, 16:], rp[:, :, :16], nope], axis=2).reshape(768, 2048))
        common["c_wukv"] = f(inp["c_w_ukv"][0])
        common["c_w_out"] = f(inp["c_w_out"][0])
        inv = (np.float32(10000.0) ** (-np.arange(0, 32, 2, dtype=np.float32) / np.float32(32))).astype(np.float32)
        ang = (np.arange(L, dtype=np.float32)[:, None] * inv[None, :]).astype(np.float32)
        cos, sin = np.cos(ang).astype(np.float32).T, np.sin(ang).astype(np.float32).T
        common["c_rope"] = f(np.concatenate([cos, cos, -sin, sin], axis=0))
        g = np.concatenate([np.asarray(inp["c_q_norm"][0]), np.asarray(inp["c_kv_norm"][0])]).astype(np.float32)
        common["c_gqkv"] = f(g.reshape(8, P).T)
        common["c_maskT"] = np.triu(np.ones((P, P), dtype=np.float32))
    if 0 in layers:
        w = np.asarray(inp["a_w_in"][0], dtype=np.float32)
        common["a_wu"] = f(w[:, :1024])
        common["a_wz"] = f(w[:, 1024:])
        common["a_wglu"] = f(inp["a_w_glu"][0])
        common["a_w_out"] = f(inp["a_w_out"][0])
        dg = np.concatenate([np.asarray(inp["a_d"][0]).reshape(8, P).T, np.asarray(inp["a_b_glu"][0]).reshape(8, P).T], axis=1)
        common["a_dg"] = f(dg)
        lam = np.stack([np.asarray(inp["a_lam_re"][0]).reshape(32, P).T, np.asarray(inp["a_lam_im"][0]).reshape(32, P).T,
                        np.repeat(np.asarray(inp["a_log_dt"][0]), 64).reshape(32, P).T], axis=1)
        common["a_lam"] = f(lam)
        brl = np.zeros((P, 32, P), np.float32); bil = np.zeros((P, 32, P), np.float32)
        crp = np.zeros((P, 32, P), np.float32); cip = np.zeros((P, 32, P), np.float32)
        b_re, b_im = np.asarray(inp["a_b_re"][0]), np.asarray(inp["a_b_im"][0])
        c_re, c_im = np.asarray(inp["a_c_re"][0]), np.asarray(inp["a_c_im"][0])
        for j in range(32):
            for gl in range(2):
                g = 2 * j + gl
                r0 = 32 * (j % 4) + gl * 16
                brl[r0:r0 + 16, j, gl * 64:(gl + 1) * 64] = b_re[g].T
                bil[r0:r0 + 16, j, gl * 64:(gl + 1) * 64] = b_im[g].T
                crp[gl * 64:(gl + 1) * 64, j, r0:r0 + 16] = c_re[g].T
                cip[gl * 64:(gl + 1) * 64, j, r0:r0 + 16] = c_im[g].T
        common["a_brl"], common["a_bil"], common["a_crp"], common["a_cip"] = brl, bil, crp, cip
    return common


def run(inp, layers=(0, 1, 2, 3), cores=8, trace=False):
    nc = bass.Bass("TRN2", target_bir_lowering=False)
    build(nc, list(layers))
    common = host_inputs(inp, list(layers))
    x = np.asarray(inp["x"], dtype=np.float32)
    in_maps = []
    for b in range(cores):
        m = dict(common)
        m["xT"] = np.ascontiguousarray(x[b].T)
        in_maps.append(m)
    res = run_bass_kernel_spmd(nc, in_maps, core_ids=list(range(cores)), trace=trace)
    out = np.stack([np.ascontiguousarray(r["outT"].T) for r in res.results], axis=0)
    return out.astype(np.float32), res


def kernel(**inputs):
    out, _ = run(inputs)
    return out
```

```python
import math
import numpy as np
from contextlib import ExitStack
import concourse.bass as bass
import concourse.mybir as mybir
from concourse.bass_utils import run_bass_kernel_spmd

F32 = mybir.dt.float32
BF16 = mybir.dt.bfloat16
ALU = mybir.AluOpType
AF = mybir.ActivationFunctionType

P = 128
L = 2048
D = 1024
NBLK = 4
BW = 512
EPS = 1e-6
SELF_SYNC = True
NDS = 12


class Buf:
    __slots__ = ("w", "r")

    def __init__(self):
        self.w = None
        self.r = {}


class WB:
    def __init__(self):
        self.blocks = []

    def cols(self, c0, c1):
        return [b for (a0, a1, bl) in self.blocks if a0 < c1 and c0 < a1 for b in bl]

    def all(self):
        return [b for (_, _, bl) in self.blocks for b in bl]


def _flat(lst):
    out = []
    for b in lst:
        if isinstance(b, WB):
            out.extend(b.all())
        elif isinstance(b, list):
            out.extend(_flat(b))
        else:
            out.append(b)
    return out


def bufs(*shape):
    if len(shape) == 1:
        return [Buf() for _ in range(shape[0])]
    return [bufs(*shape[1:]) for _ in range(shape[0])]


class KB:
    def __init__(self, nc, es):
        self.nc = nc
        self.E = dict(pe=nc.tensor, act=nc.scalar, dve=nc.vector, pool=nc.gpsimd, sp=nc.sync)
        self.sem = {e: es.enter_context(nc.semaphore("s_" + e)) for e in ("pe", "act", "dve", "pool")}
        self.cnt = {e: 0 for e in self.sem}
        self.pend = {e: False for e in self.sem}
        self.dsem = {q: [[es.enter_context(nc.semaphore("d_%s%d" % (q, i))), 0] for i in range(NDS)]
                     for q in ("sp", "pool")}
        self.dcnt = {"sp": 0, "pool": 0}
        self.seen = {e: {} for e in self.E}
        self.ps = []
        self.psB = []
        self.psrot = 0
        self.nrot = 6

    def _semh(self, key):
        if isinstance(key, str):
            return self.sem[key]
        return self.dsem[key[0]][key[1]][0]

    def _wait(self, e, toks):
        need = {}
        for key, v in toks:
            if need.get(key, 0) < v:
                need[key] = v
        for key, v in need.items():
            if key == e and (e == "pe" or not SELF_SYNC):
                continue
            if self.seen[e].get(key, 0) >= v:
                continue
            self.E[e].wait_ge(self._semh(key), v)
            self.seen[e][key] = v

    def _deps(self, reads, writes):
        toks = []
        for b in reads:
            if b.w is not None:
                toks.append(b.w)
        for b in writes:
            if b.w is not None:
                toks.append(b.w)
            toks.extend(b.r.items())
        return toks

    def _mark(self, tok, reads, writes):
        key, v = tok
        for b in reads:
            if b.r.get(key, 0) < v:
                b.r[key] = v
        for b in writes:
            b.w = tok
            b.r = {}

    def op(self, e, reads, writes, fn, inc=True):
        reads, writes = _flat(reads), _flat(writes)
        self._wait(e, self._deps(reads, writes))
        ins = fn(self.E[e])
        if inc:
            self.cnt[e] += 1
            ins.then_inc(self.sem[e], 1)
            self.pend[e] = False
            tok = (e, self.cnt[e])
        else:
            self.pend[e] = True
            tok = (e, self.cnt[e] + 1)
        self._mark(tok, reads, writes)
        return ins

    def dma(self, q, out, in_, reads, writes):
        reads, writes = _flat(reads), _flat(writes)
        self._wait(q, self._deps(reads, writes))
        i = self.dcnt[q] % NDS
        self.dcnt[q] += 1
        ent = self.dsem[q][i]
        key = (q, i)
        if ent[1] > 0:
            self._wait(q, [(key, 16 * ent[1])])
        ins = self.E[q].dma_start(out=out, in_=in_)
        ins.then_inc(ent[0], 16)
        ent[1] += 1
        tok = (key, 16 * ent[1])
        self._mark(tok, reads, writes)
        return tok

    def all_tokens(self):
        toks = [(e, c) for e, c in self.cnt.items() if c > 0]
        for q in self.dsem:
            for i, ent in enumerate(self.dsem[q]):
                if ent[1] > 0:
                    toks.append(((q, i), 16 * ent[1]))
        return toks

    def barrier(self):
        for e in self.sem:
            assert not self.pend[e], e
        toks = self.all_tokens()
        for e in self.E:
            self._wait(e, [t for t in toks if t[0] != e])

    def nextps(self):
        b = self.psrot
        self.psrot = (self.psrot + 1) % self.nrot
        return b


def build(nc, layers):
    es = ExitStack()
    with es:
        _build(nc, es, layers)
    return nc


def _build(nc, es, layers):
    k = KB(nc, es)
    op, dma = k.op, k.dma

    def dram_in(name, shape, dt=F32):
        return nc.dram_tensor(name, list(shape), dt, kind="ExternalInput").ap()

    uid = [0]

    def sb(st, name, shape, dt):
        uid[0] += 1
        return st.enter_context(nc.sbuf_tensor("%s_%d" % (name, uid[0]), list(shape), dt))

    xT_d = dram_in("xT", [D, L])
    outT_d = nc.dram_tensor("outT", [D, L], F32, kind="ExternalOutput").ap()
    gpre_d = dram_in("gpre", [P, 32])
    gpost_d = dram_in("gpost", [P, 32])
    ident_d = dram_in("ident", [P, P])
    dw = {}
    if 3 in layers:
        dw["d_w_in"] = dram_in("d_w_in", [D, 3072])
        dw["d_w_out"] = dram_in("d_w_out", [D, D])
        dw["d_ws"] = dram_in("d_ws", [P, 16, P])
        dw["d_tril"] = dram_in("d_tril", [P, P])
        dw["d_bs"] = dram_in("d_bs", [16, P])
        dw["d_lng"] = dram_in("d_lng", [1, D])
        dw["d_lnb"] = dram_in("d_lnb", [1, D])

    if 1 in layers:
        dw["b_w_in"] = dram_in("b_w_in", [D, 2432])
        dw["b_w_out"] = dram_in("b_w_out", [D, D])
        dw["b_biasT"] = dram_in("b_biasT", [P, 16, 256])
        dw["b_sinks"] = dram_in("b_sinks", [1, 16])

    if 2 in layers:
        dw["c_w1"] = dram_in("c_w1", [D, 1088])
        dw["c_wz"] = dram_in("c_wz", [D, D])
        dw["c_wuq"] = dram_in("c_wuq", [768, 2048])
        dw["c_wukv"] = dram_in("c_wukv", [256, 2048])
        dw["c_w_out"] = dram_in("c_w_out", [D, D])
        dw["c_rope"] = dram_in("c_rope", [64, L])
        dw["c_gqkv"] = dram_in("c_gqkv", [P, 8])
        dw["c_maskT"] = dram_in("c_maskT", [P, P])

    if 0 in layers:
        dw["a_wu"] = dram_in("a_wu", [D, D])
        dw["a_wz"] = dram_in("a_wz", [D, D])
        dw["a_wglu"] = dram_in("a_wglu", [D, D])
        dw["a_w_out"] = dram_in("a_w_out", [D, D])
        dw["a_dg"] = dram_in("a_dg", [P, 16])
        dw["a_lam"] = dram_in("a_lam", [P, 3, 32])
        for nm in ("a_brl", "a_bil", "a_crp", "a_cip"):
            dw[nm] = dram_in(nm, [P, 32, P])

    X = sb(es, "X", [P, 8, L], F32)
    xB = bufs(8, NBLK)
    ones = sb(es, "ones", [P, P], BF16)
    onesB = Buf()
    ident = sb(es, "identb", [P, P], BF16)
    identB = Buf()
    gpre = sb(es, "gpre_s", [P, 32], F32)
    gpost = sb(es, "gpost_s", [P, 32], F32)
    gB = Buf()
    rstd = sb(es, "rstd", [P, BW], F32)
    rstdB = Buf()
    for i in range(8):
        k.ps.append(es.enter_context(nc.psum_tensor("ps%d" % i, [P, BW], F32)))
        k.psB.append(Buf())
    ps, psB = k.ps, k.psB

    for c in range(8):
        for tb in range(NBLK):
            dma("sp", X[:, c, tb * BW:(tb + 1) * BW], xT_d[c * P:(c + 1) * P, tb * BW:(tb + 1) * BW], [], [xB[c][tb]])
    dma("sp", gpre[:], gpre_d, [], [gB])
    dma("sp", gpost[:], gpost_d, [], [gB])
    dma("pool", ident[:], ident_d, [], [identB])
    op("dve", [], [onesB], lambda e: e.memset(ones[:], 1.0))
    epsc = sb(es, "epsc", [P, 1], F32)
    op("dve", [], [onesB], lambda e: e.memset(epsc[:], EPS))

    def load_w(st, name, d_ap, ncols, q="pool"):
        K = d_ap.shape[0]
        kc_n = K // P
        t = sb(st, name, [P, kc_n, ncols], BF16)
        B = WB()
        for c0 in range(0, ncols, 512):
            c1 = min(ncols, c0 + 512)
            bl = []
            for kc in range(kc_n):
                b1 = Buf()
                dma(q, t[:, kc, c0:c1], d_ap[kc * P:(kc + 1) * P, c0:c1], [], [b1])
                bl.append(b1)
            B.blocks.append((c0, c1, bl))
        return t, B

    def rstd_from_sq(sq, sqB, n, scale, c0=0):
        b = k.nextps()
        for c in range(n):
            op("pe", [sqB[c0 + c], onesB], [psB[b]],
               lambda e, c=c: e.matmul(ps[b][:, :], lhsT=ones[:, :], rhs=sq[:, c0 + c, :], start=(c == 0), stop=(c == n - 1)),
               inc=(c == n - 1))
        op("act", [psB[b]], [rstdB],
           lambda e: e.activation(out=rstd[:], in_=ps[b][:, :], func=AF.Ln, scale=scale, bias=epsc[:, 0:1]))
        op("act", [rstdB], [rstdB], lambda e: e.activation(out=rstd[:], in_=rstd[:], func=AF.Exp, scale=-0.5))

    def prenorm(li, tb, xn, xnB, tmpA, tmpAB):
        blk = slice(tb * BW, (tb + 1) * BW)
        for c in range(8):
            op("act", [xB[c][tb]], [tmpAB[c]],
               lambda e, c=c: e.activation(out=tmpA[:, c, :], in_=X[:, c, blk], func=AF.Square))
        rstd_from_sq(tmpA, tmpAB, 8, 1.0 / D)
        for c in range(8):
            op("dve", [xB[c][tb], rstdB, gB], [xnB[c]],
               lambda e, c=c: e.scalar_tensor_tensor(out=xn[:, c, :], in0=X[:, c, blk],
                                                     scalar=gpre[:, li * 8 + c:li * 8 + c + 1], in1=rstd[:],
                                                     op0=ALU.mult, op1=ALU.mult))

    def prenorm_ap(li, tb, xn_ap, xnB, tmpA, tmpAB):
        blk = slice(tb * BW, (tb + 1) * BW)
        for c in range(8):
            op("act", [xB[c][tb]], [tmpAB[c]],
               lambda e, c=c: e.activation(out=tmpA[:, c, :], in_=X[:, c, blk], func=AF.Square))
        rstd_from_sq(tmpA, tmpAB, 8, 1.0 / D)
        for c in range(8):
            op("dve", [xB[c][tb], rstdB, gB], [xnB[c]],
               lambda e, c=c: e.scalar_tensor_tensor(out=xn_ap(c), in0=X[:, c, blk],
                                                     scalar=gpre[:, li * 8 + c:li * 8 + c + 1], in1=rstd[:],
                                                     op0=ALU.mult, op1=ALU.mult))

    def proj_ap(w, wB, col0, M, rhs_ap, rhsB, nk):
        b = k.nextps()
        wBc = wB.cols(col0, col0 + M)
        for kc in range(nk):
            op("pe", [wBc, rhsB[kc]], [psB[b]],
               lambda e, kc=kc: e.matmul(ps[b][0:M, :], lhsT=w[:, kc, col0:col0 + M], rhs=rhs_ap(kc),
                                         start=(kc == 0), stop=(kc == nk - 1)),
               inc=(kc == nk - 1))
        return b

    def proj_fm(w, wB, col0, rhs, rhsB, nk, M=P):
        b = k.nextps()
        wBc = wB.cols(col0, col0 + M)
        for kc in range(nk):
            op("pe", [wBc, rhsB[kc]], [psB[b]],
               lambda e, kc=kc: e.matmul(ps[b][0:M, :], lhsT=w[:, kc, col0:col0 + M], rhs=rhs[:, kc, :],
                                         start=(kc == 0), stop=(kc == nk - 1)),
               inc=(kc == nk - 1))
        return b

    def outproj(li, tb, G, GB, wout, woutB, ybuf, ybufB, tmpA, tmpAB):
        blk = slice(tb * BW, (tb + 1) * BW)
        for m in range(8):
            b = proj_fm(wout, woutB, m * P, G, GB, 8)
            op("act", [psB[b]], [ybufB[m]], lambda e, m=m, b=b: e.activation(out=ybuf[:, m, :], in_=ps[b][:, :], func=AF.Copy))
            op("act", [psB[b]], [tmpAB[m]], lambda e, m=m, b=b: e.activation(out=tmpA[:, m, :], in_=ps[b][:, :], func=AF.Square))
        rstd_from_sq(tmpA, tmpAB, 8, 1.0 / D)
        for m in range(8):
            op("dve", [ybufB[m], rstdB, gB], [ybufB[m]],
               lambda e, m=m: e.scalar_tensor_tensor(out=ybuf[:, m, :], in0=ybuf[:, m, :],
                                                     scalar=gpost[:, li * 8 + m:li * 8 + m + 1], in1=rstd[:],
                                                     op0=ALU.mult, op1=ALU.mult))
            op("pool", [ybufB[m], xB[m][tb]], [xB[m][tb]],
               lambda e, m=m: e.tensor_tensor(out=X[:, m, blk], in0=X[:, m, blk], in1=ybuf[:, m, :], op=ALU.add))

    def layer_sgu(li):
        st = ExitStack()
        with st:
            w_in, w_inB = load_w(st, "d_win", dw["d_w_in"], 3072)
            w_out, w_outB = load_w(st, "d_wout", dw["d_w_out"], D)
            wsf = sb(st, "wsf", [P, 16, P], F32)
            wsfB = Buf()
            tril = sb(st, "tril", [P, P], F32)
            trilB = Buf()
            wsm = sb(st, "wsm", [P, 16, P], BF16)
            wsmB = Buf()
            wsT = sb(st, "wsT", [P, 16, P], BF16)
            wsTB = bufs(16)
            bsT = sb(st, "bsT", [P, 8, P], F32)
            bsTB = Buf()
            lng = sb(st, "lng", [P, D], F32)
            lnb = sb(st, "lnb", [P, D], F32)
            lnB = Buf()
            R1 = sb(st, "R1", [P, 8, BW], F32)
            R1B = bufs(8)
            R1bf = R1[:].bitcast(BF16)
            tmpA = sb(st, "tmpA", [P, 8, BW], BF16)
            tmpAB = bufs(8)
            gu = sb(st, "gu", [P, 8, BW], BF16)
            guB = bufs(8)
            sz = sb(st, "sz", [P, 8, BW], BF16)
            szB = bufs(8)
            vtmp = sb(st, "vtmp", [P, D], F32)
            vtmpB = Buf()
            stt = sb(st, "stt", [P, 2, 6], F32)
            mv = sb(st, "mv", [P, 2], F32)
            rs1 = sb(st, "rs1", [P, 1], F32)
            sttB = Buf()
            tmpS = sb(st, "tmpS", [P, BW], F32)
            tmpSB = Buf()

            def xn_ap(c):
                return R1bf[:, c // 2, (c % 2) * BW:(c % 2) * BW + BW]

            def vln_ap(ch, c0, c1):
                return R1bf[:, 4 + ch, c0:c1]

            xnB = [R1B[c // 2] for c in range(8)]

            dma("sp", wsf[:], dw["d_ws"], [], [wsfB])
            dma("sp", tril[:], dw["d_tril"], [], [trilB])
            for h in range(2):
                src = dw["d_bs"].rearrange("(gp h) t -> h gp t", h=2)[h]
                dma("sp", bsT[h * 64:(h + 1) * 64, :, :], src.unsqueeze(0).to_broadcast([64, 8, P]), [], [bsTB])
            dma("sp", lng[:], dw["d_lng"].to_broadcast([P, D]), [], [lnB])
            dma("sp", lnb[:], dw["d_lnb"].to_broadcast([P, D]), [], [lnB])
            op("dve", [wsfB, trilB], [wsmB],
               lambda e: e.tensor_tensor(out=wsm[:], in0=wsf[:], in1=tril[:].unsqueeze(1).to_broadcast([P, 16, P]), op=ALU.mult))
            for g in range(16):
                b = k.nextps()
                op("pe", [wsmB, identB], [psB[b]],
                   lambda e, g=g, b=b: e.matmul(ps[b][:, 0:P], lhsT=wsm[:, g, :], rhs=ident[:, :], start=True, stop=True))
                op("act", [psB[b]], [wsTB[g]], lambda e, g=g, b=b: e.activation(out=wsT[:, g, :], in_=ps[b][:, 0:P], func=AF.Copy))

            for tb in range(NBLK):
                blk = slice(tb * BW, (tb + 1) * BW)
                for c in range(8):
                    op("act", [xB[c][tb]], [tmpAB[c]],
                       lambda e, c=c: e.activation(out=tmpA[:, c, :], in_=X[:, c, blk], func=AF.Square))
                rstd_from_sq(tmpA, tmpAB, 8, 1.0 / D)
                for c in range(8):
                    op("dve", [xB[c][tb], rstdB, gB], [xnB[c]],
                       lambda e, c=c: e.scalar_tensor_tensor(out=xn_ap(c), in0=X[:, c, blk],
                                                             scalar=gpre[:, li * 8 + c:li * 8 + c + 1], in1=rstd[:],
                                                             op0=ALU.mult, op1=ALU.mult))
                for ch in range(4):
                    for half in range(2):
                        b = k.nextps()
                        for kc in range(8):
                            op("pe", [w_inB, xnB[kc]], [psB[b]],
                               lambda e, kc=kc, b=b: e.matmul(ps[b][:, :], lhsT=xn_ap(kc)[:, ch * P:(ch + 1) * P],
                                                              rhs=w_in[:, kc, D + half * BW:D + (half + 1) * BW],
                                                              start=(kc == 0), stop=(kc == 7)),
                               inc=(kc == 7))
                        op("act", [psB[b]], [vtmpB],
                           lambda e, b=b, half=half: e.activation(out=vtmp[:, half * BW:(half + 1) * BW], in_=ps[b][:, :],
                                                                  func=AF.Gelu_apprx_tanh))
                    for half in range(2):
                        op("dve", [vtmpB], [sttB], lambda e, half=half: e.bn_stats(out=stt[:, half, :], in_=vtmp[:, half * BW:(half + 1) * BW]))
                    op("dve", [sttB], [sttB], lambda e: e.bn_aggr(out=mv[:], in_=stt[:].rearrange("p a b -> p (a b)")))
                    op("act", [sttB], [sttB],
                       lambda e: e.activation(out=rs1[:], in_=mv[:, 1:2], func=AF.Sqrt, scale=1.0, bias=epsc[:, 0:1]))
                    op("dve", [sttB], [sttB], lambda e: e.reciprocal(out=rs1[:], in_=rs1[:]))
                    op("dve", [vtmpB, sttB], [vtmpB],
                       lambda e: e.tensor_scalar(out=vtmp[:], in0=vtmp[:], scalar1=mv[:, 0:1], scalar2=rs1[:, 0:1],
                                                 op0=ALU.subtract, op1=ALU.mult))
                    op("pool", [vtmpB, lnB], [vtmpB], lambda e: e.tensor_tensor(out=vtmp[:], in0=vtmp[:], in1=lng[:], op=ALU.mult))
                    op("pool", [vtmpB, lnB], [R1B[4 + ch]],
                       lambda e, ch=ch: e.tensor_tensor(out=vln_ap(ch, 0, D), in0=vtmp[:], in1=lnb[:], op=ALU.add))
                for m in range(8):
                    b = k.nextps()
                    for kc in range(8):
                        op("pe", [w_inB, xnB[kc]], [psB[b]],
                           lambda e, kc=kc, b=b, m=m: e.matmul(ps[b][:, :], lhsT=w_in[:, kc, m * P:(m + 1) * P], rhs=xn_ap(kc),
                                                               start=(kc == 0), stop=(kc == 7)), inc=(kc == 7))
                    op("act", [psB[b]], [guB[m]], lambda e, b=b, m=m: e.activation(out=gu[:, m, :], in_=ps[b][:, :], func=AF.Gelu_apprx_tanh))
                for m in range(8):
                    b = k.nextps()
                    for kc in range(8):
                        op("pe", [w_inB, xnB[kc]], [psB[b]],
                           lambda e, kc=kc, b=b, m=m: e.matmul(ps[b][:, :], lhsT=w_in[:, kc, 2 * D + m * P:2 * D + (m + 1) * P], rhs=xn_ap(kc),
                                                               start=(kc == 0), stop=(kc == 7)), inc=(kc == 7))
                    op("act", [psB[b]], [szB[m]], lambda e, b=b, m=m: e.activation(out=sz[:, m, :], in_=ps[b][:, :], func=AF.Silu))
                for m in range(8):
                    op("pool", [guB[m], szB[m]], [guB[m]], lambda e, m=m: e.tensor_tensor(out=gu[:, m, :], in0=gu[:, m, :], in1=sz[:, m, :], op=ALU.mult))
                for gp in range(8):
                    b = k.nextps()
                    n = 0
                    for ch in range(4):
                        for h in range(2):
                            g = 2 * gp + h
                            n += 1
                            op("pe", [R1B[4 + ch], wsTB[g]], [psB[b]],
                               lambda e, ch=ch, h=h, g=g, b=b: e.matmul(ps[b][h * 64:(h + 1) * 64, ch * P:(ch + 1) * P],
                                                                        lhsT=vln_ap(ch, g * 64, (g + 1) * 64), rhs=wsT[:, g, :],
                                                                        start=True, stop=True),
                               inc=(n == 8))
                    op("dve", [psB[b], bsTB], [tmpSB],
                       lambda e, b=b, gp=gp: e.tensor_tensor(out=tmpS[:].rearrange("p (a t) -> p a t", a=4),
                                                             in0=ps[b][:, :].rearrange("p (a t) -> p a t", a=4),
                                                             in1=bsT[:, gp, :].unsqueeze(1).to_broadcast([P, 4, P]), op=ALU.add))
                    op("dve", [tmpSB, guB[gp]], [guB[gp]],
                       lambda e, gp=gp: e.tensor_tensor(out=gu[:, gp, :], in0=tmpS[:], in1=gu[:, gp, :], op=ALU.mult))
                outproj(li, tb, gu, guB, w_out, w_outB, R1, R1B, tmpA, tmpAB)
            k.barrier()

    def layer_swa(li):
        st = ExitStack()
        with st:
            NC_IN = 2432
            w_in, w_inB = load_w(st, "b_win", dw["b_w_in"], NC_IN)
            w_out, w_outB = load_w(st, "b_wout", dw["b_w_out"], D)
            KT2 = sb(st, "KT2", [P, 2, L], BF16)
            KTB = bufs(2, NBLK)
            VA = [[sb(st, "VA%d%d" % (kv, par), [P, 16, P], BF16) for par in range(2)] for kv in range(2)]
            VAB = bufs(2, 2, 16)
            biasT = sb(st, "biasT", [P, 16, 256], F32)
            biasB = Buf()
            esink = sb(st, "esink", [P, 16], F32)
            esinkB = Buf()
            R1 = sb(st, "R1", [P, 8, BW], F32)
            R1B = bufs(8)
            R1bf = R1[:].bitcast(BF16)
            tmpA = sb(st, "tmpA", [P, 8, BW], BF16)
            tmpAB = bufs(8)
            sz = sb(st, "sz", [P, 8, BW], BF16)
            szB = bufs(8)
            NPB = 5
            apc = [0]
            tmpP = sb(st, "tmpP", [P, NPB, 256], F32)
            tmpPB = bufs(NPB)
            PT = sb(st, "PT", [P, NPB, 256], BF16)
            PTB = bufs(NPB)
            rsb = sb(st, "rsb", [P, BW], F32)
            rsbB = bufs(2)
            tG = sb(st, "tG", [P, BW], F32)
            tGB = bufs(2)

            def xn_ap(c):
                return R1bf[:, c // 2, (c % 2) * BW:(c % 2) * BW + BW]
            xnB = [R1B[c // 2] for c in range(8)]

            def qt_ap(j, p0, p1, c0, c1):
                return R1bf[p0:p1, 4 + j // 2, (j % 2) * BW + c0:(j % 2) * BW + c1]
            qtB = [R1B[4 + j // 2] for j in range(8)]

            import os
            SK = os.environ.get("SKIP", "")
            if "a" not in SK:
                dma("sp", biasT[:], dw["b_biasT"], [], [biasB])
            if "b" not in SK:
                dma("sp", esink[:], dw["b_sinks"].to_broadcast([P, 16]), [], [esinkB])
                op("act", [esinkB], [esinkB], lambda e: e.activation(out=esink[:], in_=esink[:], func=AF.Exp))
            for kv in range(2):
                for par in range(2):
                    c0 = 64 if par == 0 else 0
                    if "c" not in SK:
                        op("pool", [], sum([[VAB[kv][par][t]] for t in range(16)], []),
                           lambda e, kv=kv, par=par, c0=c0: e.memset(VA[kv][par][:, :, c0:c0 + 64], 1.0))

            for tb in range(NBLK):
                blk = slice(tb * BW, (tb + 1) * BW)
                prenorm_ap(li, tb, xn_ap, xnB, tmpA, tmpAB)
                STG = int(os.environ.get("STG", "9"))
                for j in range(8 if STG >= 1 else 0):
                    b = proj_ap(w_in, w_inB, j * P, P, xn_ap, xnB, 8)
                    op("act", [psB[b]], [qtB[j]], lambda e, b=b, j=j: e.activation(out=qt_ap(j, 0, P, 0, BW), in_=ps[b][:, :], func=AF.Copy))
                for kv in range(2 if STG >= 2 else 0):
                    b = proj_ap(w_in, w_inB, D + kv * P, P, xn_ap, xnB, 8)
                    op("act", [psB[b]], [KTB[kv][tb]], lambda e, b=b, kv=kv: e.activation(out=KT2[:, kv, blk], in_=ps[b][:, :], func=AF.Copy))
                for ch in range(4 if STG >= 3 else 0):
                    tt = tb * 4 + ch
                    b = k.nextps()
                    for kc in range(8):
                        op("pe", [w_inB, xnB[kc]], [psB[b]],
                           lambda e, kc=kc, b=b, ch=ch: e.matmul(ps[b][:, 0:P], lhsT=xn_ap(kc)[:, ch * P:(ch + 1) * P],
                                                                 rhs=w_in[:, kc, D + 256:D + 384], start=(kc == 0), stop=(kc == 7)),
                           inc=(kc == 7))
                    for kv in range(2):
                        op("act", [psB[b]], [VAB[kv][0][tt]],
                           lambda e, b=b, kv=kv, tt=tt: e.activation(out=VA[kv][0][:, tt, 0:64], in_=ps[b][:, kv * 64:(kv + 1) * 64], func=AF.Copy))
                        op("act", [psB[b]], [VAB[kv][1][tt]],
                           lambda e, b=b, kv=kv, tt=tt: e.activation(out=VA[kv][1][:, tt, 64:128], in_=ps[b][:, kv * 64:(kv + 1) * 64], func=AF.Copy))
                for m in range(8):
                    b = proj_ap(w_in, w_inB, D + 384 + m * P, P, xn_ap, xnB, 8)
                    op("act", [psB[b]], [szB[m]], lambda e, b=b, m=m: e.activation(out=sz[:, m, :], in_=ps[b][:, :], func=AF.Silu))
                def scores(h, nbl, pi):
                    kv, par, j = h // 8, h % 2, h // 2
                    base = par * 64
                    nb = tb * 4 + nbl
                    b = k.nextps()
                    c_lo = 0 if nb > 0 else P
                    rhs_q = qt_ap(j, base, base + 64, nbl * P, (nbl + 1) * P)
                    if nb > 0:
                        tbp = (nb - 1) // 4
                        op("pe", [KTB[kv][tbp], qtB[j]], [psB[b]],
                           lambda e: e.matmul(ps[b][:, 0:P], lhsT=KT2[base:base + 64, kv, (nb - 1) * P:nb * P], rhs=rhs_q, start=True, stop=True),
                           inc=False)
                    op("pe", [KTB[kv][tb], qtB[j]], [psB[b]],
                       lambda e: e.matmul(ps[b][:, P:2 * P], lhsT=KT2[base:base + 64, kv, nb * P:(nb + 1) * P], rhs=rhs_q, start=True, stop=True))
                    op("dve", [psB[b], biasB], [tmpPB[pi]],
                       lambda e: e.scalar_tensor_tensor(out=tmpP[:, pi, c_lo:256], in0=ps[b][:, c_lo:256], scalar=0.125, in1=biasT[:, h, c_lo:256],
                                                        op0=ALU.mult, op1=ALU.add))
                    op("act", [tmpPB[pi]], [PTB[pi]],
                       lambda e: e.activation(out=PT[:, pi, c_lo:256], in_=tmpP[:, pi, c_lo:256], func=AF.Exp))

                def pv_epi(h, nbl, pi):
                    kv, par, j = h // 8, h % 2, h // 2
                    base = par * 64
                    oth = 64 - base
                    bo = 6 + (h % 2)
                    nb = tb * 4 + nbl
                    if nb > 0:
                        op("pe", [PTB[pi], VAB[kv][par][nb - 1]], [psB[bo]],
                           lambda e: e.matmul(ps[bo][:, nbl * P:(nbl + 1) * P], lhsT=VA[kv][par][:, nb - 1, :], rhs=PT[:, pi, 0:P], start=True, stop=False),
                           inc=False)
                    op("pe", [PTB[pi], VAB[kv][par][nb]], [psB[bo]],
                       lambda e: e.matmul(ps[bo][:, nbl * P:(nbl + 1) * P], lhsT=VA[kv][par][:, nb, :], rhs=PT[:, pi, P:2 * P], start=(nb == 0), stop=True))
                    if nbl == 3:
                        op("dve", [psB[bo], esinkB], [rsbB[par]],
                           lambda e: e.tensor_scalar(out=rsb[base:base + 64, :], in0=ps[bo][oth:oth + 64, :], scalar1=esink[oth:oth + 64, h:h + 1],
                                                     scalar2=None, op0=ALU.add))
                        op("act", [rsbB[par]], [rsbB[par]], lambda e: e.activation(out=rsb[base:base + 64, :], in_=rsb[base:base + 64, :], func=AF.Ln))
                        op("act", [rsbB[par]], [rsbB[par]], lambda e: e.activation(out=rsb[base:base + 64, :], in_=rsb[base:base + 64, :], func=AF.Exp, scale=-1.0))
                        op("dve", [psB[bo], rsbB[par]], [tGB[par]],
                           lambda e: e.tensor_tensor(out=tG[base:base + 64, :], in0=ps[bo][base:base + 64, :], in1=rsb[base:base + 64, :], op=ALU.mult))
                        op("pool", [tGB[par], szB[j]], [szB[j]],
                           lambda e: e.tensor_tensor(out=sz[base:base + 64, j, :], in0=tG[base:base + 64, :], in1=sz[base:base + 64, j, :], op=ALU.mult))

                import os
                tasks = [(h, nbl) for h in range(int(os.environ.get('SWA_NH', '16'))) for nbl in range(4)]
                LAS = 3
                for i in range(min(LAS, len(tasks))):
                    scores(tasks[i][0], tasks[i][1], (apc[0] + i) % NPB)
                for i, (h, nbl) in enumerate(tasks):
                    pi = apc[0] % NPB
                    apc[0] += 1
                    if i + LAS < len(tasks):
                        scores(tasks[i + LAS][0], tasks[i + LAS][1], (apc[0] + LAS - 1) % NPB)
                    pv_epi(h, nbl, pi)
                outproj(li, tb, sz, szB, w_out, w_outB, R1, R1B, tmpA, tmpAB)
            k.barrier()

    def layer_mla(li):
        SC = 96.0 ** -0.5
        st = ExitStack()
        with st:
            GT = sb(st, "GT", [P, 8, L], BF16)
            GTB = bufs(8, NBLK)
            sA = ExitStack()
            sA.__enter__()
            CQN = sb(sA, "CQN", [P, 6, L], BF16)
            CQNB = bufs(6, NBLK)
            CKVN = sb(sA, "CKVN", [P, 2, L], BF16)
            CKVNB = bufs(2, NBLK)
            KR = sb(sA, "KR", [32, L], BF16)
            KRB = bufs(NBLK)
            ROPE = sb(sA, "ROPE", [64, L], F32)
            ropeB = Buf()
            gq = sb(sA, "gq", [P, 8], F32)
            gqB = Buf()
            dma("sp", ROPE[:], dw["c_rope"], [], [ropeB])
            dma("sp", gq[:], dw["c_gqkv"], [], [gqB])
            tR = sb(sA, "tR", [32, 2, BW], F32)
            tRB = bufs(2)
            s1 = ExitStack()
            with s1:
                w1, w1B = load_w(s1, "s_c_w1", dw["c_w1"], 1088)
                R1 = sb(s1, "R1", [P, 8, BW], F32)
                R1B = bufs(8)
                xnt = sb(s1, "xnt", [P, 8, BW], BF16)
                xnB = bufs(8)
                tmpA = sb(s1, "tmpA", [P, 8, BW], BF16)
                tmpAB = bufs(8)
                xn_ap = lambda c: xnt[:, c, :]
                for tb in range(NBLK):
                    blk = slice(tb * BW, (tb + 1) * BW)
                    prenorm_ap(li, tb, xn_ap, xnB, tmpA, tmpAB)
                    for m in range(8):
                        b = proj_ap(w1, w1B, m * P, P, xn_ap, xnB, 8)
                        op("act", [psB[b]], [R1B[m]], lambda e, b=b, m=m: e.activation(out=R1[:, m, :], in_=ps[b][:, :], func=AF.Copy))
                        op("act", [psB[b]], [tmpAB[m]], lambda e, b=b, m=m: e.activation(out=tmpA[:, m, :], in_=ps[b][:, :], func=AF.Square))
                    rstd_from_sq(tmpA, tmpAB, 6, 1.0 / 768, 0)
                    for m in range(6):
                        op("dve", [R1B[m], rstdB, gqB], [CQNB[m][tb]],
                           lambda e, m=m: e.scalar_tensor_tensor(out=CQN[:, m, blk], in0=R1[:, m, :], scalar=gq[:, m:m + 1], in1=rstd[:],
                                                                 op0=ALU.mult, op1=ALU.mult))
                    rstd_from_sq(tmpA, tmpAB, 2, 1.0 / 256, 6)
                    for m in range(2):
                        op("dve", [R1B[6 + m], rstdB, gqB], [CKVNB[m][tb]],
                           lambda e, m=m: e.scalar_tensor_tensor(out=CKVN[:, m, blk], in0=R1[:, 6 + m, :], scalar=gq[:, 6 + m:7 + m], in1=rstd[:],
                                                                 op0=ALU.mult, op1=ALU.mult))
                    b = proj_ap(w1, w1B, D, 64, xn_ap, xnB, 8)
                    op("dve", [psB[b], ropeB], [tRB[0]],
                       lambda e, b=b: e.tensor_tensor(out=tR[:, 0, :], in0=ps[b][32:64, :], in1=ROPE[32:64, blk], op=ALU.mult))
                    op("dve", [psB[b], ropeB], [tRB[1]],
                       lambda e, b=b: e.tensor_tensor(out=tR[:, 1, :], in0=ps[b][0:32, :], in1=ROPE[0:32, blk], op=ALU.mult))
                    op("pool", [tRB[0], tRB[1]], [KRB[tb]],
                       lambda e: e.tensor_tensor(out=KR[:, blk], in0=tR[:, 0, :], in1=tR[:, 1, :], op=ALU.add))
                k.barrier()
            s2 = ExitStack()
            with s2:
                wuq, wuqB = load_w(s2, "s_c_wuq", dw["c_wuq"], 2048)
                wukv, wukvB = load_w(s2, "s_c_wukv", dw["c_wukv"], 2048)
                KTh = [sb(s2, "KTh%d" % i, [P, L], BF16) for i in range(2)]
                KThB = bufs(2, NBLK)
                VAh = [sb(s2, "VAh%d" % i, [P, 16, P], BF16) for i in range(2)]
                VAhB = bufs(2, 4)
                QT = [sb(s2, "QT%d" % i, [P, BW], BF16) for i in range(2)]
                QTB = bufs(2)
                NPT = 5
                LA = 3
                PT = [sb(s2, "PT%d" % i, [P, BW], BF16) for i in range(NPT)]
                PTB = bufs(NPT)
                QF = sb(s2, "QF", [64, BW], F32)
                QFB = Buf()
                maskT = sb(s2, "maskT", [P, P], BF16)
                maskB = Buf()
                rsb = sb(s2, "rsb", [P, BW], F32)
                rsbB = bufs(2)
                dma("pool", maskT[:], dw["c_maskT"], [], [maskB])
                for i in range(2):
                    op("pool", [], KThB[i], lambda e, i=i: e.memset(KTh[i][32:64, :], 0.0))
                    c0 = 64 if i == 0 else 0
                    op("pool", [], VAhB[i], lambda e, i=i, c0=c0: e.memset(VAh[i][:, :, c0:c0 + 64], 1.0))
                def kv_build(h):
                    par = h % 2
                    vc0 = par * 64
                    for tb in range(NBLK):
                        blk = slice(tb * BW, (tb + 1) * BW)
                        b = k.nextps()
                        for kc in range(2):
                            op("pe", [wukvB, CKVNB[kc][tb]], [psB[b]],
                               lambda e, kc=kc, b=b, blk=blk: e.matmul(ps[b][64:128, :], lhsT=wukv[:, kc, h * P:h * P + 64], rhs=CKVN[:, kc, blk],
                                                                      start=(kc == 0), stop=(kc == 1)), inc=(kc == 1))
                        op("act", [psB[b]], [KThB[par][tb]],
                           lambda e, b=b, blk=blk: e.activation(out=KTh[par][64:128, blk], in_=ps[b][64:128, :], func=AF.Copy))
                        op("dve", [KRB[tb]], [KThB[par][tb]],
                           lambda e, blk=blk: e.tensor_scalar(out=KTh[par][0:32, blk], in0=KR[:, blk], scalar1=1.0, scalar2=None, op0=ALU.mult))
                        b = k.nextps()
                        for i4 in range(4):
                            tt = tb * 4 + i4
                            for kc in range(2):
                                op("pe", [wukvB, CKVNB[kc][tb]], [psB[b]],
                                   lambda e, kc=kc, b=b, tt=tt, i4=i4: e.matmul(ps[b][:, i4 * 64:(i4 + 1) * 64], lhsT=CKVN[:, kc, tt * P:(tt + 1) * P],
                                                                                rhs=wukv[:, kc, h * P + 64:h * P + 128], start=(kc == 0), stop=(kc == 1)),
                                   inc=(kc == 1 and i4 == 3))
                        op("act", [psB[b]], [VAhB[par][tb]],
                           lambda e, b=b, tb=tb: e.activation(out=VAh[par][:, tb * 4:(tb + 1) * 4, vc0:vc0 + 64],
                                                              in_=ps[b][:, 0:256].rearrange("p (a c) -> p a c", a=4), func=AF.Copy))

                def q_prep(h, qb, qi):
                    qblk = slice(qb * BW, (qb + 1) * BW)
                    b = k.nextps()
                    for kc in range(6):
                        op("pe", [wuqB, CQNB[kc][qb]], [psB[b]],
                           lambda e, kc=kc, b=b: e.matmul(ps[b][:, :], lhsT=wuq[:, kc, h * P:(h + 1) * P], rhs=CQN[:, kc, qblk],
                                                          start=(kc == 0), stop=(kc == 5)), inc=(kc == 5))
                    op("act", [psB[b]], [QTB[qi]], lambda e, b=b: e.activation(out=QT[qi][:, :], in_=ps[b][:, :], func=AF.Copy))
                    op("act", [psB[b]], [QFB], lambda e, b=b: e.activation(out=QF[:, :], in_=ps[b][0:64, :], func=AF.Copy))
                    op("dve", [QFB, ropeB], [tRB[0]],
                       lambda e: e.tensor_tensor(out=tR[:, 0, :], in0=QF[32:64, :], in1=ROPE[32:64, qblk], op=ALU.mult))
                    op("dve", [QFB, ropeB], [tRB[1]],
                       lambda e: e.tensor_tensor(out=tR[:, 1, :], in0=QF[0:32, :], in1=ROPE[0:32, qblk], op=ALU.mult))
                    op("dve", [tRB[0], tRB[1]], [QTB[qi]],
                       lambda e: e.tensor_tensor(out=QT[qi][0:32, :], in0=tR[:, 0, :], in1=tR[:, 1, :], op=ALU.add))

                pti = [0]

                def attend(h, qb, qi, bo):
                    par, j = h % 2, h // 2
                    base = par * 64
                    oth = 64 - base
                    qblk = slice(qb * BW, (qb + 1) * BW)
                    nkc = 4 * qb + 4

                    def pv(kc, pi, q_lo):
                        op("pe", [PTB[pi], VAhB[par][kc // 4]], [psB[bo]],
                           lambda e: e.matmul(ps[bo][:, q_lo:BW], lhsT=VAh[par][:, kc, :], rhs=PT[pi][:, q_lo:BW],
                                              start=(kc == 0), stop=(kc == nkc - 1)))
                    pend_pv = []
                    for kc in range(nkc):
                        q_lo = max(0, kc - 4 * qb) * P
                        pi = pti[0] % NPT
                        pti[0] += 1
                        b = k.nextps()
                        op("pe", [KThB[par][kc // 4], QTB[qi]], [psB[b]],
                           lambda e, b=b, kc=kc, q_lo=q_lo: e.matmul(ps[b][:, q_lo:BW], lhsT=KTh[par][:, kc * P:(kc + 1) * P],
                                                                     rhs=QT[qi][:, q_lo:BW], start=True, stop=True))
                        op("act", [psB[b]], [PTB[pi]],
                           lambda e, b=b, pi=pi, q_lo=q_lo: e.activation(out=PT[pi][:, q_lo:BW], in_=ps[b][:, q_lo:BW], func=AF.Exp, scale=SC))
                        if kc >= 4 * qb:
                            op("dve", [PTB[pi], maskB], [PTB[pi]],
                               lambda e, pi=pi, q_lo=q_lo: e.tensor_tensor(out=PT[pi][:, q_lo:q_lo + P], in0=PT[pi][:, q_lo:q_lo + P],
                                                                           in1=maskT[:, :], op=ALU.mult))
                        pend_pv.append((kc, pi, q_lo))
                        if len(pend_pv) > LA:
                            pv(*pend_pv.pop(0))
                    while pend_pv:
                        pv(*pend_pv.pop(0))
                    op("dve", [psB[bo]], [rsbB[par]],
                       lambda e: e.tensor_scalar(out=rsb[base:base + 64, :], in0=ps[bo][oth:oth + 64, :], scalar1=1.0, scalar2=None, op0=ALU.mult))
                    op("act", [rsbB[par]], [rsbB[par]], lambda e: e.activation(out=rsb[base:base + 64, :], in_=rsb[base:base + 64, :], func=AF.Ln))
                    op("act", [rsbB[par]], [rsbB[par]], lambda e: e.activation(out=rsb[base:base + 64, :], in_=rsb[base:base + 64, :], func=AF.Exp, scale=-1.0))
                    op("dve", [psB[bo], rsbB[par]], [GTB[j][qb]],
                       lambda e: e.tensor_tensor(out=GT[base:base + 64, j, qblk], in0=ps[bo][base:base + 64, :],
                                                 in1=rsb[base:base + 64, :], op=ALU.mult))

                import os
                NH = int(os.environ.get("MLA_NH", "16"))
                tasks = [(h, qb) for h in range(NH) for qb in range(NBLK)]
                if tasks:
                    kv_build(0)
                    q_prep(0, 0, 0)
                for i, (h, qb) in enumerate(tasks):
                    if i + 1 < len(tasks):
                        h2, qb2 = tasks[i + 1]
                        if h2 != h:
                            kv_build(h2)
                        q_prep(h2, qb2, (i + 1) % 2)
                    attend(h, qb, i % 2, 6 + (i % 2))
                k.barrier()
            sA.close()
            s3 = ExitStack()
            with s3:
                wz, wzB = load_w(s3, "s_c_wz", dw["c_wz"], D)
                w_out, w_outB = load_w(s3, "s_c_wout", dw["c_w_out"], D)
                R1 = sb(s3, "R1", [P, 8, BW], F32)
                R1B = bufs(8)
                xnt = sb(s3, "xnt", [P, 8, BW], BF16)
                xnB = bufs(8)
                tmpA = sb(s3, "tmpA", [P, 8, BW], BF16)
                tmpAB = bufs(8)
                sz = sb(s3, "sz", [P, 8, BW], BF16)
                szB = bufs(8)
                xn_ap = lambda c: xnt[:, c, :]
                for tb in range(NBLK):
                    blk = slice(tb * BW, (tb + 1) * BW)
                    prenorm_ap(li, tb, xn_ap, xnB, tmpA, tmpAB)
                    for m in range(8):
                        b = proj_ap(wz, wzB, m * P, P, xn_ap, xnB, 8)
                        op("act", [psB[b]], [szB[m]], lambda e, b=b, m=m: e.activation(out=sz[:, m, :], in_=ps[b][:, :], func=AF.Silu))
                        op("pool", [szB[m], GTB[m][tb]], [szB[m]],
                           lambda e, m=m, blk=blk: e.tensor_tensor(out=sz[:, m, :], in0=sz[:, m, :], in1=GT[:, m, blk], op=ALU.mult))
                    outproj(li, tb, sz, szB, w_out, w_outB, R1, R1B, tmpA, tmpAB)
                k.barrier()

    def layer_s5(li):
        TT = ALU
        st = ExitStack()
        with st:
            UY = sb(st, "UY", [P, 8, L], BF16)
            UYB = bufs(8, NBLK)
            dT = sb(st, "dT", [P, 16], F32)
            dTB = Buf()
            dma("sp", dT[:], dw["a_dg"], [], [dTB])
            s1 = ExitStack()
            with s1:
                wu, wuB = load_w(s1, "a_wu", dw["a_wu"], D)
                xnt = sb(s1, "xnt", [P, 8, BW], BF16)
                xnB = bufs(8)
                tmpA = sb(s1, "tmpA", [P, 8, BW], BF16)
                tmpAB = bufs(8)
                xn_ap = lambda c: xnt[:, c, :]
                for tb in range(NBLK):
                    blk = slice(tb * BW, (tb + 1) * BW)
                    prenorm_ap(li, tb, xn_ap, xnB, tmpA, tmpAB)
                    for m in range(8):
                        b = proj_ap(wu, wuB, m * P, P, xn_ap, xnB, 8)
                        op("act", [psB[b]], [UYB[m][tb]], lambda e, b=b, m=m, blk=blk: e.activation(out=UY[:, m, blk], in_=ps[b][:, :], func=AF.Copy))
                k.barrier()
            s2 = ExitStack()
            with s2:
                NJ = 32
                BrL, BrLB = sb(s2, "BrL", [P, NJ, P], BF16), Buf()
                BiL, BiLB = sb(s2, "BiL", [P, NJ, P], BF16), Buf()
                CrP, CrPB = sb(s2, "CrP", [P, NJ, P], BF16), Buf()
                CiP, CiPB = sb(s2, "CiP", [P, NJ, P], BF16), Buf()
                for t_, B_, nm in ((BrL, BrLB, "a_brl"), (BiL, BiLB, "a_bil"), (CrP, CrPB, "a_crp"), (CiP, CiPB, "a_cip")):
                    for q4 in range(4):
                        dma("pool", t_[:, q4 * 8:(q4 + 1) * 8, :], dw[nm][:, q4 * 8:(q4 + 1) * 8, :], [], [B_])
                lam = sb(s2, "lam", [P, 3, NJ], F32)
                prepB = Buf()
                dma("sp", lam[:], dw["a_lam"], [], [prepB])
                W = {}
                for nm in ("dt", "lrdt", "th", "mag", "t", "t2", "c", "s", "q", "c2", "s2", "cs", "ar", "ai", "den", "nr", "fr", "fi",
                           "u1", "u2", "ir", "ii", "pr", "pi", "A128r", "nA128i", "A128i"):
                    W[nm] = sb(s2, "w_" + nm, [P, NJ], F32)
                lr, li_, ldt = lam[:, 0, :], lam[:, 1, :], lam[:, 2, :]

                def tt(o, a, b_, o_):
                    op("dve", [prepB], [prepB], lambda e: e.tensor_tensor(out=o, in0=a, in1=b_, op=o_))

                def ts(o, a, s1_, s2_, o1, o2=None):
                    if o2 is None:
                        op("dve", [prepB], [prepB], lambda e: e.tensor_scalar(out=o, in0=a, scalar1=s1_, scalar2=None, op0=o1))
                    else:
                        op("dve", [prepB], [prepB], lambda e: e.tensor_scalar(out=o, in0=a, scalar1=s1_, scalar2=s2_, op0=o1, op1=o2))

                def stt(o, a, sc, b_, o1, o2):
                    op("dve", [prepB], [prepB], lambda e: e.scalar_tensor_tensor(out=o, in0=a, scalar=sc, in1=b_, op0=o1, op1=o2))

                def csq(cr, ci):
                    tt(W["c2"][:], cr, cr, TT.mult)
                    tt(W["s2"][:], ci, ci, TT.mult)
                    tt(W["cs"][:], cr, ci, TT.mult)
                    tt(cr, W["c2"][:], W["s2"][:], TT.subtract)
                    ts(ci, W["cs"][:], 2.0, None, TT.mult)

                op("act", [prepB], [prepB], lambda e: e.activation(out=W["dt"][:], in_=ldt, func=AF.Exp))
                tt(W["lrdt"][:], lr, W["dt"][:], TT.mult)
                tt(W["th"][:], li_, W["dt"][:], TT.mult)
                op("act", [prepB], [prepB], lambda e: e.activation(out=W["mag"][:], in_=W["lrdt"][:], func=AF.Exp))
                ts(W["t"][:], W["th"][:], 1.0 / 64, None, TT.mult)
                tt(W["t2"][:], W["t"][:], W["t"][:], TT.mult)
                ts(W["q"][:], W["t2"][:], -1.0 / 720, None, TT.mult)
                stt(W["q"][:], W["q"][:], 1.0 / 24, W["t2"][:], TT.add, TT.mult)
                stt(W["q"][:], W["q"][:], -0.5, W["t2"][:], TT.add, TT.mult)
                ts(W["c"][:], W["q"][:], 1.0, None, TT.add)
                ts(W["q"][:], W["t2"][:], -1.0 / 5040, None, TT.mult)
                stt(W["q"][:], W["q"][:], 1.0 / 120, W["t2"][:], TT.add, TT.mult)
                stt(W["q"][:], W["q"][:], -1.0 / 6, W["t2"][:], TT.add, TT.mult)
                stt(W["s"][:], W["q"][:], 1.0, W["t"][:], TT.add, TT.mult)
                for _ in range(6):
                    csq(W["c"][:], W["s"][:])
                tt(W["ar"][:], W["mag"][:], W["c"][:], TT.mult)
                tt(W["ai"][:], W["mag"][:], W["s"][:], TT.mult)
                tt(W["den"][:], lr, lr, TT.mult)
                tt(W["u1"][:], li_, li_, TT.mult)
                tt(W["den"][:], W["den"][:], W["u1"][:], TT.add)
                op("dve", [prepB], [prepB], lambda e: e.reciprocal(out=W["den"][:], in_=W["den"][:]))
                ts(W["nr"][:], W["ar"][:], -1.0, None, TT.add)
                tt(W["u1"][:], W["nr"][:], lr, TT.mult)
                tt(W["u2"][:], W["ai"][:], li_, TT.mult)
                tt(W["u1"][:], W["u1"][:], W["u2"][:], TT.add)
                tt(W["fr"][:], W["u1"][:], W["den"][:], TT.mult)
                tt(W["u1"][:], W["ai"][:], lr, TT.mult)
                tt(W["u2"][:], W["nr"][:], li_, TT.mult)
                tt(W["u1"][:], W["u1"][:], W["u2"][:], TT.subtract)
                tt(W["fi"][:], W["u1"][:], W["den"][:], TT.mult)
                tt(W["u1"][:], W["ar"][:], W["ar"][:], TT.mult)
                tt(W["u2"][:], W["ai"][:], W["ai"][:], TT.mult)
                tt(W["u1"][:], W["u1"][:], W["u2"][:], TT.add)
                op("dve", [prepB], [prepB], lambda e: e.reciprocal(out=W["u1"][:], in_=W["u1"][:]))
                tt(W["ir"][:], W["ar"][:], W["u1"][:], TT.mult)
                tt(W["ii"][:], W["ai"][:], W["u1"][:], TT.mult)
                ts(W["ii"][:], W["ii"][:], -1.0, None, TT.mult)
                TPr = sb(s2, "TPr", [P, NJ, P], BF16)
                TPi = sb(s2, "TPi", [P, NJ, P], BF16)
                TNr = sb(s2, "TNr", [P, NJ, P], BF16)
                TNi = sb(s2, "TNi", [P, NJ, P], BF16)
                sT = ExitStack()
                sT.__enter__()
                m1 = sb(sT, "m1t", [P, NJ, 64], F32)
                m2 = sb(sT, "m2t", [P, NJ, 64], F32)
                op("dve", [prepB], [prepB], lambda e: e.memset(TPr[:, :, 0:1], 1.0))
                op("dve", [prepB], [prepB], lambda e: e.memset(TPi[:, :, 0:1], 0.0))
                ts(TNr[:, :, 0:1], W["fr"][:].unsqueeze(2), 1.0, None, TT.mult)
                ts(TNi[:, :, 0:1], W["fi"][:].unsqueeze(2), 1.0, None, TT.mult)
                for (Tr_, Ti_, pr0, pi0) in ((TPr, TPi, "ar", "ai"), (TNr, TNi, "ir", "ii")):
                    ts(W["pr"][:], W[pr0][:], 1.0, None, TT.mult)
                    ts(W["pi"][:], W[pi0][:], 1.0, None, TT.mult)
                    for kk in range(7):
                        n = 1 << kk
                        pr_b = W["pr"][:].unsqueeze(2).to_broadcast([P, NJ, n])
                        pi_b = W["pi"][:].unsqueeze(2).to_broadcast([P, NJ, n])
                        lo_r, lo_i = Tr_[:, :, 0:n], Ti_[:, :, 0:n]
                        tt(m1[:, :, 0:n], lo_r, pr_b, TT.mult)
                        tt(m2[:, :, 0:n], lo_i, pi_b, TT.mult)
                        tt(Tr_[:, :, n:2 * n], m1[:, :, 0:n], m2[:, :, 0:n], TT.subtract)
                        tt(m1[:, :, 0:n], lo_r, pi_b, TT.mult)
                        tt(m2[:, :, 0:n], lo_i, pr_b, TT.mult)
                        tt(Ti_[:, :, n:2 * n], m1[:, :, 0:n], m2[:, :, 0:n], TT.add)
                        csq(W["pr"][:], W["pi"][:])
                    if pr0 == "ar":
                        ts(W["A128r"][:], W["pr"][:], 1.0, None, TT.mult)
                        ts(W["nA128i"][:], W["pi"][:], -1.0, None, TT.mult)
                        ts(W["A128i"][:], W["pi"][:], 1.0, None, TT.mult)
                k.barrier()
                sT.close()
                car = sb(s2, "car", [P, 2, NJ], F32)
                carB = bufs(NJ)
                op("dve", [prepB], carB, lambda e: e.memset(car[:], 0.0))
                rmask = sb(s2, "rmask", [P, 2, 4, P], F32)
                op("dve", [prepB], [prepB], lambda e: e.memset(rmask[:], 1.0))
                op("dve", [prepB], [prepB], lambda e: e.memset(rmask[:, :, :, 0:1], 0.0))
                op("dve", [prepB], [prepB], lambda e: e.tensor_scalar(out=TNi[:], in0=TNi[:], scalar1=-1.0, scalar2=None, op0=TT.mult))
                NA, NCD, NQ = 2, 4, 2
                TA = [sb(s2, "TA%d" % i, [P, 4, P], BF16) for i in range(NA)]
                TBf = [sb(s2, "TB%d" % i, [P, 4, P], BF16) for i in range(NA)]
                CD = [sb(s2, "CD%d" % i, [P, 2, 4, P], F32) for i in range(NCD)]
                T1 = sb(s2, "T1", [P, 4, P], F32)
                T2 = sb(s2, "T2", [P, 4, P], F32)
                Q = [[sb(s2, "Q%d_%d" % (i, q_), [P, 4, P], BF16) for q_ in range(4)] for i in range(NQ)]
                AB_, BB_, CDB = bufs(NA), bufs(NA), bufs(NCD)
                T1B, T2B = Buf(), Buf()
                QB = bufs(NQ, 4)
                ea = sb(s2, "ea", [P, 2, 4], F32)
                eb = sb(s2, "eb", [P, 2, 4], F32)
                eB = Buf()
                ytmp = sb(s2, "ytmp", [P, BW], F32)
                yB = Buf()

                def flat(t_):
                    return t_[:].rearrange("p a t -> p (a t)")

                units = [(chc, c) for cp in range(4) for c in range(16) for chc in (2 * cp, 2 * cp + 1)]
                NU = len(units)

                def stA(u):
                    chc, c = units[u]
                    j0, qt = chc * 4, c // 4
                    cols = slice(c * P, (c + 1) * P)
                    ai = u % NA
                    ba = k.nextps()
                    bb = k.nextps()
                    for jl in range(4):
                        op("pe", [BrLB, UYB[chc][qt]], [psB[ba]],
                           lambda e, jl=jl: e.matmul(ps[ba][:, jl * P:(jl + 1) * P], lhsT=BrL[:, j0 + jl, :], rhs=UY[:, chc, cols], start=True, stop=True),
                           inc=(jl == 3))
                    for jl in range(4):
                        op("pe", [BiLB, UYB[chc][qt]], [psB[bb]],
                           lambda e, jl=jl: e.matmul(ps[bb][:, jl * P:(jl + 1) * P], lhsT=BiL[:, j0 + jl, :], rhs=UY[:, chc, cols], start=True, stop=True),
                           inc=(jl == 3))
                    op("act", [psB[ba]], [AB_[ai]], lambda e: e.activation(out=flat(TA[ai]), in_=ps[ba][:, :], func=AF.Copy))
                    op("act", [psB[bb]], [BB_[ai]], lambda e: e.activation(out=flat(TBf[ai]), in_=ps[bb][:, :], func=AF.Copy))

                def ctx(u):
                    chc, c = units[u]
                    j0 = chc * 4
                    ai, ci, qi = u % NA, u % NCD, u % NQ
                    return chc, c, j0, ai, ci, qi

                def stB_mul(u):
                    chc, c, j0, ai, ci, qi = ctx(u)
                    A, B_ = TA[ai], TBf[ai]
                    C, Dd = CD[ci][:, 0, :, :], CD[ci][:, 1, :, :]
                    tnr, ntni = TNr[:, j0:j0 + 4, :], TNi[:, j0:j0 + 4, :]
                    op("dve", [AB_[ai], prepB], [CDB[ci]], lambda e: e.tensor_tensor(out=C, in0=A[:], in1=tnr, op=TT.mult))
                    op("dve", [BB_[ai], prepB], [T1B], lambda e: e.tensor_tensor(out=T1[:], in0=B_[:], in1=ntni, op=TT.mult))
                    op("dve", [AB_[ai], prepB], [CDB[ci]], lambda e: e.tensor_tensor(out=Dd, in0=A[:], in1=ntni, op=TT.mult))
                    op("dve", [BB_[ai], prepB], [T2B], lambda e: e.tensor_tensor(out=T2[:], in0=B_[:], in1=tnr, op=TT.mult))

                def stB_comb(u):
                    chc, c, j0, ai, ci, qi = ctx(u)
                    C, Dd = CD[ci][:, 0, :, :], CD[ci][:, 1, :, :]
                    op("dve", [CDB[ci], T1B], [CDB[ci]], lambda e: e.tensor_tensor(out=C, in0=C, in1=T1[:], op=TT.add))
                    op("dve", [CDB[ci], T2B], [CDB[ci]], lambda e: e.tensor_tensor(out=Dd, in0=Dd, in1=T2[:], op=TT.subtract))

                def stC_inject(u):
                    chc, c, j0, ai, ci, qi = ctx(u)
                    if c > 0:
                        op("dve", [CDB[ci], carB[j0]], [CDB[ci]],
                           lambda e: e.tensor_tensor(out=CD[ci][:, :, :, 0:1], in0=CD[ci][:, :, :, 0:1],
                                                     in1=car[:, :, j0:j0 + 4].unsqueeze(3), op=TT.add))

                def stC_scan(u):
                    chc, c, j0, ai, ci, qi = ctx(u)
                    op("dve", [CDB[ci], prepB], [CDB[ci]],
                       lambda e: e.tensor_tensor_scan(out=CD[ci][:].rearrange("p a b t -> p (a b t)"), data0=rmask[:].rearrange("p a b t -> p (a b t)"),
                                                      data1=CD[ci][:].rearrange("p a b t -> p (a b t)"), initial=0.0, op0=TT.mult, op1=TT.add))

                def stD_e(u):
                    chc, c, j0, ai, ci, qi = ctx(u)
                    if c < 15:
                        xl = CD[ci][:, :, :, P - 1]
                        a_r = W["A128r"][:, j0:j0 + 4].unsqueeze(1).to_broadcast([P, 2, 4])
                        a_i = W["A128i"][:, j0:j0 + 4].unsqueeze(1).to_broadcast([P, 2, 4])
                        op("dve", [CDB[ci], prepB, carB[j0]], [eB], lambda e: e.tensor_tensor(out=ea[:], in0=xl, in1=a_r, op=TT.mult))
                        op("dve", [CDB[ci], prepB, carB[j0]], [eB], lambda e: e.tensor_tensor(out=eb[:], in0=xl, in1=a_i, op=TT.mult))

                def stD_q3(u):
                    chc, c, j0, ai, ci, qi = ctx(u)
                    C = CD[ci][:, 0, :, :]
                    tpi = TPi[:, j0:j0 + 4, :]
                    op("dve", [CDB[ci], prepB], [QB[qi][2]],
                       lambda e: e.scalar_tensor_tensor(out=Q[qi][2][:], in0=C, scalar=-1.0, in1=tpi, op0=TT.mult, op1=TT.mult))

                def stD_car(u):
                    chc, c, j0, ai, ci, qi = ctx(u)
                    if c < 15:
                        op("dve", [eB], [carB[j0]], lambda e: e.tensor_tensor(out=car[:, 0, j0:j0 + 4], in0=ea[:, 0, :], in1=eb[:, 1, :], op=TT.add))
                        op("dve", [eB], [carB[j0]], lambda e: e.tensor_tensor(out=car[:, 1, j0:j0 + 4], in0=ea[:, 1, :], in1=eb[:, 0, :], op=TT.subtract))

                def stD_rest(u):
                    chc, c, j0, ai, ci, qi = ctx(u)
                    qt, cq = c // 4, c % 4
                    C, Dd = CD[ci][:, 0, :, :], CD[ci][:, 1, :, :]
                    bo = 6 + (chc % 2)
                    tpr, tpi = TPr[:, j0:j0 + 4, :], TPi[:, j0:j0 + 4, :]
                    q = Q[qi]
                    op("dve", [CDB[ci], prepB], [QB[qi][0]], lambda e: e.tensor_tensor(out=q[0][:], in0=C, in1=tpr, op=TT.mult))
                    op("dve", [CDB[ci], prepB], [QB[qi][1]], lambda e: e.tensor_tensor(out=q[1][:], in0=Dd, in1=tpi, op=TT.mult))
                    op("pool", [CDB[ci], prepB], [QB[qi][3]], lambda e: e.tensor_tensor(out=q[3][:], in0=Dd, in1=tpr, op=TT.mult))
                    n = 0
                    for jl in range(4):
                        j = j0 + jl
                        for qq in range(4):
                            wt, wtB = (CrP, CrPB) if qq < 2 else (CiP, CiPB)
                            n += 1
                            op("pe", [QB[qi][qq], wtB], [psB[bo]],
                               lambda e, j=j, jl=jl, qq=qq, wt=wt, n=n: e.matmul(ps[bo][:, cq * P:(cq + 1) * P], lhsT=wt[:, j, :], rhs=q[qq][:, jl, :],
                                                                                  start=(n == 1), stop=(n == 16)), inc=(n == 16))
                    if cq == 3:
                        blk = slice(qt * BW, (qt + 1) * BW)
                        op("act", [psB[bo]], [yB], lambda e: e.activation(out=ytmp[:], in_=ps[bo][:, :], func=AF.Copy))
                        op("dve", [yB, UYB[chc][qt], dTB], [yB],
                           lambda e: e.scalar_tensor_tensor(out=ytmp[:], in0=UY[:, chc, blk], scalar=dT[:, chc:chc + 1], in1=ytmp[:],
                                                            op0=TT.mult, op1=TT.add))
                        op("act", [yB], [UYB[chc][qt]], lambda e: e.activation(out=UY[:, chc, blk], in_=ytmp[:], func=AF.Gelu_apprx_tanh))

                def ok(u):
                    return 0 <= u < NU

                for i in range(NU + 3):
                    if ok(i):
                        stA(i)
                    if ok(i - 2):
                        stC_inject(i - 2)
                    if ok(i - 1):
                        stB_mul(i - 1)
                    if ok(i - 2):
                        stC_scan(i - 2)
                    if ok(i - 3):
                        stD_e(i - 3)
                    if ok(i - 1):
                        stB_comb(i - 1)
                    if ok(i - 3):
                        stD_q3(i - 3)
                        stD_car(i - 3)
                        stD_rest(i - 3)
                k.barrier()
            s3 = ExitStack()
            with s3:
                wz, wzB = load_w(s3, "a_wzs", dw["a_wz"], D)
                wg, wgB = load_w(s3, "a_wgs", dw["a_wglu"], D)
                w_out, w_outB = load_w(s3, "a_wouts", dw["a_w_out"], D)
                R1 = sb(s3, "R1", [P, 8, BW], F32)
                R1B = bufs(8)
                xnt = sb(s3, "xnt", [P, 8, BW], BF16)
                xnB = bufs(8)
                tmpA = sb(s3, "tmpA", [P, 8, BW], BF16)
                tmpAB = bufs(8)
                sz = sb(s3, "sz", [P, 8, BW], BF16)
                szB = bufs(8)
                sg = sb(s3, "sg", [P, 2, BW], BF16)
                sgB = bufs(2)
                xn_ap = lambda c: xnt[:, c, :]
                for tb in range(NBLK):
                    blk = slice(tb * BW, (tb + 1) * BW)
                    prenorm_ap(li, tb, xn_ap, xnB, tmpA, tmpAB)
                    uy_ap = lambda c, blk=blk: UY[:, c, blk]
                    uyB_t = [UYB[c][tb] for c in range(8)]
                    for m in range(8):
                        b = proj_ap(wz, wzB, m * P, P, xn_ap, xnB, 8)
                        op("act", [psB[b]], [szB[m]], lambda e, b=b, m=m: e.activation(out=sz[:, m, :], in_=ps[b][:, :], func=AF.Silu))
                        b = proj_ap(wg, wgB, m * P, P, uy_ap, uyB_t, 8)
                        op("act", [psB[b], dTB], [sgB[m % 2]],
                           lambda e, b=b, m=m: e.activation(out=sg[:, m % 2, :], in_=ps[b][:, :], func=AF.Sigmoid, bias=dT[:, 8 + m:9 + m], scale=1.0))
                        op("pool", [szB[m], uyB_t[m]], [szB[m]],
                           lambda e, m=m, blk=blk: e.tensor_tensor(out=sz[:, m, :], in0=sz[:, m, :], in1=UY[:, m, blk], op=ALU.mult))
                        op("pool", [szB[m], sgB[m % 2]], [szB[m]],
                           lambda e, m=m: e.tensor_tensor(out=sz[:, m, :], in0=sz[:, m, :], in1=sg[:, m % 2, :], op=ALU.mult))
                    outproj(li, tb, sz, szB, w_out, w_outB, R1, R1B, tmpA, tmpAB)
                k.barrier()

    for li in layers:
        if li == 3:
            layer_sgu(li)
        elif li == 1:
            layer_swa(li)
        elif li == 2:
            layer_mla(li)
        elif li == 0:
            layer_s5(li)

    toks = []
    for c in range(8):
        for tb in range(NBLK):
            toks.append(dma("sp", outT_d[c * P:(c + 1) * P, tb * BW:(tb + 1) * BW], X[:, c, tb * BW:(tb + 1) * BW], [xB[c][tb]], []))
    k._wait("sp", toks)
    k.barrier()


def host_inputs(inp, layers):
    f = lambda a: np.ascontiguousarray(np.asarray(a, dtype=np.float32))
    common = {}
    common["gpre"] = f(np.asarray(inp["pre_norm"]).reshape(4, 8, P).transpose(2, 0, 1).reshape(P, 32))
    common["gpost"] = f(np.asarray(inp["post_norm"]).reshape(4, 8, P).transpose(2, 0, 1).reshape(P, 32))
    common["ident"] = np.eye(P, dtype=np.float32)
    if 3 in layers:
        common["d_w_in"] = f(inp["d_w_in"][0])
        common["d_w_out"] = f(inp["d_w_out"][0])
        common["d_ws"] = f(np.asarray(inp["d_w_s"][0]).transpose(1, 0, 2))
        common["d_tril"] = np.tril(np.ones((P, P), dtype=np.float32))
        common["d_bs"] = f(inp["d_b_s"][0])
        common["d_lng"] = f(inp["d_ln_g"])
        common["d_lnb"] = f(inp["d_ln_b"])
    if 1 in layers:
        w = np.asarray(inp["b_w_in"][0], dtype=np.float32)
        q, kk, v, z = w[:, :1024], w[:, 1024:1152], w[:, 1152:1280], w[:, 1280:]
        common["b_w_in"] = f(np.concatenate([q, kk[:, :64], kk[:, :64], kk[:, 64:], kk[:, 64:], v, z], axis=1))
        common["b_w_out"] = f(inp["b_w_out"][0])
        common["b_sinks"] = f(inp["b_sinks"])
        def bucket(d):
            if d < 16:
                return d
            v_ = 16 + int(math.log(max(d, 1) / 16.0) / math.log(128 / 16.0) * 16)
            return min(v_, 31)
        rb = np.asarray(inp["rel_bias"], dtype=np.float32)
        bt = np.full((P, 16, 256), -1e30, dtype=np.float32)
        for kj in range(P):
            for qi in range(P):
                d_prev = qi + P - kj
                if d_prev < P:
                    bt[kj, :, qi] = rb[bucket(d_prev), :]
                d_cur = qi - kj
                if d_cur >= 0:
                    bt[kj, :, P + qi] = rb[bucket(d_cur), :]
        common["b_biasT"] = bt
    if 2 in layers:
        w = np.asarray(inp["c_w_in"][0], dtype=np.float32)
        kr = w[:, 1024:1056]
        common["c_w1"] = f(np.concatenate([w[:, :1024], kr, kr[:, 16:], kr[:, :16]], axis=1))
        common["c_wz"] = f(w[:, 1056:])
        uq = np.asarray(inp["c_w_uq"][0], dtype=np.float32).reshape(768, 16, 96)
        nope, rp = uq[:, :, :64], uq[:, :, 64:]
        common["c_wuq"] = f(np.concatenate([rp, rp[:, :, 16:], rp[:, :, :16], nope], axis=2).reshape(768, 2048))
        common["c_wukv"] = f(inp["c_w_ukv"][0])
        common["c_w_out"] = f(inp["c_w_out"][0])
        inv = (np.float32(10000.0) ** (-np.arange(0, 32, 2, dtype=np.float32) / np.float32(32))).astype(np.float32)
        ang = (np.arange(L, dtype=np.float32)[:, None] * inv[None, :]).astype(np.float32)
        cos, sin = np.cos(ang).astype(np.float32).T, np.sin(ang).astype(np.float32).T
        common["c_rope"] = f(np.concatenate([cos, cos, -sin, sin], axis=0))
        g = np.concatenate([np.asarray(inp["c_q_norm"][0]), np.asarray(inp["c_kv_norm"][0])]).astype(np.float32)
        common["c_gqkv"] = f(g.reshape(8, P).T)
        common["c_maskT"] = np.triu(np.ones((P, P), dtype=np.float32))
    if 0 in layers:
        w = np.asarray(inp["a_w_in"][0], dtype=np.float32)
        common["a_wu"] = f(w[:, :1024])
        common["a_wz"] = f(w[:, 1024:])
        common["a_wglu"] = f(inp["a_w_glu"][0])
        common["a_w_out"] = f(inp["a_w_out"][0])
        dg = np.concatenate([np.asarray(inp["a_d"][0]).reshape(8, P).T, np.asarray(inp["a_b_glu"][0]).reshape(8, P).T], axis=1)
        common["a_dg"] = f(dg)
        lam = np.stack([np.asarray(inp["a_lam_re"][0]).reshape(32, P).T, np.asarray(inp["a_lam_im"][0]).reshape(32, P).T,
                        np.repeat(np.asarray(inp["a_log_dt"][0]), 64).reshape(32, P).T], axis=1)
        common["a_lam"] = f(lam)
        brl = np.zeros((P, 32, P), np.float32); bil = np.zeros((P, 32, P), np.float32)
        crp = np.zeros((P, 32, P), np.float32); cip = np.zeros((P, 32, P), np.float32)
        b_re, b_im = np.asarray(inp["a_b_re"][0]), np.asarray(inp["a_b_im"][0])
        c_re, c_im = np.asarray(inp["a_c_re"][0]), np.asarray(inp["a_c_im"][0])
        for j in range(32):
            for gl in range(2):
                g = 2 * j + gl
                r0 = 32 * (j % 4) + gl * 16
                brl[r0:r0 + 16, j, gl * 64:(gl + 1) * 64] = b_re[g].T
                bil[r0:r0 + 16, j, gl * 64:(gl + 1) * 64] = b_im[g].T
                crp[gl * 64:(gl + 1) * 64, j, r0:r0 + 16] = c_re[g].T
                cip[gl * 64:(gl + 1) * 64, j, r0:r0 + 16] = c_im[g].T
        common["a_brl"], common["a_bil"], common["a_crp"], common["a_cip"] = brl, bil, crp, cip
    return common


def run(inp, layers=(0, 1, 2, 3), cores=8, trace=False):
    nc = bass.Bass("TRN2", target_bir_lowering=False)
    build(nc, list(layers))
    common = host_inputs(inp, list(layers))
    x = np.asarray(inp["x"], dtype=np.float32)
    in_maps = []
    for b in range(cores):
        m = dict(common)
        m["xT"] = np.ascontiguousarray(x[b].T)
        in_maps.append(m)
    res = run_bass_kernel_spmd(nc, in_maps, core_ids=list(range(cores)), trace=trace)
    out = np.stack([np.ascontiguousarray(r["outT"].T) for r in res.results], axis=0)
    return out.astype(np.float32), res


def kernel(**inputs):
    out, _ = run(inputs)
    return out
```

```python
import math
import numpy as np
from contextlib import ExitStack
import concourse.bass as bass
import concourse.mybir as mybir
from concourse.bass_utils import run_bass_kernel_spmd

F32 = mybir.dt.float32
BF16 = mybir.dt.bfloat16
ALU = mybir.AluOpType
AF = mybir.ActivationFunctionType

P = 128
L = 2048
D = 1024
NBLK = 4
BW = 512
EPS = 1e-6
SELF_SYNC = True
NDS = 12


class Buf:
    __slots__ = ("w", "r")

    def __init__(self):
        self.w = None
        self.r = {}


class WB:
    def __init__(self):
        self.blocks = []

    def cols(self, c0, c1):
        return [b for (a0, a1, bl) in self.blocks if a0 < c1 and c0 < a1 for b in bl]

    def all(self):
        return [b for (_, _, bl) in self.blocks for b in bl]


def _flat(lst):
    out = []
    for b in lst:
        if isinstance(b, WB):
            out.extend(b.all())
        elif isinstance(b, list):
            out.extend(_flat(b))
        else:
            out.append(b)
    return out


def bufs(*shape):
    if len(shape) == 1:
        return [Buf() for _ in range(shape[0])]
    return [bufs(*shape[1:]) for _ in range(shape[0])]


class KB:
    def __init__(self, nc, es):
        self.nc = nc
        self.E = dict(pe=nc.tensor, act=nc.scalar, dve=nc.vector, pool=nc.gpsimd, sp=nc.sync)
        self.sem = {e: es.enter_context(nc.semaphore("s_" + e)) for e in ("pe", "act", "dve", "pool")}
        self.cnt = {e: 0 for e in self.sem}
        self.pend = {e: False for e in self.sem}
        self.dsem = {q: [[es.enter_context(nc.semaphore("d_%s%d" % (q, i))), 0] for i in range(NDS)]
                     for q in ("sp", "pool")}
        self.dcnt = {"sp": 0, "pool": 0}
        self.seen = {e: {} for e in self.E}
        self.ps = []
        self.psB = []
        self.psrot = 0
        self.nrot = 6

    def _semh(self, key):
        if isinstance(key, str):
            return self.sem[key]
        return self.dsem[key[0]][key[1]][0]

    def _wait(self, e, toks):
        need = {}
        for key, v in toks:
            if need.get(key, 0) < v:
                need[key] = v
        for key, v in need.items():
            if key == e and (e == "pe" or not SELF_SYNC):
                continue
            if self.seen[e].get(key, 0) >= v:
                continue
            self.E[e].wait_ge(self._semh(key), v)
            self.seen[e][key] = v

    def _deps(self, reads, writes):
        toks = []
        for b in reads:
            if b.w is not None:
                toks.append(b.w)
        for b in writes:
            if b.w is not None:
                toks.append(b.w)
            toks.extend(b.r.items())
        return toks

    def _mark(self, tok, reads, writes):
        key, v = tok
        for b in reads:
            if b.r.get(key, 0) < v:
                b.r[key] = v
        for b in writes:
            b.w = tok
            b.r = {}

    def op(self, e, reads, writes, fn, inc=True):
        reads, writes = _flat(reads), _flat(writes)
        self._wait(e, self._deps(reads, writes))
        ins = fn(self.E[e])
        if inc:
            self.cnt[e] += 1
            ins.then_inc(self.sem[e], 1)
            self.pend[e] = False
            tok = (e, self.cnt[e])
        else:
            self.pend[e] = True
            tok = (e, self.cnt[e] + 1)
        self._mark(tok, reads, writes)
        return ins

    def dma(self, q, out, in_, reads, writes):
        reads, writes = _flat(reads), _flat(writes)
        self._wait(q, self._deps(reads, writes))
        i = self.dcnt[q] % NDS
        self.dcnt[q] += 1
        ent = self.dsem[q][i]
        key = (q, i)
        if ent[1] > 0:
            self._wait(q, [(key, 16 * ent[1])])
        ins = self.E[q].dma_start(out=out, in_=in_)
        ins.then_inc(ent[0], 16)
        ent[1] += 1
        tok = (key, 16 * ent[1])
        self._mark(tok, reads, writes)
        return tok

    def all_tokens(self):
        toks = [(e, c) for e, c in self.cnt.items() if c > 0]
        for q in self.dsem:
            for i, ent in enumerate(self.dsem[q]):
                if ent[1] > 0:
                    toks.append(((q, i), 16 * ent[1]))
        return toks

    def barrier(self):
        for e in self.sem:
            assert not self.pend[e], e
        toks = self.all_tokens()
        for e in self.E:
            self._wait(e, toks)

    def nextps(self):
        b = self.psrot
        self.psrot = (self.psrot + 1) % self.nrot
        return b


def build(nc, layers):
    es = ExitStack()
    with es:
        _build(nc, es, layers)
    return nc


def _build(nc, es, layers):
    k = KB(nc, es)
    op, dma = k.op, k.dma

    def dram_in(name, shape, dt=F32):
        return nc.dram_tensor(name, list(shape), dt, kind="ExternalInput").ap()

    uid = [0]

    def sb(st, name, shape, dt):
        uid[0] += 1
        return st.enter_context(nc.sbuf_tensor("%s_%d" % (name, uid[0]), list(shape), dt))

    xT_d = dram_in("xT", [D, L])
    outT_d = nc.dram_tensor("outT", [D, L], F32, kind="ExternalOutput").ap()
    gpre_d = dram_in("gpre", [P, 32])
    gpost_d = dram_in("gpost", [P, 32])
    ident_d = dram_in("ident", [P, P])
    dw = {}
    if 3 in layers:
        dw["d_w_in"] = dram_in("d_w_in", [D, 3072])
        dw["d_w_out"] = dram_in("d_w_out", [D, D])
        dw["d_ws"] = dram_in("d_ws", [P, 16, P])
        dw["d_tril"] = dram_in("d_tril", [P, P])
        dw["d_bs"] = dram_in("d_bs", [16, P])
        dw["d_lng"] = dram_in("d_lng", [1, D])
        dw["d_lnb"] = dram_in("d_lnb", [1, D])

    if 1 in layers:
        dw["b_w_in"] = dram_in("b_w_in", [D, 2432])
        dw["b_w_out"] = dram_in("b_w_out", [D, D])
        dw["b_biasT"] = dram_in("b_biasT", [P, 16, 256])
        dw["b_sinks"] = dram_in("b_sinks", [1, 16])

    if 2 in layers:
        dw["c_w1"] = dram_in("c_w1", [D, 1088])
        dw["c_wz"] = dram_in("c_wz", [D, D])
        dw["c_wuq"] = dram_in("c_wuq", [768, 2048])
        dw["c_wukv"] = dram_in("c_wukv", [256, 2048])
        dw["c_w_out"] = dram_in("c_w_out", [D, D])
        dw["c_rope"] = dram_in("c_rope", [64, L])
        dw["c_gqkv"] = dram_in("c_gqkv", [P, 8])
        dw["c_maskT"] = dram_in("c_maskT", [P, P])

    if 0 in layers:
        dw["a_wu"] = dram_in("a_wu", [D, D])
        dw["a_wz"] = dram_in("a_wz", [D, D])
        dw["a_wglu"] = dram_in("a_wglu", [D, D])
        dw["a_w_out"] = dram_in("a_w_out", [D, D])
        dw["a_dg"] = dram_in("a_dg", [P, 16])
        dw["a_lam"] = dram_in("a_lam", [P, 3, 32])
        for nm in ("a_brl", "a_bil", "a_crp", "a_cip"):
            dw[nm] = dram_in(nm, [P, 32, P])
        dw["a_triu"] = dram_in("a_triu", [P, P])

    X = sb(es, "X", [P, 8, L], F32)
    xB = bufs(8, NBLK)
    ones = sb(es, "ones", [P, P], BF16)
    onesB = Buf()
    ident = sb(es, "identb", [P, P], BF16)
    identB = Buf()
    gpre = sb(es, "gpre_s", [P, 32], F32)
    gpost = sb(es, "gpost_s", [P, 32], F32)
    gB = Buf()
    rstd = sb(es, "rstd", [P, BW], F32)
    rstdB = Buf()
    for i in range(8):
        k.ps.append(es.enter_context(nc.psum_tensor("ps%d" % i, [P, BW], F32)))
        k.psB.append(Buf())
    ps, psB = k.ps, k.psB

    for c in range(8):
        for tb in range(NBLK):
            dma("sp", X[:, c, tb * BW:(tb + 1) * BW], xT_d[c * P:(c + 1) * P, tb * BW:(tb + 1) * BW], [], [xB[c][tb]])
    dma("sp", gpre[:], gpre_d, [], [gB])
    dma("sp", gpost[:], gpost_d, [], [gB])
    dma("pool", ident[:], ident_d, [], [identB])
    op("dve", [], [onesB], lambda e: e.memset(ones[:], 1.0))
    epsc = sb(es, "epsc", [P, 1], F32)
    op("dve", [], [onesB], lambda e: e.memset(epsc[:], EPS))

    def load_w(st, name, d_ap, ncols, q="pool"):
        K = d_ap.shape[0]
        kc_n = K // P
        t = sb(st, name, [P, kc_n, ncols], BF16)
        B = WB()
        for c0 in range(0, ncols, 512):
            c1 = min(ncols, c0 + 512)
            bl = []
            for kc in range(kc_n):
                b1 = Buf()
                dma(q, t[:, kc, c0:c1], d_ap[kc * P:(kc + 1) * P, c0:c1], [], [b1])
                bl.append(b1)
            B.blocks.append((c0, c1, bl))
        return t, B

    def rstd_from_sq(sq, sqB, n, scale, c0=0):
        b = k.nextps()
        for c in range(n):
            op("pe", [sqB[c0 + c], onesB], [psB[b]],
               lambda e, c=c: e.matmul(ps[b][:, :], lhsT=ones[:, :], rhs=sq[:, c0 + c, :], start=(c == 0), stop=(c == n - 1)),
               inc=(c == n - 1))
        op("act", [psB[b]], [rstdB],
           lambda e: e.activation(out=rstd[:], in_=ps[b][:, :], func=AF.Ln, scale=scale, bias=epsc[:, 0:1]))
        op("act", [rstdB], [rstdB], lambda e: e.activation(out=rstd[:], in_=rstd[:], func=AF.Exp, scale=-0.5))

    def prenorm(li, tb, xn, xnB, tmpA, tmpAB):
        blk = slice(tb * BW, (tb + 1) * BW)
        for c in range(8):
            op("act", [xB[c][tb]], [tmpAB[c]],
               lambda e, c=c: e.activation(out=tmpA[:, c, :], in_=X[:, c, blk], func=AF.Square))
        rstd_from_sq(tmpA, tmpAB, 8, 1.0 / D)
        for c in range(8):
            op("dve", [xB[c][tb], rstdB, gB], [xnB[c]],
               lambda e, c=c: e.scalar_tensor_tensor(out=xn[:, c, :], in0=X[:, c, blk],
                                                     scalar=gpre[:, li * 8 + c:li * 8 + c + 1], in1=rstd[:],
                                                     op0=ALU.mult, op1=ALU.mult))

    def prenorm_ap(li, tb, xn_ap, xnB, tmpA, tmpAB):
        blk = slice(tb * BW, (tb + 1) * BW)
        for c in range(8):
            op("act", [xB[c][tb]], [tmpAB[c]],
               lambda e, c=c: e.activation(out=tmpA[:, c, :], in_=X[:, c, blk], func=AF.Square))
        rstd_from_sq(tmpA, tmpAB, 8, 1.0 / D)
        for c in range(8):
            op("dve", [xB[c][tb], rstdB, gB], [xnB[c]],
               lambda e, c=c: e.scalar_tensor_tensor(out=xn_ap(c), in0=X[:, c, blk],
                                                     scalar=gpre[:, li * 8 + c:li * 8 + c + 1], in1=rstd[:],
                                                     op0=ALU.mult, op1=ALU.mult))

    def proj_ap(w, wB, col0, M, rhs_ap, rhsB, nk):
        b = k.nextps()
        wBc = wB.cols(col0, col0 + M)
        for kc in range(nk):
            op("pe", [wBc, rhsB[kc]], [psB[b]],
               lambda e, kc=kc: e.matmul(ps[b][0:M, :], lhsT=w[:, kc, col0:col0 + M], rhs=rhs_ap(kc),
                                         start=(kc == 0), stop=(kc == nk - 1)),
               inc=(kc == nk - 1))
        return b

    def proj_fm(w, wB, col0, rhs, rhsB, nk, M=P):
        b = k.nextps()
        wBc = wB.cols(col0, col0 + M)
        for kc in range(nk):
            op("pe", [wBc, rhsB[kc]], [psB[b]],
               lambda e, kc=kc: e.matmul(ps[b][0:M, :], lhsT=w[:, kc, col0:col0 + M], rhs=rhs[:, kc, :],
                                         start=(kc == 0), stop=(kc == nk - 1)),
               inc=(kc == nk - 1))
        return b

    def outproj(li, tb, G, GB, wout, woutB, ybuf, ybufB, tmpA, tmpAB):
        blk = slice(tb * BW, (tb + 1) * BW)
        for m in range(8):
            b = proj_fm(wout, woutB, m * P, G, GB, 8)
            op("act", [psB[b]], [ybufB[m]], lambda e, m=m, b=b: e.activation(out=ybuf[:, m, :], in_=ps[b][:, :], func=AF.Copy))
            op("act", [psB[b]], [tmpAB[m]], lambda e, m=m, b=b: e.activation(out=tmpA[:, m, :], in_=ps[b][:, :], func=AF.Square))
        rstd_from_sq(tmpA, tmpAB, 8, 1.0 / D)
        for m in range(8):
            op("dve", [ybufB[m], rstdB, gB], [ybufB[m]],
               lambda e, m=m: e.scalar_tensor_tensor(out=ybuf[:, m, :], in0=ybuf[:, m, :],
                                                     scalar=gpost[:, li * 8 + m:li * 8 + m + 1], in1=rstd[:],
                                                     op0=ALU.mult, op1=ALU.mult))
            op("pool", [ybufB[m], xB[m][tb]], [xB[m][tb]],
               lambda e, m=m: e.tensor_tensor(out=X[:, m, blk], in0=X[:, m, blk], in1=ybuf[:, m, :], op=ALU.add))

    def layer_sgu(li):
        st = ExitStack()
        with st:
            w_in, w_inB = load_w(st, "d_win", dw["d_w_in"], 3072)
            w_out, w_outB = load_w(st, "d_wout", dw["d_w_out"], D)
            wsf = sb(st, "wsf", [P, 16, P], F32)
            wsfB = Buf()
            tril = sb(st, "tril", [P, P], F32)
            trilB = Buf()
            wsm = sb(st, "wsm", [P, 16, P], BF16)
            wsmB = Buf()
            wsT = sb(st, "wsT", [P, 16, P], BF16)
            wsTB = bufs(16)
            bsT = sb(st, "bsT", [P, 8, P], F32)
            bsTB = Buf()
            lng = sb(st, "lng", [P, D], F32)
            lnb = sb(st, "lnb", [P, D], F32)
            lnB = Buf()
            R1 = sb(st, "R1", [P, 8, BW], F32)
            R1B = bufs(8)
            R1bf = R1[:].bitcast(BF16)
            tmpA = sb(st, "tmpA", [P, 8, BW], BF16)
            tmpAB = bufs(8)
            gu = sb(st, "gu", [P, 8, BW], BF16)
            guB = bufs(8)
            sz = sb(st, "sz", [P, 8, BW], BF16)
            szB = bufs(8)
            vtmp = sb(st, "vtmp", [P, D], F32)
            vtmpB = Buf()
            stt = sb(st, "stt", [P, 2, 6], F32)
            mv = sb(st, "mv", [P, 2], F32)
            rs1 = sb(st, "rs1", [P, 1], F32)
            sttB = Buf()
            tmpS = sb(st, "tmpS", [P, BW], F32)
            tmpSB = Buf()

            def xn_ap(c):
                return R1bf[:, c // 2, (c % 2) * BW:(c % 2) * BW + BW]

            def vln_ap(ch, c0, c1):
                return R1bf[:, 4 + ch, c0:c1]

            xnB = [R1B[c // 2] for c in range(8)]

            dma("sp", wsf[:], dw["d_ws"], [], [wsfB])
            dma("sp", tril[:], dw["d_tril"], [], [trilB])
            for h in range(2):
                src = dw["d_bs"].rearrange("(gp h) t -> h gp t", h=2)[h]
                dma("sp", bsT[h * 64:(h + 1) * 64, :, :], src.unsqueeze(0).to_broadcast([64, 8, P]), [], [bsTB])
            dma("sp", lng[:], dw["d_lng"].to_broadcast([P, D]), [], [lnB])
            dma("sp", lnb[:], dw["d_lnb"].to_broadcast([P, D]), [], [lnB])
            op("dve", [wsfB, trilB], [wsmB],
               lambda e: e.tensor_tensor(out=wsm[:], in0=wsf[:], in1=tril[:].unsqueeze(1).to_broadcast([P, 16, P]), op=ALU.mult))
            for g in range(16):
                b = k.nextps()
                op("pe", [wsmB, identB], [psB[b]],
                   lambda e, g=g, b=b: e.matmul(ps[b][:, 0:P], lhsT=wsm[:, g, :], rhs=ident[:, :], start=True, stop=True))
                op("act", [psB[b]], [wsTB[g]], lambda e, g=g, b=b: e.activation(out=wsT[:, g, :], in_=ps[b][:, 0:P], func=AF.Copy))

            for tb in range(NBLK):
                blk = slice(tb * BW, (tb + 1) * BW)
                for c in range(8):
                    op("act", [xB[c][tb]], [tmpAB[c]],
                       lambda e, c=c: e.activation(out=tmpA[:, c, :], in_=X[:, c, blk], func=AF.Square))
                rstd_from_sq(tmpA, tmpAB, 8, 1.0 / D)
                for c in range(8):
                    op("dve", [xB[c][tb], rstdB, gB], [xnB[c]],
                       lambda e, c=c: e.scalar_tensor_tensor(out=xn_ap(c), in0=X[:, c, blk],
                                                             scalar=gpre[:, li * 8 + c:li * 8 + c + 1], in1=rstd[:],
                                                             op0=ALU.mult, op1=ALU.mult))
                for ch in range(4):
                    for half in range(2):
                        b = k.nextps()
                        for kc in range(8):
                            op("pe", [w_inB, xnB[kc]], [psB[b]],
                               lambda e, kc=kc, b=b: e.matmul(ps[b][:, :], lhsT=xn_ap(kc)[:, ch * P:(ch + 1) * P],
                                                              rhs=w_in[:, kc, D + half * BW:D + (half + 1) * BW],
                                                              start=(kc == 0), stop=(kc == 7)),
                               inc=(kc == 7))
                        op("act", [psB[b]], [vtmpB],
                           lambda e, b=b, half=half: e.activation(out=vtmp[:, half * BW:(half + 1) * BW], in_=ps[b][:, :],
                                                                  func=AF.Gelu_apprx_tanh))
                    for half in range(2):
                        op("dve", [vtmpB], [sttB], lambda e, half=half: e.bn_stats(out=stt[:, half, :], in_=vtmp[:, half * BW:(half + 1) * BW]))
                    op("dve", [sttB], [sttB], lambda e: e.bn_aggr(out=mv[:], in_=stt[:].rearrange("p a b -> p (a b)")))
                    op("act", [sttB], [sttB],
                       lambda e: e.activation(out=rs1[:], in_=mv[:, 1:2], func=AF.Sqrt, scale=1.0, bias=epsc[:, 0:1]))
                    op("dve", [sttB], [sttB], lambda e: e.reciprocal(out=rs1[:], in_=rs1[:]))
                    op("dve", [vtmpB, sttB], [vtmpB],
                       lambda e: e.tensor_scalar(out=vtmp[:], in0=vtmp[:], scalar1=mv[:, 0:1], scalar2=rs1[:, 0:1],
                                                 op0=ALU.subtract, op1=ALU.mult))
                    op("pool", [vtmpB, lnB], [vtmpB], lambda e: e.tensor_tensor(out=vtmp[:], in0=vtmp[:], in1=lng[:], op=ALU.mult))
                    op("pool", [vtmpB, lnB], [R1B[4 + ch]],
                       lambda e, ch=ch: e.tensor_tensor(out=vln_ap(ch, 0, D), in0=vtmp[:], in1=lnb[:], op=ALU.add))
                for m in range(8):
                    b = k.nextps()
                    for kc in range(8):
                        op("pe", [w_inB, xnB[kc]], [psB[b]],
                           lambda e, kc=kc, b=b, m=m: e.matmul(ps[b][:, :], lhsT=w_in[:, kc, m * P:(m + 1) * P], rhs=xn_ap(kc),
                                                               start=(kc == 0), stop=(kc == 7)), inc=(kc == 7))
                    op("act", [psB[b]], [guB[m]], lambda e, b=b, m=m: e.activation(out=gu[:, m, :], in_=ps[b][:, :], func=AF.Gelu_apprx_tanh))
                for m in range(8):
                    b = k.nextps()
                    for kc in range(8):
                        op("pe", [w_inB, xnB[kc]], [psB[b]],
                           lambda e, kc=kc, b=b, m=m: e.matmul(ps[b][:, :], lhsT=w_in[:, kc, 2 * D + m * P:2 * D + (m + 1) * P], rhs=xn_ap(kc),
                                                               start=(kc == 0), stop=(kc == 7)), inc=(kc == 7))
                    op("act", [psB[b]], [szB[m]], lambda e, b=b, m=m: e.activation(out=sz[:, m, :], in_=ps[b][:, :], func=AF.Silu))
                for m in range(8):
                    op("pool", [guB[m], szB[m]], [guB[m]], lambda e, m=m: e.tensor_tensor(out=gu[:, m, :], in0=gu[:, m, :], in1=sz[:, m, :], op=ALU.mult))
                for gp in range(8):
                    b = k.nextps()
                    n = 0
                    for ch in range(4):
                        for h in range(2):
                            g = 2 * gp + h
                            n += 1
                            op("pe", [R1B[4 + ch], wsTB[g]], [psB[b]],
                               lambda e, ch=ch, h=h, g=g, b=b: e.matmul(ps[b][h * 64:(h + 1) * 64, ch * P:(ch + 1) * P],
                                                                        lhsT=vln_ap(ch, g * 64, (g + 1) * 64), rhs=wsT[:, g, :],
                                                                        start=True, stop=True),
                               inc=(n == 8))
                    op("dve", [psB[b], bsTB], [tmpSB],
                       lambda e, b=b, gp=gp: e.tensor_tensor(out=tmpS[:].rearrange("p (a t) -> p a t", a=4),
                                                             in0=ps[b][:, :].rearrange("p (a t) -> p a t", a=4),
                                                             in1=bsT[:, gp, :].unsqueeze(1).to_broadcast([P, 4, P]), op=ALU.add))
                    op("dve", [tmpSB, guB[gp]], [guB[gp]],
                       lambda e, gp=gp: e.tensor_tensor(out=gu[:, gp, :], in0=tmpS[:], in1=gu[:, gp, :], op=ALU.mult))
                outproj(li, tb, gu, guB, w_out, w_outB, R1, R1B, tmpA, tmpAB)
            k.barrier()

    def layer_swa(li):
        st = ExitStack()
        with st:
            NC_IN = 2432
            w_in, w_inB = load_w(st, "b_win", dw["b_w_in"], NC_IN)
            w_out, w_outB = load_w(st, "b_wout", dw["b_w_out"], D)
            KT2 = sb(st, "KT2", [P, 2, L], BF16)
            KTB = bufs(2, NBLK)
            VA = [[sb(st, "VA%d%d" % (kv, par), [P, 16, P], BF16) for par in range(2)] for kv in range(2)]
            VAB = bufs(2, 2, 16)
            biasT = sb(st, "biasT", [P, 16, 256], F32)
            biasB = Buf()
            esink = sb(st, "esink", [P, 16], F32)
            esinkB = Buf()
            R1 = sb(st, "R1", [P, 8, BW], F32)
            R1B = bufs(8)
            R1bf = R1[:].bitcast(BF16)
            tmpA = sb(st, "tmpA", [P, 8, BW], BF16)
            tmpAB = bufs(8)
            sz = sb(st, "sz", [P, 8, BW], BF16)
            szB = bufs(8)
            NPB = 5
            apc = [0]
            tmpP = sb(st, "tmpP", [P, NPB, 256], F32)
            tmpPB = bufs(NPB)
            PT = sb(st, "PT", [P, NPB, 256], BF16)
            PTB = bufs(NPB)
            rsb = sb(st, "rsb", [P, BW], F32)
            rsbB = bufs(2)
            tG = sb(st, "tG", [P, BW], F32)
            tGB = bufs(2)

            def xn_ap(c):
                return R1bf[:, c // 2, (c % 2) * BW:(c % 2) * BW + BW]
            xnB = [R1B[c // 2] for c in range(8)]

            def qt_ap(j, p0, p1, c0, c1):
                return R1bf[p0:p1, 4 + j // 2, (j % 2) * BW + c0:(j % 2) * BW + c1]
            qtB = [R1B[4 + j // 2] for j in range(8)]

            import os
            SK = os.environ.get("SKIP", "")
            if "a" not in SK:
                dma("sp", biasT[:], dw["b_biasT"], [], [biasB])
            if "b" not in SK:
                dma("sp", esink[:], dw["b_sinks"].to_broadcast([P, 16]), [], [esinkB])
                op("act", [esinkB], [esinkB], lambda e: e.activation(out=esink[:], in_=esink[:], func=AF.Exp))
            for kv in range(2):
                for par in range(2):
                    c0 = 64 if par == 0 else 0
                    if "c" not in SK:
                        op("pool", [], sum([[VAB[kv][par][t]] for t in range(16)], []),
                           lambda e, kv=kv, par=par, c0=c0: e.memset(VA[kv][par][:, :, c0:c0 + 64], 1.0))

            for tb in range(NBLK):
                blk = slice(tb * BW, (tb + 1) * BW)
                prenorm_ap(li, tb, xn_ap, xnB, tmpA, tmpAB)
                STG = int(os.environ.get("STG", "9"))
                for j in range(8 if STG >= 1 else 0):
                    b = proj_ap(w_in, w_inB, j * P, P, xn_ap, xnB, 8)
                    op("act", [psB[b]], [qtB[j]], lambda e, b=b, j=j: e.activation(out=qt_ap(j, 0, P, 0, BW), in_=ps[b][:, :], func=AF.Copy))
                for kv in range(2 if STG >= 2 else 0):
                    b = proj_ap(w_in, w_inB, D + kv * P, P, xn_ap, xnB, 8)
                    op("act", [psB[b]], [KTB[kv][tb]], lambda e, b=b, kv=kv: e.activation(out=KT2[:, kv, blk], in_=ps[b][:, :], func=AF.Copy))
                for ch in range(4 if STG >= 3 else 0):
                    tt = tb * 4 + ch
                    b = k.nextps()
                    for kc in range(8):
                        op("pe", [w_inB, xnB[kc]], [psB[b]],
                           lambda e, kc=kc, b=b, ch=ch: e.matmul(ps[b][:, 0:P], lhsT=xn_ap(kc)[:, ch * P:(ch + 1) * P],
                                                                 rhs=w_in[:, kc, D + 256:D + 384], start=(kc == 0), stop=(kc == 7)),
                           inc=(kc == 7))
                    for kv in range(2):
                        op("act", [psB[b]], [VAB[kv][0][tt]],
                           lambda e, b=b, kv=kv, tt=tt: e.activation(out=VA[kv][0][:, tt, 0:64], in_=ps[b][:, kv * 64:(kv + 1) * 64], func=AF.Copy))
                        op("act", [psB[b]], [VAB[kv][1][tt]],
                           lambda e, b=b, kv=kv, tt=tt: e.activation(out=VA[kv][1][:, tt, 64:128], in_=ps[b][:, kv * 64:(kv + 1) * 64], func=AF.Copy))
                for m in range(8):
                    b = proj_ap(w_in, w_inB, D + 384 + m * P, P, xn_ap, xnB, 8)
                    op("act", [psB[b]], [szB[m]], lambda e, b=b, m=m: e.activation(out=sz[:, m, :], in_=ps[b][:, :], func=AF.Silu))
                def scores(h, nbl, pi):
                    kv, par, j = h // 8, h % 2, h // 2
                    base = par * 64
                    nb = tb * 4 + nbl
                    b = k.nextps()
                    c_lo = 0 if nb > 0 else P
                    rhs_q = qt_ap(j, base, base + 64, nbl * P, (nbl + 1) * P)
                    if nb > 0:
                        tbp = (nb - 1) // 4
                        op("pe", [KTB[kv][tbp], qtB[j]], [psB[b]],
                           lambda e: e.matmul(ps[b][:, 0:P], lhsT=KT2[base:base + 64, kv, (nb - 1) * P:nb * P], rhs=rhs_q, start=True, stop=True),
                           inc=False)
                    op("pe", [KTB[kv][tb], qtB[j]], [psB[b]],
                       lambda e: e.matmul(ps[b][:, P:2 * P], lhsT=KT2[base:base + 64, kv, nb * P:(nb + 1) * P], rhs=rhs_q, start=True, stop=True))
                    op("dve", [psB[b], biasB], [tmpPB[pi]],
                       lambda e: e.scalar_tensor_tensor(out=tmpP[:, pi, c_lo:256], in0=ps[b][:, c_lo:256], scalar=0.125, in1=biasT[:, h, c_lo:256],
                                                        op0=ALU.mult, op1=ALU.add))
                    op("act", [tmpPB[pi]], [PTB[pi]],
                       lambda e: e.activation(out=PT[:, pi, c_lo:256], in_=tmpP[:, pi, c_lo:256], func=AF.Exp))

                def pv_epi(h, nbl, pi):
                    kv, par, j = h // 8, h % 2, h // 2
                    base = par * 64
                    oth = 64 - base
                    bo = 6 + (h % 2)
                    nb = tb * 4 + nbl
                    if nb > 0:
                        op("pe", [PTB[pi], VAB[kv][par][nb - 1]], [psB[bo]],
                           lambda e: e.matmul(ps[bo][:, nbl * P:(nbl + 1) * P], lhsT=VA[kv][par][:, nb - 1, :], rhs=PT[:, pi, 0:P], start=True, stop=False),
                           inc=False)
                    op("pe", [PTB[pi], VAB[kv][par][nb]], [psB[bo]],
                       lambda e: e.matmul(ps[bo][:, nbl * P:(nbl + 1) * P], lhsT=VA[kv][par][:, nb, :], rhs=PT[:, pi, P:2 * P], start=(nb == 0), stop=True))
                    if nbl == 3:
                        op("dve", [psB[bo], esinkB], [rsbB[par]],
                           lambda e: e.tensor_scalar(out=rsb[base:base + 64, :], in0=ps[bo][oth:oth + 64, :], scalar1=esink[oth:oth + 64, h:h + 1],
                                                     scalar2=None, op0=ALU.add))
                        op("act", [rsbB[par]], [rsbB[par]], lambda e: e.activation(out=rsb[base:base + 64, :], in_=rsb[base:base + 64, :], func=AF.Ln))
                        op("act", [rsbB[par]], [rsbB[par]], lambda e: e.activation(out=rsb[base:base + 64, :], in_=rsb[base:base + 64, :], func=AF.Exp, scale=-1.0))
                        op("dve", [psB[bo], rsbB[par]], [tGB[par]],
                           lambda e: e.tensor_tensor(out=tG[base:base + 64, :], in0=ps[bo][base:base + 64, :], in1=rsb[base:base + 64, :], op=ALU.mult))
                        op("pool", [tGB[par], szB[j]], [szB[j]],
                           lambda e: e.tensor_tensor(out=sz[base:base + 64, j, :], in0=tG[base:base + 64, :], in1=sz[base:base + 64, j, :], op=ALU.mult))

                import os
                tasks = [(h, nbl) for h in range(int(os.environ.get('SWA_NH', '16'))) for nbl in range(4)]
                LAS = 3
                for i in range(min(LAS, len(tasks))):
                    scores(tasks[i][0], tasks[i][1], (apc[0] + i) % NPB)
                for i, (h, nbl) in enumerate(tasks):
                    pi = apc[0] % NPB
                    apc[0] += 1
                    if i + LAS < len(tasks):
                        scores(tasks[i + LAS][0], tasks[i + LAS][1], (apc[0] + LAS - 1) % NPB)
                    pv_epi(h, nbl, pi)
                outproj(li, tb, sz, szB, w_out, w_outB, R1, R1B, tmpA, tmpAB)
            k.barrier()

    def layer_mla(li):
        SC = 96.0 ** -0.5
        st = ExitStack()
        with st:
            GT = sb(st, "GT", [P, 8, L], BF16)
            GTB = bufs(8, NBLK)
            sA = ExitStack()
            sA.__enter__()
            CQN = sb(sA, "CQN", [P, 6, L], BF16)
            CQNB = bufs(6, NBLK)
            CKVN = sb(sA, "CKVN", [P, 2, L], BF16)
            CKVNB = bufs(2, NBLK)
            KR = sb(sA, "KR", [32, L], BF16)
            KRB = bufs(NBLK)
            ROPE = sb(sA, "ROPE", [64, L], F32)
            ropeB = Buf()
            gq = sb(sA, "gq", [P, 8], F32)
            gqB = Buf()
            dma("sp", ROPE[:], dw["c_rope"], [], [ropeB])
            dma("sp", gq[:], dw["c_gqkv"], [], [gqB])
            tR = sb(sA, "tR", [32, 2, BW], F32)
            tRB = bufs(2)
            s1 = ExitStack()
            with s1:
                w1, w1B = load_w(s1, "s_c_w1", dw["c_w1"], 1088)
                R1 = sb(s1, "R1", [P, 8, BW], F32)
                R1B = bufs(8)
                xnt = sb(s1, "xnt", [P, 8, BW], BF16)
                xnB = bufs(8)
                tmpA = sb(s1, "tmpA", [P, 8, BW], BF16)
                tmpAB = bufs(8)
                xn_ap = lambda c: xnt[:, c, :]
                for tb in range(NBLK):
                    blk = slice(tb * BW, (tb + 1) * BW)
                    prenorm_ap(li, tb, xn_ap, xnB, tmpA, tmpAB)
                    for m in range(8):
                        b = proj_ap(w1, w1B, m * P, P, xn_ap, xnB, 8)
                        op("act", [psB[b]], [R1B[m]], lambda e, b=b, m=m: e.activation(out=R1[:, m, :], in_=ps[b][:, :], func=AF.Copy))
                        op("act", [psB[b]], [tmpAB[m]], lambda e, b=b, m=m: e.activation(out=tmpA[:, m, :], in_=ps[b][:, :], func=AF.Square))
                    rstd_from_sq(tmpA, tmpAB, 6, 1.0 / 768, 0)
                    for m in range(6):
                        op("dve", [R1B[m], rstdB, gqB], [CQNB[m][tb]],
                           lambda e, m=m: e.scalar_tensor_tensor(out=CQN[:, m, blk], in0=R1[:, m, :], scalar=gq[:, m:m + 1], in1=rstd[:],
                                                                 op0=ALU.mult, op1=ALU.mult))
                    rstd_from_sq(tmpA, tmpAB, 2, 1.0 / 256, 6)
                    for m in range(2):
                        op("dve", [R1B[6 + m], rstdB, gqB], [CKVNB[m][tb]],
                           lambda e, m=m: e.scalar_tensor_tensor(out=CKVN[:, m, blk], in0=R1[:, 6 + m, :], scalar=gq[:, 6 + m:7 + m], in1=rstd[:],
                                                                 op0=ALU.mult, op1=ALU.mult))
                    b = proj_ap(w1, w1B, D, 64, xn_ap, xnB, 8)
                    op("dve", [psB[b], ropeB], [tRB[0]],
                       lambda e, b=b: e.tensor_tensor(out=tR[:, 0, :], in0=ps[b][32:64, :], in1=ROPE[32:64, blk], op=ALU.mult))
                    op("dve", [psB[b], ropeB], [tRB[1]],
                       lambda e, b=b: e.tensor_tensor(out=tR[:, 1, :], in0=ps[b][0:32, :], in1=ROPE[0:32, blk], op=ALU.mult))
                    op("pool", [tRB[0], tRB[1]], [KRB[tb]],
                       lambda e: e.tensor_tensor(out=KR[:, blk], in0=tR[:, 0, :], in1=tR[:, 1, :], op=ALU.add))
                k.barrier()
            s2 = ExitStack()
            with s2:
                wuq, wuqB = load_w(s2, "s_c_wuq", dw["c_wuq"], 2048)
                wukv, wukvB = load_w(s2, "s_c_wukv", dw["c_wukv"], 2048)
                KTh = [sb(s2, "KTh%d" % i, [P, L], BF16) for i in range(2)]
                KThB = bufs(2, NBLK)
                VAh = [sb(s2, "VAh%d" % i, [P, 16, P], BF16) for i in range(2)]
                VAhB = bufs(2, 4)
                QT = [sb(s2, "QT%d" % i, [P, BW], BF16) for i in range(2)]
                QTB = bufs(2)
                NPT = 5
                LA = 3
                PT = [sb(s2, "PT%d" % i, [P, BW], BF16) for i in range(NPT)]
                PTB = bufs(NPT)
                QF = sb(s2, "QF", [64, BW], F32)
                QFB = Buf()
                maskT = sb(s2, "maskT", [P, P], BF16)
                maskB = Buf()
                rsb = sb(s2, "rsb", [P, BW], F32)
                rsbB = bufs(2)
                dma("pool", maskT[:], dw["c_maskT"], [], [maskB])
                for i in range(2):
                    op("pool", [], KThB[i], lambda e, i=i: e.memset(KTh[i][32:64, :], 0.0))
                    c0 = 64 if i == 0 else 0
                    op("pool", [], VAhB[i], lambda e, i=i, c0=c0: e.memset(VAh[i][:, :, c0:c0 + 64], 1.0))
                def kv_build(h):
                    par = h % 2
                    vc0 = par * 64
                    for tb in range(NBLK):
                        blk = slice(tb * BW, (tb + 1) * BW)
                        b = k.nextps()
                        for kc in range(2):
                            op("pe", [wukvB, CKVNB[kc][tb]], [psB[b]],
                               lambda e, kc=kc, b=b, blk=blk: e.matmul(ps[b][64:128, :], lhsT=wukv[:, kc, h * P:h * P + 64], rhs=CKVN[:, kc, blk],
                                                                      start=(kc == 0), stop=(kc == 1)), inc=(kc == 1))
                        op("act", [psB[b]], [KThB[par][tb]],
                           lambda e, b=b, blk=blk: e.activation(out=KTh[par][64:128, blk], in_=ps[b][64:128, :], func=AF.Copy))
                        op("dve", [KRB[tb]], [KThB[par][tb]],
                           lambda e, blk=blk: e.tensor_scalar(out=KTh[par][0:32, blk], in0=KR[:, blk], scalar1=1.0, scalar2=None, op0=ALU.mult))
                        b = k.nextps()
                        for i4 in range(4):
                            tt = tb * 4 + i4
                            for kc in range(2):
                                op("pe", [wukvB, CKVNB[kc][tb]], [psB[b]],
                                   lambda e, kc=kc, b=b, tt=tt, i4=i4: e.matmul(ps[b][:, i4 * 64:(i4 + 1) * 64], lhsT=CKVN[:, kc, tt * P:(tt + 1) * P],
                                                                                rhs=wukv[:, kc, h * P + 64:h * P + 128], start=(kc == 0), stop=(kc == 1)),
                                   inc=(kc == 1 and i4 == 3))
                        op("act", [psB[b]], [VAhB[par][tb]],
                           lambda e, b=b, tb=tb: e.activation(out=VAh[par][:, tb * 4:(tb + 1) * 4, vc0:vc0 + 64],
                                                              in_=ps[b][:, 0:256].rearrange("p (a c) -> p a c", a=4), func=AF.Copy))

                def q_prep(h, qb, qi):
                    qblk = slice(qb * BW, (qb + 1) * BW)
                    b = k.nextps()
                    for kc in range(6):
                        op("pe", [wuqB, CQNB[kc][qb]], [psB[b]],
                           lambda e, kc=kc, b=b: e.matmul(ps[b][:, :], lhsT=wuq[:, kc, h * P:(h + 1) * P], rhs=CQN[:, kc, qblk],
                                                          start=(kc == 0), stop=(kc == 5)), inc=(kc == 5))
                    op("act", [psB[b]], [QTB[qi]], lambda e, b=b: e.activation(out=QT[qi][:, :], in_=ps[b][:, :], func=AF.Copy))
                    op("act", [psB[b]], [QFB], lambda e, b=b: e.activation(out=QF[:, :], in_=ps[b][0:64, :], func=AF.Copy))
                    op("dve", [QFB, ropeB], [tRB[0]],
                       lambda e: e.tensor_tensor(out=tR[:, 0, :], in0=QF[32:64, :], in1=ROPE[32:64, qblk], op=ALU.mult))
                    op("dve", [QFB, ropeB], [tRB[1]],
                       lambda e: e.tensor_tensor(out=tR[:, 1, :], in0=QF[0:32, :], in1=ROPE[0:32, qblk], op=ALU.mult))
                    op("dve", [tRB[0], tRB[1]], [QTB[qi]],
                       lambda e: e.tensor_tensor(out=QT[qi][0:32, :], in0=tR[:, 0, :], in1=tR[:, 1, :], op=ALU.add))

                pti = [0]

                def attend(h, qb, qi, bo):
                    par, j = h % 2, h // 2
                    base = par * 64
                    oth = 64 - base
                    qblk = slice(qb * BW, (qb + 1) * BW)
                    nkc = 4 * qb + 4

                    def pv(kc, pi, q_lo):
                        op("pe", [PTB[pi], VAhB[par][kc // 4]], [psB[bo]],
                           lambda e: e.matmul(ps[bo][:, q_lo:BW], lhsT=VAh[par][:, kc, :], rhs=PT[pi][:, q_lo:BW],
                                              start=(kc == 0), stop=(kc == nkc - 1)))
                    pend_pv = []
                    for kc in range(nkc):
                        q_lo = max(0, kc - 4 * qb) * P
                        pi = pti[0] % NPT
                        pti[0] += 1
                        b = k.nextps()
                        op("pe", [KThB[par][kc // 4], QTB[qi]], [psB[b]],
                           lambda e, b=b, kc=kc, q_lo=q_lo: e.matmul(ps[b][:, q_lo:BW], lhsT=KTh[par][:, kc * P:(kc + 1) * P],
                                                                     rhs=QT[qi][:, q_lo:BW], start=True, stop=True))
                        op("act", [psB[b]], [PTB[pi]],
                           lambda e, b=b, pi=pi, q_lo=q_lo: e.activation(out=PT[pi][:, q_lo:BW], in_=ps[b][:, q_lo:BW], func=AF.Exp, scale=SC))
                        if kc >= 4 * qb:
                            op("dve", [PTB[pi], maskB], [PTB[pi]],
                               lambda e, pi=pi, q_lo=q_lo: e.tensor_tensor(out=PT[pi][:, q_lo:q_lo + P], in0=PT[pi][:, q_lo:q_lo + P],
                                                                           in1=maskT[:, :], op=ALU.mult))
                        pend_pv.append((kc, pi, q_lo))
                        if len(pend_pv) > LA:
                            pv(*pend_pv.pop(0))
                    while pend_pv:
                        pv(*pend_pv.pop(0))
                    op("dve", [psB[bo]], [rsbB[par]],
                       lambda e: e.tensor_scalar(out=rsb[base:base + 64, :], in0=ps[bo][oth:oth + 64, :], scalar1=1.0, scalar2=None, op0=ALU.mult))
                    op("act", [rsbB[par]], [rsbB[par]], lambda e: e.activation(out=rsb[base:base + 64, :], in_=rsb[base:base + 64, :], func=AF.Ln))
                    op("act", [rsbB[par]], [rsbB[par]], lambda e: e.activation(out=rsb[base:base + 64, :], in_=rsb[base:base + 64, :], func=AF.Exp, scale=-1.0))
                    op("dve", [psB[bo], rsbB[par]], [GTB[j][qb]],
                       lambda e: e.tensor_tensor(out=GT[base:base + 64, j, qblk], in0=ps[bo][base:base + 64, :],
                                                 in1=rsb[base:base + 64, :], op=ALU.mult))

                import os
                NH = int(os.environ.get("MLA_NH", "16"))
                tasks = [(h, qb) for h in range(NH) for qb in range(NBLK)]
                if tasks:
                    kv_build(0)
                    q_prep(0, 0, 0)
                for i, (h, qb) in enumerate(tasks):
                    if i + 1 < len(tasks):
                        h2, qb2 = tasks[i + 1]
                        if h2 != h:
                            kv_build(h2)
                        q_prep(h2, qb2, (i + 1) % 2)
                    attend(h, qb, i % 2, 6 + (i % 2))
                k.barrier()
            sA.close()
            s3 = ExitStack()
            with s3:
                wz, wzB = load_w(s3, "s_c_wz", dw["c_wz"], D)
                w_out, w_outB = load_w(s3, "s_c_wout", dw["c_w_out"], D)
                R1 = sb(s3, "R1", [P, 8, BW], F32)
                R1B = bufs(8)
                xnt = [sb(s3, "xnt%d" % i, [P, 8, BW], BF16) for i in range(2)]
                xnB = bufs(2, 8)
                tmpA = sb(s3, "tmpA", [P, 8, BW], BF16)
                tmpAB = bufs(8)
                tmpX = sb(s3, "tmpX", [P, 8, BW], BF16)
                tmpXB = bufs(8)
                sz = [sb(s3, "sz%d" % i, [P, 8, BW], BF16) for i in range(2)]
                szB = bufs(2, 8)

                def front(tb):
                    i = tb % 2
                    blk = slice(tb * BW, (tb + 1) * BW)
                    xn_ap = lambda c: xnt[i][:, c, :]
                    prenorm_ap(li, tb, xn_ap, xnB[i], tmpX, tmpXB)
                    for m in range(8):
                        b = proj_ap(wz, wzB, m * P, P, xn_ap, xnB[i], 8)
                        op("act", [psB[b]], [szB[i][m]], lambda e, b=b, m=m: e.activation(out=sz[i][:, m, :], in_=ps[b][:, :], func=AF.Silu))
                        op("pool", [szB[i][m], GTB[m][tb]], [szB[i][m]],
                           lambda e, m=m: e.tensor_tensor(out=sz[i][:, m, :], in0=sz[i][:, m, :], in1=GT[:, m, blk], op=ALU.mult))

                front(0)
                for tb in range(NBLK):
                    if tb + 1 < NBLK:
                        front(tb + 1)
                    outproj(li, tb, sz[tb % 2], szB[tb % 2], w_out, w_outB, R1, R1B, tmpA, tmpAB)
                k.barrier()

    def layer_s5(li):
        TT = ALU
        st = ExitStack()
        with st:
            UY = sb(st, "UY", [P, 8, L], BF16)
            UYB = bufs(8, NBLK)
            dT = sb(st, "dT", [P, 16], F32)
            dTB = Buf()
            dma("sp", dT[:], dw["a_dg"], [], [dTB])
            sP = ExitStack()
            sP.__enter__()
            NJ = 32
            BrL, BrLB = sb(sP, "BrL", [P, NJ, P], BF16), Buf()
            BiL, BiLB = sb(sP, "BiL", [P, NJ, P], BF16), Buf()
            CrP, CrPB = sb(sP, "CrP", [P, NJ, P], BF16), Buf()
            CiP, CiPB = sb(sP, "CiP", [P, NJ, P], BF16), Buf()
            for t_, B_, nm in ((BrL, BrLB, "a_brl"), (BiL, BiLB, "a_bil"), (CrP, CrPB, "a_crp"), (CiP, CiPB, "a_cip")):
                for q4 in range(4):
                    dma("pool", t_[:, q4 * 8:(q4 + 1) * 8, :], dw[nm][:, q4 * 8:(q4 + 1) * 8, :], [], [B_])
            lam = sb(sP, "lam", [P, 3, NJ], F32)
            prepB = Buf()
            dma("sp", lam[:], dw["a_lam"], [], [prepB])
            W = {}
            for nm in ("dt", "lrdt", "th", "mag", "t", "t2", "c", "s", "q", "c2", "s2", "cs", "ar", "ai", "den", "nr", "fr", "fi",
                       "u1", "u2", "ir", "ii", "pr", "pi", "A128r", "nA128i", "A128i"):
                W[nm] = sb(sP, "w_" + nm, [P, NJ], F32)
            lr, li_, ldt = lam[:, 0, :], lam[:, 1, :], lam[:, 2, :]

            TPr = sb(sP, "TPr", [P, NJ, P], BF16)
            TPi = sb(sP, "TPi", [P, NJ, P], BF16)
            TNr = sb(sP, "TNr", [P, NJ, P], BF16)
            TNi = sb(sP, "TNi", [P, NJ, P], BF16)
            car = sb(sP, "car", [P, 2, NJ], F32)
            carB = bufs(NJ)
            s1 = ExitStack()
            with s1:
                wu, wuB = load_w(s1, "a_wu", dw["a_wu"], D)
                xnt = sb(s1, "xnt", [P, 8, BW], BF16)
                xnB = bufs(8)
                tmpA = sb(s1, "tmpA", [P, 8, BW], BF16)
                tmpAB = bufs(8)
                m1 = sb(s1, "m1t", [P, NJ, 16], F32)
                m2 = sb(s1, "m2t", [P, NJ, 16], F32)
                prep_ops = []

                def dop(e_, r_, w_, fn_):
                    prep_ops.append((e_, r_, w_, fn_))

                def tt(o, a, b_, o_):
                    dop("dve", [prepB], [prepB], lambda e: e.tensor_tensor(out=o, in0=a, in1=b_, op=o_))

                def ts(o, a, s1_, s2_, o1, o2=None):
                    if o2 is None:
                        dop("dve", [prepB], [prepB], lambda e: e.tensor_scalar(out=o, in0=a, scalar1=s1_, scalar2=None, op0=o1))
                    else:
                        dop("dve", [prepB], [prepB], lambda e: e.tensor_scalar(out=o, in0=a, scalar1=s1_, scalar2=s2_, op0=o1, op1=o2))

                def stt(o, a, sc, b_, o1, o2):
                    dop("dve", [prepB], [prepB], lambda e: e.tensor_scalar(out=o, in0=a, scalar1=sc, scalar2=None, op0=o1))
                    dop("dve", [prepB], [prepB], lambda e: e.tensor_tensor(out=o, in0=o, in1=b_, op=o2))

                def csq(cr, ci):
                    tt(W["c2"][:], cr, cr, TT.mult)
                    tt(W["s2"][:], ci, ci, TT.mult)
                    tt(W["cs"][:], cr, ci, TT.mult)
                    tt(cr, W["c2"][:], W["s2"][:], TT.subtract)
                    ts(ci, W["cs"][:], 2.0, None, TT.mult)

                dop("act", [prepB], [prepB], lambda e: e.activation(out=W["dt"][:], in_=ldt, func=AF.Exp))
                tt(W["lrdt"][:], lr, W["dt"][:], TT.mult)
                tt(W["th"][:], li_, W["dt"][:], TT.mult)
                dop("act", [prepB], [prepB], lambda e: e.activation(out=W["mag"][:], in_=W["lrdt"][:], func=AF.Exp))
                ts(W["t"][:], W["th"][:], 1.0 / 64, None, TT.mult)
                tt(W["t2"][:], W["t"][:], W["t"][:], TT.mult)
                ts(W["q"][:], W["t2"][:], -1.0 / 720, None, TT.mult)
                stt(W["q"][:], W["q"][:], 1.0 / 24, W["t2"][:], TT.add, TT.mult)
                stt(W["q"][:], W["q"][:], -0.5, W["t2"][:], TT.add, TT.mult)
                ts(W["c"][:], W["q"][:], 1.0, None, TT.add)
                ts(W["q"][:], W["t2"][:], -1.0 / 5040, None, TT.mult)
                stt(W["q"][:], W["q"][:], 1.0 / 120, W["t2"][:], TT.add, TT.mult)
                stt(W["q"][:], W["q"][:], -1.0 / 6, W["t2"][:], TT.add, TT.mult)
                stt(W["s"][:], W["q"][:], 1.0, W["t"][:], TT.add, TT.mult)
                for _ in range(6):
                    csq(W["c"][:], W["s"][:])
                tt(W["ar"][:], W["mag"][:], W["c"][:], TT.mult)
                tt(W["ai"][:], W["mag"][:], W["s"][:], TT.mult)
                tt(W["den"][:], lr, lr, TT.mult)
                tt(W["u1"][:], li_, li_, TT.mult)
                tt(W["den"][:], W["den"][:], W["u1"][:], TT.add)
                dop("act", [prepB], [prepB], lambda e: e.activation(out=W["den"][:], in_=W["den"][:], func=AF.Ln))
                dop("act", [prepB], [prepB], lambda e: e.activation(out=W["den"][:], in_=W["den"][:], func=AF.Exp, scale=-1.0))
                ts(W["nr"][:], W["ar"][:], -1.0, None, TT.add)
                tt(W["u1"][:], W["nr"][:], lr, TT.mult)
                tt(W["u2"][:], W["ai"][:], li_, TT.mult)
                tt(W["u1"][:], W["u1"][:], W["u2"][:], TT.add)
                tt(W["fr"][:], W["u1"][:], W["den"][:], TT.mult)
                tt(W["u1"][:], W["ai"][:], lr, TT.mult)
                tt(W["u2"][:], W["nr"][:], li_, TT.mult)
                tt(W["u1"][:], W["u1"][:], W["u2"][:], TT.subtract)
                tt(W["fi"][:], W["u1"][:], W["den"][:], TT.mult)
                dop("act", [prepB], [prepB], lambda e: e.activation(out=W["u1"][:], in_=W["lrdt"][:], func=AF.Exp, scale=-2.0))
                tt(W["ir"][:], W["ar"][:], W["u1"][:], TT.mult)
                tt(W["ii"][:], W["ai"][:], W["u1"][:], TT.mult)
                ts(W["ii"][:], W["ii"][:], -1.0, None, TT.mult)
                dop("dve", [prepB], [prepB], lambda e: e.memset(TPr[:, :, 0:1], 1.0))
                dop("dve", [prepB], [prepB], lambda e: e.memset(TPi[:, :, 0:1], 0.0))
                ts(TNr[:, :, 0:1], W["fr"][:].unsqueeze(2), 1.0, None, TT.mult)
                ts(TNi[:, :, 0:1], W["fi"][:].unsqueeze(2), 1.0, None, TT.mult)
                for (Tr_, Ti_, pr0, pi0) in ((TPr, TPi, "ar", "ai"), (TNr, TNi, "ir", "ii")):
                    ts(W["pr"][:], W[pr0][:], 1.0, None, TT.mult)
                    ts(W["pi"][:], W[pi0][:], 1.0, None, TT.mult)
                    for kk in range(7):
                        n = 1 << kk
                        for c0 in range(0, n, 16):
                            w = min(16, n - c0)
                            pr_b = W["pr"][:].unsqueeze(2).to_broadcast([P, NJ, w])
                            pi_b = W["pi"][:].unsqueeze(2).to_broadcast([P, NJ, w])
                            lo_r, lo_i = Tr_[:, :, c0:c0 + w], Ti_[:, :, c0:c0 + w]
                            tt(m1[:, :, 0:w], lo_r, pr_b, TT.mult)
                            tt(m2[:, :, 0:w], lo_i, pi_b, TT.mult)
                            tt(Tr_[:, :, n + c0:n + c0 + w], m1[:, :, 0:w], m2[:, :, 0:w], TT.subtract)
                            tt(m1[:, :, 0:w], lo_r, pi_b, TT.mult)
                            tt(m2[:, :, 0:w], lo_i, pr_b, TT.mult)
                            tt(Ti_[:, :, n + c0:n + c0 + w], m1[:, :, 0:w], m2[:, :, 0:w], TT.add)
                        csq(W["pr"][:], W["pi"][:])
                    if pr0 == "ar":
                        ts(W["A128r"][:], W["pr"][:], 1.0, None, TT.mult)
                        ts(W["nA128i"][:], W["pi"][:], -1.0, None, TT.mult)
                        ts(W["A128i"][:], W["pi"][:], 1.0, None, TT.mult)
                dop("dve", [prepB], [prepB], lambda e: e.tensor_scalar(out=TNi[:], in0=TNi[:], scalar1=-1.0, scalar2=None, op0=TT.mult))
                dop("dve", [prepB], carB, lambda e: e.memset(car[:], 0.0))
                xn_ap = lambda c: xnt[:, c, :]
                npo = len(prep_ops)
                for tb in range(NBLK):
                    blk = slice(tb * BW, (tb + 1) * BW)
                    prenorm_ap(li, tb, xn_ap, xnB, tmpA, tmpAB)
                    for (e_, r_, w_, fn_) in prep_ops[tb * npo // NBLK:(tb + 1) * npo // NBLK]:
                        op(e_, r_, w_, fn_)
                    for m in range(8):
                        b = proj_ap(wu, wuB, m * P, P, xn_ap, xnB, 8)
                        op("act", [psB[b]], [UYB[m][tb]], lambda e, b=b, m=m, blk=blk: e.activation(out=UY[:, m, blk], in_=ps[b][:, :], func=AF.Copy))
                k.barrier()
            s2 = ExitStack()
            with s2:
                NA, NCD, NQ = 2, 2, 2
                TA = [sb(s2, "TA%d" % i, [P, 4, P], BF16) for i in range(NA)]
                TBf = [sb(s2, "TB%d" % i, [P, 4, P], BF16) for i in range(NA)]
                CD = [sb(s2, "CD%d" % i, [P, 2, 4, P], F32) for i in range(NCD)]
                T4 = [sb(s2, "T4_%d" % i, [P, 4, P], F32) for i in range(4)]
                T4B = Buf()
                KBt = [sb(s2, "KB%d" % i, [P, 2, 4, P], BF16) for i in range(2)]
                KBB = bufs(2)
                triu = sb(s2, "triu", [P, P], BF16)
                triuB = Buf()
                dma("pool", triu[:], dw["a_triu"], [], [triuB])
                for T_ in (TNr, TNi):
                    for j4 in range(8):
                        b = k.nextps()
                        for jl in range(4):
                            op("pe", [prepB, identB], [psB[b]],
                               lambda e, b=b, T_=T_, j4=j4, jl=jl: e.matmul(ps[b][:, jl * P:(jl + 1) * P], lhsT=T_[:, j4 * 4 + jl, :], rhs=ident[:, :],
                                                                            start=True, stop=True), inc=(jl == 3))
                        op("act", [psB[b]], [prepB],
                           lambda e, b=b, T_=T_, j4=j4: e.activation(out=T_[:, j4 * 4:j4 * 4 + 4, :].rearrange("p a t -> p (a t)"), in_=ps[b][:, :], func=AF.Copy))
                Q = [[sb(s2, "Q%d_%d" % (i, q_), [P, 4, P], BF16) for q_ in range(4)] for i in range(NQ)]
                AB_, BB_, CDB = bufs(NA), bufs(NA), bufs(NCD)
                QB = bufs(NQ, 4)
                ea = sb(s2, "ea", [P, 2, 4], F32)
                eb = sb(s2, "eb", [P, 2, 4], F32)
                eB = Buf()
                ytmp = sb(s2, "ytmp", [P, BW], F32)
                yB = Buf()

                def flat(t_):
                    return t_[:].rearrange("p a t -> p (a t)")

                units = [(chc, c) for cp in range(4) for c in range(16) for chc in (2 * cp, 2 * cp + 1)]
                NU = len(units)

                def stA(u):
                    chc, c = units[u]
                    j0, qt = chc * 4, c // 4
                    cols = slice(c * P, (c + 1) * P)
                    ai = u % NA
                    ba = k.nextps()
                    bb = k.nextps()
                    op("pe", [BrLB, UYB[chc][qt]], [psB[ba]],
                       lambda e: e.matmul(ps[ba][:, :], lhsT=UY[:, chc, cols], rhs=BrL[:, j0:j0 + 4, :].rearrange("p a t -> p (a t)"), start=True, stop=True))
                    op("pe", [BiLB, UYB[chc][qt]], [psB[bb]],
                       lambda e: e.matmul(ps[bb][:, :], lhsT=UY[:, chc, cols], rhs=BiL[:, j0:j0 + 4, :].rearrange("p a t -> p (a t)"), start=True, stop=True))
                    op("act", [psB[ba]], [AB_[ai]], lambda e: e.activation(out=flat(TA[ai]), in_=ps[ba][:, :], func=AF.Copy))
                    op("act", [psB[bb]], [BB_[ai]], lambda e: e.activation(out=flat(TBf[ai]), in_=ps[bb][:, :], func=AF.Copy))

                def ctx(u):
                    chc, c = units[u]
                    j0 = chc * 4
                    ai, ci, qi = u % NA, u % NCD, u % NQ
                    return chc, c, j0, ai, ci, qi

                def stB_mul(u):
                    chc, c, j0, ai, ci, qi = ctx(u)
                    A, B_ = TA[ai], TBf[ai]
                    tnr, ntni = TNr[:, j0:j0 + 4, :], TNi[:, j0:j0 + 4, :]
                    op("dve", [AB_[ai], prepB], [T4B], lambda e: e.tensor_tensor(out=T4[0][:], in0=A[:], in1=tnr, op=TT.mult))
                    op("dve", [BB_[ai], prepB], [T4B], lambda e: e.tensor_tensor(out=T4[1][:], in0=B_[:], in1=ntni, op=TT.mult))
                    op("dve", [AB_[ai], prepB], [T4B], lambda e: e.tensor_tensor(out=T4[2][:], in0=A[:], in1=ntni, op=TT.mult))
                    op("dve", [BB_[ai], prepB], [T4B], lambda e: e.tensor_tensor(out=T4[3][:], in0=B_[:], in1=tnr, op=TT.mult))

                def stB_comb(u):
                    kb = KBt[u % 2]
                    op("dve", [T4B], [KBB[u % 2]], lambda e: e.tensor_tensor(out=kb[:, 0, :, :], in0=T4[0][:], in1=T4[1][:], op=TT.add))
                    op("dve", [T4B], [KBB[u % 2]], lambda e: e.tensor_tensor(out=kb[:, 1, :, :], in0=T4[2][:], in1=T4[3][:], op=TT.subtract))

                def stT_cs(u):
                    chc, c, j0, ai, ci, qi = ctx(u)
                    kt = KBt[u % 2]
                    for ri in range(2):
                        b = k.nextps()
                        for jl in range(4):
                            op("pe", [KBB[u % 2], triuB], [psB[b]],
                               lambda e, b=b, ri=ri, jl=jl: e.matmul(ps[b][:, jl * P:(jl + 1) * P], lhsT=kt[:, ri, jl, :], rhs=triu[:, :], start=True, stop=True),
                               inc=(jl == 3))
                        for jl in range(4):
                            op("act", [psB[b], carB[j0]], [CDB[ci]],
                               lambda e, b=b, ri=ri, jl=jl: e.activation(out=CD[ci][:, ri, jl, :], in_=ps[b][:, jl * P:(jl + 1) * P], func=AF.Identity,
                                                                         bias=car[:, ri, j0 + jl:j0 + jl + 1], scale=1.0))

                def stD_e(u):
                    chc, c, j0, ai, ci, qi = ctx(u)
                    if c < 15:
                        xl = CD[ci][:, :, :, P - 1]
                        a_r = W["A128r"][:, j0:j0 + 4].unsqueeze(1).to_broadcast([P, 2, 4])
                        a_i = W["A128i"][:, j0:j0 + 4].unsqueeze(1).to_broadcast([P, 2, 4])
                        op("dve", [CDB[ci], prepB, carB[j0]], [eB], lambda e: e.tensor_tensor(out=ea[:], in0=xl, in1=a_r, op=TT.mult))
                        op("dve", [CDB[ci], prepB, carB[j0]], [eB], lambda e: e.tensor_tensor(out=eb[:], in0=xl, in1=a_i, op=TT.mult))

                def stD_q3(u):
                    chc, c, j0, ai, ci, qi = ctx(u)
                    C = CD[ci][:, 0, :, :]
                    tpi = TPi[:, j0:j0 + 4, :]
                    op("dve", [CDB[ci], prepB], [QB[qi][2]],
                       lambda e: e.scalar_tensor_tensor(out=Q[qi][2][:], in0=C, scalar=-1.0, in1=tpi, op0=TT.mult, op1=TT.mult))

                def stD_car(u):
                    chc, c, j0, ai, ci, qi = ctx(u)
                    if c < 15:
                        op("dve", [eB], [carB[j0]], lambda e: e.tensor_tensor(out=car[:, 0, j0:j0 + 4], in0=ea[:, 0, :], in1=eb[:, 1, :], op=TT.add))
                        op("dve", [eB], [carB[j0]], lambda e: e.tensor_tensor(out=car[:, 1, j0:j0 + 4], in0=ea[:, 1, :], in1=eb[:, 0, :], op=TT.subtract))

                def stD_rest(u):
                    chc, c, j0, ai, ci, qi = ctx(u)
                    qt, cq = c // 4, c % 4
                    C, Dd = CD[ci][:, 0, :, :], CD[ci][:, 1, :, :]
                    bo = 6 + (chc % 2)
                    tpr, tpi = TPr[:, j0:j0 + 4, :], TPi[:, j0:j0 + 4, :]
                    q = Q[qi]
                    op("dve", [CDB[ci], prepB], [QB[qi][0]], lambda e: e.tensor_tensor(out=q[0][:], in0=C, in1=tpr, op=TT.mult))
                    op("dve", [CDB[ci], prepB], [QB[qi][1]], lambda e: e.tensor_tensor(out=q[1][:], in0=Dd, in1=tpi, op=TT.mult))
                    op("pool", [CDB[ci], prepB], [QB[qi][3]], lambda e: e.tensor_tensor(out=q[3][:], in0=Dd, in1=tpr, op=TT.mult))

                def stE_mm(u):
                    chc, c, j0, ai, ci, qi = ctx(u)
                    qt, cq = c // 4, c % 4
                    bo = 6 + (chc % 2)
                    q = Q[qi]
                    n = 0
                    for jl in range(4):
                        j = j0 + jl
                        for qq in range(4):
                            wt, wtB = (CrP, CrPB) if qq < 2 else (CiP, CiPB)
                            n += 1
                            op("pe", [QB[qi][qq], wtB], [psB[bo]],
                               lambda e, j=j, jl=jl, qq=qq, wt=wt, n=n: e.matmul(ps[bo][:, cq * P:(cq + 1) * P], lhsT=wt[:, j, :], rhs=q[qq][:, jl, :],
                                                                                  start=(n == 1), stop=(n == 16)), inc=(n == 16))
                    if cq == 3:
                        blk = slice(qt * BW, (qt + 1) * BW)
                        op("act", [psB[bo]], [yB], lambda e: e.activation(out=ytmp[:], in_=ps[bo][:, :], func=AF.Copy))
                        op("dve", [yB, UYB[chc][qt], dTB], [yB],
                           lambda e: e.scalar_tensor_tensor(out=ytmp[:], in0=UY[:, chc, blk], scalar=dT[:, chc:chc + 1], in1=ytmp[:],
                                                            op0=TT.mult, op1=TT.add))
                        op("act", [yB], [UYB[chc][qt]], lambda e: e.activation(out=UY[:, chc, blk], in_=ytmp[:], func=AF.Gelu_apprx_tanh))

                def ok(u):
                    return 0 <= u < NU

                for i in range(NU + 4):
                    if ok(i):
                        stA(i)
                    if ok(i - 2):
                        stT_cs(i - 2)
                    if ok(i - 4):
                        stE_mm(i - 4)
                    if ok(i - 1):
                        stB_mul(i - 1)
                    if ok(i - 3):
                        stD_e(i - 3)
                        stD_q3(i - 3)
                        stD_car(i - 3)
                        stD_rest(i - 3)
                    if ok(i - 1):
                        stB_comb(i - 1)
                k.barrier()
            sP.close()
            s3 = ExitStack()
            with s3:
                wz, wzB = load_w(s3, "a_wzs", dw["a_wz"], D)
                wg, wgB = load_w(s3, "a_wgs", dw["a_wglu"], D)
                w_out, w_outB = load_w(s3, "a_wouts", dw["a_w_out"], D)
                R1 = sb(s3, "R1", [P, 8, BW], F32)
                R1B = bufs(8)
                xnt = sb(s3, "xnt", [P, 8, BW], BF16)
                xnB = bufs(8)
                tmpA = sb(s3, "tmpA", [P, 8, BW], BF16)
                tmpAB = bufs(8)
                sz = sb(s3, "sz", [P, 8, BW], BF16)
                szB = bufs(8)
                sg = sb(s3, "sg", [P, 2, BW], BF16)
                sgB = bufs(2)
                xn_ap = lambda c: xnt[:, c, :]
                for tb in range(NBLK):
                    blk = slice(tb * BW, (tb + 1) * BW)
                    prenorm_ap(li, tb, xn_ap, xnB, tmpA, tmpAB)
                    uy_ap = lambda c, blk=blk: UY[:, c, blk]
                    uyB_t = [UYB[c][tb] for c in range(8)]
                    for m in range(8):
                        b = proj_ap(wz, wzB, m * P, P, xn_ap, xnB, 8)
                        op("act", [psB[b]], [szB[m]], lambda e, b=b, m=m: e.activation(out=sz[:, m, :], in_=ps[b][:, :], func=AF.Silu))
                        b = proj_ap(wg, wgB, m * P, P, uy_ap, uyB_t, 8)
                        op("act", [psB[b], dTB], [sgB[m % 2]],
                           lambda e, b=b, m=m: e.activation(out=sg[:, m % 2, :], in_=ps[b][:, :], func=AF.Sigmoid, bias=dT[:, 8 + m:9 + m], scale=1.0))
                        op("pool", [szB[m], uyB_t[m]], [szB[m]],
                           lambda e, m=m, blk=blk: e.tensor_tensor(out=sz[:, m, :], in0=sz[:, m, :], in1=UY[:, m, blk], op=ALU.mult))
                        op("pool", [szB[m], sgB[m % 2]], [szB[m]],
                           lambda e, m=m: e.tensor_tensor(out=sz[:, m, :], in0=sz[:, m, :], in1=sg[:, m % 2, :], op=ALU.mult))
                    outproj(li, tb, sz, szB, w_out, w_outB, R1, R1B, tmpA, tmpAB)
                k.barrier()

    for li in layers:
        if li == 3:
            layer_sgu(li)
        elif li == 1:
            layer_swa(li)
        elif li == 2:
            layer_mla(li)
        elif li == 0:
            layer_s5(li)

    toks = []
    for c in range(8):
        for tb in range(NBLK):
            toks.append(dma("sp", outT_d[c * P:(c + 1) * P, tb * BW:(tb + 1) * BW], X[:, c, tb * BW:(tb + 1) * BW], [xB[c][tb]], []))
    k._wait("sp", toks)
    k.barrier()


def host_inputs(inp, layers):
    f = lambda a: np.ascontiguousarray(np.asarray(a, dtype=np.float32))
    common = {}
    common["gpre"] = f(np.asarray(inp["pre_norm"]).reshape(4, 8, P).transpose(2, 0, 1).reshape(P, 32))
    common["gpost"] = f(np.asarray(inp["post_norm"]).reshape(4, 8, P).transpose(2, 0, 1).reshape(P, 32))
    common["ident"] = np.eye(P, dtype=np.float32)
    if 3 in layers:
        common["d_w_in"] = f(inp["d_w_in"][0])
        common["d_w_out"] = f(inp["d_w_out"][0])
        common["d_ws"] = f(np.asarray(inp["d_w_s"][0]).transpose(1, 0, 2))
        common["d_tril"] = np.tril(np.ones((P, P), dtype=np.float32))
        common["d_bs"] = f(inp["d_b_s"][0])
        common["d_lng"] = f(inp["d_ln_g"])
        common["d_lnb"] = f(inp["d_ln_b"])
    if 1 in layers:
        w = np.asarray(inp["b_w_in"][0], dtype=np.float32)
        q, kk, v, z = w[:, :1024], w[:, 1024:1152], w[:, 1152:1280], w[:, 1280:]
        common["b_w_in"] = f(np.concatenate([q, kk[:, :64], kk[:, :64], kk[:, 64:], kk[:, 64:], v, z], axis=1))
        common["b_w_out"] = f(inp["b_w_out"][0])
        common["b_sinks"] = f(inp["b_sinks"])
        def bucket(d):
            if d < 16:
                return d
            v_ = 16 + int(math.log(max(d, 1) / 16.0) / math.log(128 / 16.0) * 16)
            return min(v_, 31)
        rb = np.asarray(inp["rel_bias"], dtype=np.float32)
        bt = np.full((P, 16, 256), -1e30, dtype=np.float32)
        for kj in range(P):
            for qi in range(P):
                d_prev = qi + P - kj
                if d_prev < P:
                    bt[kj, :, qi] = rb[bucket(d_prev), :]
                d_cur = qi - kj
                if d_cur >= 0:
                    bt[kj, :, P + qi] = rb[bucket(d_cur), :]
        common["b_biasT"] = bt
    if 2 in layers:
        w = np.asarray(inp["c_w_in"][0], dtype=np.float32)
        kr = w[:, 1024:1056]
        common["c_w1"] = f(np.concatenate([w[:, :1024], kr, kr[:, 16:], kr[:, :16]], axis=1))
        common["c_wz"] = f(w[:, 1056:])
        uq = np.asarray(inp["c_w_uq"][0], dtype=np.float32).reshape(768, 16, 96)
        nope, rp = uq[:, :, :64], uq[:, :, 64:]
        common["c_wuq"] = f(np.concatenate([rp, rp[:, :, 16:], rp[:, :, :16], nope], axis=2).reshape(768, 2048))
        common["c_wukv"] = f(inp["c_w_ukv"][0])
        common["c_w_out"] = f(inp["c_w_out"][0])
        inv = (np.float32(10000.0) ** (-np.arange(0, 32, 2, dtype=np.float32) / np.float32(32))).astype(np.float32)
        ang = (np.arange(L, dtype=np.float32)[:, None] * inv[None, :]).astype(np.float32)
        cos, sin = np.cos(ang).astype(np.float32).T, np.sin(ang).astype(np.float32).T
        common["c_rope"] = f(np.concatenate([cos, cos, -sin, sin], axis=0))
        g = np.concatenate([np.asarray(inp["c_q_norm"][0]), np.asarray(inp["c_kv_norm"][0])]).astype(np.float32)
        common["c_gqkv"] = f(g.reshape(8, P).T)
        common["c_maskT"] = np.triu(np.ones((P, P), dtype=np.float32))
    if 0 in layers:
        w = np.asarray(inp["a_w_in"][0], dtype=np.float32)
        common["a_wu"] = f(w[:, :1024])
        common["a_wz"] = f(w[:, 1024:])
        common["a_wglu"] = f(inp["a_w_glu"][0])
        common["a_w_out"] = f(inp["a_w_out"][0])
        dg = np.concatenate([np.asarray(inp["a_d"][0]).reshape(8, P).T, np.asarray(inp["a_b_glu"][0]).reshape(8, P).T], axis=1)
        common["a_dg"] = f(dg)
        lam = np.stack([np.asarray(inp["a_lam_re"][0]).reshape(32, P).T, np.asarray(inp["a_lam_im"][0]).reshape(32, P).T,
                        np.repeat(np.asarray(inp["a_log_dt"][0]), 64).reshape(32, P).T], axis=1)
        common["a_lam"] = f(lam)
        brl = np.zeros((P, 32, P), np.float32); bil = np.zeros((P, 32, P), np.float32)
        crp = np.zeros((P, 32, P), np.float32); cip = np.zeros((P, 32, P), np.float32)
        b_re, b_im = np.asarray(inp["a_b_re"][0]), np.asarray(inp["a_b_im"][0])
        c_re, c_im = np.asarray(inp["a_c_re"][0]), np.asarray(inp["a_c_im"][0])
        for j in range(32):
            for gl in range(2):
                g = 2 * j + gl
                r0 = 32 * (j % 4) + gl * 16
                brl[r0:r0 + 16, j, gl * 64:(gl + 1) * 64] = b_re[g].T
                bil[r0:r0 + 16, j, gl * 64:(gl + 1) * 64] = b_im[g].T
                crp[gl * 64:(gl + 1) * 64, j, r0:r0 + 16] = c_re[g].T
                cip[gl * 64:(gl + 1) * 64, j, r0:r0 + 16] = c_im[g].T
        common["a_brl"], common["a_bil"], common["a_crp"], common["a_cip"] = brl, bil, crp, cip
        common["a_triu"] = np.triu(np.ones((P, P), dtype=np.float32))
    return common


def run(inp, layers=(0, 1, 2, 3), cores=8, trace=False):
    nc = bass.Bass("TRN2", target_bir_lowering=False)
    build(nc, list(layers))
    common = host_inputs(inp, list(layers))
    x = np.asarray(inp["x"], dtype=np.float32)
    in_maps = []
    for b in range(cores):
        m = dict(common)
        m["xT"] = np.ascontiguousarray(x[b].T)
        in_maps.append(m)
    res = run_bass_kernel_spmd(nc, in_maps, core_ids=list(range(cores)), trace=trace)
    out = np.stack([np.ascontiguousarray(r["outT"].T) for r in res.results], axis=0)
    return out.astype(np.float32), res


def kernel(**inputs):
    out, _ = run(inputs)
    return out
```

```python
import math
import numpy as np
from contextlib import ExitStack
import concourse.bass as bass
import concourse.mybir as mybir
from concourse.bass_utils import run_bass_kernel_spmd

F32 = mybir.dt.float32
BF16 = mybir.dt.bfloat16
ALU = mybir.AluOpType
AF = mybir.ActivationFunctionType

P = 128
L = 2048
D = 1024
NBLK = 4
BW = 512
EPS = 1e-6
SELF_SYNC = True
NDS = 12


class Buf:
    __slots__ = ("w", "r")

    def __init__(self):
        self.w = None
        self.r = {}


class WB:
    def __init__(self):
        self.blocks = []

    def cols(self, c0, c1):
        return [b for (a0, a1, bl) in self.blocks if a0 < c1 and c0 < a1 for b in bl]

    def all(self):
        return [b for (_, _, bl) in self.blocks for b in bl]


def _flat(lst):
    out = []
    for b in lst:
        if isinstance(b, WB):
            out.extend(b.all())
        elif isinstance(b, list):
            out.extend(_flat(b))
        else:
            out.append(b)
    return out


def bufs(*shape):
    if len(shape) == 1:
        return [Buf() for _ in range(shape[0])]
    return [bufs(*shape[1:]) for _ in range(shape[0])]


class KB:
    def __init__(self, nc, es):
        self.nc = nc
        self.E = dict(pe=nc.tensor, act=nc.scalar, dve=nc.vector, pool=nc.gpsimd, sp=nc.sync)
        self.sem = {e: es.enter_context(nc.semaphore("s_" + e)) for e in ("pe", "act", "dve", "pool")}
        self.cnt = {e: 0 for e in self.sem}
        self.pend = {e: False for e in self.sem}
        self.dsem = {q: [[es.enter_context(nc.semaphore("d_%s%d" % (q, i))), 0] for i in range(NDS)]
                     for q in ("sp", "pool")}
        self.dcnt = {"sp": 0, "pool": 0}
        self.seen = {e: {} for e in self.E}
        self.ps = []
        self.psB = []
        self.psrot = 0
        self.nrot = 6

    def _semh(self, key):
        if isinstance(key, str):
            return self.sem[key]
        return self.dsem[key[0]][key[1]][0]

    def _wait(self, e, toks):
        need = {}
        for key, v in toks:
            if need.get(key, 0) < v:
                need[key] = v
        for key, v in need.items():
            if key == e and (e == "pe" or not SELF_SYNC):
                continue
            if self.seen[e].get(key, 0) >= v:
                continue
            self.E[e].wait_ge(self._semh(key), v)
            self.seen[e][key] = v

    def _deps(self, reads, writes):
        toks = []
        for b in reads:
            if b.w is not None:
                toks.append(b.w)
        for b in writes:
            if b.w is not None:
                toks.append(b.w)
            toks.extend(b.r.items())
        return toks

    def _mark(self, tok, reads, writes):
        key, v = tok
        for b in reads:
            if b.r.get(key, 0) < v:
                b.r[key] = v
        for b in writes:
            b.w = tok
            b.r = {}

    def op(self, e, reads, writes, fn, inc=True):
        reads, writes = _flat(reads), _flat(writes)
        self._wait(e, self._deps(reads, writes))
        ins = fn(self.E[e])
        if inc:
            self.cnt[e] += 1
            ins.then_inc(self.sem[e], 1)
            self.pend[e] = False
            tok = (e, self.cnt[e])
        else:
            self.pend[e] = True
            tok = (e, self.cnt[e] + 1)
        self._mark(tok, reads, writes)
        return ins

    def dma(self, q, out, in_, reads, writes):
        reads, writes = _flat(reads), _flat(writes)
        self._wait(q, self._deps(reads, writes))
        i = self.dcnt[q] % NDS
        self.dcnt[q] += 1
        ent = self.dsem[q][i]
        key = (q, i)
        if ent[1] > 0:
            self._wait(q, [(key, 16 * ent[1])])
        ins = self.E[q].dma_start(out=out, in_=in_)
        ins.then_inc(ent[0], 16)
        ent[1] += 1
        tok = (key, 16 * ent[1])
        self._mark(tok, reads, writes)
        return tok

    def all_tokens(self):
        toks = [(e, c) for e, c in self.cnt.items() if c > 0]
        for q in self.dsem:
            for i, ent in enumerate(self.dsem[q]):
                if ent[1] > 0:
                    toks.append(((q, i), 16 * ent[1]))
        return toks

    def barrier(self):
        for e in self.sem:
            assert not self.pend[e], e
        toks = self.all_tokens()
        for e in self.E:
            self._wait(e, toks)

    def nextps(self):
        b = self.psrot
        self.psrot = (self.psrot + 1) % self.nrot
        return b


def build(nc, layers):
    es = ExitStack()
    with es:
        _build(nc, es, layers)
    return nc


def _build(nc, es, layers):
    k = KB(nc, es)
    op, dma = k.op, k.dma

    def dram_in(name, shape, dt=F32):
        return nc.dram_tensor(name, list(shape), dt, kind="ExternalInput").ap()

    uid = [0]

    def sb(st, name, shape, dt):
        uid[0] += 1
        return st.enter_context(nc.sbuf_tensor("%s_%d" % (name, uid[0]), list(shape), dt))

    xT_d = dram_in("xT", [D, L])
    outT_d = nc.dram_tensor("outT", [D, L], F32, kind="ExternalOutput").ap()
    gpre_d = dram_in("gpre", [P, 32])
    gpost_d = dram_in("gpost", [P, 32])
    ident_d = dram_in("ident", [P, P])
    dw = {}
    if 3 in layers:
        dw["d_w_in"] = dram_in("d_w_in", [D, 3072])
        dw["d_w_out"] = dram_in("d_w_out", [D, D])
        dw["d_ws"] = dram_in("d_ws", [P, 16, P])
        dw["d_tril"] = dram_in("d_tril", [P, P])
        dw["d_bs"] = dram_in("d_bs", [16, P])
        dw["d_lng"] = dram_in("d_lng", [1, D])
        dw["d_lnb"] = dram_in("d_lnb", [1, D])

    if 1 in layers:
        dw["b_w_in"] = dram_in("b_w_in", [D, 2432])
        dw["b_w_out"] = dram_in("b_w_out", [D, D])
        dw["b_biasT"] = dram_in("b_biasT", [P, 16, 256])
        dw["b_sinks"] = dram_in("b_sinks", [1, 16])

    if 2 in layers:
        dw["c_w1"] = dram_in("c_w1", [D, 1088])
        dw["c_wz"] = dram_in("c_wz", [D, D])
        dw["c_wuq"] = dram_in("c_wuq", [768, 2048])
        dw["c_wukv"] = dram_in("c_wukv", [256, 2048])
        dw["c_w_out"] = dram_in("c_w_out", [D, D])
        dw["c_rope"] = dram_in("c_rope", [64, L])
        dw["c_gqkv"] = dram_in("c_gqkv", [P, 8])
        dw["c_maskT"] = dram_in("c_maskT", [P, P])

    if 0 in layers:
        dw["a_wu"] = dram_in("a_wu", [D, D])
        dw["a_wz"] = dram_in("a_wz", [D, D])
        dw["a_wglu"] = dram_in("a_wglu", [D, D])
        dw["a_w_out"] = dram_in("a_w_out", [D, D])
        dw["a_dg"] = dram_in("a_dg", [P, 16])
        dw["a_lam"] = dram_in("a_lam", [P, 3, 32])
        for nm in ("a_brl", "a_bil", "a_crp", "a_cip"):
            dw[nm] = dram_in(nm, [P, 32, P])
        dw["a_triu"] = dram_in("a_triu", [P, P])
        dw["a_ntriu"] = dram_in("a_ntriu", [P, P])

    X = sb(es, "X", [P, 8, L], F32)
    xB = bufs(8, NBLK)
    ones = sb(es, "ones", [P, P], BF16)
    onesB = Buf()
    ident = sb(es, "identb", [P, P], BF16)
    identB = Buf()
    gpre = sb(es, "gpre_s", [P, 32], F32)
    gpost = sb(es, "gpost_s", [P, 32], F32)
    gB = Buf()
    rstd = sb(es, "rstd", [P, BW], F32)
    rstdB = Buf()
    for i in range(8):
        k.ps.append(es.enter_context(nc.psum_tensor("ps%d" % i, [P, BW], F32)))
        k.psB.append(Buf())
    ps, psB = k.ps, k.psB

    for c in range(8):
        for tb in range(NBLK):
            dma("sp", X[:, c, tb * BW:(tb + 1) * BW], xT_d[c * P:(c + 1) * P, tb * BW:(tb + 1) * BW], [], [xB[c][tb]])
    dma("sp", gpre[:], gpre_d, [], [gB])
    dma("sp", gpost[:], gpost_d, [], [gB])
    dma("pool", ident[:], ident_d, [], [identB])
    op("dve", [], [onesB], lambda e: e.memset(ones[:], 1.0))
    epsc = sb(es, "epsc", [P, 1], F32)
    op("dve", [], [onesB], lambda e: e.memset(epsc[:], EPS))

    def load_w(st, name, d_ap, ncols, q="pool"):
        K = d_ap.shape[0]
        kc_n = K // P
        t = sb(st, name, [P, kc_n, ncols], BF16)
        B = WB()
        for c0 in range(0, ncols, 512):
            c1 = min(ncols, c0 + 512)
            bl = []
            for kc in range(kc_n):
                b1 = Buf()
                dma(q, t[:, kc, c0:c1], d_ap[kc * P:(kc + 1) * P, c0:c1], [], [b1])
                bl.append(b1)
            B.blocks.append((c0, c1, bl))
        return t, B

    def rstd_from_sq(sq, sqB, n, scale, c0=0):
        b = k.nextps()
        for c in range(n):
            op("pe", [sqB[c0 + c], onesB], [psB[b]],
               lambda e, c=c: e.matmul(ps[b][:, :], lhsT=ones[:, :], rhs=sq[:, c0 + c, :], start=(c == 0), stop=(c == n - 1)),
               inc=(c == n - 1))
        op("act", [psB[b]], [rstdB],
           lambda e: e.activation(out=rstd[:], in_=ps[b][:, :], func=AF.Ln, scale=scale, bias=epsc[:, 0:1]))
        op("act", [rstdB], [rstdB], lambda e: e.activation(out=rstd[:], in_=rstd[:], func=AF.Exp, scale=-0.5))

    def prenorm(li, tb, xn, xnB, tmpA, tmpAB):
        blk = slice(tb * BW, (tb + 1) * BW)
        for c in range(8):
            op("act", [xB[c][tb]], [tmpAB[c]],
               lambda e, c=c: e.activation(out=tmpA[:, c, :], in_=X[:, c, blk], func=AF.Square))
        rstd_from_sq(tmpA, tmpAB, 8, 1.0 / D)
        for c in range(8):
            op("dve", [xB[c][tb], rstdB, gB], [xnB[c]],
               lambda e, c=c: e.scalar_tensor_tensor(out=xn[:, c, :], in0=X[:, c, blk],
                                                     scalar=gpre[:, li * 8 + c:li * 8 + c + 1], in1=rstd[:],
                                                     op0=ALU.mult, op1=ALU.mult))

    def prenorm_ap(li, tb, xn_ap, xnB, tmpA, tmpAB):
        blk = slice(tb * BW, (tb + 1) * BW)
        for c in range(8):
            op("act", [xB[c][tb]], [tmpAB[c]],
               lambda e, c=c: e.activation(out=tmpA[:, c, :], in_=X[:, c, blk], func=AF.Square))
        rstd_from_sq(tmpA, tmpAB, 8, 1.0 / D)
        for c in range(8):
            op("dve", [xB[c][tb], rstdB, gB], [xnB[c]],
               lambda e, c=c: e.scalar_tensor_tensor(out=xn_ap(c), in0=X[:, c, blk],
                                                     scalar=gpre[:, li * 8 + c:li * 8 + c + 1], in1=rstd[:],
                                                     op0=ALU.mult, op1=ALU.mult))

    def proj_ap(w, wB, col0, M, rhs_ap, rhsB, nk):
        b = k.nextps()
        wBc = wB.cols(col0, col0 + M)
        for kc in range(nk):
            op("pe", [wBc, rhsB[kc]], [psB[b]],
               lambda e, kc=kc: e.matmul(ps[b][0:M, :], lhsT=w[:, kc, col0:col0 + M], rhs=rhs_ap(kc),
                                         start=(kc == 0), stop=(kc == nk - 1)),
               inc=(kc == nk - 1))
        return b

    def proj_fm(w, wB, col0, rhs, rhsB, nk, M=P):
        b = k.nextps()
        wBc = wB.cols(col0, col0 + M)
        for kc in range(nk):
            op("pe", [wBc, rhsB[kc]], [psB[b]],
               lambda e, kc=kc: e.matmul(ps[b][0:M, :], lhsT=w[:, kc, col0:col0 + M], rhs=rhs[:, kc, :],
                                         start=(kc == 0), stop=(kc == nk - 1)),
               inc=(kc == nk - 1))
        return b

    def outproj(li, tb, G, GB, wout, woutB, ybuf, ybufB, tmpA, tmpAB):
        blk = slice(tb * BW, (tb + 1) * BW)
        for m in range(8):
            b = proj_fm(wout, woutB, m * P, G, GB, 8)
            op("act", [psB[b]], [ybufB[m]], lambda e, m=m, b=b: e.activation(out=ybuf[:, m, :], in_=ps[b][:, :], func=AF.Copy))
            op("act", [psB[b]], [tmpAB[m]], lambda e, m=m, b=b: e.activation(out=tmpA[:, m, :], in_=ps[b][:, :], func=AF.Square))
        rstd_from_sq(tmpA, tmpAB, 8, 1.0 / D)
        for m in range(8):
            op("dve", [ybufB[m], rstdB, gB], [ybufB[m]],
               lambda e, m=m: e.scalar_tensor_tensor(out=ybuf[:, m, :], in0=ybuf[:, m, :],
                                                     scalar=gpost[:, li * 8 + m:li * 8 + m + 1], in1=rstd[:],
                                                     op0=ALU.mult, op1=ALU.mult))
            op("pool", [ybufB[m], xB[m][tb]], [xB[m][tb]],
               lambda e, m=m: e.tensor_tensor(out=X[:, m, blk], in0=X[:, m, blk], in1=ybuf[:, m, :], op=ALU.add))

    def layer_sgu(li):
        st = ExitStack()
        with st:
            w_in, w_inB = load_w(st, "d_win", dw["d_w_in"], 3072)
            w_out, w_outB = load_w(st, "d_wout", dw["d_w_out"], D)
            wsf = sb(st, "wsf", [P, 16, P], F32)
            wsfB = Buf()
            tril = sb(st, "tril", [P, P], F32)
            trilB = Buf()
            wsm = sb(st, "wsm", [P, 16, P], BF16)
            wsmB = Buf()
            wsT = sb(st, "wsT", [P, 16, P], BF16)
            wsTB = bufs(16)
            bsT = sb(st, "bsT", [P, 8, P], F32)
            bsTB = Buf()
            lng = sb(st, "lng", [P, D], F32)
            lnb = sb(st, "lnb", [P, D], F32)
            lnB = Buf()
            R1 = sb(st, "R1", [P, 8, BW], F32)
            R1B = bufs(8)
            R1bf = R1[:].bitcast(BF16)
            tmpA = sb(st, "tmpA", [P, 8, BW], BF16)
            tmpAB = bufs(8)
            gu = sb(st, "gu", [P, 8, BW], BF16)
            guB = bufs(8)
            sz = sb(st, "sz", [P, 8, BW], BF16)
            szB = bufs(8)
            vtmp = sb(st, "vtmp", [P, D], F32)
            vtmpB = Buf()
            stt = sb(st, "stt", [P, 2, 6], F32)
            mv = sb(st, "mv", [P, 2], F32)
            rs1 = sb(st, "rs1", [P, 1], F32)
            sttB = Buf()
            tmpS = sb(st, "tmpS", [P, BW], F32)
            tmpSB = Buf()

            def xn_ap(c):
                return R1bf[:, c // 2, (c % 2) * BW:(c % 2) * BW + BW]

            def vln_ap(ch, c0, c1):
                return R1bf[:, 4 + ch, c0:c1]

            xnB = [R1B[c // 2] for c in range(8)]

            dma("sp", wsf[:], dw["d_ws"], [], [wsfB])
            dma("sp", tril[:], dw["d_tril"], [], [trilB])
            for h in range(2):
                src = dw["d_bs"].rearrange("(gp h) t -> h gp t", h=2)[h]
                dma("sp", bsT[h * 64:(h + 1) * 64, :, :], src.unsqueeze(0).to_broadcast([64, 8, P]), [], [bsTB])
            dma("sp", lng[:], dw["d_lng"].to_broadcast([P, D]), [], [lnB])
            dma("sp", lnb[:], dw["d_lnb"].to_broadcast([P, D]), [], [lnB])
            op("dve", [wsfB, trilB], [wsmB],
               lambda e: e.tensor_tensor(out=wsm[:], in0=wsf[:], in1=tril[:].unsqueeze(1).to_broadcast([P, 16, P]), op=ALU.mult))
            for g in range(16):
                b = k.nextps()
                op("pe", [wsmB, identB], [psB[b]],
                   lambda e, g=g, b=b: e.matmul(ps[b][:, 0:P], lhsT=wsm[:, g, :], rhs=ident[:, :], start=True, stop=True))
                op("act", [psB[b]], [wsTB[g]], lambda e, g=g, b=b: e.activation(out=wsT[:, g, :], in_=ps[b][:, 0:P], func=AF.Copy))

            for tb in range(NBLK):
                blk = slice(tb * BW, (tb + 1) * BW)
                for c in range(8):
                    op("act", [xB[c][tb]], [tmpAB[c]],
                       lambda e, c=c: e.activation(out=tmpA[:, c, :], in_=X[:, c, blk], func=AF.Square))
                rstd_from_sq(tmpA, tmpAB, 8, 1.0 / D)
                for c in range(8):
                    op("dve", [xB[c][tb], rstdB, gB], [xnB[c]],
                       lambda e, c=c: e.scalar_tensor_tensor(out=xn_ap(c), in0=X[:, c, blk],
                                                             scalar=gpre[:, li * 8 + c:li * 8 + c + 1], in1=rstd[:],
                                                             op0=ALU.mult, op1=ALU.mult))
                for ch in range(4):
                    for half in range(2):
                        b = k.nextps()
                        for kc in range(8):
                            op("pe", [w_inB, xnB[kc]], [psB[b]],
                               lambda e, kc=kc, b=b: e.matmul(ps[b][:, :], lhsT=xn_ap(kc)[:, ch * P:(ch + 1) * P],
                                                              rhs=w_in[:, kc, D + half * BW:D + (half + 1) * BW],
                                                              start=(kc == 0), stop=(kc == 7)),
                               inc=(kc == 7))
                        op("act", [psB[b]], [vtmpB],
                           lambda e, b=b, half=half: e.activation(out=vtmp[:, half * BW:(half + 1) * BW], in_=ps[b][:, :],
                                                                  func=AF.Gelu_apprx_tanh))
                    for half in range(2):
                        op("dve", [vtmpB], [sttB], lambda e, half=half: e.bn_stats(out=stt[:, half, :], in_=vtmp[:, half * BW:(half + 1) * BW]))
                    op("dve", [sttB], [sttB], lambda e: e.bn_aggr(out=mv[:], in_=stt[:].rearrange("p a b -> p (a b)")))
                    op("act", [sttB], [sttB],
                       lambda e: e.activation(out=rs1[:], in_=mv[:, 1:2], func=AF.Sqrt, scale=1.0, bias=epsc[:, 0:1]))
                    op("dve", [sttB], [sttB], lambda e: e.reciprocal(out=rs1[:], in_=rs1[:]))
                    op("dve", [vtmpB, sttB], [vtmpB],
                       lambda e: e.tensor_scalar(out=vtmp[:], in0=vtmp[:], scalar1=mv[:, 0:1], scalar2=rs1[:, 0:1],
                                                 op0=ALU.subtract, op1=ALU.mult))
                    op("pool", [vtmpB, lnB], [vtmpB], lambda e: e.tensor_tensor(out=vtmp[:], in0=vtmp[:], in1=lng[:], op=ALU.mult))
                    op("pool", [vtmpB, lnB], [R1B[4 + ch]],
                       lambda e, ch=ch: e.tensor_tensor(out=vln_ap(ch, 0, D), in0=vtmp[:], in1=lnb[:], op=ALU.add))
                for m in range(8):
                    b = k.nextps()
                    for kc in range(8):
                        op("pe", [w_inB, xnB[kc]], [psB[b]],
                           lambda e, kc=kc, b=b, m=m: e.matmul(ps[b][:, :], lhsT=w_in[:, kc, m * P:(m + 1) * P], rhs=xn_ap(kc),
                                                               start=(kc == 0), stop=(kc == 7)), inc=(kc == 7))
                    op("act", [psB[b]], [guB[m]], lambda e, b=b, m=m: e.activation(out=gu[:, m, :], in_=ps[b][:, :], func=AF.Gelu_apprx_tanh))
                for m in range(8):
                    b = k.nextps()
                    for kc in range(8):
                        op("pe", [w_inB, xnB[kc]], [psB[b]],
                           lambda e, kc=kc, b=b, m=m: e.matmul(ps[b][:, :], lhsT=w_in[:, kc, 2 * D + m * P:2 * D + (m + 1) * P], rhs=xn_ap(kc),
                                                               start=(kc == 0), stop=(kc == 7)), inc=(kc == 7))
                    op("act", [psB[b]], [szB[m]], lambda e, b=b, m=m: e.activation(out=sz[:, m, :], in_=ps[b][:, :], func=AF.Silu))
                for m in range(8):
                    op("pool", [guB[m], szB[m]], [guB[m]], lambda e, m=m: e.tensor_tensor(out=gu[:, m, :], in0=gu[:, m, :], in1=sz[:, m, :], op=ALU.mult))
                for gp in range(8):
                    b = k.nextps()
                    n = 0
                    for ch in range(4):
                        for h in range(2):
                            g = 2 * gp + h
                            n += 1
                            op("pe", [R1B[4 + ch], wsTB[g]], [psB[b]],
                               lambda e, ch=ch, h=h, g=g, b=b: e.matmul(ps[b][h * 64:(h + 1) * 64, ch * P:(ch + 1) * P],
                                                                        lhsT=vln_ap(ch, g * 64, (g + 1) * 64), rhs=wsT[:, g, :],
                                                                        start=True, stop=True),
                               inc=(n == 8))
                    op("dve", [psB[b], bsTB], [tmpSB],
                       lambda e, b=b, gp=gp: e.tensor_tensor(out=tmpS[:].rearrange("p (a t) -> p a t", a=4),
                                                             in0=ps[b][:, :].rearrange("p (a t) -> p a t", a=4),
                                                             in1=bsT[:, gp, :].unsqueeze(1).to_broadcast([P, 4, P]), op=ALU.add))
                    op("dve", [tmpSB, guB[gp]], [guB[gp]],
                       lambda e, gp=gp: e.tensor_tensor(out=gu[:, gp, :], in0=tmpS[:], in1=gu[:, gp, :], op=ALU.mult))
                outproj(li, tb, gu, guB, w_out, w_outB, R1, R1B, tmpA, tmpAB)
            k.barrier()

    def layer_swa(li):
        st = ExitStack()
        with st:
            NC_IN = 2432
            w_in, w_inB = load_w(st, "b_win", dw["b_w_in"], NC_IN)
            w_out, w_outB = load_w(st, "b_wout", dw["b_w_out"], D)
            KT2 = sb(st, "KT2", [P, 2, L], BF16)
            KTB = bufs(2, NBLK)
            VA = [[sb(st, "VA%d%d" % (kv, par), [P, 16, P], BF16) for par in range(2)] for kv in range(2)]
            VAB = bufs(2, 2, 16)
            biasT = sb(st, "biasT", [P, 16, 256], F32)
            biasB = Buf()
            esink = sb(st, "esink", [P, 16], F32)
            esinkB = Buf()
            R1 = sb(st, "R1", [P, 8, BW], F32)
            R1B = bufs(8)
            R1bf = R1[:].bitcast(BF16)
            tmpA = sb(st, "tmpA", [P, 8, BW], BF16)
            tmpAB = bufs(8)
            sz = sb(st, "sz", [P, 8, BW], BF16)
            szB = bufs(8)
            NPB = 5
            apc = [0]
            tmpP = sb(st, "tmpP", [P, NPB, 256], F32)
            tmpPB = bufs(NPB)
            PT = sb(st, "PT", [P, NPB, 256], BF16)
            PTB = bufs(NPB)
            rsb = sb(st, "rsb", [P, BW], F32)
            rsbB = bufs(2)
            tG = sb(st, "tG", [P, BW], F32)
            tGB = bufs(2)

            def xn_ap(c):
                return R1bf[:, c // 2, (c % 2) * BW:(c % 2) * BW + BW]
            xnB = [R1B[c // 2] for c in range(8)]

            def qt_ap(j, p0, p1, c0, c1):
                return R1bf[p0:p1, 4 + j // 2, (j % 2) * BW + c0:(j % 2) * BW + c1]
            qtB = [R1B[4 + j // 2] for j in range(8)]

            import os
            SK = os.environ.get("SKIP", "")
            if "a" not in SK:
                dma("sp", biasT[:], dw["b_biasT"], [], [biasB])
            if "b" not in SK:
                dma("sp", esink[:], dw["b_sinks"].to_broadcast([P, 16]), [], [esinkB])
                op("act", [esinkB], [esinkB], lambda e: e.activation(out=esink[:], in_=esink[:], func=AF.Exp))
            for kv in range(2):
                for par in range(2):
                    c0 = 64 if par == 0 else 0
                    if "c" not in SK:
                        op("pool", [], sum([[VAB[kv][par][t]] for t in range(16)], []),
                           lambda e, kv=kv, par=par, c0=c0: e.memset(VA[kv][par][:, :, c0:c0 + 64], 1.0))

            for tb in range(NBLK):
                blk = slice(tb * BW, (tb + 1) * BW)
                prenorm_ap(li, tb, xn_ap, xnB, tmpA, tmpAB)
                STG = int(os.environ.get("STG", "9"))
                for j in range(8 if STG >= 1 else 0):
                    b = proj_ap(w_in, w_inB, j * P, P, xn_ap, xnB, 8)
                    op("act", [psB[b]], [qtB[j]], lambda e, b=b, j=j: e.activation(out=qt_ap(j, 0, P, 0, BW), in_=ps[b][:, :], func=AF.Copy))
                for kv in range(2 if STG >= 2 else 0):
                    b = proj_ap(w_in, w_inB, D + kv * P, P, xn_ap, xnB, 8)
                    op("act", [psB[b]], [KTB[kv][tb]], lambda e, b=b, kv=kv: e.activation(out=KT2[:, kv, blk], in_=ps[b][:, :], func=AF.Copy))
                for ch in range(4 if STG >= 3 else 0):
                    tt = tb * 4 + ch
                    b = k.nextps()
                    for kc in range(8):
                        op("pe", [w_inB, xnB[kc]], [psB[b]],
                           lambda e, kc=kc, b=b, ch=ch: e.matmul(ps[b][:, 0:P], lhsT=xn_ap(kc)[:, ch * P:(ch + 1) * P],
                                                                 rhs=w_in[:, kc, D + 256:D + 384], start=(kc == 0), stop=(kc == 7)),
                           inc=(kc == 7))
                    for kv in range(2):
                        op("act", [psB[b]], [VAB[kv][0][tt]],
                           lambda e, b=b, kv=kv, tt=tt: e.activation(out=VA[kv][0][:, tt, 0:64], in_=ps[b][:, kv * 64:(kv + 1) * 64], func=AF.Copy))
                        op("act", [psB[b]], [VAB[kv][1][tt]],
                           lambda e, b=b, kv=kv, tt=tt: e.activation(out=VA[kv][1][:, tt, 64:128], in_=ps[b][:, kv * 64:(kv + 1) * 64], func=AF.Copy))
                for m in range(8):
                    b = proj_ap(w_in, w_inB, D + 384 + m * P, P, xn_ap, xnB, 8)
                    op("act", [psB[b]], [szB[m]], lambda e, b=b, m=m: e.activation(out=sz[:, m, :], in_=ps[b][:, :], func=AF.Silu))
                def scores(h, nbl, pi):
                    kv, par, j = h // 8, h % 2, h // 2
                    base = par * 64
                    nb = tb * 4 + nbl
                    b = k.nextps()
                    c_lo = 0 if nb > 0 else P
                    rhs_q = qt_ap(j, base, base + 64, nbl * P, (nbl + 1) * P)
                    if nb > 0:
                        tbp = (nb - 1) // 4
                        op("pe", [KTB[kv][tbp], qtB[j]], [psB[b]],
                           lambda e: e.matmul(ps[b][:, 0:P], lhsT=KT2[base:base + 64, kv, (nb - 1) * P:nb * P], rhs=rhs_q, start=True, stop=True),
                           inc=False)
                    op("pe", [KTB[kv][tb], qtB[j]], [psB[b]],
                       lambda e: e.matmul(ps[b][:, P:2 * P], lhsT=KT2[base:base + 64, kv, nb * P:(nb + 1) * P], rhs=rhs_q, start=True, stop=True))
                    op("dve", [psB[b], biasB], [tmpPB[pi]],
                       lambda e: e.scalar_tensor_tensor(out=tmpP[:, pi, c_lo:256], in0=ps[b][:, c_lo:256], scalar=0.125, in1=biasT[:, h, c_lo:256],
                                                        op0=ALU.mult, op1=ALU.add))
                    op("act", [tmpPB[pi]], [PTB[pi]],
                       lambda e: e.activation(out=PT[:, pi, c_lo:256], in_=tmpP[:, pi, c_lo:256], func=AF.Exp))

                def pv_epi(h, nbl, pi):
                    kv, par, j = h // 8, h % 2, h // 2
                    base = par * 64
                    oth = 64 - base
                    bo = 6 + (h % 2)
                    nb = tb * 4 + nbl
                    if nb > 0:
                        op("pe", [PTB[pi], VAB[kv][par][nb - 1]], [psB[bo]],
                           lambda e: e.matmul(ps[bo][:, nbl * P:(nbl + 1) * P], lhsT=VA[kv][par][:, nb - 1, :], rhs=PT[:, pi, 0:P], start=True, stop=False),
                           inc=False)
                    op("pe", [PTB[pi], VAB[kv][par][nb]], [psB[bo]],
                       lambda e: e.matmul(ps[bo][:, nbl * P:(nbl + 1) * P], lhsT=VA[kv][par][:, nb, :], rhs=PT[:, pi, P:2 * P], start=(nb == 0), stop=True))
                    if nbl == 3:
                        op("dve", [psB[bo], esinkB], [rsbB[par]],
                           lambda e: e.tensor_scalar(out=rsb[base:base + 64, :], in0=ps[bo][oth:oth + 64, :], scalar1=esink[oth:oth + 64, h:h + 1],
                                                     scalar2=None, op0=ALU.add))
                        op("act", [rsbB[par]], [rsbB[par]], lambda e: e.activation(out=rsb[base:base + 64, :], in_=rsb[base:base + 64, :], func=AF.Ln))
                        op("act", [rsbB[par]], [rsbB[par]], lambda e: e.activation(out=rsb[base:base + 64, :], in_=rsb[base:base + 64, :], func=AF.Exp, scale=-1.0))
                        op("dve", [psB[bo], rsbB[par]], [tGB[par]],
                           lambda e: e.tensor_tensor(out=tG[base:base + 64, :], in0=ps[bo][base:base + 64, :], in1=rsb[base:base + 64, :], op=ALU.mult))
                        op("pool", [tGB[par], szB[j]], [szB[j]],
                           lambda e: e.tensor_tensor(out=sz[base:base + 64, j, :], in0=tG[base:base + 64, :], in1=sz[base:base + 64, j, :], op=ALU.mult))

                import os
                tasks = [(h, nbl) for h in range(int(os.environ.get('SWA_NH', '16'))) for nbl in range(4)]
                LAS = 3
                for i in range(min(LAS, len(tasks))):
                    scores(tasks[i][0], tasks[i][1], (apc[0] + i) % NPB)
                for i, (h, nbl) in enumerate(tasks):
                    pi = apc[0] % NPB
                    apc[0] += 1
                    if i + LAS < len(tasks):
                        scores(tasks[i + LAS][0], tasks[i + LAS][1], (apc[0] + LAS - 1) % NPB)
                    pv_epi(h, nbl, pi)
                outproj(li, tb, sz, szB, w_out, w_outB, R1, R1B, tmpA, tmpAB)
            k.barrier()

    def layer_mla(li):
        SC = 96.0 ** -0.5
        st = ExitStack()
        with st:
            GT = sb(st, "GT", [P, 8, L], BF16)
            GTB = bufs(8, NBLK)
            sA = ExitStack()
            sA.__enter__()
            CQN = sb(sA, "CQN", [P, 6, L], BF16)
            CQNB = bufs(6, NBLK)
            CKVN = sb(sA, "CKVN", [P, 2, L], BF16)
            CKVNB = bufs(2, NBLK)
            KR = sb(sA, "KR", [32, L], BF16)
            KRB = bufs(NBLK)
            ROPE = sb(sA, "ROPE", [64, L], F32)
            ropeB = Buf()
            gq = sb(sA, "gq", [P, 8], F32)
            gqB = Buf()
            dma("sp", ROPE[:], dw["c_rope"], [], [ropeB])
            dma("sp", gq[:], dw["c_gqkv"], [], [gqB])
            tR = sb(sA, "tR", [32, 2, BW], F32)
            tRB = bufs(2)
            s1 = ExitStack()
            with s1:
                w1, w1B = load_w(s1, "s_c_w1", dw["c_w1"], 1088)
                R1 = sb(s1, "R1", [P, 8, BW], F32)
                R1B = bufs(8)
                xnt = sb(s1, "xnt", [P, 8, BW], BF16)
                xnB = bufs(8)
                tmpA = sb(s1, "tmpA", [P, 8, BW], BF16)
                tmpAB = bufs(8)
                xn_ap = lambda c: xnt[:, c, :]
                for tb in range(NBLK):
                    blk = slice(tb * BW, (tb + 1) * BW)
                    prenorm_ap(li, tb, xn_ap, xnB, tmpA, tmpAB)
                    for m in range(8):
                        b = proj_ap(w1, w1B, m * P, P, xn_ap, xnB, 8)
                        op("act", [psB[b]], [R1B[m]], lambda e, b=b, m=m: e.activation(out=R1[:, m, :], in_=ps[b][:, :], func=AF.Copy))
                        op("act", [psB[b]], [tmpAB[m]], lambda e, b=b, m=m: e.activation(out=tmpA[:, m, :], in_=ps[b][:, :], func=AF.Square))
                    rstd_from_sq(tmpA, tmpAB, 6, 1.0 / 768, 0)
                    for m in range(6):
                        op("dve", [R1B[m], rstdB, gqB], [CQNB[m][tb]],
                           lambda e, m=m: e.scalar_tensor_tensor(out=CQN[:, m, blk], in0=R1[:, m, :], scalar=gq[:, m:m + 1], in1=rstd[:],
                                                                 op0=ALU.mult, op1=ALU.mult))
                    rstd_from_sq(tmpA, tmpAB, 2, 1.0 / 256, 6)
                    for m in range(2):
                        op("dve", [R1B[6 + m], rstdB, gqB], [CKVNB[m][tb]],
                           lambda e, m=m: e.scalar_tensor_tensor(out=CKVN[:, m, blk], in0=R1[:, 6 + m, :], scalar=gq[:, 6 + m:7 + m], in1=rstd[:],
                                                                 op0=ALU.mult, op1=ALU.mult))
                    b = proj_ap(w1, w1B, D, 64, xn_ap, xnB, 8)
                    op("dve", [psB[b], ropeB], [tRB[0]],
                       lambda e, b=b: e.tensor_tensor(out=tR[:, 0, :], in0=ps[b][32:64, :], in1=ROPE[32:64, blk], op=ALU.mult))
                    op("dve", [psB[b], ropeB], [tRB[1]],
                       lambda e, b=b: e.tensor_tensor(out=tR[:, 1, :], in0=ps[b][0:32, :], in1=ROPE[0:32, blk], op=ALU.mult))
                    op("pool", [tRB[0], tRB[1]], [KRB[tb]],
                       lambda e: e.tensor_tensor(out=KR[:, blk], in0=tR[:, 0, :], in1=tR[:, 1, :], op=ALU.add))
                k.barrier()
            s2 = ExitStack()
            with s2:
                wuq, wuqB = load_w(s2, "s_c_wuq", dw["c_wuq"], 2048)
                wukv, wukvB = load_w(s2, "s_c_wukv", dw["c_wukv"], 2048)
                KTh = [sb(s2, "KTh%d" % i, [P, L], BF16) for i in range(2)]
                KThB = bufs(2, NBLK)
                VAh = [sb(s2, "VAh%d" % i, [P, 16, P], BF16) for i in range(2)]
                VAhB = bufs(2, 4)
                QT = [sb(s2, "QT%d" % i, [P, BW], BF16) for i in range(2)]
                QTB = bufs(2)
                NPT = 5
                LA = 3
                PT = [sb(s2, "PT%d" % i, [P, BW], BF16) for i in range(NPT)]
                PTB = bufs(NPT)
                QF = sb(s2, "QF", [64, BW], F32)
                QFB = Buf()
                maskT = sb(s2, "maskT", [P, P], BF16)
                maskB = Buf()
                rsb = sb(s2, "rsb", [P, BW], F32)
                rsbB = bufs(2)
                dma("pool", maskT[:], dw["c_maskT"], [], [maskB])
                for i in range(2):
                    op("pool", [], KThB[i], lambda e, i=i: e.memset(KTh[i][32:64, :], 0.0))
                    c0 = 64 if i == 0 else 0
                    op("pool", [], VAhB[i], lambda e, i=i, c0=c0: e.memset(VAh[i][:, :, c0:c0 + 64], 1.0))
                def kv_build(h):
                    par = h % 2
                    vc0 = par * 64
                    for tb in range(NBLK):
                        blk = slice(tb * BW, (tb + 1) * BW)
                        b = k.nextps()
                        for kc in range(2):
                            op("pe", [wukvB, CKVNB[kc][tb]], [psB[b]],
                               lambda e, kc=kc, b=b, blk=blk: e.matmul(ps[b][64:128, :], lhsT=wukv[:, kc, h * P:h * P + 64], rhs=CKVN[:, kc, blk],
                                                                      start=(kc == 0), stop=(kc == 1)), inc=(kc == 1))
                        op("act", [psB[b]], [KThB[par][tb]],
                           lambda e, b=b, blk=blk: e.activation(out=KTh[par][64:128, blk], in_=ps[b][64:128, :], func=AF.Copy))
                        op("dve", [KRB[tb]], [KThB[par][tb]],
                           lambda e, blk=blk: e.tensor_scalar(out=KTh[par][0:32, blk], in0=KR[:, blk], scalar1=1.0, scalar2=None, op0=ALU.mult))
                        b = k.nextps()
                        for i4 in range(4):
                            tt = tb * 4 + i4
                            for kc in range(2):
                                op("pe", [wukvB, CKVNB[kc][tb]], [psB[b]],
                                   lambda e, kc=kc, b=b, tt=tt, i4=i4: e.matmul(ps[b][:, i4 * 64:(i4 + 1) * 64], lhsT=CKVN[:, kc, tt * P:(tt + 1) * P],
                                                                                rhs=wukv[:, kc, h * P + 64:h * P + 128], start=(kc == 0), stop=(kc == 1)),
                                   inc=(kc == 1 and i4 == 3))
                        op("act", [psB[b]], [VAhB[par][tb]],
                           lambda e, b=b, tb=tb: e.activation(out=VAh[par][:, tb * 4:(tb + 1) * 4, vc0:vc0 + 64],
                                                              in_=ps[b][:, 0:256].rearrange("p (a c) -> p a c", a=4), func=AF.Copy))

                def q_prep(h, qb, qi):
                    qblk = slice(qb * BW, (qb + 1) * BW)
                    b = k.nextps()
                    for kc in range(6):
                        op("pe", [wuqB, CQNB[kc][qb]], [psB[b]],
                           lambda e, kc=kc, b=b: e.matmul(ps[b][:, :], lhsT=wuq[:, kc, h * P:(h + 1) * P], rhs=CQN[:, kc, qblk],
                                                          start=(kc == 0), stop=(kc == 5)), inc=(kc == 5))
                    op("act", [psB[b]], [QTB[qi]], lambda e, b=b: e.activation(out=QT[qi][:, :], in_=ps[b][:, :], func=AF.Copy))
                    op("act", [psB[b]], [QFB], lambda e, b=b: e.activation(out=QF[:, :], in_=ps[b][0:64, :], func=AF.Copy))
                    op("dve", [QFB, ropeB], [tRB[0]],
                       lambda e: e.tensor_tensor(out=tR[:, 0, :], in0=QF[32:64, :], in1=ROPE[32:64, qblk], op=ALU.mult))
                    op("dve", [QFB, ropeB], [tRB[1]],
                       lambda e: e.tensor_tensor(out=tR[:, 1, :], in0=QF[0:32, :], in1=ROPE[0:32, qblk], op=ALU.mult))
                    op("dve", [tRB[0], tRB[1]], [QTB[qi]],
                       lambda e: e.tensor_tensor(out=QT[qi][0:32, :], in0=tR[:, 0, :], in1=tR[:, 1, :], op=ALU.add))

                pti = [0]

                def attend(h, qb, qi, bo):
                    par, j = h % 2, h // 2
                    base = par * 64
                    oth = 64 - base
                    qblk = slice(qb * BW, (qb + 1) * BW)
                    nkc = 4 * qb + 4

                    def pv(kc, pi, q_lo):
                        op("pe", [PTB[pi], VAhB[par][kc // 4]], [psB[bo]],
                           lambda e: e.matmul(ps[bo][:, q_lo:BW], lhsT=VAh[par][:, kc, :], rhs=PT[pi][:, q_lo:BW],
                                              start=(kc == 0), stop=(kc == nkc - 1)))
                    pend_pv = []
                    for kc in range(nkc):
                        q_lo = max(0, kc - 4 * qb) * P
                        pi = pti[0] % NPT
                        pti[0] += 1
                        b = k.nextps()
                        op("pe", [KThB[par][kc // 4], QTB[qi]], [psB[b]],
                           lambda e, b=b, kc=kc, q_lo=q_lo: e.matmul(ps[b][:, q_lo:BW], lhsT=KTh[par][:, kc * P:(kc + 1) * P],
                                                                     rhs=QT[qi][:, q_lo:BW], start=True, stop=True))
                        op("act", [psB[b]], [PTB[pi]],
                           lambda e, b=b, pi=pi, q_lo=q_lo: e.activation(out=PT[pi][:, q_lo:BW], in_=ps[b][:, q_lo:BW], func=AF.Exp, scale=SC))
                        if kc >= 4 * qb:
                            op("dve", [PTB[pi], maskB], [PTB[pi]],
                               lambda e, pi=pi, q_lo=q_lo: e.tensor_tensor(out=PT[pi][:, q_lo:q_lo + P], in0=PT[pi][:, q_lo:q_lo + P],
                                                                           in1=maskT[:, :], op=ALU.mult))
                        pend_pv.append((kc, pi, q_lo))
                        if len(pend_pv) > LA:
                            pv(*pend_pv.pop(0))
                    while pend_pv:
                        pv(*pend_pv.pop(0))
                    op("dve", [psB[bo]], [rsbB[par]],
                       lambda e: e.tensor_scalar(out=rsb[base:base + 64, :], in0=ps[bo][oth:oth + 64, :], scalar1=1.0, scalar2=None, op0=ALU.mult))
                    op("act", [rsbB[par]], [rsbB[par]], lambda e: e.activation(out=rsb[base:base + 64, :], in_=rsb[base:base + 64, :], func=AF.Ln))
                    op("act", [rsbB[par]], [rsbB[par]], lambda e: e.activation(out=rsb[base:base + 64, :], in_=rsb[base:base + 64, :], func=AF.Exp, scale=-1.0))
                    op("dve", [psB[bo], rsbB[par]], [GTB[j][qb]],
                       lambda e: e.tensor_tensor(out=GT[base:base + 64, j, qblk], in0=ps[bo][base:base + 64, :],
                                                 in1=rsb[base:base + 64, :], op=ALU.mult))

                import os
                NH = int(os.environ.get("MLA_NH", "16"))
                tasks = [(h, qb) for h in range(NH) for qb in range(NBLK)]
                if tasks:
                    kv_build(0)
                    q_prep(0, 0, 0)
                for i, (h, qb) in enumerate(tasks):
                    if i + 1 < len(tasks):
                        h2, qb2 = tasks[i + 1]
                        if h2 != h:
                            kv_build(h2)
                        q_prep(h2, qb2, (i + 1) % 2)
                    attend(h, qb, i % 2, 6 + (i % 2))
                k.barrier()
            sA.close()
            s3 = ExitStack()
            with s3:
                wz, wzB = load_w(s3, "s_c_wz", dw["c_wz"], D)
                w_out, w_outB = load_w(s3, "s_c_wout", dw["c_w_out"], D)
                R1 = sb(s3, "R1", [P, 8, BW], F32)
                R1B = bufs(8)
                xnt = [sb(s3, "xnt%d" % i, [P, 8, BW], BF16) for i in range(2)]
                xnB = bufs(2, 8)
                tmpA = sb(s3, "tmpA", [P, 8, BW], BF16)
                tmpAB = bufs(8)
                tmpX = sb(s3, "tmpX", [P, 8, BW], BF16)
                tmpXB = bufs(8)
                sz = [sb(s3, "sz%d" % i, [P, 8, BW], BF16) for i in range(2)]
                szB = bufs(2, 8)

                def front(tb):
                    i = tb % 2
                    blk = slice(tb * BW, (tb + 1) * BW)
                    xn_ap = lambda c: xnt[i][:, c, :]
                    prenorm_ap(li, tb, xn_ap, xnB[i], tmpX, tmpXB)
                    for m in range(8):
                        b = proj_ap(wz, wzB, m * P, P, xn_ap, xnB[i], 8)
                        op("act", [psB[b]], [szB[i][m]], lambda e, b=b, m=m: e.activation(out=sz[i][:, m, :], in_=ps[b][:, :], func=AF.Silu))
                        op("pool", [szB[i][m], GTB[m][tb]], [szB[i][m]],
                           lambda e, m=m: e.tensor_tensor(out=sz[i][:, m, :], in0=sz[i][:, m, :], in1=GT[:, m, blk], op=ALU.mult))

                front(0)
                for tb in range(NBLK):
                    if tb + 1 < NBLK:
                        front(tb + 1)
                    outproj(li, tb, sz[tb % 2], szB[tb % 2], w_out, w_outB, R1, R1B, tmpA, tmpAB)
                k.barrier()

    def layer_s5(li):
        TT = ALU
        st = ExitStack()
        with st:
            UY = sb(st, "UY", [P, 8, L], BF16)
            UYB = bufs(8, NBLK)
            dT = sb(st, "dT", [P, 16], F32)
            dTB = Buf()
            dma("sp", dT[:], dw["a_dg"], [], [dTB])
            sP = ExitStack()
            sP.__enter__()
            NJ = 32
            BrL, BrLB = sb(sP, "BrL", [P, NJ, P], BF16), Buf()
            BiL, BiLB = sb(sP, "BiL", [P, NJ, P], BF16), Buf()
            CrP, CrPB = sb(sP, "CrP", [P, NJ, P], BF16), Buf()
            CiP, CiPB = sb(sP, "CiP", [P, NJ, P], BF16), Buf()
            for t_, B_, nm in ((BrL, BrLB, "a_brl"), (BiL, BiLB, "a_bil"), (CrP, CrPB, "a_crp"), (CiP, CiPB, "a_cip")):
                for q4 in range(4):
                    dma("pool", t_[:, q4 * 8:(q4 + 1) * 8, :], dw[nm][:, q4 * 8:(q4 + 1) * 8, :], [], [B_])
            lam = sb(sP, "lam", [P, 3, NJ], F32)
            prepB = Buf()
            dma("sp", lam[:], dw["a_lam"], [], [prepB])
            W = {}
            for nm in ("dt", "lrdt", "th", "mag", "t", "t2", "c", "s", "q", "c2", "s2", "cs", "ar", "ai", "den", "nr", "fr", "fi",
                       "u1", "u2", "ir", "ii", "pr", "pi", "A128r", "nA128i", "A128i"):
                W[nm] = sb(sP, "w_" + nm, [P, NJ], F32)
            lr, li_, ldt = lam[:, 0, :], lam[:, 1, :], lam[:, 2, :]

            TPr = sb(sP, "TPr", [P, NJ, P], BF16)
            TPi = sb(sP, "TPi", [P, NJ, P], BF16)
            TNr = sb(sP, "TNr", [P, NJ, P], BF16)
            TNi = sb(sP, "TNi", [P, NJ, P], BF16)
            car = sb(sP, "car", [P, 2, NJ], F32)
            carB = bufs(NJ)
            s1 = ExitStack()
            with s1:
                wu, wuB = load_w(s1, "a_wu", dw["a_wu"], D)
                xnt = sb(s1, "xnt", [P, 8, BW], BF16)
                xnB = bufs(8)
                tmpA = sb(s1, "tmpA", [P, 8, BW], BF16)
                tmpAB = bufs(8)
                m1 = sb(s1, "m1t", [P, NJ, 16], F32)
                m2 = sb(s1, "m2t", [P, NJ, 16], F32)
                prep_ops = []

                def dop(e_, r_, w_, fn_):
                    prep_ops.append((e_, r_, w_, fn_))

                def tt(o, a, b_, o_):
                    dop("dve", [prepB], [prepB], lambda e: e.tensor_tensor(out=o, in0=a, in1=b_, op=o_))

                def ts(o, a, s1_, s2_, o1, o2=None):
                    if o2 is None:
                        dop("dve", [prepB], [prepB], lambda e: e.tensor_scalar(out=o, in0=a, scalar1=s1_, scalar2=None, op0=o1))
                    else:
                        dop("dve", [prepB], [prepB], lambda e: e.tensor_scalar(out=o, in0=a, scalar1=s1_, scalar2=s2_, op0=o1, op1=o2))

                def stt(o, a, sc, b_, o1, o2):
                    dop("dve", [prepB], [prepB], lambda e: e.tensor_scalar(out=o, in0=a, scalar1=sc, scalar2=None, op0=o1))
                    dop("dve", [prepB], [prepB], lambda e: e.tensor_tensor(out=o, in0=o, in1=b_, op=o2))

                def csq(cr, ci):
                    tt(W["c2"][:], cr, cr, TT.mult)
                    tt(W["s2"][:], ci, ci, TT.mult)
                    tt(W["cs"][:], cr, ci, TT.mult)
                    tt(cr, W["c2"][:], W["s2"][:], TT.subtract)
                    ts(ci, W["cs"][:], 2.0, None, TT.mult)

                dop("act", [prepB], [prepB], lambda e: e.activation(out=W["dt"][:], in_=ldt, func=AF.Exp))
                tt(W["lrdt"][:], lr, W["dt"][:], TT.mult)
                tt(W["th"][:], li_, W["dt"][:], TT.mult)
                dop("act", [prepB], [prepB], lambda e: e.activation(out=W["mag"][:], in_=W["lrdt"][:], func=AF.Exp))
                ts(W["t"][:], W["th"][:], 1.0 / 64, None, TT.mult)
                tt(W["t2"][:], W["t"][:], W["t"][:], TT.mult)
                ts(W["q"][:], W["t2"][:], -1.0 / 720, None, TT.mult)
                stt(W["q"][:], W["q"][:], 1.0 / 24, W["t2"][:], TT.add, TT.mult)
                stt(W["q"][:], W["q"][:], -0.5, W["t2"][:], TT.add, TT.mult)
                ts(W["c"][:], W["q"][:], 1.0, None, TT.add)
                ts(W["q"][:], W["t2"][:], -1.0 / 5040, None, TT.mult)
                stt(W["q"][:], W["q"][:], 1.0 / 120, W["t2"][:], TT.add, TT.mult)
                stt(W["q"][:], W["q"][:], -1.0 / 6, W["t2"][:], TT.add, TT.mult)
                stt(W["s"][:], W["q"][:], 1.0, W["t"][:], TT.add, TT.mult)
                for _ in range(6):
                    csq(W["c"][:], W["s"][:])
                tt(W["ar"][:], W["mag"][:], W["c"][:], TT.mult)
                tt(W["ai"][:], W["mag"][:], W["s"][:], TT.mult)
                tt(W["den"][:], lr, lr, TT.mult)
                tt(W["u1"][:], li_, li_, TT.mult)
                tt(W["den"][:], W["den"][:], W["u1"][:], TT.add)
                dop("act", [prepB], [prepB], lambda e: e.activation(out=W["den"][:], in_=W["den"][:], func=AF.Ln))
                dop("act", [prepB], [prepB], lambda e: e.activation(out=W["den"][:], in_=W["den"][:], func=AF.Exp, scale=-1.0))
                ts(W["nr"][:], W["ar"][:], -1.0, None, TT.add)
                tt(W["u1"][:], W["nr"][:], lr, TT.mult)
                tt(W["u2"][:], W["ai"][:], li_, TT.mult)
                tt(W["u1"][:], W["u1"][:], W["u2"][:], TT.add)
                tt(W["fr"][:], W["u1"][:], W["den"][:], TT.mult)
                tt(W["u1"][:], W["ai"][:], lr, TT.mult)
                tt(W["u2"][:], W["nr"][:], li_, TT.mult)
                tt(W["u1"][:], W["u1"][:], W["u2"][:], TT.subtract)
                tt(W["fi"][:], W["u1"][:], W["den"][:], TT.mult)
                dop("act", [prepB], [prepB], lambda e: e.activation(out=W["u1"][:], in_=W["lrdt"][:], func=AF.Exp, scale=-2.0))
                tt(W["ir"][:], W["ar"][:], W["u1"][:], TT.mult)
                tt(W["ii"][:], W["ai"][:], W["u1"][:], TT.mult)
                ts(W["ii"][:], W["ii"][:], -1.0, None, TT.mult)
                dop("dve", [prepB], [prepB], lambda e: e.memset(TPr[:, :, 0:1], 1.0))
                dop("dve", [prepB], [prepB], lambda e: e.memset(TPi[:, :, 0:1], 0.0))
                ts(TNr[:, :, 0:1], W["fr"][:].unsqueeze(2), 1.0, None, TT.mult)
                ts(TNi[:, :, 0:1], W["fi"][:].unsqueeze(2), 1.0, None, TT.mult)
                for (Tr_, Ti_, pr0, pi0) in ((TPr, TPi, "ar", "ai"), (TNr, TNi, "ir", "ii")):
                    ts(W["pr"][:], W[pr0][:], 1.0, None, TT.mult)
                    ts(W["pi"][:], W[pi0][:], 1.0, None, TT.mult)
                    for kk in range(7):
                        n = 1 << kk
                        for c0 in range(0, n, 16):
                            w = min(16, n - c0)
                            pr_b = W["pr"][:].unsqueeze(2).to_broadcast([P, NJ, w])
                            pi_b = W["pi"][:].unsqueeze(2).to_broadcast([P, NJ, w])
                            lo_r, lo_i = Tr_[:, :, c0:c0 + w], Ti_[:, :, c0:c0 + w]
                            tt(m1[:, :, 0:w], lo_r, pr_b, TT.mult)
                            tt(m2[:, :, 0:w], lo_i, pi_b, TT.mult)
                            tt(Tr_[:, :, n + c0:n + c0 + w], m1[:, :, 0:w], m2[:, :, 0:w], TT.subtract)
                            tt(m1[:, :, 0:w], lo_r, pi_b, TT.mult)
                            tt(m2[:, :, 0:w], lo_i, pr_b, TT.mult)
                            tt(Ti_[:, :, n + c0:n + c0 + w], m1[:, :, 0:w], m2[:, :, 0:w], TT.add)
                        csq(W["pr"][:], W["pi"][:])
                    if pr0 == "ar":
                        ts(W["A128r"][:], W["pr"][:], 1.0, None, TT.mult)
                        ts(W["nA128i"][:], W["pi"][:], -1.0, None, TT.mult)
                        ts(W["A128i"][:], W["pi"][:], 1.0, None, TT.mult)
                dop("dve", [prepB], [prepB], lambda e: e.tensor_scalar(out=TNi[:], in0=TNi[:], scalar1=-1.0, scalar2=None, op0=TT.mult))
                dop("dve", [prepB], carB, lambda e: e.memset(car[:], 0.0))
                xn_ap = lambda c: xnt[:, c, :]
                npo = len(prep_ops)
                for tb in range(NBLK):
                    blk = slice(tb * BW, (tb + 1) * BW)
                    prenorm_ap(li, tb, xn_ap, xnB, tmpA, tmpAB)
                    for (e_, r_, w_, fn_) in prep_ops[tb * npo // NBLK:(tb + 1) * npo // NBLK]:
                        op(e_, r_, w_, fn_)
                    for m in range(8):
                        b = proj_ap(wu, wuB, m * P, P, xn_ap, xnB, 8)
                        op("act", [psB[b]], [UYB[m][tb]], lambda e, b=b, m=m, blk=blk: e.activation(out=UY[:, m, blk], in_=ps[b][:, :], func=AF.Copy))
                k.barrier()
            s2 = ExitStack()
            with s2:
                NA, NCD, NQ = 2, 2, 2
                TA = [sb(s2, "TA%d" % i, [P, 4, P], BF16) for i in range(NA)]
                TBf = [sb(s2, "TB%d" % i, [P, 4, P], BF16) for i in range(NA)]
                CD = [sb(s2, "CD%d" % i, [P, 2, 4, P], F32) for i in range(NCD)]
                KBt = [sb(s2, "KB%d" % i, [P, 4, 4, P], BF16) for i in range(2)]
                KBB = bufs(2)
                ntriu = sb(s2, "ntriu", [P, P], BF16)
                triu = sb(s2, "triu", [P, P], BF16)
                triuB = Buf()
                dma("pool", triu[:], dw["a_triu"], [], [triuB])
                dma("pool", ntriu[:], dw["a_ntriu"], [], [triuB])
                for T_ in (TNr, TNi):
                    for j4 in range(8):
                        b = k.nextps()
                        for jl in range(4):
                            op("pe", [prepB, identB], [psB[b]],
                               lambda e, b=b, T_=T_, j4=j4, jl=jl: e.matmul(ps[b][:, jl * P:(jl + 1) * P], lhsT=T_[:, j4 * 4 + jl, :], rhs=ident[:, :],
                                                                            start=True, stop=True), inc=(jl == 3))
                        op("act", [psB[b]], [prepB],
                           lambda e, b=b, T_=T_, j4=j4: e.activation(out=T_[:, j4 * 4:j4 * 4 + 4, :].rearrange("p a t -> p (a t)"), in_=ps[b][:, :], func=AF.Copy))
                Q = [[sb(s2, "Q%d_%d" % (i, q_), [P, 4, P], BF16) for q_ in range(4)] for i in range(NQ)]
                AB_, BB_, CDB = bufs(NA), bufs(NA), bufs(NCD)
                QB = bufs(NQ, 4)
                ea = sb(s2, "ea", [P, 2, 4], F32)
                eb = sb(s2, "eb", [P, 2, 4], F32)
                eB = Buf()
                ytmp = sb(s2, "ytmp", [P, BW], F32)
                yB = Buf()

                def flat(t_):
                    return t_[:].rearrange("p a t -> p (a t)")

                units = [(chc, c) for cp in range(4) for c in range(16) for chc in (2 * cp, 2 * cp + 1)]
                NU = len(units)

                def stA(u):
                    chc, c = units[u]
                    j0, qt = chc * 4, c // 4
                    cols = slice(c * P, (c + 1) * P)
                    ai = u % NA
                    ba = k.nextps()
                    bb = k.nextps()
                    op("pe", [BrLB, UYB[chc][qt]], [psB[ba]],
                       lambda e: e.matmul(ps[ba][:, :], lhsT=UY[:, chc, cols], rhs=BrL[:, j0:j0 + 4, :].rearrange("p a t -> p (a t)"), start=True, stop=True))
                    op("pe", [BiLB, UYB[chc][qt]], [psB[bb]],
                       lambda e: e.matmul(ps[bb][:, :], lhsT=UY[:, chc, cols], rhs=BiL[:, j0:j0 + 4, :].rearrange("p a t -> p (a t)"), start=True, stop=True))
                    op("act", [psB[ba]], [AB_[ai]], lambda e: e.activation(out=flat(TA[ai]), in_=ps[ba][:, :], func=AF.Copy))
                    op("act", [psB[bb]], [BB_[ai]], lambda e: e.activation(out=flat(TBf[ai]), in_=ps[bb][:, :], func=AF.Copy))

                def ctx(u):
                    chc, c = units[u]
                    j0 = chc * 4
                    ai, ci, qi = u % NA, u % NCD, u % NQ
                    return chc, c, j0, ai, ci, qi

                def stB_mul(u):
                    chc, c, j0, ai, ci, qi = ctx(u)
                    A, B_ = TA[ai], TBf[ai]
                    kb = KBt[u % 2]
                    tnr, ntni = TNr[:, j0:j0 + 4, :], TNi[:, j0:j0 + 4, :]
                    op("dve", [AB_[ai], prepB], [KBB[u % 2]], lambda e: e.tensor_tensor(out=kb[:, 0, :, :], in0=A[:], in1=tnr, op=TT.mult))
                    op("dve", [BB_[ai], prepB], [KBB[u % 2]], lambda e: e.tensor_tensor(out=kb[:, 1, :, :], in0=B_[:], in1=ntni, op=TT.mult))
                    op("dve", [AB_[ai], prepB], [KBB[u % 2]], lambda e: e.tensor_tensor(out=kb[:, 2, :, :], in0=A[:], in1=ntni, op=TT.mult))
                    op("dve", [BB_[ai], prepB], [KBB[u % 2]], lambda e: e.tensor_tensor(out=kb[:, 3, :, :], in0=B_[:], in1=tnr, op=TT.mult))

                def stT_cs(u):
                    chc, c, j0, ai, ci, qi = ctx(u)
                    kt = KBt[u % 2]
                    for ri in range(2):
                        b = k.nextps()
                        for jl in range(4):
                            op("pe", [KBB[u % 2], triuB], [psB[b]],
                               lambda e, b=b, ri=ri, jl=jl: e.matmul(ps[b][:, jl * P:(jl + 1) * P], lhsT=kt[:, 2 * ri, jl, :], rhs=triu[:, :], start=True, stop=False),
                               inc=False)
                            op("pe", [KBB[u % 2], triuB], [psB[b]],
                               lambda e, b=b, ri=ri, jl=jl: e.matmul(ps[b][:, jl * P:(jl + 1) * P], lhsT=kt[:, 2 * ri + 1, jl, :],
                                                                     rhs=(triu if ri == 0 else ntriu)[:, :], start=False, stop=True),
                               inc=(jl == 3))
                        for jl in range(4):
                            op("act", [psB[b], carB[j0]], [CDB[ci]],
                               lambda e, b=b, ri=ri, jl=jl: e.activation(out=CD[ci][:, ri, jl, :], in_=ps[b][:, jl * P:(jl + 1) * P], func=AF.Identity,
                                                                         bias=car[:, ri, j0 + jl:j0 + jl + 1], scale=1.0))

                def stD_e(u):
                    chc, c, j0, ai, ci, qi = ctx(u)
                    if c < 15:
                        xl = CD[ci][:, :, :, P - 1]
                        a_r = W["A128r"][:, j0:j0 + 4].unsqueeze(1).to_broadcast([P, 2, 4])
                        a_i = W["A128i"][:, j0:j0 + 4].unsqueeze(1).to_broadcast([P, 2, 4])
                        op("dve", [CDB[ci], prepB, carB[j0]], [eB], lambda e: e.tensor_tensor(out=ea[:], in0=xl, in1=a_r, op=TT.mult))
                        op("dve", [CDB[ci], prepB, carB[j0]], [eB], lambda e: e.tensor_tensor(out=eb[:], in0=xl, in1=a_i, op=TT.mult))

                def stD_q3(u):
                    chc, c, j0, ai, ci, qi = ctx(u)
                    C = CD[ci][:, 0, :, :]
                    tpi = TPi[:, j0:j0 + 4, :]
                    op("dve", [CDB[ci], prepB], [QB[qi][2]],
                       lambda e: e.scalar_tensor_tensor(out=Q[qi][2][:], in0=C, scalar=-1.0, in1=tpi, op0=TT.mult, op1=TT.mult))

                def stD_car(u):
                    chc, c, j0, ai, ci, qi = ctx(u)
                    if c < 15:
                        op("dve", [eB], [carB[j0]], lambda e: e.tensor_tensor(out=car[:, 0, j0:j0 + 4], in0=ea[:, 0, :], in1=eb[:, 1, :], op=TT.add))
                        op("dve", [eB], [carB[j0]], lambda e: e.tensor_tensor(out=car[:, 1, j0:j0 + 4], in0=ea[:, 1, :], in1=eb[:, 0, :], op=TT.subtract))

                def stD_rest(u):
                    chc, c, j0, ai, ci, qi = ctx(u)
                    qt, cq = c // 4, c % 4
                    C, Dd = CD[ci][:, 0, :, :], CD[ci][:, 1, :, :]
                    bo = 6 + (chc % 2)
                    tpr, tpi = TPr[:, j0:j0 + 4, :], TPi[:, j0:j0 + 4, :]
                    q = Q[qi]
                    op("dve", [CDB[ci], prepB], [QB[qi][0]], lambda e: e.tensor_tensor(out=q[0][:], in0=C, in1=tpr, op=TT.mult))
                    op("dve", [CDB[ci], prepB], [QB[qi][1]], lambda e: e.tensor_tensor(out=q[1][:], in0=Dd, in1=tpi, op=TT.mult))
                    op("dve", [CDB[ci], prepB], [QB[qi][3]], lambda e: e.tensor_tensor(out=q[3][:], in0=Dd, in1=tpr, op=TT.mult))

                def stE_mm(u):
                    chc, c, j0, ai, ci, qi = ctx(u)
                    qt, cq = c // 4, c % 4
                    bo = 6 + (chc % 2)
                    q = Q[qi]
                    n = 0
                    for jl in range(4):
                        j = j0 + jl
                        for qq in range(4):
                            wt, wtB = (CrP, CrPB) if qq < 2 else (CiP, CiPB)
                            n += 1
                            op("pe", [QB[qi][qq], wtB], [psB[bo]],
                               lambda e, j=j, jl=jl, qq=qq, wt=wt, n=n: e.matmul(ps[bo][:, cq * P:(cq + 1) * P], lhsT=wt[:, j, :], rhs=q[qq][:, jl, :],
                                                                                  start=(n == 1), stop=(n == 16)), inc=(n == 16))
                    if cq == 3:
                        blk = slice(qt * BW, (qt + 1) * BW)
                        op("act", [psB[bo]], [yB], lambda e: e.activation(out=ytmp[:], in_=ps[bo][:, :], func=AF.Copy))
                        op("dve", [yB, UYB[chc][qt], dTB], [yB],
                           lambda e: e.scalar_tensor_tensor(out=ytmp[:], in0=UY[:, chc, blk], scalar=dT[:, chc:chc + 1], in1=ytmp[:],
                                                            op0=TT.mult, op1=TT.add))
                        op("act", [yB], [UYB[chc][qt]], lambda e: e.activation(out=UY[:, chc, blk], in_=ytmp[:], func=AF.Gelu_apprx_tanh))

                def ok(u):
                    return 0 <= u < NU

                for i in range(NU + 4):
                    if ok(i):
                        stA(i)
                    if ok(i - 2):
                        stT_cs(i - 2)
                    if ok(i - 4):
                        stE_mm(i - 4)
                    if ok(i - 1):
                        stB_mul(i - 1)
                    if ok(i - 3):
                        stD_e(i - 3)
                        stD_q3(i - 3)
                        stD_car(i - 3)
                        stD_rest(i - 3)
                k.barrier()
            sP.close()
            s3 = ExitStack()
            with s3:
                wz, wzB = load_w(s3, "a_wzs", dw["a_wz"], D)
                wg, wgB = load_w(s3, "a_wgs", dw["a_wglu"], D)
                w_out, w_outB = load_w(s3, "a_wouts", dw["a_w_out"], D)
                R1 = sb(s3, "R1", [P, 8, BW], F32)
                R1B = bufs(8)
                xnt = sb(s3, "xnt", [P, 8, BW], BF16)
                xnB = bufs(8)
                tmpA = sb(s3, "tmpA", [P, 8, BW], BF16)
                tmpAB = bufs(8)
                sz = sb(s3, "sz", [P, 8, BW], BF16)
                szB = bufs(8)
                sg = sb(s3, "sg", [P, 2, BW], BF16)
                sgB = bufs(2)
                xn_ap = lambda c: xnt[:, c, :]
                for tb in range(NBLK):
                    blk = slice(tb * BW, (tb + 1) * BW)
                    prenorm_ap(li, tb, xn_ap, xnB, tmpA, tmpAB)
                    uy_ap = lambda c, blk=blk: UY[:, c, blk]
                    uyB_t = [UYB[c][tb] for c in range(8)]
                    for m in range(8):
                        b = proj_ap(wz, wzB, m * P, P, xn_ap, xnB, 8)
                        op("act", [psB[b]], [szB[m]], lambda e, b=b, m=m: e.activation(out=sz[:, m, :], in_=ps[b][:, :], func=AF.Silu))
                        b = proj_ap(wg, wgB, m * P, P, uy_ap, uyB_t, 8)
                        op("act", [psB[b], dTB], [sgB[m % 2]],
                           lambda e, b=b, m=m: e.activation(out=sg[:, m % 2, :], in_=ps[b][:, :], func=AF.Sigmoid, bias=dT[:, 8 + m:9 + m], scale=1.0))
                        op("pool", [szB[m], uyB_t[m]], [szB[m]],
                           lambda e, m=m, blk=blk: e.tensor_tensor(out=sz[:, m, :], in0=sz[:, m, :], in1=UY[:, m, blk], op=ALU.mult))
                        op("pool", [szB[m], sgB[m % 2]], [szB[m]],
                           lambda e, m=m: e.tensor_tensor(out=sz[:, m, :], in0=sz[:, m, :], in1=sg[:, m % 2, :], op=ALU.mult))
                    outproj(li, tb, sz, szB, w_out, w_outB, R1, R1B, tmpA, tmpAB)
                k.barrier()

    for li in layers:
        if li == 3:
            layer_sgu(li)
        elif li == 1:
            layer_swa(li)
        elif li == 2:
            layer_mla(li)
        elif li == 0:
            layer_s5(li)

    toks = []
    for c in range(8):
        for tb in range(NBLK):
            toks.append(dma("sp", outT_d[c * P:(c + 1) * P, tb * BW:(tb + 1) * BW], X[:, c, tb * BW:(tb + 1) * BW], [xB[c][tb]], []))
    k._wait("sp", toks)
    k.barrier()


def host_inputs(inp, layers):
    f = lambda a: np.ascontiguousarray(np.asarray(a, dtype=np.float32))
    common = {}
    common["gpre"] = f(np.asarray(inp["pre_norm"]).reshape(4, 8, P).transpose(2, 0, 1).reshape(P, 32))
    common["gpost"] = f(np.asarray(inp["post_norm"]).reshape(4, 8, P).transpose(2, 0, 1).reshape(P, 32))
    common["ident"] = np.eye(P, dtype=np.float32)
    if 3 in layers:
        common["d_w_in"] = f(inp["d_w_in"][0])
        common["d_w_out"] = f(inp["d_w_out"][0])
        common["d_ws"] = f(np.asarray(inp["d_w_s"][0]).transpose(1, 0, 2))
        common["d_tril"] = np.tril(np.ones((P, P), dtype=np.float32))
        common["d_bs"] = f(inp["d_b_s"][0])
        common["d_lng"] = f(inp["d_ln_g"])
        common["d_lnb"] = f(inp["d_ln_b"])
    if 1 in layers:
        w = np.asarray(inp["b_w_in"][0], dtype=np.float32)
        q, kk, v, z = w[:, :1024], w[:, 1024:1152], w[:, 1152:1280], w[:, 1280:]
        common["b_w_in"] = f(np.concatenate([q, kk[:, :64], kk[:, :64], kk[:, 64:], kk[:, 64:], v, z], axis=1))
        common["b_w_out"] = f(inp["b_w_out"][0])
        common["b_sinks"] = f(inp["b_sinks"])
        def bucket(d):
            if d < 16:
                return d
            v_ = 16 + int(math.log(max(d, 1) / 16.0) / math.log(128 / 16.0) * 16)
            return min(v_, 31)
        rb = np.asarray(inp["rel_bias"], dtype=np.float32)
        bt = np.full((P, 16, 256), -1e30, dtype=np.float32)
        for kj in range(P):
            for qi in range(P):
                d_prev = qi + P - kj
                if d_prev < P:
                    bt[kj, :, qi] = rb[bucket(d_prev), :]
                d_cur = qi - kj
                if d_cur >= 0:
                    bt[kj, :, P + qi] = rb[bucket(d_cur), :]
        common["b_biasT"] = bt
    if 2 in layers:
        w = np.asarray(inp["c_w_in"][0], dtype=np.float32)
        kr = w[:, 1024:1056]
        common["c_w1"] = f(np.concatenate([w[:, :1024], kr, kr[:, 16:], kr[:, :16]], axis=1))
        common["c_wz"] = f(w[:, 1056:])
        uq = np.asarray(inp["c_w_uq"][0], dtype=np.float32).reshape(768, 16, 96)
        nope, rp = uq[:, :, :64], uq[:, :, 64:]
        common["c_wuq"] = f(np.concatenate([rp, rp[:, :, 16:], rp[:, :, :16], nope], axis=2).reshape(768, 2048))
        common["c_wukv"] = f(inp["c_w_ukv"][0])
        common["c_w_out"] = f(inp["c_w_out"][0])
        inv = (np.float32(10000.0) ** (-np.arange(0, 32, 2, dtype=np.float32) / np.float32(32))).astype(np.float32)
        ang = (np.arange(L, dtype=np.float32)[:, None] * inv[None, :]).astype(np.float32)
        cos, sin = np.cos(ang).astype(np.float32).T, np.sin(ang).astype(np.float32).T
        common["c_rope"] = f(np.concatenate([cos, cos, -sin, sin], axis=0))
        g = np.concatenate([np.asarray(inp["c_q_norm"][0]), np.asarray(inp["c_kv_norm"][0])]).astype(np.float32)
        common["c_gqkv"] = f(g.reshape(8, P).T)
        common["c_maskT"] = np.triu(np.ones((P, P), dtype=np.float32))
    if 0 in layers:
        w = np.asarray(inp["a_w_in"][0], dtype=np.float32)
        common["a_wu"] = f(w[:, :1024])
        common["a_wz"] = f(w[:, 1024:])
        common["a_wglu"] = f(inp["a_w_glu"][0])
        common["a_w_out"] = f(inp["a_w_out"][0])
        dg = np.concatenate([np.asarray(inp["a_d"][0]).reshape(8, P).T, np.asarray(inp["a_b_glu"][0]).reshape(8, P).T], axis=1)
        common["a_dg"] = f(dg)
        lam = np.stack([np.asarray(inp["a_lam_re"][0]).reshape(32, P).T, np.asarray(inp["a_lam_im"][0]).reshape(32, P).T,
                        np.repeat(np.asarray(inp["a_log_dt"][0]), 64).reshape(32, P).T], axis=1)
        common["a_lam"] = f(lam)
        brl = np.zeros((P, 32, P), np.float32); bil = np.zeros((P, 32, P), np.float32)
        crp = np.zeros((P, 32, P), np.float32); cip = np.zeros((P, 32, P), np.float32)
        b_re, b_im = np.asarray(inp["a_b_re"][0]), np.asarray(inp["a_b_im"][0])
        c_re, c_im = np.asarray(inp["a_c_re"][0]), np.asarray(inp["a_c_im"][0])
        for j in range(32):
            for gl in range(2):
                g = 2 * j + gl
                r0 = 32 * (j % 4) + gl * 16
                brl[r0:r0 + 16, j, gl * 64:(gl + 1) * 64] = b_re[g].T
                bil[r0:r0 + 16, j, gl * 64:(gl + 1) * 64] = b_im[g].T
                crp[gl * 64:(gl + 1) * 64, j, r0:r0 + 16] = c_re[g].T
                cip[gl * 64:(gl + 1) * 64, j, r0:r0 + 16] = c_im[g].T
        common["a_brl"], common["a_bil"], common["a_crp"], common["a_cip"] = brl, bil, crp, cip
        common["a_triu"] = np.triu(np.ones((P, P), dtype=np.float32))
        common["a_ntriu"] = -np.triu(np.ones((P, P), dtype=np.float32))
    return common


def run(inp, layers=(0, 1, 2, 3), cores=8, trace=False):
    nc = bass.Bass("TRN2", target_bir_lowering=False)
    build(nc, list(layers))
    common = host_inputs(inp, list(layers))
    x = np.asarray(inp["x"], dtype=np.float32)
    in_maps = []
    for b in range(cores):
        m = dict(common)
        m["xT"] = np.ascontiguousarray(x[b].T)
        in_maps.append(m)
    res = run_bass_kernel_spmd(nc, in_maps, core_ids=list(range(cores)), trace=trace)
    out = np.stack([np.ascontiguousarray(r["outT"].T) for r in res.results], axis=0)
    return out.astype(np.float32), res


def kernel(**inputs):
    out, _ = run(inputs)
    return out
```

```python
import math
import numpy as np
from contextlib import ExitStack
import concourse.bass as bass
import concourse.mybir as mybir
from concourse.bass_utils import run_bass_kernel_spmd

F32 = mybir.dt.float32
BF16 = mybir.dt.bfloat16
ALU = mybir.AluOpType
AF = mybir.ActivationFunctionType

P = 128
L = 2048
D = 1024
NBLK = 4
BW = 512
EPS = 1e-6
SELF_SYNC = True
NDS = 12


class Buf:
    __slots__ = ("w", "r")

    def __init__(self):
        self.w = None
        self.r = {}


class WB:
    def __init__(self):
        self.blocks = []

    def cols(self, c0, c1):
        return [b for (a0, a1, bl) in self.blocks if a0 < c1 and c0 < a1 for b in bl]

    def all(self):
        return [b for (_, _, bl) in self.blocks for b in bl]


def _flat(lst):
    out = []
    for b in lst:
        if isinstance(b, WB):
            out.extend(b.all())
        elif isinstance(b, list):
            out.extend(_flat(b))
        else:
            out.append(b)
    return out


def bufs(*shape):
    if len(shape) == 1:
        return [Buf() for _ in range(shape[0])]
    return [bufs(*shape[1:]) for _ in range(shape[0])]


class KB:
    def __init__(self, nc, es):
        self.nc = nc
        self.E = dict(pe=nc.tensor, act=nc.scalar, dve=nc.vector, pool=nc.gpsimd, sp=nc.sync)
        self.sem = {e: es.enter_context(nc.semaphore("s_" + e)) for e in ("pe", "act", "dve", "pool")}
        self.cnt = {e: 0 for e in self.sem}
        self.pend = {e: False for e in self.sem}
        self.dsem = {q: [[es.enter_context(nc.semaphore("d_%s%d" % (q, i))), 0] for i in range(NDS)]
                     for q in ("sp", "pool")}
        self.dcnt = {"sp": 0, "pool": 0}
        self.seen = {e: {} for e in self.E}
        self.ps = []
        self.psB = []
        self.psrot = 0
        self.nrot = 6

    def _semh(self, key):
        if isinstance(key, str):
            return self.sem[key]
        return self.dsem[key[0]][key[1]][0]

    def _wait(self, e, toks):
        need = {}
        for key, v in toks:
            if need.get(key, 0) < v:
                need[key] = v
        for key, v in need.items():
            if key == e and (e == "pe" or not SELF_SYNC):
                continue
            if self.seen[e].get(key, 0) >= v:
                continue
            self.E[e].wait_ge(self._semh(key), v)
            self.seen[e][key] = v

    def _deps(self, reads, writes):
        toks = []
        for b in reads:
            if b.w is not None:
                toks.append(b.w)
        for b in writes:
            if b.w is not None:
                toks.append(b.w)
            toks.extend(b.r.items())
        return toks

    def _mark(self, tok, reads, writes):
        key, v = tok
        for b in reads:
            if b.r.get(key, 0) < v:
                b.r[key] = v
        for b in writes:
            b.w = tok
            b.r = {}

    def op(self, e, reads, writes, fn, inc=True):
        reads, writes = _flat(reads), _flat(writes)
        self._wait(e, self._deps(reads, writes))
        ins = fn(self.E[e])
        if inc:
            self.cnt[e] += 1
            ins.then_inc(self.sem[e], 1)
            self.pend[e] = False
            tok = (e, self.cnt[e])
        else:
            self.pend[e] = True
            tok = (e, self.cnt[e] + 1)
        self._mark(tok, reads, writes)
        return ins

    def dma(self, q, out, in_, reads, writes):
        reads, writes = _flat(reads), _flat(writes)
        self._wait(q, self._deps(reads, writes))
        i = self.dcnt[q] % NDS
        self.dcnt[q] += 1
        ent = self.dsem[q][i]
        key = (q, i)
        if ent[1] > 0:
            self._wait(q, [(key, 16 * ent[1])])
        ins = self.E[q].dma_start(out=out, in_=in_)
        ins.then_inc(ent[0], 16)
        ent[1] += 1
        tok = (key, 16 * ent[1])
        self._mark(tok, reads, writes)
        return tok

    def all_tokens(self):
        toks = [(e, c) for e, c in self.cnt.items() if c > 0]
        for q in self.dsem:
            for i, ent in enumerate(self.dsem[q]):
                if ent[1] > 0:
                    toks.append(((q, i), 16 * ent[1]))
        return toks

    def barrier(self):
        for e in self.sem:
            assert not self.pend[e], e
        toks = self.all_tokens()
        for e in self.E:
            self._wait(e, toks)

    def nextps(self):
        b = self.psrot
        self.psrot = (self.psrot + 1) % self.nrot
        return b


def build(nc, layers):
    es = ExitStack()
    with es:
        _build(nc, es, layers)
    return nc


def _build(nc, es, layers):
    k = KB(nc, es)
    op, dma = k.op, k.dma

    def dram_in(name, shape, dt=F32):
        return nc.dram_tensor(name, list(shape), dt, kind="ExternalInput").ap()

    uid = [0]

    def sb(st, name, shape, dt):
        uid[0] += 1
        return st.enter_context(nc.sbuf_tensor("%s_%d" % (name, uid[0]), list(shape), dt))

    xT_d = dram_in("xT", [D, L])
    outT_d = nc.dram_tensor("outT", [D, L], F32, kind="ExternalOutput").ap()
    gpre_d = dram_in("gpre", [P, 32])
    gpost_d = dram_in("gpost", [P, 32])
    ident_d = dram_in("ident", [P, P])
    dw = {}
    if 3 in layers:
        dw["d_w_in"] = dram_in("d_w_in", [D, 3072])
        dw["d_w_out"] = dram_in("d_w_out", [D, D])
        dw["d_ws"] = dram_in("d_ws", [P, 16, P])
        dw["d_tril"] = dram_in("d_tril", [P, P])
        dw["d_bs"] = dram_in("d_bs", [16, P])
        dw["d_lng"] = dram_in("d_lng", [1, D])
        dw["d_lnb"] = dram_in("d_lnb", [1, D])

    if 1 in layers:
        dw["b_w_in"] = dram_in("b_w_in", [D, 2432])
        dw["b_w_out"] = dram_in("b_w_out", [D, D])
        dw["b_biasT"] = dram_in("b_biasT", [P, 16, 256])
        dw["b_sinks"] = dram_in("b_sinks", [1, 16])

    if 2 in layers:
        dw["c_w1"] = dram_in("c_w1", [D, 1088])
        dw["c_wz"] = dram_in("c_wz", [D, D])
        dw["c_wuq"] = dram_in("c_wuq", [768, 2048])
        dw["c_wukv"] = dram_in("c_wukv", [256, 2048])
        dw["c_w_out"] = dram_in("c_w_out", [D, D])
        dw["c_rope"] = dram_in("c_rope", [64, L])
        dw["c_gqkv"] = dram_in("c_gqkv", [P, 8])
        dw["c_maskT"] = dram_in("c_maskT", [P, P])

    if 0 in layers:
        dw["a_wu"] = dram_in("a_wu", [D, D])
        dw["a_wz"] = dram_in("a_wz", [D, D])
        dw["a_wglu"] = dram_in("a_wglu", [D, D])
        dw["a_w_out"] = dram_in("a_w_out", [D, D])
        dw["a_dg"] = dram_in("a_dg", [P, 16])
        dw["a_lam"] = dram_in("a_lam", [P, 3, 32])
        for nm in ("a_brl", "a_bil", "a_crp", "a_cip"):
            dw[nm] = dram_in(nm, [P, 32, P])
        dw["a_triu"] = dram_in("a_triu", [P, P])
        dw["a_ntriu"] = dram_in("a_ntriu", [P, P])

    X = sb(es, "X", [P, 8, L], F32)
    xB = bufs(8, NBLK)
    ones = sb(es, "ones", [P, P], BF16)
    onesB = Buf()
    ident = sb(es, "identb", [P, P], BF16)
    identB = Buf()
    gpre = sb(es, "gpre_s", [P, 32], F32)
    gpost = sb(es, "gpost_s", [P, 32], F32)
    gB = Buf()
    rstd = sb(es, "rstd", [P, BW], F32)
    rstdB = Buf()
    for i in range(8):
        k.ps.append(es.enter_context(nc.psum_tensor("ps%d" % i, [P, BW], F32)))
        k.psB.append(Buf())
    ps, psB = k.ps, k.psB

    for c in range(8):
        for tb in range(NBLK):
            dma("sp", X[:, c, tb * BW:(tb + 1) * BW], xT_d[c * P:(c + 1) * P, tb * BW:(tb + 1) * BW], [], [xB[c][tb]])
    dma("sp", gpre[:], gpre_d, [], [gB])
    dma("sp", gpost[:], gpost_d, [], [gB])
    dma("pool", ident[:], ident_d, [], [identB])
    op("dve", [], [onesB], lambda e: e.memset(ones[:], 1.0))
    epsc = sb(es, "epsc", [P, 1], F32)
    op("dve", [], [onesB], lambda e: e.memset(epsc[:], EPS))

    def load_w(st, name, d_ap, ncols, q="pool"):
        K = d_ap.shape[0]
        kc_n = K // P
        t = sb(st, name, [P, kc_n, ncols], BF16)
        B = WB()
        for c0 in range(0, ncols, 512):
            c1 = min(ncols, c0 + 512)
            bl = []
            for kc in range(kc_n):
                b1 = Buf()
                dma(q, t[:, kc, c0:c1], d_ap[kc * P:(kc + 1) * P, c0:c1], [], [b1])
                bl.append(b1)
            B.blocks.append((c0, c1, bl))
        return t, B

    def rstd_from_sq(sq, sqB, n, scale, c0=0):
        b = k.nextps()
        for c in range(n):
            op("pe", [sqB[c0 + c], onesB], [psB[b]],
               lambda e, c=c: e.matmul(ps[b][:, :], lhsT=ones[:, :], rhs=sq[:, c0 + c, :], start=(c == 0), stop=(c == n - 1)),
               inc=(c == n - 1))
        op("act", [psB[b]], [rstdB],
           lambda e: e.activation(out=rstd[:], in_=ps[b][:, :], func=AF.Ln, scale=scale, bias=epsc[:, 0:1]))
        op("act", [rstdB], [rstdB], lambda e: e.activation(out=rstd[:], in_=rstd[:], func=AF.Exp, scale=-0.5))

    def prenorm(li, tb, xn, xnB, tmpA, tmpAB):
        blk = slice(tb * BW, (tb + 1) * BW)
        for c in range(8):
            op("act", [xB[c][tb]], [tmpAB[c]],
               lambda e, c=c: e.activation(out=tmpA[:, c, :], in_=X[:, c, blk], func=AF.Square))
        rstd_from_sq(tmpA, tmpAB, 8, 1.0 / D)
        for c in range(8):
            op("dve", [xB[c][tb], rstdB, gB], [xnB[c]],
               lambda e, c=c: e.scalar_tensor_tensor(out=xn[:, c, :], in0=X[:, c, blk],
                                                     scalar=gpre[:, li * 8 + c:li * 8 + c + 1], in1=rstd[:],
                                                     op0=ALU.mult, op1=ALU.mult))

    def prenorm_ap(li, tb, xn_ap, xnB, tmpA, tmpAB):
        blk = slice(tb * BW, (tb + 1) * BW)
        for c in range(8):
            op("act", [xB[c][tb]], [tmpAB[c]],
               lambda e, c=c: e.activation(out=tmpA[:, c, :], in_=X[:, c, blk], func=AF.Square))
        rstd_from_sq(tmpA, tmpAB, 8, 1.0 / D)
        for c in range(8):
            op("dve", [xB[c][tb], rstdB, gB], [xnB[c]],
               lambda e, c=c: e.scalar_tensor_tensor(out=xn_ap(c), in0=X[:, c, blk],
                                                     scalar=gpre[:, li * 8 + c:li * 8 + c + 1], in1=rstd[:],
                                                     op0=ALU.mult, op1=ALU.mult))

    def proj_ap(w, wB, col0, M, rhs_ap, rhsB, nk):
        b = k.nextps()
        wBc = wB.cols(col0, col0 + M)
        for kc in range(nk):
            op("pe", [wBc, rhsB[kc]], [psB[b]],
               lambda e, kc=kc: e.matmul(ps[b][0:M, :], lhsT=w[:, kc, col0:col0 + M], rhs=rhs_ap(kc),
                                         start=(kc == 0), stop=(kc == nk - 1)),
               inc=(kc == nk - 1))
        return b

    def proj_fm(w, wB, col0, rhs, rhsB, nk, M=P):
        b = k.nextps()
        wBc = wB.cols(col0, col0 + M)
        for kc in range(nk):
            op("pe", [wBc, rhsB[kc]], [psB[b]],
               lambda e, kc=kc: e.matmul(ps[b][0:M, :], lhsT=w[:, kc, col0:col0 + M], rhs=rhs[:, kc, :],
                                         start=(kc == 0), stop=(kc == nk - 1)),
               inc=(kc == nk - 1))
        return b

    def outproj(li, tb, G, GB, wout, woutB, ybuf, ybufB, tmpA, tmpAB):
        blk = slice(tb * BW, (tb + 1) * BW)
        for m in range(8):
            b = proj_fm(wout, woutB, m * P, G, GB, 8)
            op("act", [psB[b]], [ybufB[m]], lambda e, m=m, b=b: e.activation(out=ybuf[:, m, :], in_=ps[b][:, :], func=AF.Copy))
            op("act", [psB[b]], [tmpAB[m]], lambda e, m=m, b=b: e.activation(out=tmpA[:, m, :], in_=ps[b][:, :], func=AF.Square))
        rstd_from_sq(tmpA, tmpAB, 8, 1.0 / D)
        for m in range(8):
            op("dve", [ybufB[m], rstdB, gB], [ybufB[m]],
               lambda e, m=m: e.scalar_tensor_tensor(out=ybuf[:, m, :], in0=ybuf[:, m, :],
                                                     scalar=gpost[:, li * 8 + m:li * 8 + m + 1], in1=rstd[:],
                                                     op0=ALU.mult, op1=ALU.mult))
            op("pool", [ybufB[m], xB[m][tb]], [xB[m][tb]],
               lambda e, m=m: e.tensor_tensor(out=X[:, m, blk], in0=X[:, m, blk], in1=ybuf[:, m, :], op=ALU.add))

    def layer_sgu(li):
        st = ExitStack()
        with st:
            w_in, w_inB = load_w(st, "d_win", dw["d_w_in"], 3072)
            w_out, w_outB = load_w(st, "d_wout", dw["d_w_out"], D)
            wsf = sb(st, "wsf", [P, 16, P], F32)
            wsfB = Buf()
            tril = sb(st, "tril", [P, P], F32)
            trilB = Buf()
            wsm = sb(st, "wsm", [P, 16, P], BF16)
            wsmB = Buf()
            wsT = sb(st, "wsT", [P, 16, P], BF16)
            wsTB = bufs(16)
            bsT = sb(st, "bsT", [P, 8, P], F32)
            bsTB = Buf()
            lng = sb(st, "lng", [P, D], F32)
            lnb = sb(st, "lnb", [P, D], F32)
            lnB = Buf()
            R1 = sb(st, "R1", [P, 8, BW], F32)
            R1B = bufs(8)
            R1bf = R1[:].bitcast(BF16)
            tmpA = sb(st, "tmpA", [P, 8, BW], BF16)
            tmpAB = bufs(8)
            gu = sb(st, "gu", [P, 8, BW], BF16)
            guB = bufs(8)
            sz = sb(st, "sz", [P, 8, BW], BF16)
            szB = bufs(8)
            vtmp = sb(st, "vtmp", [P, D], F32)
            vtmpB = Buf()
            stt = sb(st, "stt", [P, 2, 6], F32)
            mv = sb(st, "mv", [P, 2], F32)
            rs1 = sb(st, "rs1", [P, 1], F32)
            sttB = Buf()
            tmpS2 = sb(st, "tmpS", [P, 2, BW], F32)
            tmpSB2 = bufs(2)

            def xn_ap(c):
                return R1bf[:, c // 2, (c % 2) * BW:(c % 2) * BW + BW]

            def vln_ap(ch, c0, c1):
                return R1bf[:, 4 + ch, c0:c1]

            xnB = [R1B[c // 2] for c in range(8)]

            dma("sp", wsf[:], dw["d_ws"], [], [wsfB])
            dma("sp", tril[:], dw["d_tril"], [], [trilB])
            for h in range(2):
                src = dw["d_bs"].rearrange("(gp h) t -> h gp t", h=2)[h]
                dma("sp", bsT[h * 64:(h + 1) * 64, :, :], src.unsqueeze(0).to_broadcast([64, 8, P]), [], [bsTB])
            dma("sp", lng[:], dw["d_lng"].to_broadcast([P, D]), [], [lnB])
            dma("sp", lnb[:], dw["d_lnb"].to_broadcast([P, D]), [], [lnB])
            op("dve", [wsfB, trilB], [wsmB],
               lambda e: e.tensor_tensor(out=wsm[:], in0=wsf[:], in1=tril[:].unsqueeze(1).to_broadcast([P, 16, P]), op=ALU.mult))
            for g in range(16):
                b = k.nextps()
                op("pe", [wsmB, identB], [psB[b]],
                   lambda e, g=g, b=b: e.matmul(ps[b][:, 0:P], lhsT=wsm[:, g, :], rhs=ident[:, :], start=True, stop=True))
                op("act", [psB[b]], [wsTB[g]], lambda e, g=g, b=b: e.activation(out=wsT[:, g, :], in_=ps[b][:, 0:P], func=AF.Copy))

            for tb in range(NBLK):
                blk = slice(tb * BW, (tb + 1) * BW)
                for c in range(8):
                    op("act", [xB[c][tb]], [tmpAB[c]],
                       lambda e, c=c: e.activation(out=tmpA[:, c, :], in_=X[:, c, blk], func=AF.Square))
                rstd_from_sq(tmpA, tmpAB, 8, 1.0 / D)
                for c in range(8):
                    op("dve", [xB[c][tb], rstdB, gB], [xnB[c]],
                       lambda e, c=c: e.scalar_tensor_tensor(out=xn_ap(c), in0=X[:, c, blk],
                                                             scalar=gpre[:, li * 8 + c:li * 8 + c + 1], in1=rstd[:],
                                                             op0=ALU.mult, op1=ALU.mult))
                for ch in range(4):
                    for half in range(2):
                        b = k.nextps()
                        for kc in range(8):
                            op("pe", [w_inB, xnB[kc]], [psB[b]],
                               lambda e, kc=kc, b=b: e.matmul(ps[b][:, :], lhsT=xn_ap(kc)[:, ch * P:(ch + 1) * P],
                                                              rhs=w_in[:, kc, D + half * BW:D + (half + 1) * BW],
                                                              start=(kc == 0), stop=(kc == 7)),
                               inc=(kc == 7))
                        op("act", [psB[b]], [vtmpB],
                           lambda e, b=b, half=half: e.activation(out=vtmp[:, half * BW:(half + 1) * BW], in_=ps[b][:, :],
                                                                  func=AF.Gelu_apprx_tanh))
                    for half in range(2):
                        op("dve", [vtmpB], [sttB], lambda e, half=half: e.bn_stats(out=stt[:, half, :], in_=vtmp[:, half * BW:(half + 1) * BW]))
                    op("dve", [sttB], [sttB], lambda e: e.bn_aggr(out=mv[:], in_=stt[:].rearrange("p a b -> p (a b)")))
                    op("act", [sttB], [sttB],
                       lambda e: e.activation(out=rs1[:], in_=mv[:, 1:2], func=AF.Sqrt, scale=1.0, bias=epsc[:, 0:1]))
                    op("dve", [sttB], [sttB], lambda e: e.reciprocal(out=rs1[:], in_=rs1[:]))
                    op("dve", [vtmpB, sttB], [vtmpB],
                       lambda e: e.tensor_scalar(out=vtmp[:], in0=vtmp[:], scalar1=mv[:, 0:1], scalar2=rs1[:, 0:1],
                                                 op0=ALU.subtract, op1=ALU.mult))
                    op("pool", [vtmpB, lnB], [vtmpB], lambda e: e.tensor_tensor(out=vtmp[:], in0=vtmp[:], in1=lng[:], op=ALU.mult))
                    op("pool", [vtmpB, lnB], [R1B[4 + ch]],
                       lambda e, ch=ch: e.tensor_tensor(out=vln_ap(ch, 0, D), in0=vtmp[:], in1=lnb[:], op=ALU.add))
                for m in range(8):
                    b = k.nextps()
                    for kc in range(8):
                        op("pe", [w_inB, xnB[kc]], [psB[b]],
                           lambda e, kc=kc, b=b, m=m: e.matmul(ps[b][:, :], lhsT=w_in[:, kc, m * P:(m + 1) * P], rhs=xn_ap(kc),
                                                               start=(kc == 0), stop=(kc == 7)), inc=(kc == 7))
                    op("act", [psB[b]], [guB[m]], lambda e, b=b, m=m: e.activation(out=gu[:, m, :], in_=ps[b][:, :], func=AF.Gelu_apprx_tanh))
                for m in range(8):
                    b = k.nextps()
                    for kc in range(8):
                        op("pe", [w_inB, xnB[kc]], [psB[b]],
                           lambda e, kc=kc, b=b, m=m: e.matmul(ps[b][:, :], lhsT=w_in[:, kc, 2 * D + m * P:2 * D + (m + 1) * P], rhs=xn_ap(kc),
                                                               start=(kc == 0), stop=(kc == 7)), inc=(kc == 7))
                    op("act", [psB[b]], [szB[m]], lambda e, b=b, m=m: e.activation(out=sz[:, m, :], in_=ps[b][:, :], func=AF.Silu))
                for m in range(8):
                    op("pool", [guB[m], szB[m]], [guB[m]], lambda e, m=m: e.tensor_tensor(out=gu[:, m, :], in0=gu[:, m, :], in1=sz[:, m, :], op=ALU.mult))
                for gp in range(8):
                    b = k.nextps()
                    n = 0
                    for ch in range(4):
                        for h in range(2):
                            g = 2 * gp + h
                            n += 1
                            op("pe", [R1B[4 + ch], wsTB[g]], [psB[b]],
                               lambda e, ch=ch, h=h, g=g, b=b: e.matmul(ps[b][h * 64:(h + 1) * 64, ch * P:(ch + 1) * P],
                                                                        lhsT=vln_ap(ch, g * 64, (g + 1) * 64), rhs=wsT[:, g, :],
                                                                        start=True, stop=True),
                               inc=(n == 8))
                    op("dve", [psB[b], bsTB], [tmpSB2[gp % 2]],
                       lambda e, b=b, gp=gp: e.tensor_tensor(out=tmpS2[:, gp % 2, :].rearrange("p (a t) -> p a t", a=4),
                                                             in0=ps[b][:, :].rearrange("p (a t) -> p a t", a=4),
                                                             in1=bsT[:, gp, :].unsqueeze(1).to_broadcast([P, 4, P]), op=ALU.add))
                    if gp > 0:
                        op("dve", [tmpSB2[(gp - 1) % 2], guB[gp - 1]], [guB[gp - 1]],
                           lambda e, gp=gp: e.tensor_tensor(out=gu[:, gp - 1, :], in0=tmpS2[:, (gp - 1) % 2, :], in1=gu[:, gp - 1, :], op=ALU.mult))
                op("dve", [tmpSB2[1], guB[7]], [guB[7]],
                   lambda e: e.tensor_tensor(out=gu[:, 7, :], in0=tmpS2[:, 1, :], in1=gu[:, 7, :], op=ALU.mult))
                outproj(li, tb, gu, guB, w_out, w_outB, R1, R1B, tmpA, tmpAB)
            k.barrier()

    def layer_swa(li):
        st = ExitStack()
        with st:
            NC_IN = 2432
            w_in, w_inB = load_w(st, "b_win", dw["b_w_in"], NC_IN)
            w_out, w_outB = load_w(st, "b_wout", dw["b_w_out"], D)
            KT2 = sb(st, "KT2", [P, 2, L], BF16)
            KTB = bufs(2, NBLK)
            VA = [[sb(st, "VA%d%d" % (kv, par), [P, 16, P], BF16) for par in range(2)] for kv in range(2)]
            VAB = bufs(2, 2, 16)
            biasT = sb(st, "biasT", [P, 16, 256], F32)
            biasB = Buf()
            esink = sb(st, "esink", [P, 16], F32)
            esinkB = Buf()
            R1 = sb(st, "R1", [P, 8, BW], F32)
            R1B = bufs(8)
            R1bf = R1[:].bitcast(BF16)
            tmpA = sb(st, "tmpA", [P, 8, BW], BF16)
            tmpAB = bufs(8)
            sz = sb(st, "sz", [P, 8, BW], BF16)
            szB = bufs(8)
            NPB = 5
            apc = [0]
            tmpP = sb(st, "tmpP", [P, NPB, 256], F32)
            tmpPB = bufs(NPB)
            PT = sb(st, "PT", [P, NPB, 256], BF16)
            PTB = bufs(NPB)
            rsb = sb(st, "rsb", [P, BW], F32)
            rsbB = bufs(2)
            tG = sb(st, "tG", [P, BW], F32)
            tGB = bufs(2)

            def xn_ap(c):
                return R1bf[:, c // 2, (c % 2) * BW:(c % 2) * BW + BW]
            xnB = [R1B[c // 2] for c in range(8)]

            def qt_ap(j, p0, p1, c0, c1):
                return R1bf[p0:p1, 4 + j // 2, (j % 2) * BW + c0:(j % 2) * BW + c1]
            qtB = [R1B[4 + j // 2] for j in range(8)]

            import os
            SK = os.environ.get("SKIP", "")
            if "a" not in SK:
                dma("sp", biasT[:], dw["b_biasT"], [], [biasB])
            if "b" not in SK:
                dma("sp", esink[:], dw["b_sinks"].to_broadcast([P, 16]), [], [esinkB])
                op("act", [esinkB], [esinkB], lambda e: e.activation(out=esink[:], in_=esink[:], func=AF.Exp))
            for kv in range(2):
                for par in range(2):
                    c0 = 64 if par == 0 else 0
                    if "c" not in SK:
                        op("pool", [], sum([[VAB[kv][par][t]] for t in range(16)], []),
                           lambda e, kv=kv, par=par, c0=c0: e.memset(VA[kv][par][:, :, c0:c0 + 64], 1.0))

            for tb in range(NBLK):
                blk = slice(tb * BW, (tb + 1) * BW)
                prenorm_ap(li, tb, xn_ap, xnB, tmpA, tmpAB)
                STG = int(os.environ.get("STG", "9"))
                for j in range(8 if STG >= 1 else 0):
                    b = proj_ap(w_in, w_inB, j * P, P, xn_ap, xnB, 8)
                    op("act", [psB[b]], [qtB[j]], lambda e, b=b, j=j: e.activation(out=qt_ap(j, 0, P, 0, BW), in_=ps[b][:, :], func=AF.Copy))
                for kv in range(2 if STG >= 2 else 0):
                    b = proj_ap(w_in, w_inB, D + kv * P, P, xn_ap, xnB, 8)
                    op("act", [psB[b]], [KTB[kv][tb]], lambda e, b=b, kv=kv: e.activation(out=KT2[:, kv, blk], in_=ps[b][:, :], func=AF.Copy))
                for ch in range(4 if STG >= 3 else 0):
                    tt = tb * 4 + ch
                    b = k.nextps()
                    for kc in range(8):
                        op("pe", [w_inB, xnB[kc]], [psB[b]],
                           lambda e, kc=kc, b=b, ch=ch: e.matmul(ps[b][:, 0:P], lhsT=xn_ap(kc)[:, ch * P:(ch + 1) * P],
                                                                 rhs=w_in[:, kc, D + 256:D + 384], start=(kc == 0), stop=(kc == 7)),
                           inc=(kc == 7))
                    for kv in range(2):
                        op("act", [psB[b]], [VAB[kv][0][tt]],
                           lambda e, b=b, kv=kv, tt=tt: e.activation(out=VA[kv][0][:, tt, 0:64], in_=ps[b][:, kv * 64:(kv + 1) * 64], func=AF.Copy))
                        op("act", [psB[b]], [VAB[kv][1][tt]],
                           lambda e, b=b, kv=kv, tt=tt: e.activation(out=VA[kv][1][:, tt, 64:128], in_=ps[b][:, kv * 64:(kv + 1) * 64], func=AF.Copy))
                for m in range(8):
                    b = proj_ap(w_in, w_inB, D + 384 + m * P, P, xn_ap, xnB, 8)
                    op("act", [psB[b]], [szB[m]], lambda e, b=b, m=m: e.activation(out=sz[:, m, :], in_=ps[b][:, :], func=AF.Silu))
                def scores(h, nbl, pi):
                    kv, par, j = h // 8, h % 2, h // 2
                    base = par * 64
                    nb = tb * 4 + nbl
                    b = k.nextps()
                    c_lo = 0 if nb > 0 else P
                    rhs_q = qt_ap(j, base, base + 64, nbl * P, (nbl + 1) * P)
                    if nb > 0:
                        tbp = (nb - 1) // 4
                        op("pe", [KTB[kv][tbp], qtB[j]], [psB[b]],
                           lambda e: e.matmul(ps[b][:, 0:P], lhsT=KT2[base:base + 64, kv, (nb - 1) * P:nb * P], rhs=rhs_q, start=True, stop=True),
                           inc=False)
                    op("pe", [KTB[kv][tb], qtB[j]], [psB[b]],
                       lambda e: e.matmul(ps[b][:, P:2 * P], lhsT=KT2[base:base + 64, kv, nb * P:(nb + 1) * P], rhs=rhs_q, start=True, stop=True))
                    op("dve", [psB[b], biasB], [tmpPB[pi]],
                       lambda e: e.scalar_tensor_tensor(out=tmpP[:, pi, c_lo:256], in0=ps[b][:, c_lo:256], scalar=0.125, in1=biasT[:, h, c_lo:256],
                                                        op0=ALU.mult, op1=ALU.add))
                    op("act", [tmpPB[pi]], [PTB[pi]],
                       lambda e: e.activation(out=PT[:, pi, c_lo:256], in_=tmpP[:, pi, c_lo:256], func=AF.Exp))

                def pv_epi(h, nbl, pi):
                    kv, par, j = h // 8, h % 2, h // 2
                    base = par * 64
                    oth = 64 - base
                    bo = 6 + (h % 2)
                    nb = tb * 4 + nbl
                    if nb > 0:
                        op("pe", [PTB[pi], VAB[kv][par][nb - 1]], [psB[bo]],
                           lambda e: e.matmul(ps[bo][:, nbl * P:(nbl + 1) * P], lhsT=VA[kv][par][:, nb - 1, :], rhs=PT[:, pi, 0:P], start=True, stop=False),
                           inc=False)
                    op("pe", [PTB[pi], VAB[kv][par][nb]], [psB[bo]],
                       lambda e: e.matmul(ps[bo][:, nbl * P:(nbl + 1) * P], lhsT=VA[kv][par][:, nb, :], rhs=PT[:, pi, P:2 * P], start=(nb == 0), stop=True))
                    if nbl == 3:
                        op("dve", [psB[bo], esinkB], [rsbB[par]],
                           lambda e: e.tensor_scalar(out=rsb[base:base + 64, :], in0=ps[bo][oth:oth + 64, :], scalar1=esink[oth:oth + 64, h:h + 1],
                                                     scalar2=None, op0=ALU.add))
                        op("act", [rsbB[par]], [rsbB[par]], lambda e: e.activation(out=rsb[base:base + 64, :], in_=rsb[base:base + 64, :], func=AF.Ln))
                        op("act", [rsbB[par]], [rsbB[par]], lambda e: e.activation(out=rsb[base:base + 64, :], in_=rsb[base:base + 64, :], func=AF.Exp, scale=-1.0))
                        op("dve", [psB[bo], rsbB[par]], [tGB[par]],
                           lambda e: e.tensor_tensor(out=tG[base:base + 64, :], in0=ps[bo][base:base + 64, :], in1=rsb[base:base + 64, :], op=ALU.mult))
                        op("pool", [tGB[par], szB[j]], [szB[j]],
                           lambda e: e.tensor_tensor(out=sz[base:base + 64, j, :], in0=tG[base:base + 64, :], in1=sz[base:base + 64, j, :], op=ALU.mult))

                import os
                tasks = [(h, nbl) for h in range(int(os.environ.get('SWA_NH', '16'))) for nbl in range(4)]
                LAS = 3
                for i in range(min(LAS, len(tasks))):
                    scores(tasks[i][0], tasks[i][1], (apc[0] + i) % NPB)
                for i, (h, nbl) in enumerate(tasks):
                    pi = apc[0] % NPB
                    apc[0] += 1
                    if i + LAS < len(tasks):
                        scores(tasks[i + LAS][0], tasks[i + LAS][1], (apc[0] + LAS - 1) % NPB)
                    pv_epi(h, nbl, pi)
                outproj(li, tb, sz, szB, w_out, w_outB, R1, R1B, tmpA, tmpAB)
            k.barrier()

    def layer_mla(li):
        SC = 96.0 ** -0.5
        st = ExitStack()
        with st:
            GT = sb(st, "GT", [P, 8, L], BF16)
            GTB = bufs(8, NBLK)
            sA = ExitStack()
            sA.__enter__()
            CQN = sb(sA, "CQN", [P, 6, L], BF16)
            CQNB = bufs(6, NBLK)
            CKVN = sb(sA, "CKVN", [P, 2, L], BF16)
            CKVNB = bufs(2, NBLK)
            KR = sb(sA, "KR", [32, L], BF16)
            KRB = bufs(NBLK)
            ROPE = sb(sA, "ROPE", [64, L], F32)
            ropeB = Buf()
            gq = sb(sA, "gq", [P, 8], F32)
            gqB = Buf()
            dma("sp", ROPE[:], dw["c_rope"], [], [ropeB])
            dma("sp", gq[:], dw["c_gqkv"], [], [gqB])
            tR = sb(sA, "tR", [32, 2, BW], F32)
            tRB = bufs(2)
            s1 = ExitStack()
            with s1:
                w1, w1B = load_w(s1, "s_c_w1", dw["c_w1"], 1088)
                R1 = sb(s1, "R1", [P, 8, BW], F32)
                R1B = bufs(8)
                xnt = sb(s1, "xnt", [P, 8, BW], BF16)
                xnB = bufs(8)
                tmpA = sb(s1, "tmpA", [P, 8, BW], BF16)
                tmpAB = bufs(8)
                xn_ap = lambda c: xnt[:, c, :]
                for tb in range(NBLK):
                    blk = slice(tb * BW, (tb + 1) * BW)
                    prenorm_ap(li, tb, xn_ap, xnB, tmpA, tmpAB)
                    for m in range(8):
                        b = proj_ap(w1, w1B, m * P, P, xn_ap, xnB, 8)
                        op("act", [psB[b]], [R1B[m]], lambda e, b=b, m=m: e.activation(out=R1[:, m, :], in_=ps[b][:, :], func=AF.Copy))
                        op("act", [psB[b]], [tmpAB[m]], lambda e, b=b, m=m: e.activation(out=tmpA[:, m, :], in_=ps[b][:, :], func=AF.Square))
                    rstd_from_sq(tmpA, tmpAB, 6, 1.0 / 768, 0)
                    for m in range(6):
                        op("dve", [R1B[m], rstdB, gqB], [CQNB[m][tb]],
                           lambda e, m=m: e.scalar_tensor_tensor(out=CQN[:, m, blk], in0=R1[:, m, :], scalar=gq[:, m:m + 1], in1=rstd[:],
                                                                 op0=ALU.mult, op1=ALU.mult))
                    rstd_from_sq(tmpA, tmpAB, 2, 1.0 / 256, 6)
                    for m in range(2):
                        op("dve", [R1B[6 + m], rstdB, gqB], [CKVNB[m][tb]],
                           lambda e, m=m: e.scalar_tensor_tensor(out=CKVN[:, m, blk], in0=R1[:, 6 + m, :], scalar=gq[:, 6 + m:7 + m], in1=rstd[:],
                                                                 op0=ALU.mult, op1=ALU.mult))
                    b = proj_ap(w1, w1B, D, 64, xn_ap, xnB, 8)
                    op("dve", [psB[b], ropeB], [tRB[0]],
                       lambda e, b=b: e.tensor_tensor(out=tR[:, 0, :], in0=ps[b][32:64, :], in1=ROPE[32:64, blk], op=ALU.mult))
                    op("dve", [psB[b], ropeB], [tRB[1]],
                       lambda e, b=b: e.tensor_tensor(out=tR[:, 1, :], in0=ps[b][0:32, :], in1=ROPE[0:32, blk], op=ALU.mult))
                    op("pool", [tRB[0], tRB[1]], [KRB[tb]],
                       lambda e: e.tensor_tensor(out=KR[:, blk], in0=tR[:, 0, :], in1=tR[:, 1, :], op=ALU.add))
                k.barrier()
            s2 = ExitStack()
            with s2:
                wuq, wuqB = load_w(s2, "s_c_wuq", dw["c_wuq"], 2048)
                wukv, wukvB = load_w(s2, "s_c_wukv", dw["c_wukv"], 2048)
                KTh = [sb(s2, "KTh%d" % i, [P, L], BF16) for i in range(2)]
                KThB = bufs(2, NBLK)
                VAh = [sb(s2, "VAh%d" % i, [P, 16, P], BF16) for i in range(2)]
                VAhB = bufs(2, 4)
                QT = [sb(s2, "QT%d" % i, [P, BW], BF16) for i in range(2)]
                QTB = bufs(2)
                NPT = 5
                LA = 3
                PT = [sb(s2, "PT%d" % i, [P, BW], BF16) for i in range(NPT)]
                PTB = bufs(NPT)
                QF = sb(s2, "QF", [64, BW], F32)
                QFB = Buf()
                maskT = sb(s2, "maskT", [P, P], BF16)
                maskB = Buf()
                rsb = sb(s2, "rsb", [P, BW], F32)
                rsbB = bufs(2)
                dma("pool", maskT[:], dw["c_maskT"], [], [maskB])
                for i in range(2):
                    op("pool", [], KThB[i], lambda e, i=i: e.memset(KTh[i][32:64, :], 0.0))
                    c0 = 64 if i == 0 else 0
                    op("pool", [], VAhB[i], lambda e, i=i, c0=c0: e.memset(VAh[i][:, :, c0:c0 + 64], 1.0))
                def kv_build(h):
                    par = h % 2
                    vc0 = par * 64
                    for tb in range(NBLK):
                        blk = slice(tb * BW, (tb + 1) * BW)
                        b = k.nextps()
                        for kc in range(2):
                            op("pe", [wukvB, CKVNB[kc][tb]], [psB[b]],
                               lambda e, kc=kc, b=b, blk=blk: e.matmul(ps[b][64:128, :], lhsT=wukv[:, kc, h * P:h * P + 64], rhs=CKVN[:, kc, blk],
                                                                      start=(kc == 0), stop=(kc == 1)), inc=(kc == 1))
                        op("act", [psB[b]], [KThB[par][tb]],
                           lambda e, b=b, blk=blk: e.activation(out=KTh[par][64:128, blk], in_=ps[b][64:128, :], func=AF.Copy))
                        op("dve", [KRB[tb]], [KThB[par][tb]],
                           lambda e, blk=blk: e.tensor_scalar(out=KTh[par][0:32, blk], in0=KR[:, blk], scalar1=1.0, scalar2=None, op0=ALU.mult))
                        b = k.nextps()
                        for i4 in range(4):
                            tt = tb * 4 + i4
                            for kc in range(2):
                                op("pe", [wukvB, CKVNB[kc][tb]], [psB[b]],
                                   lambda e, kc=kc, b=b, tt=tt, i4=i4: e.matmul(ps[b][:, i4 * 64:(i4 + 1) * 64], lhsT=CKVN[:, kc, tt * P:(tt + 1) * P],
                                                                                rhs=wukv[:, kc, h * P + 64:h * P + 128], start=(kc == 0), stop=(kc == 1)),
                                   inc=(kc == 1 and i4 == 3))
                        op("act", [psB[b]], [VAhB[par][tb]],
                           lambda e, b=b, tb=tb: e.activation(out=VAh[par][:, tb * 4:(tb + 1) * 4, vc0:vc0 + 64],
                                                              in_=ps[b][:, 0:256].rearrange("p (a c) -> p a c", a=4), func=AF.Copy))

                def q_prep(h, qb, qi):
                    qblk = slice(qb * BW, (qb + 1) * BW)
                    b = k.nextps()
                    for kc in range(6):
                        op("pe", [wuqB, CQNB[kc][qb]], [psB[b]],
                           lambda e, kc=kc, b=b: e.matmul(ps[b][:, :], lhsT=wuq[:, kc, h * P:(h + 1) * P], rhs=CQN[:, kc, qblk],
                                                          start=(kc == 0), stop=(kc == 5)), inc=(kc == 5))
                    op("act", [psB[b]], [QTB[qi]], lambda e, b=b: e.activation(out=QT[qi][:, :], in_=ps[b][:, :], func=AF.Copy))
                    op("act", [psB[b]], [QFB], lambda e, b=b: e.activation(out=QF[:, :], in_=ps[b][0:64, :], func=AF.Copy))
                    op("dve", [QFB, ropeB], [tRB[0]],
                       lambda e: e.tensor_tensor(out=tR[:, 0, :], in0=QF[32:64, :], in1=ROPE[32:64, qblk], op=ALU.mult))
                    op("dve", [QFB, ropeB], [tRB[1]],
                       lambda e: e.tensor_tensor(out=tR[:, 1, :], in0=QF[0:32, :], in1=ROPE[0:32, qblk], op=ALU.mult))
                    op("dve", [tRB[0], tRB[1]], [QTB[qi]],
                       lambda e: e.tensor_tensor(out=QT[qi][0:32, :], in0=tR[:, 0, :], in1=tR[:, 1, :], op=ALU.add))

                pti = [0]

                def attend(h, qb, qi, bo):
                    par, j = h % 2, h // 2
                    base = par * 64
                    oth = 64 - base
                    qblk = slice(qb * BW, (qb + 1) * BW)
                    nkc = 4 * qb + 4

                    def pv(kc, pi, q_lo):
                        op("pe", [PTB[pi], VAhB[par][kc // 4]], [psB[bo]],
                           lambda e: e.matmul(ps[bo][:, q_lo:BW], lhsT=VAh[par][:, kc, :], rhs=PT[pi][:, q_lo:BW],
                                              start=(kc == 0), stop=(kc == nkc - 1)))
                    pend_pv = []
                    for kc in range(nkc):
                        q_lo = max(0, kc - 4 * qb) * P
                        pi = pti[0] % NPT
                        pti[0] += 1
                        b = k.nextps()
                        op("pe", [KThB[par][kc // 4], QTB[qi]], [psB[b]],
                           lambda e, b=b, kc=kc, q_lo=q_lo: e.matmul(ps[b][:, q_lo:BW], lhsT=KTh[par][:, kc * P:(kc + 1) * P],
                                                                     rhs=QT[qi][:, q_lo:BW], start=True, stop=True))
                        op("act", [psB[b]], [PTB[pi]],
                           lambda e, b=b, pi=pi, q_lo=q_lo: e.activation(out=PT[pi][:, q_lo:BW], in_=ps[b][:, q_lo:BW], func=AF.Exp, scale=SC))
                        if kc >= 4 * qb:
                            op("dve", [PTB[pi], maskB], [PTB[pi]],
                               lambda e, pi=pi, q_lo=q_lo: e.tensor_tensor(out=PT[pi][:, q_lo:q_lo + P], in0=PT[pi][:, q_lo:q_lo + P],
                                                                           in1=maskT[:, :], op=ALU.mult))
                        pend_pv.append((kc, pi, q_lo))
                        if len(pend_pv) > LA:
                            pv(*pend_pv.pop(0))
                    while pend_pv:
                        pv(*pend_pv.pop(0))
                    op("dve", [psB[bo]], [rsbB[par]],
                       lambda e: e.tensor_scalar(out=rsb[base:base + 64, :], in0=ps[bo][oth:oth + 64, :], scalar1=1.0, scalar2=None, op0=ALU.mult))
                    op("act", [rsbB[par]], [rsbB[par]], lambda e: e.activation(out=rsb[base:base + 64, :], in_=rsb[base:base + 64, :], func=AF.Ln))
                    op("act", [rsbB[par]], [rsbB[par]], lambda e: e.activation(out=rsb[base:base + 64, :], in_=rsb[base:base + 64, :], func=AF.Exp, scale=-1.0))
                    op("dve", [psB[bo], rsbB[par]], [GTB[j][qb]],
                       lambda e: e.tensor_tensor(out=GT[base:base + 64, j, qblk], in0=ps[bo][base:base + 64, :],
                                                 in1=rsb[base:base + 64, :], op=ALU.mult))

                import os
                NH = int(os.environ.get("MLA_NH", "16"))
                tasks = [(h, qb) for h in range(NH) for qb in range(NBLK)]
                if tasks:
                    kv_build(0)
                    q_prep(0, 0, 0)
                for i, (h, qb) in enumerate(tasks):
                    if i + 1 < len(tasks):
                        h2, qb2 = tasks[i + 1]
                        if h2 != h:
                            kv_build(h2)
                        q_prep(h2, qb2, (i + 1) % 2)
                    attend(h, qb, i % 2, 6 + (i % 2))
                k.barrier()
            sA.close()
            s3 = ExitStack()
            with s3:
                wz, wzB = load_w(s3, "s_c_wz", dw["c_wz"], D)
                w_out, w_outB = load_w(s3, "s_c_wout", dw["c_w_out"], D)
                R1 = sb(s3, "R1", [P, 8, BW], F32)
                R1B = bufs(8)
                xnt = [sb(s3, "xnt%d" % i, [P, 8, BW], BF16) for i in range(2)]
                xnB = bufs(2, 8)
                tmpA = sb(s3, "tmpA", [P, 8, BW], BF16)
                tmpAB = bufs(8)
                tmpX = sb(s3, "tmpX", [P, 8, BW], BF16)
                tmpXB = bufs(8)
                sz = [sb(s3, "sz%d" % i, [P, 8, BW], BF16) for i in range(2)]
                szB = bufs(2, 8)

                def front(tb):
                    i = tb % 2
                    blk = slice(tb * BW, (tb + 1) * BW)
                    xn_ap = lambda c: xnt[i][:, c, :]
                    prenorm_ap(li, tb, xn_ap, xnB[i], tmpX, tmpXB)
                    for m in range(8):
                        b = proj_ap(wz, wzB, m * P, P, xn_ap, xnB[i], 8)
                        op("act", [psB[b]], [szB[i][m]], lambda e, b=b, m=m: e.activation(out=sz[i][:, m, :], in_=ps[b][:, :], func=AF.Silu))
                        op("pool", [szB[i][m], GTB[m][tb]], [szB[i][m]],
                           lambda e, m=m: e.tensor_tensor(out=sz[i][:, m, :], in0=sz[i][:, m, :], in1=GT[:, m, blk], op=ALU.mult))

                front(0)
                for tb in range(NBLK):
                    if tb + 1 < NBLK:
                        front(tb + 1)
                    outproj(li, tb, sz[tb % 2], szB[tb % 2], w_out, w_outB, R1, R1B, tmpA, tmpAB)
                k.barrier()

    def layer_s5(li):
        TT = ALU
        st = ExitStack()
        with st:
            UY = sb(st, "UY", [P, 8, L], BF16)
            UYB = bufs(8, NBLK)
            dT = sb(st, "dT", [P, 16], F32)
            dTB = Buf()
            dma("sp", dT[:], dw["a_dg"], [], [dTB])
            sP = ExitStack()
            sP.__enter__()
            NJ = 32
            BrL, BrLB = sb(sP, "BrL", [P, NJ, P], BF16), Buf()
            BiL, BiLB = sb(sP, "BiL", [P, NJ, P], BF16), Buf()
            CrP, CrPB = sb(sP, "CrP", [P, NJ, P], BF16), Buf()
            CiP, CiPB = sb(sP, "CiP", [P, NJ, P], BF16), Buf()
            for t_, B_, nm in ((BrL, BrLB, "a_brl"), (BiL, BiLB, "a_bil"), (CrP, CrPB, "a_crp"), (CiP, CiPB, "a_cip")):
                for q4 in range(4):
                    dma("pool", t_[:, q4 * 8:(q4 + 1) * 8, :], dw[nm][:, q4 * 8:(q4 + 1) * 8, :], [], [B_])
            lam = sb(sP, "lam", [P, 3, NJ], F32)
            prepB = Buf()
            dma("sp", lam[:], dw["a_lam"], [], [prepB])
            W = {}
            for nm in ("dt", "lrdt", "th", "mag", "t", "t2", "c", "s", "q", "c2", "s2", "cs", "ar", "ai", "den", "nr", "fr", "fi",
                       "u1", "u2", "ir", "ii", "pr", "pi", "A128r", "nA128i", "A128i"):
                W[nm] = sb(sP, "w_" + nm, [P, NJ], F32)
            lr, li_, ldt = lam[:, 0, :], lam[:, 1, :], lam[:, 2, :]

            TPr = sb(sP, "TPr", [P, NJ, P], BF16)
            TPi = sb(sP, "TPi", [P, NJ, P], BF16)
            TNr = sb(sP, "TNr", [P, NJ, P], BF16)
            TNi = sb(sP, "TNi", [P, NJ, P], BF16)
            car = sb(sP, "car", [P, 2, NJ], F32)
            carB = bufs(NJ)
            s1 = ExitStack()
            with s1:
                wu, wuB = load_w(s1, "a_wu", dw["a_wu"], D)
                xnt = sb(s1, "xnt", [P, 8, BW], BF16)
                xnB = bufs(8)
                tmpA = sb(s1, "tmpA", [P, 8, BW], BF16)
                tmpAB = bufs(8)
                m1 = sb(s1, "m1t", [P, NJ, 16], F32)
                m2 = sb(s1, "m2t", [P, NJ, 16], F32)
                prep_ops = []

                def dop(e_, r_, w_, fn_):
                    prep_ops.append((e_, r_, w_, fn_))

                def tt(o, a, b_, o_):
                    dop("dve", [prepB], [prepB], lambda e: e.tensor_tensor(out=o, in0=a, in1=b_, op=o_))

                def ts(o, a, s1_, s2_, o1, o2=None):
                    if o2 is None:
                        dop("dve", [prepB], [prepB], lambda e: e.tensor_scalar(out=o, in0=a, scalar1=s1_, scalar2=None, op0=o1))
                    else:
                        dop("dve", [prepB], [prepB], lambda e: e.tensor_scalar(out=o, in0=a, scalar1=s1_, scalar2=s2_, op0=o1, op1=o2))

                def stt(o, a, sc, b_, o1, o2):
                    dop("dve", [prepB], [prepB], lambda e: e.tensor_scalar(out=o, in0=a, scalar1=sc, scalar2=None, op0=o1))
                    dop("dve", [prepB], [prepB], lambda e: e.tensor_tensor(out=o, in0=o, in1=b_, op=o2))

                def csq(cr, ci):
                    tt(W["c2"][:], cr, cr, TT.mult)
                    tt(W["s2"][:], ci, ci, TT.mult)
                    tt(W["cs"][:], cr, ci, TT.mult)
                    tt(cr, W["c2"][:], W["s2"][:], TT.subtract)
                    ts(ci, W["cs"][:], 2.0, None, TT.mult)

                dop("act", [prepB], [prepB], lambda e: e.activation(out=W["dt"][:], in_=ldt, func=AF.Exp))
                tt(W["lrdt"][:], lr, W["dt"][:], TT.mult)
                tt(W["th"][:], li_, W["dt"][:], TT.mult)
                dop("act", [prepB], [prepB], lambda e: e.activation(out=W["mag"][:], in_=W["lrdt"][:], func=AF.Exp))
                ts(W["t"][:], W["th"][:], 1.0 / 64, None, TT.mult)
                tt(W["t2"][:], W["t"][:], W["t"][:], TT.mult)
                ts(W["q"][:], W["t2"][:], -1.0 / 720, None, TT.mult)
                stt(W["q"][:], W["q"][:], 1.0 / 24, W["t2"][:], TT.add, TT.mult)
                stt(W["q"][:], W["q"][:], -0.5, W["t2"][:], TT.add, TT.mult)
                ts(W["c"][:], W["q"][:], 1.0, None, TT.add)
                ts(W["q"][:], W["t2"][:], -1.0 / 5040, None, TT.mult)
                stt(W["q"][:], W["q"][:], 1.0 / 120, W["t2"][:], TT.add, TT.mult)
                stt(W["q"][:], W["q"][:], -1.0 / 6, W["t2"][:], TT.add, TT.mult)
                stt(W["s"][:], W["q"][:], 1.0, W["t"][:], TT.add, TT.mult)
                for _ in range(6):
                    csq(W["c"][:], W["s"][:])
                tt(W["ar"][:], W["mag"][:], W["c"][:], TT.mult)
                tt(W["ai"][:], W["mag"][:], W["s"][:], TT.mult)
                tt(W["den"][:], lr, lr, TT.mult)
                tt(W["u1"][:], li_, li_, TT.mult)
                tt(W["den"][:], W["den"][:], W["u1"][:], TT.add)
                dop("act", [prepB], [prepB], lambda e: e.activation(out=W["den"][:], in_=W["den"][:], func=AF.Ln))
                dop("act", [prepB], [prepB], lambda e: e.activation(out=W["den"][:], in_=W["den"][:], func=AF.Exp, scale=-1.0))
                ts(W["nr"][:], W["ar"][:], -1.0, None, TT.add)
                tt(W["u1"][:], W["nr"][:], lr, TT.mult)
                tt(W["u2"][:], W["ai"][:], li_, TT.mult)
                tt(W["u1"][:], W["u1"][:], W["u2"][:], TT.add)
                tt(W["fr"][:], W["u1"][:], W["den"][:], TT.mult)
                tt(W["u1"][:], W["ai"][:], lr, TT.mult)
                tt(W["u2"][:], W["nr"][:], li_, TT.mult)
                tt(W["u1"][:], W["u1"][:], W["u2"][:], TT.subtract)
                tt(W["fi"][:], W["u1"][:], W["den"][:], TT.mult)
                dop("act", [prepB], [prepB], lambda e: e.activation(out=W["u1"][:], in_=W["lrdt"][:], func=AF.Exp, scale=-2.0))
                tt(W["ir"][:], W["ar"][:], W["u1"][:], TT.mult)
                tt(W["ii"][:], W["ai"][:], W["u1"][:], TT.mult)
                ts(W["ii"][:], W["ii"][:], -1.0, None, TT.mult)
                dop("dve", [prepB], [prepB], lambda e: e.memset(TPr[:, :, 0:1], 1.0))
                dop("dve", [prepB], [prepB], lambda e: e.memset(TPi[:, :, 0:1], 0.0))
                ts(TNr[:, :, 0:1], W["fr"][:].unsqueeze(2), 1.0, None, TT.mult)
                ts(TNi[:, :, 0:1], W["fi"][:].unsqueeze(2), 1.0, None, TT.mult)
                for (Tr_, Ti_, pr0, pi0) in ((TPr, TPi, "ar", "ai"), (TNr, TNi, "ir", "ii")):
                    ts(W["pr"][:], W[pr0][:], 1.0, None, TT.mult)
                    ts(W["pi"][:], W[pi0][:], 1.0, None, TT.mult)
                    for kk in range(7):
                        n = 1 << kk
                        for c0 in range(0, n, 16):
                            w = min(16, n - c0)
                            pr_b = W["pr"][:].unsqueeze(2).to_broadcast([P, NJ, w])
                            pi_b = W["pi"][:].unsqueeze(2).to_broadcast([P, NJ, w])
                            lo_r, lo_i = Tr_[:, :, c0:c0 + w], Ti_[:, :, c0:c0 + w]
                            tt(m1[:, :, 0:w], lo_r, pr_b, TT.mult)
                            tt(m2[:, :, 0:w], lo_i, pi_b, TT.mult)
                            tt(Tr_[:, :, n + c0:n + c0 + w], m1[:, :, 0:w], m2[:, :, 0:w], TT.subtract)
                            tt(m1[:, :, 0:w], lo_r, pi_b, TT.mult)
                            tt(m2[:, :, 0:w], lo_i, pr_b, TT.mult)
                            tt(Ti_[:, :, n + c0:n + c0 + w], m1[:, :, 0:w], m2[:, :, 0:w], TT.add)
                        csq(W["pr"][:], W["pi"][:])
                    if pr0 == "ar":
                        ts(W["A128r"][:], W["pr"][:], 1.0, None, TT.mult)
                        ts(W["nA128i"][:], W["pi"][:], -1.0, None, TT.mult)
                        ts(W["A128i"][:], W["pi"][:], 1.0, None, TT.mult)
                dop("dve", [prepB], [prepB], lambda e: e.tensor_scalar(out=TNi[:], in0=TNi[:], scalar1=-1.0, scalar2=None, op0=TT.mult))
                dop("dve", [prepB], carB, lambda e: e.memset(car[:], 0.0))
                xn_ap = lambda c: xnt[:, c, :]
                npo = len(prep_ops)
                for tb in range(NBLK):
                    blk = slice(tb * BW, (tb + 1) * BW)
                    prenorm_ap(li, tb, xn_ap, xnB, tmpA, tmpAB)
                    for (e_, r_, w_, fn_) in prep_ops[tb * npo // NBLK:(tb + 1) * npo // NBLK]:
                        op(e_, r_, w_, fn_)
                    for m in range(8):
                        b = proj_ap(wu, wuB, m * P, P, xn_ap, xnB, 8)
                        op("act", [psB[b]], [UYB[m][tb]], lambda e, b=b, m=m, blk=blk: e.activation(out=UY[:, m, blk], in_=ps[b][:, :], func=AF.Copy))
                k.barrier()
            s2 = ExitStack()
            with s2:
                NA, NCD, NQ = 2, 2, 2
                TA = [sb(s2, "TA%d" % i, [P, 4, P], BF16) for i in range(NA)]
                TBf = [sb(s2, "TB%d" % i, [P, 4, P], BF16) for i in range(NA)]
                CD = [sb(s2, "CD%d" % i, [P, 2, 4, P], F32) for i in range(NCD)]
                KBt = [sb(s2, "KB%d" % i, [P, 4, 4, P], BF16) for i in range(2)]
                KBB = bufs(2, 4)
                ntriu = sb(s2, "ntriu", [P, P], BF16)
                triu = sb(s2, "triu", [P, P], BF16)
                triuB = Buf()
                dma("pool", triu[:], dw["a_triu"], [], [triuB])
                dma("pool", ntriu[:], dw["a_ntriu"], [], [triuB])
                for T_ in (TNr, TNi):
                    for j4 in range(8):
                        b = k.nextps()
                        for jl in range(4):
                            op("pe", [prepB, identB], [psB[b]],
                               lambda e, b=b, T_=T_, j4=j4, jl=jl: e.matmul(ps[b][:, jl * P:(jl + 1) * P], lhsT=T_[:, j4 * 4 + jl, :], rhs=ident[:, :],
                                                                            start=True, stop=True), inc=(jl == 3))
                        op("act", [psB[b]], [prepB],
                           lambda e, b=b, T_=T_, j4=j4: e.activation(out=T_[:, j4 * 4:j4 * 4 + 4, :].rearrange("p a t -> p (a t)"), in_=ps[b][:, :], func=AF.Copy))
                Q = [[sb(s2, "Q%d_%d" % (i, q_), [P, 4, P], BF16) for q_ in range(4)] for i in range(NQ)]
                AB_, BB_, CDB = bufs(NA), bufs(NA), bufs(NCD, 2, 4)
                QB = bufs(NQ, 4)
                ea = sb(s2, "ea", [P, 2, 4], F32)
                eb = sb(s2, "eb", [P, 2, 4], F32)
                eB = Buf()
                eaB, ebB = Buf(), Buf()
                ytmp = sb(s2, "ytmp", [P, BW], F32)
                yB = Buf()

                def flat(t_):
                    return t_[:].rearrange("p a t -> p (a t)")

                units = [(chc, c) for cp in range(4) for c in range(16) for chc in (2 * cp, 2 * cp + 1)]
                NU = len(units)

                def stA(u):
                    chc, c = units[u]
                    j0, qt = chc * 4, c // 4
                    cols = slice(c * P, (c + 1) * P)
                    ai = u % NA
                    ba = k.nextps()
                    bb = k.nextps()
                    op("pe", [BrLB, UYB[chc][qt]], [psB[ba]],
                       lambda e: e.matmul(ps[ba][:, :], lhsT=UY[:, chc, cols], rhs=BrL[:, j0:j0 + 4, :].rearrange("p a t -> p (a t)"), start=True, stop=True))
                    op("pe", [BiLB, UYB[chc][qt]], [psB[bb]],
                       lambda e: e.matmul(ps[bb][:, :], lhsT=UY[:, chc, cols], rhs=BiL[:, j0:j0 + 4, :].rearrange("p a t -> p (a t)"), start=True, stop=True))
                    op("act", [psB[ba]], [AB_[ai]], lambda e: e.activation(out=flat(TA[ai]), in_=ps[ba][:, :], func=AF.Copy))
                    op("act", [psB[bb]], [BB_[ai]], lambda e: e.activation(out=flat(TBf[ai]), in_=ps[bb][:, :], func=AF.Copy))

                def ctx(u):
                    chc, c = units[u]
                    j0 = chc * 4
                    ai, ci, qi = u % NA, u % NCD, u % NQ
                    return chc, c, j0, ai, ci, qi

                def stB_mul(u):
                    chc, c, j0, ai, ci, qi = ctx(u)
                    A, B_ = TA[ai], TBf[ai]
                    kb = KBt[u % 2]
                    tnr, ntni = TNr[:, j0:j0 + 4, :], TNi[:, j0:j0 + 4, :]
                    op("dve", [AB_[ai], prepB], [KBB[u % 2][0]], lambda e: e.tensor_tensor(out=kb[:, 0, :, :], in0=A[:], in1=tnr, op=TT.mult))
                    op("dve", [BB_[ai], prepB], [KBB[u % 2][1]], lambda e: e.tensor_tensor(out=kb[:, 1, :, :], in0=B_[:], in1=ntni, op=TT.mult))
                    op("dve", [AB_[ai], prepB], [KBB[u % 2][2]], lambda e: e.tensor_tensor(out=kb[:, 2, :, :], in0=A[:], in1=ntni, op=TT.mult))
                    op("dve", [BB_[ai], prepB], [KBB[u % 2][3]], lambda e: e.tensor_tensor(out=kb[:, 3, :, :], in0=B_[:], in1=tnr, op=TT.mult))

                def stT_cs(u):
                    chc, c, j0, ai, ci, qi = ctx(u)
                    kt = KBt[u % 2]
                    for ri in range(2):
                        b = k.nextps()
                        for jl in range(4):
                            op("pe", [KBB[u % 2], triuB], [psB[b]],
                               lambda e, b=b, ri=ri, jl=jl: e.matmul(ps[b][:, jl * P:(jl + 1) * P], lhsT=kt[:, 2 * ri, jl, :], rhs=triu[:, :], start=True, stop=False),
                               inc=False)
                            op("pe", [KBB[u % 2], triuB], [psB[b]],
                               lambda e, b=b, ri=ri, jl=jl: e.matmul(ps[b][:, jl * P:(jl + 1) * P], lhsT=kt[:, 2 * ri + 1, jl, :],
                                                                     rhs=(triu if ri == 0 else ntriu)[:, :], start=False, stop=True),
                               inc=(jl == 3))
                        for jl in range(4):
                            op("act", [psB[b], carB[j0], carB[j0 + 1]], [CDB[ci][ri][jl]],
                               lambda e, b=b, ri=ri, jl=jl: e.activation(out=CD[ci][:, ri, jl, :], in_=ps[b][:, jl * P:(jl + 1) * P], func=AF.Identity,
                                                                         bias=car[:, ri, j0 + jl:j0 + jl + 1], scale=1.0))

                def stD_e(u):
                    chc, c, j0, ai, ci, qi = ctx(u)
                    if c < 15:
                        xl = CD[ci][:, :, :, P - 1]
                        a_r = W["A128r"][:, j0:j0 + 4].unsqueeze(1).to_broadcast([P, 2, 4])
                        a_i = W["A128i"][:, j0:j0 + 4].unsqueeze(1).to_broadcast([P, 2, 4])
                        op("dve", [CDB[ci], prepB], [eaB], lambda e: e.tensor_tensor(out=ea[:], in0=xl, in1=a_r, op=TT.mult))
                        op("dve", [CDB[ci], prepB], [ebB], lambda e: e.tensor_tensor(out=eb[:], in0=xl, in1=a_i, op=TT.mult))

                def stD_q3(u):
                    chc, c, j0, ai, ci, qi = ctx(u)
                    C = CD[ci][:, 0, :, :]
                    tpi = TPi[:, j0:j0 + 4, :]
                    op("dve", [CDB[ci], prepB], [QB[qi][2]],
                       lambda e: e.scalar_tensor_tensor(out=Q[qi][2][:], in0=C, scalar=-1.0, in1=tpi, op0=TT.mult, op1=TT.mult))

                def stD_car(u):
                    chc, c, j0, ai, ci, qi = ctx(u)
                    if c < 15:
                        op("dve", [eaB, ebB], [carB[j0]], lambda e: e.tensor_tensor(out=car[:, 0, j0:j0 + 4], in0=ea[:, 0, :], in1=eb[:, 1, :], op=TT.add))
                        op("dve", [eaB, ebB], [carB[j0 + 1]], lambda e: e.tensor_tensor(out=car[:, 1, j0:j0 + 4], in0=ea[:, 1, :], in1=eb[:, 0, :], op=TT.subtract))

                def stD_rest(u):
                    chc, c, j0, ai, ci, qi = ctx(u)
                    qt, cq = c // 4, c % 4
                    C, Dd = CD[ci][:, 0, :, :], CD[ci][:, 1, :, :]
                    bo = 6 + (chc % 2)
                    tpr, tpi = TPr[:, j0:j0 + 4, :], TPi[:, j0:j0 + 4, :]
                    q = Q[qi]
                    op("dve", [CDB[ci], prepB], [QB[qi][0]], lambda e: e.tensor_tensor(out=q[0][:], in0=C, in1=tpr, op=TT.mult))
                    op("dve", [CDB[ci], prepB], [QB[qi][1]], lambda e: e.tensor_tensor(out=q[1][:], in0=Dd, in1=tpi, op=TT.mult))
                    op("dve", [CDB[ci], prepB], [QB[qi][3]], lambda e: e.tensor_tensor(out=q[3][:], in0=Dd, in1=tpr, op=TT.mult))

                def stE_mm(u):
                    chc, c, j0, ai, ci, qi = ctx(u)
                    qt, cq = c // 4, c % 4
                    bo = 6 + (chc % 2)
                    q = Q[qi]
                    n = 0
                    for jl in range(4):
                        j = j0 + jl
                        for qq in range(4):
                            wt, wtB = (CrP, CrPB) if qq < 2 else (CiP, CiPB)
                            n += 1
                            op("pe", [QB[qi][qq], wtB], [psB[bo]],
                               lambda e, j=j, jl=jl, qq=qq, wt=wt, n=n: e.matmul(ps[bo][:, cq * P:(cq + 1) * P], lhsT=wt[:, j, :], rhs=q[qq][:, jl, :],
                                                                                  start=(n == 1), stop=(n == 16)), inc=(n == 16))
                    if cq == 3:
                        blk = slice(qt * BW, (qt + 1) * BW)
                        op("act", [psB[bo]], [yB], lambda e: e.activation(out=ytmp[:], in_=ps[bo][:, :], func=AF.Copy))
                        op("dve", [yB, UYB[chc][qt], dTB], [yB],
                           lambda e: e.scalar_tensor_tensor(out=ytmp[:], in0=UY[:, chc, blk], scalar=dT[:, chc:chc + 1], in1=ytmp[:],
                                                            op0=TT.mult, op1=TT.add))
                        op("act", [yB], [UYB[chc][qt]], lambda e: e.activation(out=UY[:, chc, blk], in_=ytmp[:], func=AF.Gelu_apprx_tanh))

                def ok(u):
                    return 0 <= u < NU

                for i in range(NU + 4):
                    if ok(i):
                        stA(i)
                    if ok(i - 2):
                        stT_cs(i - 2)
                    if ok(i - 4):
                        stE_mm(i - 4)
                    if ok(i - 1):
                        stB_mul(i - 1)
                    if ok(i - 3):
                        stD_e(i - 3)
                        stD_q3(i - 3)
                        stD_car(i - 3)
                        stD_rest(i - 3)
                k.barrier()
            sP.close()
            s3 = ExitStack()
            with s3:
                wz, wzB = load_w(s3, "a_wzs", dw["a_wz"], D)
                wg, wgB = load_w(s3, "a_wgs", dw["a_wglu"], D)
                w_out, w_outB = load_w(s3, "a_wouts", dw["a_w_out"], D)
                R1 = sb(s3, "R1", [P, 8, BW], F32)
                R1B = bufs(8)
                xnt = sb(s3, "xnt", [P, 8, BW], BF16)
                xnB = bufs(8)
                tmpA = sb(s3, "tmpA", [P, 8, BW], BF16)
                tmpAB = bufs(8)
                sz = sb(s3, "sz", [P, 8, BW], BF16)
                szB = bufs(8)
                sg = sb(s3, "sg", [P, 2, BW], BF16)
                sgB = bufs(2)
                xn_ap = lambda c: xnt[:, c, :]
                for tb in range(NBLK):
                    blk = slice(tb * BW, (tb + 1) * BW)
                    prenorm_ap(li, tb, xn_ap, xnB, tmpA, tmpAB)
                    uy_ap = lambda c, blk=blk: UY[:, c, blk]
                    uyB_t = [UYB[c][tb] for c in range(8)]
                    for m in range(8):
                        b = proj_ap(wz, wzB, m * P, P, xn_ap, xnB, 8)
                        op("act", [psB[b]], [szB[m]], lambda e, b=b, m=m: e.activation(out=sz[:, m, :], in_=ps[b][:, :], func=AF.Silu))
                        b = proj_ap(wg, wgB, m * P, P, uy_ap, uyB_t, 8)
                        op("act", [psB[b], dTB], [sgB[m % 2]],
                           lambda e, b=b, m=m: e.activation(out=sg[:, m % 2, :], in_=ps[b][:, :], func=AF.Sigmoid, bias=dT[:, 8 + m:9 + m], scale=1.0))
                        op("pool", [szB[m], uyB_t[m]], [szB[m]],
                           lambda e, m=m, blk=blk: e.tensor_tensor(out=sz[:, m, :], in0=sz[:, m, :], in1=UY[:, m, blk], op=ALU.mult))
                        op("pool", [szB[m], sgB[m % 2]], [szB[m]],
                           lambda e, m=m: e.tensor_tensor(out=sz[:, m, :], in0=sz[:, m, :], in1=sg[:, m % 2, :], op=ALU.mult))
                    outproj(li, tb, sz, szB, w_out, w_outB, R1, R1B, tmpA, tmpAB)
                k.barrier()

    for li in layers:
        if li == 3:
            layer_sgu(li)
        elif li == 1:
            layer_swa(li)
        elif li == 2:
            layer_mla(li)
        elif li == 0:
            layer_s5(li)

    toks = []
    for c in range(8):
        for tb in range(NBLK):
            toks.append(dma("sp", outT_d[c * P:(c + 1) * P, tb * BW:(tb + 1) * BW], X[:, c, tb * BW:(tb + 1) * BW], [xB[c][tb]], []))
    k._wait("sp", toks)
    k.barrier()


def host_inputs(inp, layers):
    f = lambda a: np.ascontiguousarray(np.asarray(a, dtype=np.float32))
    common = {}
    common["gpre"] = f(np.asarray(inp["pre_norm"]).reshape(4, 8, P).transpose(2, 0, 1).reshape(P, 32))
    common["gpost"] = f(np.asarray(inp["post_norm"]).reshape(4, 8, P).transpose(2, 0, 1).reshape(P, 32))
    common["ident"] = np.eye(P, dtype=np.float32)
    if 3 in layers:
        common["d_w_in"] = f(inp["d_w_in"][0])
        common["d_w_out"] = f(inp["d_w_out"][0])
        common["d_ws"] = f(np.asarray(inp["d_w_s"][0]).transpose(1, 0, 2))
        common["d_tril"] = np.tril(np.ones((P, P), dtype=np.float32))
        common["d_bs"] = f(inp["d_b_s"][0])
        common["d_lng"] = f(inp["d_ln_g"])
        common["d_lnb"] = f(inp["d_ln_b"])
    if 1 in layers:
        w = np.asarray(inp["b_w_in"][0], dtype=np.float32)
        q, kk, v, z = w[:, :1024], w[:, 1024:1152], w[:, 1152:1280], w[:, 1280:]
        common["b_w_in"] = f(np.concatenate([q, kk[:, :64], kk[:, :64], kk[:, 64:], kk[:, 64:], v, z], axis=1))
        common["b_w_out"] = f(inp["b_w_out"][0])
        common["b_sinks"] = f(inp["b_sinks"])
        def bucket(d):
            if d < 16:
                return d
            v_ = 16 + int(math.log(max(d, 1) / 16.0) / math.log(128 / 16.0) * 16)
            return min(v_, 31)
        rb = np.asarray(inp["rel_bias"], dtype=np.float32)
        bt = np.full((P, 16, 256), -1e30, dtype=np.float32)
        for kj in range(P):
            for qi in range(P):
                d_prev = qi + P - kj
                if d_prev < P:
                    bt[kj, :, qi] = rb[bucket(d_prev), :]
                d_cur = qi - kj
                if d_cur >= 0:
                    bt[kj, :, P + qi] = rb[bucket(d_cur), :]
        common["b_biasT"] = bt
    if 2 in layers:
        w = np.asarray(inp["c_w_in"][0], dtype=np.float32)
        kr = w[:, 1024:1056]
        common["c_w1"] = f(np.concatenate([w[:, :1024], kr, kr[:, 16:], kr[:, :16]], axis=1))
        common["c_wz"] = f(w[:, 1056:])
        uq = np.asarray(inp["c_w_uq"][0], dtype=np.float32).reshape(768, 16, 96)
        nope, rp = uq[:, :, :64], uq[:, :, 64:]
        common["c_wuq"] = f(np.concatenate([rp, rp[:, :, 16:], rp[:, :, :16], nope], axis=2).reshape(768, 2048))
        common["c_wukv"] = f(inp["c_w_ukv"][0])
        common["c_w_out"] = f(inp["c_w_out"][0])
        inv = (np.float32(10000.0) ** (-np.arange(0, 32, 2, dtype=np.float32) / np.float32(32))).astype(np.float32)
        ang = (np.arange(L, dtype=np.float32)[:, None] * inv[None, :]).astype(np.float32)
        cos, sin = np.cos(ang).astype(np.float32).T, np.sin(ang).astype(np.float32).T
        common["c_rope"] = f(np.concatenate([cos, cos, -sin, sin], axis=0))
        g = np.concatenate([np.asarray(inp["c_q_norm"][0]), np.asarray(inp["c_kv_norm"][0])]).astype(np.float32)
        common["c_gqkv"] = f(g.reshape(8, P).T)
        common["c_maskT"] = np.triu(np.ones((P, P), dtype=np.float32))
    if 0 in layers:
        w = np.asarray(inp["a_w_in"][0], dtype=np.float32)
        common["a_wu"] = f(w[:, :1024])
        common["a_wz"] = f(w[:, 1024:])
        common["a_wglu"] = f(inp["a_w_glu"][0])
        common["a_w_out"] = f(inp["a_w_out"][0])
        dg = np.concatenate([np.asarray(inp["a_d"][0]).reshape(8, P).T, np.asarray(inp["a_b_glu"][0]).reshape(8, P).T], axis=1)
        common["a_dg"] = f(dg)
        lam = np.stack([np.asarray(inp["a_lam_re"][0]).reshape(32, P).T, np.asarray(inp["a_lam_im"][0]).reshape(32, P).T,
                        np.repeat(np.asarray(inp["a_log_dt"][0]), 64).reshape(32, P).T], axis=1)
        common["a_lam"] = f(lam)
        brl = np.zeros((P, 32, P), np.float32); bil = np.zeros((P, 32, P), np.float32)
        crp = np.zeros((P, 32, P), np.float32); cip = np.zeros((P, 32, P), np.float32)
        b_re, b_im = np.asarray(inp["a_b_re"][0]), np.asarray(inp["a_b_im"][0])
        c_re, c_im = np.asarray(inp["a_c_re"][0]), np.asarray(inp["a_c_im"][0])
        for j in range(32):
            for gl in range(2):
                g = 2 * j + gl
                r0 = 32 * (j % 4) + gl * 16
                brl[r0:r0 + 16, j, gl * 64:(gl + 1) * 64] = b_re[g].T
                bil[r0:r0 + 16, j, gl * 64:(gl + 1) * 64] = b_im[g].T
                crp[gl * 64:(gl + 1) * 64, j, r0:r0 + 16] = c_re[g].T
                cip[gl * 64:(gl + 1) * 64, j, r0:r0 + 16] = c_im[g].T
        common["a_brl"], common["a_bil"], common["a_crp"], common["a_cip"] = brl, bil, crp, cip
        common["a_triu"] = np.triu(np.ones((P, P), dtype=np.float32))
        common["a_ntriu"] = -np.triu(np.ones((P, P), dtype=np.float32))
    return common


def run(inp, layers=(0, 1, 2, 3), cores=8, trace=False):
    nc = bass.Bass("TRN2", target_bir_lowering=False)
    build(nc, list(layers))
    common = host_inputs(inp, list(layers))
    x = np.asarray(inp["x"], dtype=np.float32)
    in_maps = []
    for b in range(cores):
        m = dict(common)
        m["xT"] = np.ascontiguousarray(x[b].T)
        in_maps.append(m)
    res = run_bass_kernel_spmd(nc, in_maps, core_ids=list(range(cores)), trace=trace)
    out = np.stack([np.ascontiguousarray(r["outT"].T) for r in res.results], axis=0)
    return out.astype(np.float32), res


def kernel(**inputs):
    out, _ = run(inputs)
    return out
```

```python
import math
import numpy as np
from contextlib import ExitStack
import concourse.bass as bass
import concourse.mybir as mybir
from concourse.bass_utils import run_bass_kernel_spmd

F32 = mybir.dt.float32
BF16 = mybir.dt.bfloat16
ALU = mybir.AluOpType
AF = mybir.ActivationFunctionType

P = 128
L = 2048
D = 1024
NBLK = 4
BW = 512
EPS = 1e-6
SELF_SYNC = True
NDS = 12


class Buf:
    __slots__ = ("w", "r")

    def __init__(self):
        self.w = None
        self.r = {}


class WB:
    def __init__(self):
        self.blocks = []

    def cols(self, c0, c1):
        return [b for (a0, a1, bl) in self.blocks if a0 < c1 and c0 < a1 for b in bl]

    def all(self):
        return [b for (_, _, bl) in self.blocks for b in bl]


def _flat(lst):
    out = []
    for b in lst:
        if isinstance(b, WB):
            out.extend(b.all())
        elif isinstance(b, list):
            out.extend(_flat(b))
        else:
            out.append(b)
    return out


def bufs(*shape):
    if len(shape) == 1:
        return [Buf() for _ in range(shape[0])]
    return [bufs(*shape[1:]) for _ in range(shape[0])]


class KB:
    def __init__(self, nc, es):
        self.nc = nc
        self.E = dict(pe=nc.tensor, act=nc.scalar, dve=nc.vector, pool=nc.gpsimd, sp=nc.sync)
        self.sem = {e: es.enter_context(nc.semaphore("s_" + e)) for e in ("pe", "act", "dve", "pool")}
        self.cnt = {e: 0 for e in self.sem}
        self.pend = {e: False for e in self.sem}
        self.dsem = {q: [[es.enter_context(nc.semaphore("d_%s%d" % (q, i))), 0] for i in range(NDS)]
                     for q in ("sp", "pool")}
        self.dcnt = {"sp": 0, "pool": 0}
        self.seen = {e: {} for e in self.E}
        self.ps = []
        self.psB = []
        self.psrot = 0
        self.nrot = 6

    def _semh(self, key):
        if isinstance(key, str):
            return self.sem[key]
        return self.dsem[key[0]][key[1]][0]

    def _wait(self, e, toks):
        need = {}
        for key, v in toks:
            if need.get(key, 0) < v:
                need[key] = v
        for key, v in need.items():
            if key == e and (e == "pe" or not SELF_SYNC):
                continue
            if self.seen[e].get(key, 0) >= v:
                continue
            self.E[e].wait_ge(self._semh(key), v)
            self.seen[e][key] = v

    def _deps(self, reads, writes):
        toks = []
        for b in reads:
            if b.w is not None:
                toks.append(b.w)
        for b in writes:
            if b.w is not None:
                toks.append(b.w)
            toks.extend(b.r.items())
        return toks

    def _mark(self, tok, reads, writes):
        key, v = tok
        for b in reads:
            if b.r.get(key, 0) < v:
                b.r[key] = v
        for b in writes:
            b.w = tok
            b.r = {}

    def op(self, e, reads, writes, fn, inc=True):
        reads, writes = _flat(reads), _flat(writes)
        self._wait(e, self._deps(reads, writes))
        ins = fn(self.E[e])
        if inc:
            self.cnt[e] += 1
            ins.then_inc(self.sem[e], 1)
            self.pend[e] = False
            tok = (e, self.cnt[e])
        else:
            self.pend[e] = True
            tok = (e, self.cnt[e] + 1)
        self._mark(tok, reads, writes)
        return ins

    def dma(self, q, out, in_, reads, writes):
        reads, writes = _flat(reads), _flat(writes)
        self._wait(q, self._deps(reads, writes))
        i = self.dcnt[q] % NDS
        self.dcnt[q] += 1
        ent = self.dsem[q][i]
        key = (q, i)
        if ent[1] > 0:
            self._wait(q, [(key, 16 * ent[1])])
        ins = self.E[q].dma_start(out=out, in_=in_)
        ins.then_inc(ent[0], 16)
        ent[1] += 1
        tok = (key, 16 * ent[1])
        self._mark(tok, reads, writes)
        return tok

    def all_tokens(self):
        toks = [(e, c) for e, c in self.cnt.items() if c > 0]
        for q in self.dsem:
            for i, ent in enumerate(self.dsem[q]):
                if ent[1] > 0:
                    toks.append(((q, i), 16 * ent[1]))
        return toks

    def barrier(self):
        for e in self.sem:
            assert not self.pend[e], e
        toks = self.all_tokens()
        for e in self.E:
            self._wait(e, toks)

    def nextps(self):
        b = self.psrot
        self.psrot = (self.psrot + 1) % self.nrot
        return b


def build(nc, layers):
    es = ExitStack()
    with es:
        _build(nc, es, layers)
    return nc


def _build(nc, es, layers):
    k = KB(nc, es)
    op, dma = k.op, k.dma

    def dram_in(name, shape, dt=F32):
        return nc.dram_tensor(name, list(shape), dt, kind="ExternalInput").ap()

    uid = [0]

    def sb(st, name, shape, dt):
        uid[0] += 1
        return st.enter_context(nc.sbuf_tensor("%s_%d" % (name, uid[0]), list(shape), dt))

    xT_d = dram_in("xT", [D, L])
    outT_d = nc.dram_tensor("outT", [D, L], F32, kind="ExternalOutput").ap()
    gpre_d = dram_in("gpre", [P, 32])
    gpost_d = dram_in("gpost", [P, 32])
    ident_d = dram_in("ident", [P, P])
    dw = {}
    if 3 in layers:
        dw["d_w_in"] = dram_in("d_w_in", [D, 3072])
        dw["d_w_out"] = dram_in("d_w_out", [D, D])
        dw["d_ws"] = dram_in("d_ws", [P, 16, P])
        dw["d_tril"] = dram_in("d_tril", [P, P])
        dw["d_bs"] = dram_in("d_bs", [16, P])
        dw["d_lng"] = dram_in("d_lng", [1, D])
        dw["d_lnb"] = dram_in("d_lnb", [1, D])

    if 1 in layers:
        dw["b_w_in"] = dram_in("b_w_in", [D, 2432])
        dw["b_w_out"] = dram_in("b_w_out", [D, D])
        dw["b_biasT"] = dram_in("b_biasT", [P, 16, 256])
        dw["b_sinks"] = dram_in("b_sinks", [1, 16])

    if 2 in layers:
        dw["c_w1"] = dram_in("c_w1", [D, 1088])
        dw["c_wz"] = dram_in("c_wz", [D, D])
        dw["c_wuq"] = dram_in("c_wuq", [768, 2048])
        dw["c_wukv"] = dram_in("c_wukv", [256, 2048])
        dw["c_w_out"] = dram_in("c_w_out", [D, D])
        dw["c_rope"] = dram_in("c_rope", [64, L])
        dw["c_gqkv"] = dram_in("c_gqkv", [P, 8])
        dw["c_maskT"] = dram_in("c_maskT", [P, P])

    if 0 in layers:
        dw["a_wu"] = dram_in("a_wu", [D, D])
        dw["a_wz"] = dram_in("a_wz", [D, D])
        dw["a_wglu"] = dram_in("a_wglu", [D, D])
        dw["a_w_out"] = dram_in("a_w_out", [D, D])
        dw["a_dg"] = dram_in("a_dg", [P, 16])
        dw["a_lam"] = dram_in("a_lam", [P, 3, 32])
        for nm in ("a_brl", "a_bil", "a_crp", "a_cip"):
            dw[nm] = dram_in(nm, [P, 32, P])
        dw["a_triu"] = dram_in("a_triu", [P, P])
        dw["a_ntriu"] = dram_in("a_ntriu", [P, P])

    X = sb(es, "X", [P, 8, L], F32)
    xB = bufs(8, NBLK)
    ones = sb(es, "ones", [P, P], BF16)
    onesB = Buf()
    ident = sb(es, "identb", [P, P], BF16)
    identB = Buf()
    gpre = sb(es, "gpre_s", [P, 32], F32)
    gpost = sb(es, "gpost_s", [P, 32], F32)
    gB = Buf()
    rstd = sb(es, "rstd", [P, BW], F32)
    rstdB = Buf()
    for i in range(8):
        k.ps.append(es.enter_context(nc.psum_tensor("ps%d" % i, [P, BW], F32)))
        k.psB.append(Buf())
    ps, psB = k.ps, k.psB

    for c in range(8):
        for tb in range(NBLK):
            dma("sp", X[:, c, tb * BW:(tb + 1) * BW], xT_d[c * P:(c + 1) * P, tb * BW:(tb + 1) * BW], [], [xB[c][tb]])
    dma("sp", gpre[:], gpre_d, [], [gB])
    dma("sp", gpost[:], gpost_d, [], [gB])
    dma("pool", ident[:], ident_d, [], [identB])
    op("dve", [], [onesB], lambda e: e.memset(ones[:], 1.0))
    epsc = sb(es, "epsc", [P, 1], F32)
    op("dve", [], [onesB], lambda e: e.memset(epsc[:], EPS))

    def load_w(st, name, d_ap, ncols, q="pool"):
        K = d_ap.shape[0]
        kc_n = K // P
        t = sb(st, name, [P, kc_n, ncols], BF16)
        B = WB()
        for c0 in range(0, ncols, 512):
            c1 = min(ncols, c0 + 512)
            bl = []
            for kc in range(kc_n):
                b1 = Buf()
                dma(q, t[:, kc, c0:c1], d_ap[kc * P:(kc + 1) * P, c0:c1], [], [b1])
                bl.append(b1)
            B.blocks.append((c0, c1, bl))
        return t, B

    def rstd_from_sq(sq, sqB, n, scale, c0=0):
        b = k.nextps()
        for c in range(n):
            op("pe", [sqB[c0 + c], onesB], [psB[b]],
               lambda e, c=c: e.matmul(ps[b][:, :], lhsT=ones[:, :], rhs=sq[:, c0 + c, :], start=(c == 0), stop=(c == n - 1)),
               inc=(c == n - 1))
        op("act", [psB[b]], [rstdB],
           lambda e: e.activation(out=rstd[:], in_=ps[b][:, :], func=AF.Ln, scale=scale, bias=epsc[:, 0:1]))
        op("act", [rstdB], [rstdB], lambda e: e.activation(out=rstd[:], in_=rstd[:], func=AF.Exp, scale=-0.5))

    def prenorm(li, tb, xn, xnB, tmpA, tmpAB):
        blk = slice(tb * BW, (tb + 1) * BW)
        for c in range(8):
            op("act", [xB[c][tb]], [tmpAB[c]],
               lambda e, c=c: e.activation(out=tmpA[:, c, :], in_=X[:, c, blk], func=AF.Square))
        rstd_from_sq(tmpA, tmpAB, 8, 1.0 / D)
        for c in range(8):
            op("dve", [xB[c][tb], rstdB, gB], [xnB[c]],
               lambda e, c=c: e.scalar_tensor_tensor(out=xn[:, c, :], in0=X[:, c, blk],
                                                     scalar=gpre[:, li * 8 + c:li * 8 + c + 1], in1=rstd[:],
                                                     op0=ALU.mult, op1=ALU.mult))

    def prenorm_ap(li, tb, xn_ap, xnB, tmpA, tmpAB):
        blk = slice(tb * BW, (tb + 1) * BW)
        for c in range(8):
            op("act", [xB[c][tb]], [tmpAB[c]],
               lambda e, c=c: e.activation(out=tmpA[:, c, :], in_=X[:, c, blk], func=AF.Square))
        rstd_from_sq(tmpA, tmpAB, 8, 1.0 / D)
        for c in range(8):
            op("dve", [xB[c][tb], rstdB, gB], [xnB[c]],
               lambda e, c=c: e.scalar_tensor_tensor(out=xn_ap(c), in0=X[:, c, blk],
                                                     scalar=gpre[:, li * 8 + c:li * 8 + c + 1], in1=rstd[:],
                                                     op0=ALU.mult, op1=ALU.mult))

    def proj_ap(w, wB, col0, M, rhs_ap, rhsB, nk):
        b = k.nextps()
        wBc = wB.cols(col0, col0 + M)
        for kc in range(nk):
            op("pe", [wBc, rhsB[kc]], [psB[b]],
               lambda e, kc=kc: e.matmul(ps[b][0:M, :], lhsT=w[:, kc, col0:col0 + M], rhs=rhs_ap(kc),
                                         start=(kc == 0), stop=(kc == nk - 1)),
               inc=(kc == nk - 1))
        return b

    def proj_fm(w, wB, col0, rhs, rhsB, nk, M=P):
        b = k.nextps()
        wBc = wB.cols(col0, col0 + M)
        for kc in range(nk):
            op("pe", [wBc, rhsB[kc]], [psB[b]],
               lambda e, kc=kc: e.matmul(ps[b][0:M, :], lhsT=w[:, kc, col0:col0 + M], rhs=rhs[:, kc, :],
                                         start=(kc == 0), stop=(kc == nk - 1)),
               inc=(kc == nk - 1))
        return b

    def outproj(li, tb, G, GB, wout, woutB, ybuf, ybufB, tmpA, tmpAB):
        blk = slice(tb * BW, (tb + 1) * BW)
        for m in range(8):
            b = proj_fm(wout, woutB, m * P, G, GB, 8)
            op("act", [psB[b]], [ybufB[m]], lambda e, m=m, b=b: e.activation(out=ybuf[:, m, :], in_=ps[b][:, :], func=AF.Copy))
            op("act", [psB[b]], [tmpAB[m]], lambda e, m=m, b=b: e.activation(out=tmpA[:, m, :], in_=ps[b][:, :], func=AF.Square))
        rstd_from_sq(tmpA, tmpAB, 8, 1.0 / D)
        for m in range(8):
            op("dve", [ybufB[m], rstdB, gB], [ybufB[m]],
               lambda e, m=m: e.scalar_tensor_tensor(out=ybuf[:, m, :], in0=ybuf[:, m, :],
                                                     scalar=gpost[:, li * 8 + m:li * 8 + m + 1], in1=rstd[:],
                                                     op0=ALU.mult, op1=ALU.mult))
            op("pool", [ybufB[m], xB[m][tb]], [xB[m][tb]],
               lambda e, m=m: e.tensor_tensor(out=X[:, m, blk], in0=X[:, m, blk], in1=ybuf[:, m, :], op=ALU.add))

    out_toks = []
    stored = [False]

    def store_out():
        stored[0] = True
        for tb in range(NBLK):
            for c in range(8):
                out_toks.append(dma("sp", outT_d[c * P:(c + 1) * P, tb * BW:(tb + 1) * BW], X[:, c, tb * BW:(tb + 1) * BW], [xB[c][tb]], []))

    def layer_sgu(li):
        st = ExitStack()
        with st:
            w_in, w_inB = load_w(st, "d_win", dw["d_w_in"], 3072)
            w_out, w_outB = load_w(st, "d_wout", dw["d_w_out"], D)
            wsf = sb(st, "wsf", [P, 16, P], F32)
            wsfB = Buf()
            tril = sb(st, "tril", [P, P], F32)
            trilB = Buf()
            wsm = sb(st, "wsm", [P, 16, P], BF16)
            wsmB = Buf()
            wsT = sb(st, "wsT", [P, 16, P], BF16)
            wsTB = bufs(16)
            bsT = sb(st, "bsT", [P, 8, P], F32)
            bsTB = Buf()
            lng = sb(st, "lng", [P, D], F32)
            lnb = sb(st, "lnb", [P, D], F32)
            lnB = Buf()
            R1 = sb(st, "R1", [P, 8, BW], F32)
            R1B = bufs(8)
            R1bf = R1[:].bitcast(BF16)
            tmpA = sb(st, "tmpA", [P, 8, BW], BF16)
            tmpAB = bufs(8)
            gu = sb(st, "gu", [P, 8, BW], BF16)
            guB = bufs(8)
            sz = sb(st, "sz", [P, 8, BW], BF16)
            szB = bufs(8)
            vtmp = sb(st, "vtmp", [P, D], F32)
            vtmpB = Buf()
            stt = sb(st, "stt", [P, 2, 6], F32)
            mv = sb(st, "mv", [P, 2], F32)
            rs1 = sb(st, "rs1", [P, 1], F32)
            sttB = Buf()
            tmpS2 = sb(st, "tmpS", [P, 2, BW], F32)
            tmpSB2 = bufs(2)

            def xn_ap(c):
                return R1bf[:, c // 2, (c % 2) * BW:(c % 2) * BW + BW]

            def vln_ap(ch, c0, c1):
                return R1bf[:, 4 + ch, c0:c1]

            xnB = [R1B[c // 2] for c in range(8)]

            dma("sp", wsf[:], dw["d_ws"], [], [wsfB])
            dma("sp", tril[:], dw["d_tril"], [], [trilB])
            for h in range(2):
                src = dw["d_bs"].rearrange("(gp h) t -> h gp t", h=2)[h]
                dma("sp", bsT[h * 64:(h + 1) * 64, :, :], src.unsqueeze(0).to_broadcast([64, 8, P]), [], [bsTB])
            dma("sp", lng[:], dw["d_lng"].to_broadcast([P, D]), [], [lnB])
            dma("sp", lnb[:], dw["d_lnb"].to_broadcast([P, D]), [], [lnB])
            op("dve", [wsfB, trilB], [wsmB],
               lambda e: e.tensor_tensor(out=wsm[:], in0=wsf[:], in1=tril[:].unsqueeze(1).to_broadcast([P, 16, P]), op=ALU.mult))
            for g in range(16):
                b = k.nextps()
                op("pe", [wsmB, identB], [psB[b]],
                   lambda e, g=g, b=b: e.matmul(ps[b][:, 0:P], lhsT=wsm[:, g, :], rhs=ident[:, :], start=True, stop=True))
                op("act", [psB[b]], [wsTB[g]], lambda e, g=g, b=b: e.activation(out=wsT[:, g, :], in_=ps[b][:, 0:P], func=AF.Copy))

            for tb in range(NBLK):
                blk = slice(tb * BW, (tb + 1) * BW)
                for c in range(8):
                    op("act", [xB[c][tb]], [tmpAB[c]],
                       lambda e, c=c: e.activation(out=tmpA[:, c, :], in_=X[:, c, blk], func=AF.Square))
                rstd_from_sq(tmpA, tmpAB, 8, 1.0 / D)
                for c in range(8):
                    op("dve", [xB[c][tb], rstdB, gB], [xnB[c]],
                       lambda e, c=c: e.scalar_tensor_tensor(out=xn_ap(c), in0=X[:, c, blk],
                                                             scalar=gpre[:, li * 8 + c:li * 8 + c + 1], in1=rstd[:],
                                                             op0=ALU.mult, op1=ALU.mult))
                for ch in range(4):
                    for half in range(2):
                        b = k.nextps()
                        for kc in range(8):
                            op("pe", [w_inB, xnB[kc]], [psB[b]],
                               lambda e, kc=kc, b=b: e.matmul(ps[b][:, :], lhsT=xn_ap(kc)[:, ch * P:(ch + 1) * P],
                                                              rhs=w_in[:, kc, D + half * BW:D + (half + 1) * BW],
                                                              start=(kc == 0), stop=(kc == 7)),
                               inc=(kc == 7))
                        op("act", [psB[b]], [vtmpB],
                           lambda e, b=b, half=half: e.activation(out=vtmp[:, half * BW:(half + 1) * BW], in_=ps[b][:, :],
                                                                  func=AF.Gelu_apprx_tanh))
                    for half in range(2):
                        op("dve", [vtmpB], [sttB], lambda e, half=half: e.bn_stats(out=stt[:, half, :], in_=vtmp[:, half * BW:(half + 1) * BW]))
                    op("dve", [sttB], [sttB], lambda e: e.bn_aggr(out=mv[:], in_=stt[:].rearrange("p a b -> p (a b)")))
                    op("act", [sttB], [sttB],
                       lambda e: e.activation(out=rs1[:], in_=mv[:, 1:2], func=AF.Sqrt, scale=1.0, bias=epsc[:, 0:1]))
                    op("dve", [sttB], [sttB], lambda e: e.reciprocal(out=rs1[:], in_=rs1[:]))
                    op("dve", [vtmpB, sttB], [vtmpB],
                       lambda e: e.tensor_scalar(out=vtmp[:], in0=vtmp[:], scalar1=mv[:, 0:1], scalar2=rs1[:, 0:1],
                                                 op0=ALU.subtract, op1=ALU.mult))
                    op("pool", [vtmpB, lnB], [vtmpB], lambda e: e.tensor_tensor(out=vtmp[:], in0=vtmp[:], in1=lng[:], op=ALU.mult))
                    op("pool", [vtmpB, lnB], [R1B[4 + ch]],
                       lambda e, ch=ch: e.tensor_tensor(out=vln_ap(ch, 0, D), in0=vtmp[:], in1=lnb[:], op=ALU.add))
                for m in range(8):
                    b = k.nextps()
                    for kc in range(8):
                        op("pe", [w_inB, xnB[kc]], [psB[b]],
                           lambda e, kc=kc, b=b, m=m: e.matmul(ps[b][:, :], lhsT=w_in[:, kc, m * P:(m + 1) * P], rhs=xn_ap(kc),
                                                               start=(kc == 0), stop=(kc == 7)), inc=(kc == 7))
                    op("act", [psB[b]], [guB[m]], lambda e, b=b, m=m: e.activation(out=gu[:, m, :], in_=ps[b][:, :], func=AF.Gelu_apprx_tanh))
                for m in range(8):
                    b = k.nextps()
                    for kc in range(8):
                        op("pe", [w_inB, xnB[kc]], [psB[b]],
                           lambda e, kc=kc, b=b, m=m: e.matmul(ps[b][:, :], lhsT=w_in[:, kc, 2 * D + m * P:2 * D + (m + 1) * P], rhs=xn_ap(kc),
                                                               start=(kc == 0), stop=(kc == 7)), inc=(kc == 7))
                    op("act", [psB[b]], [szB[m]], lambda e, b=b, m=m: e.activation(out=sz[:, m, :], in_=ps[b][:, :], func=AF.Silu))
                for m in range(8):
                    op("pool", [guB[m], szB[m]], [guB[m]], lambda e, m=m: e.tensor_tensor(out=gu[:, m, :], in0=gu[:, m, :], in1=sz[:, m, :], op=ALU.mult))
                for gp in range(8):
                    b = k.nextps()
                    n = 0
                    for ch in range(4):
                        for h in range(2):
                            g = 2 * gp + h
                            n += 1
                            op("pe", [R1B[4 + ch], wsTB[g]], [psB[b]],
                               lambda e, ch=ch, h=h, g=g, b=b: e.matmul(ps[b][h * 64:(h + 1) * 64, ch * P:(ch + 1) * P],
                                                                        lhsT=vln_ap(ch, g * 64, (g + 1) * 64), rhs=wsT[:, g, :],
                                                                        start=True, stop=True),
                               inc=(n == 8))
                    op("dve", [psB[b], bsTB], [tmpSB2[gp % 2]],
                       lambda e, b=b, gp=gp: e.tensor_tensor(out=tmpS2[:, gp % 2, :].rearrange("p (a t) -> p a t", a=4),
                                                             in0=ps[b][:, :].rearrange("p (a t) -> p a t", a=4),
                                                             in1=bsT[:, gp, :].unsqueeze(1).to_broadcast([P, 4, P]), op=ALU.add))
                    if gp > 0:
                        op("dve", [tmpSB2[(gp - 1) % 2], guB[gp - 1]], [guB[gp - 1]],
                           lambda e, gp=gp: e.tensor_tensor(out=gu[:, gp - 1, :], in0=tmpS2[:, (gp - 1) % 2, :], in1=gu[:, gp - 1, :], op=ALU.mult))
                op("dve", [tmpSB2[1], guB[7]], [guB[7]],
                   lambda e: e.tensor_tensor(out=gu[:, 7, :], in0=tmpS2[:, 1, :], in1=gu[:, 7, :], op=ALU.mult))
                outproj(li, tb, gu, guB, w_out, w_outB, R1, R1B, tmpA, tmpAB)
            if li == layers[-1]:
                store_out()
            k.barrier()

    def layer_swa(li):
        st = ExitStack()
        with st:
            NC_IN = 2432
            w_in, w_inB = load_w(st, "b_win", dw["b_w_in"], NC_IN)
            w_out, w_outB = load_w(st, "b_wout", dw["b_w_out"], D)
            KT2 = sb(st, "KT2", [P, 2, L], BF16)
            KTB = bufs(2, NBLK)
            VA = [[sb(st, "VA%d%d" % (kv, par), [P, 16, P], BF16) for par in range(2)] for kv in range(2)]
            VAB = bufs(2, 2, 16)
            biasT = sb(st, "biasT", [P, 16, 256], F32)
            biasB = Buf()
            esink = sb(st, "esink", [P, 16], F32)
            esinkB = Buf()
            R1 = sb(st, "R1", [P, 8, BW], F32)
            R1B = bufs(8)
            R1bf = R1[:].bitcast(BF16)
            tmpA = sb(st, "tmpA", [P, 8, BW], BF16)
            tmpAB = bufs(8)
            sz = sb(st, "sz", [P, 8, BW], BF16)
            szB = bufs(8)
            NPB = 5
            apc = [0]
            tmpP = sb(st, "tmpP", [P, NPB, 256], F32)
            tmpPB = bufs(NPB)
            PT = sb(st, "PT", [P, NPB, 256], BF16)
            PTB = bufs(NPB)
            rsb = sb(st, "rsb", [P, BW], F32)
            rsbB = bufs(2)
            tG = sb(st, "tG", [P, BW], F32)
            tGB = bufs(2)

            def xn_ap(c):
                return R1bf[:, c // 2, (c % 2) * BW:(c % 2) * BW + BW]
            xnB = [R1B[c // 2] for c in range(8)]

            def qt_ap(j, p0, p1, c0, c1):
                return R1bf[p0:p1, 4 + j // 2, (j % 2) * BW + c0:(j % 2) * BW + c1]
            qtB = [R1B[4 + j // 2] for j in range(8)]

            import os
            SK = os.environ.get("SKIP", "")
            if "a" not in SK:
                dma("sp", biasT[:], dw["b_biasT"], [], [biasB])
            if "b" not in SK:
                dma("sp", esink[:], dw["b_sinks"].to_broadcast([P, 16]), [], [esinkB])
                op("act", [esinkB], [esinkB], lambda e: e.activation(out=esink[:], in_=esink[:], func=AF.Exp))
            for kv in range(2):
                for par in range(2):
                    c0 = 64 if par == 0 else 0
                    if "c" not in SK:
                        op("pool", [], sum([[VAB[kv][par][t]] for t in range(16)], []),
                           lambda e, kv=kv, par=par, c0=c0: e.memset(VA[kv][par][:, :, c0:c0 + 64], 1.0))

            for tb in range(NBLK):
                blk = slice(tb * BW, (tb + 1) * BW)
                prenorm_ap(li, tb, xn_ap, xnB, tmpA, tmpAB)
                STG = int(os.environ.get("STG", "9"))
                for j in range(8 if STG >= 1 else 0):
                    b = proj_ap(w_in, w_inB, j * P, P, xn_ap, xnB, 8)
                    op("act", [psB[b]], [qtB[j]], lambda e, b=b, j=j: e.activation(out=qt_ap(j, 0, P, 0, BW), in_=ps[b][:, :], func=AF.Copy))
                for kv in range(2 if STG >= 2 else 0):
                    b = proj_ap(w_in, w_inB, D + kv * P, P, xn_ap, xnB, 8)
                    op("act", [psB[b]], [KTB[kv][tb]], lambda e, b=b, kv=kv: e.activation(out=KT2[:, kv, blk], in_=ps[b][:, :], func=AF.Copy))
                for ch in range(4 if STG >= 3 else 0):
                    tt = tb * 4 + ch
                    b = k.nextps()
                    for kc in range(8):
                        op("pe", [w_inB, xnB[kc]], [psB[b]],
                           lambda e, kc=kc, b=b, ch=ch: e.matmul(ps[b][:, 0:P], lhsT=xn_ap(kc)[:, ch * P:(ch + 1) * P],
                                                                 rhs=w_in[:, kc, D + 256:D + 384], start=(kc == 0), stop=(kc == 7)),
                           inc=(kc == 7))
                    for kv in range(2):
                        op("act", [psB[b]], [VAB[kv][0][tt]],
                           lambda e, b=b, kv=kv, tt=tt: e.activation(out=VA[kv][0][:, tt, 0:64], in_=ps[b][:, kv * 64:(kv + 1) * 64], func=AF.Copy))
                        op("act", [psB[b]], [VAB[kv][1][tt]],
                           lambda e, b=b, kv=kv, tt=tt: e.activation(out=VA[kv][1][:, tt, 64:128], in_=ps[b][:, kv * 64:(kv + 1) * 64], func=AF.Copy))
                for m in range(8):
                    b = proj_ap(w_in, w_inB, D + 384 + m * P, P, xn_ap, xnB, 8)
                    op("act", [psB[b]], [szB[m]], lambda e, b=b, m=m: e.activation(out=sz[:, m, :], in_=ps[b][:, :], func=AF.Silu))
                def scores(h, nbl, pi):
                    kv, par, j = h // 8, h % 2, h // 2
                    base = par * 64
                    nb = tb * 4 + nbl
                    b = k.nextps()
                    c_lo = 0 if nb > 0 else P
                    rhs_q = qt_ap(j, base, base + 64, nbl * P, (nbl + 1) * P)
                    if nb > 0:
                        tbp = (nb - 1) // 4
                        op("pe", [KTB[kv][tbp], qtB[j]], [psB[b]],
                           lambda e: e.matmul(ps[b][:, 0:P], lhsT=KT2[base:base + 64, kv, (nb - 1) * P:nb * P], rhs=rhs_q, start=True, stop=True),
                           inc=False)
                    op("pe", [KTB[kv][tb], qtB[j]], [psB[b]],
                       lambda e: e.matmul(ps[b][:, P:2 * P], lhsT=KT2[base:base + 64, kv, nb * P:(nb + 1) * P], rhs=rhs_q, start=True, stop=True))
                    op("dve", [psB[b], biasB], [tmpPB[pi]],
                       lambda e: e.scalar_tensor_tensor(out=tmpP[:, pi, c_lo:256], in0=ps[b][:, c_lo:256], scalar=0.125, in1=biasT[:, h, c_lo:256],
                                                        op0=ALU.mult, op1=ALU.add))
                    op("act", [tmpPB[pi]], [PTB[pi]],
                       lambda e: e.activation(out=PT[:, pi, c_lo:256], in_=tmpP[:, pi, c_lo:256], func=AF.Exp))

                def pv_epi(h, nbl, pi):
                    kv, par, j = h // 8, h % 2, h // 2
                    base = par * 64
                    oth = 64 - base
                    bo = 6 + (h % 2)
                    nb = tb * 4 + nbl
                    if nb > 0:
                        op("pe", [PTB[pi], VAB[kv][par][nb - 1]], [psB[bo]],
                           lambda e: e.matmul(ps[bo][:, nbl * P:(nbl + 1) * P], lhsT=VA[kv][par][:, nb - 1, :], rhs=PT[:, pi, 0:P], start=True, stop=False),
                           inc=False)
                    op("pe", [PTB[pi], VAB[kv][par][nb]], [psB[bo]],
                       lambda e: e.matmul(ps[bo][:, nbl * P:(nbl + 1) * P], lhsT=VA[kv][par][:, nb, :], rhs=PT[:, pi, P:2 * P], start=(nb == 0), stop=True))
                    if nbl == 3:
                        op("dve", [psB[bo], esinkB], [rsbB[par]],
                           lambda e: e.tensor_scalar(out=rsb[base:base + 64, :], in0=ps[bo][oth:oth + 64, :], scalar1=esink[oth:oth + 64, h:h + 1],
                                                     scalar2=None, op0=ALU.add))
                        op("act", [rsbB[par]], [rsbB[par]], lambda e: e.activation(out=rsb[base:base + 64, :], in_=rsb[base:base + 64, :], func=AF.Ln))
                        op("act", [rsbB[par]], [rsbB[par]], lambda e: e.activation(out=rsb[base:base + 64, :], in_=rsb[base:base + 64, :], func=AF.Exp, scale=-1.0))
                        op("dve", [psB[bo], rsbB[par]], [tGB[par]],
                           lambda e: e.tensor_tensor(out=tG[base:base + 64, :], in0=ps[bo][base:base + 64, :], in1=rsb[base:base + 64, :], op=ALU.mult))
                        op("pool", [tGB[par], szB[j]], [szB[j]],
                           lambda e: e.tensor_tensor(out=sz[base:base + 64, j, :], in0=tG[base:base + 64, :], in1=sz[base:base + 64, j, :], op=ALU.mult))

                import os
                tasks = [(h, nbl) for h in range(int(os.environ.get('SWA_NH', '16'))) for nbl in range(4)]
                LAS = 3
                for i in range(min(LAS, len(tasks))):
                    scores(tasks[i][0], tasks[i][1], (apc[0] + i) % NPB)
                for i, (h, nbl) in enumerate(tasks):
                    pi = apc[0] % NPB
                    apc[0] += 1
                    if i + LAS < len(tasks):
                        scores(tasks[i + LAS][0], tasks[i + LAS][1], (apc[0] + LAS - 1) % NPB)
                    pv_epi(h, nbl, pi)
                outproj(li, tb, sz, szB, w_out, w_outB, R1, R1B, tmpA, tmpAB)
            k.barrier()

    def layer_mla(li):
        SC = 96.0 ** -0.5
        st = ExitStack()
        with st:
            GT = sb(st, "GT", [P, 8, L], BF16)
            GTB = bufs(8, NBLK)
            sA = ExitStack()
            sA.__enter__()
            CQN = sb(sA, "CQN", [P, 6, L], BF16)
            CQNB = bufs(6, NBLK)
            CKVN = sb(sA, "CKVN", [P, 2, L], BF16)
            CKVNB = bufs(2, NBLK)
            KR = sb(sA, "KR", [32, L], BF16)
            KRB = bufs(NBLK)
            ROPE = sb(sA, "ROPE", [64, L], F32)
            ropeB = Buf()
            gq = sb(sA, "gq", [P, 8], F32)
            gqB = Buf()
            dma("sp", ROPE[:], dw["c_rope"], [], [ropeB])
            dma("sp", gq[:], dw["c_gqkv"], [], [gqB])
            tR = sb(sA, "tR", [32, 2, BW], F32)
            tRB = bufs(2)
            s1 = ExitStack()
            with s1:
                w1, w1B = load_w(s1, "s_c_w1", dw["c_w1"], 1088)
                R1 = sb(s1, "R1", [P, 8, BW], F32)
                R1B = bufs(8)
                xnt = sb(s1, "xnt", [P, 8, BW], BF16)
                xnB = bufs(8)
                tmpA = sb(s1, "tmpA", [P, 8, BW], BF16)
                tmpAB = bufs(8)
                xn_ap = lambda c: xnt[:, c, :]
                for tb in range(NBLK):
                    blk = slice(tb * BW, (tb + 1) * BW)
                    prenorm_ap(li, tb, xn_ap, xnB, tmpA, tmpAB)
                    for m in range(8):
                        b = proj_ap(w1, w1B, m * P, P, xn_ap, xnB, 8)
                        op("act", [psB[b]], [R1B[m]], lambda e, b=b, m=m: e.activation(out=R1[:, m, :], in_=ps[b][:, :], func=AF.Copy))
                        op("act", [psB[b]], [tmpAB[m]], lambda e, b=b, m=m: e.activation(out=tmpA[:, m, :], in_=ps[b][:, :], func=AF.Square))
                    rstd_from_sq(tmpA, tmpAB, 6, 1.0 / 768, 0)
                    for m in range(6):
                        op("dve", [R1B[m], rstdB, gqB], [CQNB[m][tb]],
                           lambda e, m=m: e.scalar_tensor_tensor(out=CQN[:, m, blk], in0=R1[:, m, :], scalar=gq[:, m:m + 1], in1=rstd[:],
                                                                 op0=ALU.mult, op1=ALU.mult))
                    rstd_from_sq(tmpA, tmpAB, 2, 1.0 / 256, 6)
                    for m in range(2):
                        op("dve", [R1B[6 + m], rstdB, gqB], [CKVNB[m][tb]],
                           lambda e, m=m: e.scalar_tensor_tensor(out=CKVN[:, m, blk], in0=R1[:, 6 + m, :], scalar=gq[:, 6 + m:7 + m], in1=rstd[:],
                                                                 op0=ALU.mult, op1=ALU.mult))
                    b = proj_ap(w1, w1B, D, 64, xn_ap, xnB, 8)
                    op("dve", [psB[b], ropeB], [tRB[0]],
                       lambda e, b=b: e.tensor_tensor(out=tR[:, 0, :], in0=ps[b][32:64, :], in1=ROPE[32:64, blk], op=ALU.mult))
                    op("dve", [psB[b], ropeB], [tRB[1]],
                       lambda e, b=b: e.tensor_tensor(out=tR[:, 1, :], in0=ps[b][0:32, :], in1=ROPE[0:32, blk], op=ALU.mult))
                    op("pool", [tRB[0], tRB[1]], [KRB[tb]],
                       lambda e: e.tensor_tensor(out=KR[:, blk], in0=tR[:, 0, :], in1=tR[:, 1, :], op=ALU.add))
                k.barrier()
            s2 = ExitStack()
            with s2:
                wuq, wuqB = load_w(s2, "s_c_wuq", dw["c_wuq"], 2048)
                wukv, wukvB = load_w(s2, "s_c_wukv", dw["c_wukv"], 2048)
                KTh = [sb(s2, "KTh%d" % i, [P, L], BF16) for i in range(2)]
                KThB = bufs(2, NBLK)
                VAh = [sb(s2, "VAh%d" % i, [P, 16, P], BF16) for i in range(2)]
                VAhB = bufs(2, 4)
                QT = [sb(s2, "QT%d" % i, [P, BW], BF16) for i in range(2)]
                QTB = bufs(2)
                NPT = 5
                LA = 3
                PT = [sb(s2, "PT%d" % i, [P, BW], BF16) for i in range(NPT)]
                PTB = bufs(NPT)
                QF = sb(s2, "QF", [64, BW], F32)
                QFB = Buf()
                maskT = sb(s2, "maskT", [P, P], BF16)
                maskB = Buf()
                rsb = sb(s2, "rsb", [P, BW], F32)
                rsbB = bufs(2)
                dma("pool", maskT[:], dw["c_maskT"], [], [maskB])
                for i in range(2):
                    op("pool", [], KThB[i], lambda e, i=i: e.memset(KTh[i][32:64, :], 0.0))
                    c0 = 64 if i == 0 else 0
                    op("pool", [], VAhB[i], lambda e, i=i, c0=c0: e.memset(VAh[i][:, :, c0:c0 + 64], 1.0))
                def kv_build(h):
                    par = h % 2
                    vc0 = par * 64
                    for tb in range(NBLK):
                        blk = slice(tb * BW, (tb + 1) * BW)
                        b = k.nextps()
                        for kc in range(2):
                            op("pe", [wukvB, CKVNB[kc][tb]], [psB[b]],
                               lambda e, kc=kc, b=b, blk=blk: e.matmul(ps[b][64:128, :], lhsT=wukv[:, kc, h * P:h * P + 64], rhs=CKVN[:, kc, blk],
                                                                      start=(kc == 0), stop=(kc == 1)), inc=(kc == 1))
                        op("act", [psB[b]], [KThB[par][tb]],
                           lambda e, b=b, blk=blk: e.activation(out=KTh[par][64:128, blk], in_=ps[b][64:128, :], func=AF.Copy))
                        op("dve", [KRB[tb]], [KThB[par][tb]],
                           lambda e, blk=blk: e.tensor_scalar(out=KTh[par][0:32, blk], in0=KR[:, blk], scalar1=1.0, scalar2=None, op0=ALU.mult))
                        b = k.nextps()
                        for i4 in range(4):
                            tt = tb * 4 + i4
                            for kc in range(2):
                                op("pe", [wukvB, CKVNB[kc][tb]], [psB[b]],
                                   lambda e, kc=kc, b=b, tt=tt, i4=i4: e.matmul(ps[b][:, i4 * 64:(i4 + 1) * 64], lhsT=CKVN[:, kc, tt * P:(tt + 1) * P],
                                                                                rhs=wukv[:, kc, h * P + 64:h * P + 128], start=(kc == 0), stop=(kc == 1)),
                                   inc=(kc == 1 and i4 == 3))
                        op("act", [psB[b]], [VAhB[par][tb]],
                           lambda e, b=b, tb=tb: e.activation(out=VAh[par][:, tb * 4:(tb + 1) * 4, vc0:vc0 + 64],
                                                              in_=ps[b][:, 0:256].rearrange("p (a c) -> p a c", a=4), func=AF.Copy))

                def q_prep(h, qb, qi):
                    qblk = slice(qb * BW, (qb + 1) * BW)
                    b = k.nextps()
                    for kc in range(6):
                        op("pe", [wuqB, CQNB[kc][qb]], [psB[b]],
                           lambda e, kc=kc, b=b: e.matmul(ps[b][:, :], lhsT=wuq[:, kc, h * P:(h + 1) * P], rhs=CQN[:, kc, qblk],
                                                          start=(kc == 0), stop=(kc == 5)), inc=(kc == 5))
                    op("act", [psB[b]], [QTB[qi]], lambda e, b=b: e.activation(out=QT[qi][:, :], in_=ps[b][:, :], func=AF.Copy))
                    op("act", [psB[b]], [QFB], lambda e, b=b: e.activation(out=QF[:, :], in_=ps[b][0:64, :], func=AF.Copy))
                    op("dve", [QFB, ropeB], [tRB[0]],
                       lambda e: e.tensor_tensor(out=tR[:, 0, :], in0=QF[32:64, :], in1=ROPE[32:64, qblk], op=ALU.mult))
                    op("dve", [QFB, ropeB], [tRB[1]],
                       lambda e: e.tensor_tensor(out=tR[:, 1, :], in0=QF[0:32, :], in1=ROPE[0:32, qblk], op=ALU.mult))
                    op("dve", [tRB[0], tRB[1]], [QTB[qi]],
                       lambda e: e.tensor_tensor(out=QT[qi][0:32, :], in0=tR[:, 0, :], in1=tR[:, 1, :], op=ALU.add))

                pti = [0]

                def attend(h, qb, qi, bo):
                    par, j = h % 2, h // 2
                    base = par * 64
                    oth = 64 - base
                    qblk = slice(qb * BW, (qb + 1) * BW)
                    nkc = 4 * qb + 4

                    def pv(kc, pi, q_lo):
                        op("pe", [PTB[pi], VAhB[par][kc // 4]], [psB[bo]],
                           lambda e: e.matmul(ps[bo][:, q_lo:BW], lhsT=VAh[par][:, kc, :], rhs=PT[pi][:, q_lo:BW],
                                              start=(kc == 0), stop=(kc == nkc - 1)))
                    pend_pv = []
                    for kc in range(nkc):
                        q_lo = max(0, kc - 4 * qb) * P
                        pi = pti[0] % NPT
                        pti[0] += 1
                        b = k.nextps()
                        op("pe", [KThB[par][kc // 4], QTB[qi]], [psB[b]],
                           lambda e, b=b, kc=kc, q_lo=q_lo: e.matmul(ps[b][:, q_lo:BW], lhsT=KTh[par][:, kc * P:(kc + 1) * P],
                                                                     rhs=QT[qi][:, q_lo:BW], start=True, stop=True))
                        op("act", [psB[b]], [PTB[pi]],
                           lambda e, b=b, pi=pi, q_lo=q_lo: e.activation(out=PT[pi][:, q_lo:BW], in_=ps[b][:, q_lo:BW], func=AF.Exp, scale=SC))
                        if kc >= 4 * qb:
                            op("dve", [PTB[pi], maskB], [PTB[pi]],
                               lambda e, pi=pi, q_lo=q_lo: e.tensor_tensor(out=PT[pi][:, q_lo:q_lo + P], in0=PT[pi][:, q_lo:q_lo + P],
                                                                           in1=maskT[:, :], op=ALU.mult))
                        pend_pv.append((kc, pi, q_lo))
                        if len(pend_pv) > LA:
                            pv(*pend_pv.pop(0))
                    while pend_pv:
                        pv(*pend_pv.pop(0))
                    op("dve", [psB[bo]], [rsbB[par]],
                       lambda e: e.tensor_scalar(out=rsb[base:base + 64, :], in0=ps[bo][oth:oth + 64, :], scalar1=1.0, scalar2=None, op0=ALU.mult))
                    op("act", [rsbB[par]], [rsbB[par]], lambda e: e.activation(out=rsb[base:base + 64, :], in_=rsb[base:base + 64, :], func=AF.Ln))
                    op("act", [rsbB[par]], [rsbB[par]], lambda e: e.activation(out=rsb[base:base + 64, :], in_=rsb[base:base + 64, :], func=AF.Exp, scale=-1.0))
                    op("dve", [psB[bo], rsbB[par]], [GTB[j][qb]],
                       lambda e: e.tensor_tensor(out=GT[base:base + 64, j, qblk], in0=ps[bo][base:base + 64, :],
                                                 in1=rsb[base:base + 64, :], op=ALU.mult))

                import os
                NH = int(os.environ.get("MLA_NH", "16"))
                tasks = [(h, qb) for h in range(NH) for qb in range(NBLK)]
                if tasks:
                    kv_build(0)
                    q_prep(0, 0, 0)
                for i, (h, qb) in enumerate(tasks):
                    if i + 1 < len(tasks):
                        h2, qb2 = tasks[i + 1]
                        if h2 != h:
                            kv_build(h2)
                        q_prep(h2, qb2, (i + 1) % 2)
                    attend(h, qb, i % 2, 6 + (i % 2))
                k.barrier()
            sA.close()
            s3 = ExitStack()
            with s3:
                wz, wzB = load_w(s3, "s_c_wz", dw["c_wz"], D)
                w_out, w_outB = load_w(s3, "s_c_wout", dw["c_w_out"], D)
                R1 = sb(s3, "R1", [P, 8, BW], F32)
                R1B = bufs(8)
                xnt = [sb(s3, "xnt%d" % i, [P, 8, BW], BF16) for i in range(2)]
                xnB = bufs(2, 8)
                tmpA = sb(s3, "tmpA", [P, 8, BW], BF16)
                tmpAB = bufs(8)
                tmpX = sb(s3, "tmpX", [P, 8, BW], BF16)
                tmpXB = bufs(8)
                sz = [sb(s3, "sz%d" % i, [P, 8, BW], BF16) for i in range(2)]
                szB = bufs(2, 8)

                def front(tb):
                    i = tb % 2
                    blk = slice(tb * BW, (tb + 1) * BW)
                    xn_ap = lambda c: xnt[i][:, c, :]
                    prenorm_ap(li, tb, xn_ap, xnB[i], tmpX, tmpXB)
                    for m in range(8):
                        b = proj_ap(wz, wzB, m * P, P, xn_ap, xnB[i], 8)
                        op("act", [psB[b]], [szB[i][m]], lambda e, b=b, m=m: e.activation(out=sz[i][:, m, :], in_=ps[b][:, :], func=AF.Silu))
                        op("pool", [szB[i][m], GTB[m][tb]], [szB[i][m]],
                           lambda e, m=m: e.tensor_tensor(out=sz[i][:, m, :], in0=sz[i][:, m, :], in1=GT[:, m, blk], op=ALU.mult))

                front(0)
                for tb in range(NBLK):
                    if tb + 1 < NBLK:
                        front(tb + 1)
                    outproj(li, tb, sz[tb % 2], szB[tb % 2], w_out, w_outB, R1, R1B, tmpA, tmpAB)
                k.barrier()

    def layer_s5(li):
        TT = ALU
        st = ExitStack()
        with st:
            UY = sb(st, "UY", [P, 8, L], BF16)
            UYB = bufs(8, NBLK)
            dT = sb(st, "dT", [P, 16], F32)
            dTB = Buf()
            dma("sp", dT[:], dw["a_dg"], [], [dTB])
            sP = ExitStack()
            sP.__enter__()
            NJ = 32
            BrL, BrLB = sb(sP, "BrL", [P, NJ, P], BF16), Buf()
            BiL, BiLB = sb(sP, "BiL", [P, NJ, P], BF16), Buf()
            CrP, CrPB = sb(sP, "CrP", [P, NJ, P], BF16), Buf()
            CiP, CiPB = sb(sP, "CiP", [P, NJ, P], BF16), Buf()
            for t_, B_, nm in ((BrL, BrLB, "a_brl"), (BiL, BiLB, "a_bil"), (CrP, CrPB, "a_crp"), (CiP, CiPB, "a_cip")):
                for q4 in range(4):
                    dma("pool", t_[:, q4 * 8:(q4 + 1) * 8, :], dw[nm][:, q4 * 8:(q4 + 1) * 8, :], [], [B_])
            lam = sb(sP, "lam", [P, 3, NJ], F32)
            prepB = Buf()
            dma("sp", lam[:], dw["a_lam"], [], [prepB])
            W = {}
            for nm in ("dt", "lrdt", "th", "mag", "t", "t2", "c", "s", "q", "c2", "s2", "cs", "ar", "ai", "den", "nr", "fr", "fi",
                       "u1", "u2", "ir", "ii", "pr", "pi", "A128r", "nA128i", "A128i"):
                W[nm] = sb(sP, "w_" + nm, [P, NJ], F32)
            lr, li_, ldt = lam[:, 0, :], lam[:, 1, :], lam[:, 2, :]

            TPr = sb(sP, "TPr", [P, NJ, P], BF16)
            TPi = sb(sP, "TPi", [P, NJ, P], BF16)
            TNr = sb(sP, "TNr", [P, NJ, P], BF16)
            TNi = sb(sP, "TNi", [P, NJ, P], BF16)
            car = sb(sP, "car", [P, 2, NJ], F32)
            carB = bufs(NJ)
            s1 = ExitStack()
            with s1:
                wu, wuB = load_w(s1, "a_wu", dw["a_wu"], D)
                xnt = sb(s1, "xnt", [P, 8, BW], BF16)
                xnB = bufs(8)
                tmpA = sb(s1, "tmpA", [P, 8, BW], BF16)
                tmpAB = bufs(8)
                m1 = sb(s1, "m1t", [P, NJ, 16], F32)
                m2 = sb(s1, "m2t", [P, NJ, 16], F32)
                prep_ops = []

                def dop(e_, r_, w_, fn_):
                    prep_ops.append((e_, r_, w_, fn_))

                def tt(o, a, b_, o_):
                    dop("dve", [prepB], [prepB], lambda e: e.tensor_tensor(out=o, in0=a, in1=b_, op=o_))

                def ts(o, a, s1_, s2_, o1, o2=None):
                    if o2 is None:
                        dop("dve", [prepB], [prepB], lambda e: e.tensor_scalar(out=o, in0=a, scalar1=s1_, scalar2=None, op0=o1))
                    else:
                        dop("dve", [prepB], [prepB], lambda e: e.tensor_scalar(out=o, in0=a, scalar1=s1_, scalar2=s2_, op0=o1, op1=o2))

                def stt(o, a, sc, b_, o1, o2):
                    dop("dve", [prepB], [prepB], lambda e: e.tensor_scalar(out=o, in0=a, scalar1=sc, scalar2=None, op0=o1))
                    dop("dve", [prepB], [prepB], lambda e: e.tensor_tensor(out=o, in0=o, in1=b_, op=o2))

                def csq(cr, ci):
                    tt(W["c2"][:], cr, cr, TT.mult)
                    tt(W["s2"][:], ci, ci, TT.mult)
                    tt(W["cs"][:], cr, ci, TT.mult)
                    tt(cr, W["c2"][:], W["s2"][:], TT.subtract)
                    ts(ci, W["cs"][:], 2.0, None, TT.mult)

                dop("act", [prepB], [prepB], lambda e: e.activation(out=W["dt"][:], in_=ldt, func=AF.Exp))
                tt(W["lrdt"][:], lr, W["dt"][:], TT.mult)
                tt(W["th"][:], li_, W["dt"][:], TT.mult)
                dop("act", [prepB], [prepB], lambda e: e.activation(out=W["mag"][:], in_=W["lrdt"][:], func=AF.Exp))
                ts(W["t"][:], W["th"][:], 1.0 / 64, None, TT.mult)
                tt(W["t2"][:], W["t"][:], W["t"][:], TT.mult)
                ts(W["q"][:], W["t2"][:], -1.0 / 720, None, TT.mult)
                stt(W["q"][:], W["q"][:], 1.0 / 24, W["t2"][:], TT.add, TT.mult)
                stt(W["q"][:], W["q"][:], -0.5, W["t2"][:], TT.add, TT.mult)
                ts(W["c"][:], W["q"][:], 1.0, None, TT.add)
                ts(W["q"][:], W["t2"][:], -1.0 / 5040, None, TT.mult)
                stt(W["q"][:], W["q"][:], 1.0 / 120, W["t2"][:], TT.add, TT.mult)
                stt(W["q"][:], W["q"][:], -1.0 / 6, W["t2"][:], TT.add, TT.mult)
                stt(W["s"][:], W["q"][:], 1.0, W["t"][:], TT.add, TT.mult)
                for _ in range(6):
                    csq(W["c"][:], W["s"][:])
                tt(W["ar"][:], W["mag"][:], W["c"][:], TT.mult)
                tt(W["ai"][:], W["mag"][:], W["s"][:], TT.mult)
                tt(W["den"][:], lr, lr, TT.mult)
                tt(W["u1"][:], li_, li_, TT.mult)
                tt(W["den"][:], W["den"][:], W["u1"][:], TT.add)
                dop("act", [prepB], [prepB], lambda e: e.activation(out=W["den"][:], in_=W["den"][:], func=AF.Ln))
                dop("act", [prepB], [prepB], lambda e: e.activation(out=W["den"][:], in_=W["den"][:], func=AF.Exp, scale=-1.0))
                ts(W["nr"][:], W["ar"][:], -1.0, None, TT.add)
                tt(W["u1"][:], W["nr"][:], lr, TT.mult)
                tt(W["u2"][:], W["ai"][:], li_, TT.mult)
                tt(W["u1"][:], W["u1"][:], W["u2"][:], TT.add)
                tt(W["fr"][:], W["u1"][:], W["den"][:], TT.mult)
                tt(W["u1"][:], W["ai"][:], lr, TT.mult)
                tt(W["u2"][:], W["nr"][:], li_, TT.mult)
                tt(W["u1"][:], W["u1"][:], W["u2"][:], TT.subtract)
                tt(W["fi"][:], W["u1"][:], W["den"][:], TT.mult)
                dop("act", [prepB], [prepB], lambda e: e.activation(out=W["u1"][:], in_=W["lrdt"][:], func=AF.Exp, scale=-2.0))
                tt(W["ir"][:], W["ar"][:], W["u1"][:], TT.mult)
                tt(W["ii"][:], W["ai"][:], W["u1"][:], TT.mult)
                ts(W["ii"][:], W["ii"][:], -1.0, None, TT.mult)
                dop("dve", [prepB], [prepB], lambda e: e.memset(TPr[:, :, 0:1], 1.0))
                dop("dve", [prepB], [prepB], lambda e: e.memset(TPi[:, :, 0:1], 0.0))
                ts(TNr[:, :, 0:1], W["fr"][:].unsqueeze(2), 1.0, None, TT.mult)
                ts(TNi[:, :, 0:1], W["fi"][:].unsqueeze(2), 1.0, None, TT.mult)
                for (Tr_, Ti_, pr0, pi0) in ((TPr, TPi, "ar", "ai"), (TNr, TNi, "ir", "ii")):
                    ts(W["pr"][:], W[pr0][:], 1.0, None, TT.mult)
                    ts(W["pi"][:], W[pi0][:], 1.0, None, TT.mult)
                    for kk in range(7):
                        n = 1 << kk
                        for c0 in range(0, n, 16):
                            w = min(16, n - c0)
                            pr_b = W["pr"][:].unsqueeze(2).to_broadcast([P, NJ, w])
                            pi_b = W["pi"][:].unsqueeze(2).to_broadcast([P, NJ, w])
                            lo_r, lo_i = Tr_[:, :, c0:c0 + w], Ti_[:, :, c0:c0 + w]
                            tt(m1[:, :, 0:w], lo_r, pr_b, TT.mult)
                            tt(m2[:, :, 0:w], lo_i, pi_b, TT.mult)
                            tt(Tr_[:, :, n + c0:n + c0 + w], m1[:, :, 0:w], m2[:, :, 0:w], TT.subtract)
                            tt(m1[:, :, 0:w], lo_r, pi_b, TT.mult)
                            tt(m2[:, :, 0:w], lo_i, pr_b, TT.mult)
                            tt(Ti_[:, :, n + c0:n + c0 + w], m1[:, :, 0:w], m2[:, :, 0:w], TT.add)
                        csq(W["pr"][:], W["pi"][:])
                    if pr0 == "ar":
                        ts(W["A128r"][:], W["pr"][:], 1.0, None, TT.mult)
                        ts(W["nA128i"][:], W["pi"][:], -1.0, None, TT.mult)
                        ts(W["A128i"][:], W["pi"][:], 1.0, None, TT.mult)
                dop("dve", [prepB], [prepB], lambda e: e.tensor_scalar(out=TNi[:], in0=TNi[:], scalar1=-1.0, scalar2=None, op0=TT.mult))
                dop("dve", [prepB], carB, lambda e: e.memset(car[:], 0.0))
                xn_ap = lambda c: xnt[:, c, :]
                npo = len(prep_ops)
                for tb in range(NBLK):
                    blk = slice(tb * BW, (tb + 1) * BW)
                    prenorm_ap(li, tb, xn_ap, xnB, tmpA, tmpAB)
                    for (e_, r_, w_, fn_) in prep_ops[tb * npo // NBLK:(tb + 1) * npo // NBLK]:
                        op(e_, r_, w_, fn_)
                    for m in range(8):
                        b = proj_ap(wu, wuB, m * P, P, xn_ap, xnB, 8)
                        op("act", [psB[b]], [UYB[m][tb]], lambda e, b=b, m=m, blk=blk: e.activation(out=UY[:, m, blk], in_=ps[b][:, :], func=AF.Copy))
                k.barrier()
            s2 = ExitStack()
            with s2:
                NA, NCD, NQ = 2, 2, 2
                TA = [sb(s2, "TA%d" % i, [P, 4, P], BF16) for i in range(NA)]
                TBf = [sb(s2, "TB%d" % i, [P, 4, P], BF16) for i in range(NA)]
                CD = [sb(s2, "CD%d" % i, [P, 2, 4, P], F32) for i in range(NCD)]
                KBt = [sb(s2, "KB%d" % i, [P, 4, 4, P], BF16) for i in range(2)]
                KBB = bufs(2, 4)
                ntriu = sb(s2, "ntriu", [P, P], BF16)
                triu = sb(s2, "triu", [P, P], BF16)
                triuB = Buf()
                dma("pool", triu[:], dw["a_triu"], [], [triuB])
                dma("pool", ntriu[:], dw["a_ntriu"], [], [triuB])
                for T_ in (TNr, TNi):
                    for j4 in range(8):
                        b = k.nextps()
                        for jl in range(4):
                            op("pe", [prepB, identB], [psB[b]],
                               lambda e, b=b, T_=T_, j4=j4, jl=jl: e.matmul(ps[b][:, jl * P:(jl + 1) * P], lhsT=T_[:, j4 * 4 + jl, :], rhs=ident[:, :],
                                                                            start=True, stop=True), inc=(jl == 3))
                        op("act", [psB[b]], [prepB],
                           lambda e, b=b, T_=T_, j4=j4: e.activation(out=T_[:, j4 * 4:j4 * 4 + 4, :].rearrange("p a t -> p (a t)"), in_=ps[b][:, :], func=AF.Copy))
                Q = [[sb(s2, "Q%d_%d" % (i, q_), [P, 4, P], BF16) for q_ in range(4)] for i in range(NQ)]
                AB_, BB_, CDB = bufs(NA), bufs(NA), bufs(NCD, 2, 4)
                QB = bufs(NQ, 4)
                ea = sb(s2, "ea", [P, 2, 4], F32)
                eb = sb(s2, "eb", [P, 2, 4], F32)
                eB = Buf()
                eaB, ebB = Buf(), Buf()
                ytmp = sb(s2, "ytmp", [P, BW], F32)
                yB = Buf()

                def flat(t_):
                    return t_[:].rearrange("p a t -> p (a t)")

                units = [(chc, c) for cp in range(4) for c in range(16) for chc in (2 * cp, 2 * cp + 1)]
                NU = len(units)

                def stA(u):
                    chc, c = units[u]
                    j0, qt = chc * 4, c // 4
                    cols = slice(c * P, (c + 1) * P)
                    ai = u % NA
                    ba = k.nextps()
                    bb = k.nextps()
                    op("pe", [BrLB, UYB[chc][qt]], [psB[ba]],
                       lambda e: e.matmul(ps[ba][:, :], lhsT=UY[:, chc, cols], rhs=BrL[:, j0:j0 + 4, :].rearrange("p a t -> p (a t)"), start=True, stop=True))
                    op("pe", [BiLB, UYB[chc][qt]], [psB[bb]],
                       lambda e: e.matmul(ps[bb][:, :], lhsT=UY[:, chc, cols], rhs=BiL[:, j0:j0 + 4, :].rearrange("p a t -> p (a t)"), start=True, stop=True))
                    op("act", [psB[ba]], [AB_[ai]], lambda e: e.activation(out=flat(TA[ai]), in_=ps[ba][:, :], func=AF.Copy))
                    op("act", [psB[bb]], [BB_[ai]], lambda e: e.activation(out=flat(TBf[ai]), in_=ps[bb][:, :], func=AF.Copy))

                def ctx(u):
                    chc, c = units[u]
                    j0 = chc * 4
                    ai, ci, qi = u % NA, u % NCD, u % NQ
                    return chc, c, j0, ai, ci, qi

                def stB_mul(u):
                    chc, c, j0, ai, ci, qi = ctx(u)
                    A, B_ = TA[ai], TBf[ai]
                    kb = KBt[u % 2]
                    tnr, ntni = TNr[:, j0:j0 + 4, :], TNi[:, j0:j0 + 4, :]
                    op("dve", [AB_[ai], prepB], [KBB[u % 2][0]], lambda e: e.tensor_tensor(out=kb[:, 0, :, :], in0=A[:], in1=tnr, op=TT.mult))
                    op("dve", [BB_[ai], prepB], [KBB[u % 2][1]], lambda e: e.tensor_tensor(out=kb[:, 1, :, :], in0=B_[:], in1=ntni, op=TT.mult))
                    op("dve", [AB_[ai], prepB], [KBB[u % 2][2]], lambda e: e.tensor_tensor(out=kb[:, 2, :, :], in0=A[:], in1=ntni, op=TT.mult))
                    op("dve", [BB_[ai], prepB], [KBB[u % 2][3]], lambda e: e.tensor_tensor(out=kb[:, 3, :, :], in0=B_[:], in1=tnr, op=TT.mult))

                def stT_cs(u):
                    chc, c, j0, ai, ci, qi = ctx(u)
                    kt = KBt[u % 2]
                    for ri in range(2):
                        b = k.nextps()
                        for jl in range(4):
                            op("pe", [KBB[u % 2], triuB], [psB[b]],
                               lambda e, b=b, ri=ri, jl=jl: e.matmul(ps[b][:, jl * P:(jl + 1) * P], lhsT=kt[:, 2 * ri, jl, :], rhs=triu[:, :], start=True, stop=False),
                               inc=False)
                            op("pe", [KBB[u % 2], triuB], [psB[b]],
                               lambda e, b=b, ri=ri, jl=jl: e.matmul(ps[b][:, jl * P:(jl + 1) * P], lhsT=kt[:, 2 * ri + 1, jl, :],
                                                                     rhs=(triu if ri == 0 else ntriu)[:, :], start=False, stop=True),
                               inc=(jl == 3))
                        for jl in range(4):
                            op("act", [psB[b], carB[j0], carB[j0 + 1]], [CDB[ci][ri][jl]],
                               lambda e, b=b, ri=ri, jl=jl: e.activation(out=CD[ci][:, ri, jl, :], in_=ps[b][:, jl * P:(jl + 1) * P], func=AF.Identity,
                                                                         bias=car[:, ri, j0 + jl:j0 + jl + 1], scale=1.0))

                def stD_e(u):
                    chc, c, j0, ai, ci, qi = ctx(u)
                    if c < 15:
                        xl = CD[ci][:, :, :, P - 1]
                        a_r = W["A128r"][:, j0:j0 + 4].unsqueeze(1).to_broadcast([P, 2, 4])
                        a_i = W["A128i"][:, j0:j0 + 4].unsqueeze(1).to_broadcast([P, 2, 4])
                        op("dve", [CDB[ci], prepB], [eaB], lambda e: e.tensor_tensor(out=ea[:], in0=xl, in1=a_r, op=TT.mult))
                        op("dve", [CDB[ci], prepB], [ebB], lambda e: e.tensor_tensor(out=eb[:], in0=xl, in1=a_i, op=TT.mult))

                def stD_q3(u):
                    chc, c, j0, ai, ci, qi = ctx(u)
                    C = CD[ci][:, 0, :, :]
                    tpi = TPi[:, j0:j0 + 4, :]
                    op("dve", [CDB[ci], prepB], [QB[qi][2]],
                       lambda e: e.scalar_tensor_tensor(out=Q[qi][2][:], in0=C, scalar=-1.0, in1=tpi, op0=TT.mult, op1=TT.mult))

                def stD_car(u):
                    chc, c, j0, ai, ci, qi = ctx(u)
                    if c < 15:
                        op("dve", [eaB, ebB], [carB[j0]], lambda e: e.tensor_tensor(out=car[:, 0, j0:j0 + 4], in0=ea[:, 0, :], in1=eb[:, 1, :], op=TT.add))
                        op("dve", [eaB, ebB], [carB[j0 + 1]], lambda e: e.tensor_tensor(out=car[:, 1, j0:j0 + 4], in0=ea[:, 1, :], in1=eb[:, 0, :], op=TT.subtract))

                def stD_rest(u):
                    chc, c, j0, ai, ci, qi = ctx(u)
                    qt, cq = c // 4, c % 4
                    C, Dd = CD[ci][:, 0, :, :], CD[ci][:, 1, :, :]
                    bo = 6 + (chc % 2)
                    tpr, tpi = TPr[:, j0:j0 + 4, :], TPi[:, j0:j0 + 4, :]
                    q = Q[qi]
                    op("dve", [CDB[ci], prepB], [QB[qi][0]], lambda e: e.tensor_tensor(out=q[0][:], in0=C, in1=tpr, op=TT.mult))
                    op("dve", [CDB[ci], prepB], [QB[qi][1]], lambda e: e.tensor_tensor(out=q[1][:], in0=Dd, in1=tpi, op=TT.mult))
                    op("dve", [CDB[ci], prepB], [QB[qi][3]], lambda e: e.tensor_tensor(out=q[3][:], in0=Dd, in1=tpr, op=TT.mult))

                def stE_mm(u):
                    chc, c, j0, ai, ci, qi = ctx(u)
                    qt, cq = c // 4, c % 4
                    bo = 6 + (chc % 2)
                    q = Q[qi]
                    n = 0
                    for jl in range(4):
                        j = j0 + jl
                        for qq in range(4):
                            wt, wtB = (CrP, CrPB) if qq < 2 else (CiP, CiPB)
                            n += 1
                            op("pe", [QB[qi][qq], wtB], [psB[bo]],
                               lambda e, j=j, jl=jl, qq=qq, wt=wt, n=n: e.matmul(ps[bo][:, cq * P:(cq + 1) * P], lhsT=wt[:, j, :], rhs=q[qq][:, jl, :],
                                                                                  start=(n == 1), stop=(n == 16)), inc=(n == 16))
                    if cq == 3:
                        blk = slice(qt * BW, (qt + 1) * BW)
                        op("act", [psB[bo]], [yB], lambda e: e.activation(out=ytmp[:], in_=ps[bo][:, :], func=AF.Copy))
                        op("dve", [yB, UYB[chc][qt], dTB], [yB],
                           lambda e: e.scalar_tensor_tensor(out=ytmp[:], in0=UY[:, chc, blk], scalar=dT[:, chc:chc + 1], in1=ytmp[:],
                                                            op0=TT.mult, op1=TT.add))
                        op("act", [yB], [UYB[chc][qt]], lambda e: e.activation(out=UY[:, chc, blk], in_=ytmp[:], func=AF.Gelu_apprx_tanh))

                def ok(u):
                    return 0 <= u < NU

                for i in range(NU + 4):
                    if ok(i):
                        stA(i)
                    if ok(i - 2):
                        stT_cs(i - 2)
                    if ok(i - 4):
                        stE_mm(i - 4)
                    if ok(i - 1):
                        stB_mul(i - 1)
                    if ok(i - 3):
                        stD_e(i - 3)
                        stD_q3(i - 3)
                        stD_car(i - 3)
                        stD_rest(i - 3)
                k.barrier()
            sP.close()
            s3 = ExitStack()
            with s3:
                wz, wzB = load_w(s3, "a_wzs", dw["a_wz"], D)
                wg, wgB = load_w(s3, "a_wgs", dw["a_wglu"], D)
                w_out, w_outB = load_w(s3, "a_wouts", dw["a_w_out"], D)
                R1 = sb(s3, "R1", [P, 8, BW], F32)
                R1B = bufs(8)
                xnt = sb(s3, "xnt", [P, 8, BW], BF16)
                xnB = bufs(8)
                tmpA = sb(s3, "tmpA", [P, 8, BW], BF16)
                tmpAB = bufs(8)
                sz = sb(s3, "sz", [P, 8, BW], BF16)
                szB = bufs(8)
                sg = sb(s3, "sg", [P, 2, BW], BF16)
                sgB = bufs(2)
                xn_ap = lambda c: xnt[:, c, :]
                for tb in range(NBLK):
                    blk = slice(tb * BW, (tb + 1) * BW)
                    prenorm_ap(li, tb, xn_ap, xnB, tmpA, tmpAB)
                    uy_ap = lambda c, blk=blk: UY[:, c, blk]
                    uyB_t = [UYB[c][tb] for c in range(8)]
                    for m in range(8):
                        b = proj_ap(wz, wzB, m * P, P, xn_ap, xnB, 8)
                        op("act", [psB[b]], [szB[m]], lambda e, b=b, m=m: e.activation(out=sz[:, m, :], in_=ps[b][:, :], func=AF.Silu))
                        b = proj_ap(wg, wgB, m * P, P, uy_ap, uyB_t, 8)
                        op("act", [psB[b], dTB], [sgB[m % 2]],
                           lambda e, b=b, m=m: e.activation(out=sg[:, m % 2, :], in_=ps[b][:, :], func=AF.Sigmoid, bias=dT[:, 8 + m:9 + m], scale=1.0))
                        op("pool", [szB[m], uyB_t[m]], [szB[m]],
                           lambda e, m=m, blk=blk: e.tensor_tensor(out=sz[:, m, :], in0=sz[:, m, :], in1=UY[:, m, blk], op=ALU.mult))
                        op("pool", [szB[m], sgB[m % 2]], [szB[m]],
                           lambda e, m=m: e.tensor_tensor(out=sz[:, m, :], in0=sz[:, m, :], in1=sg[:, m % 2, :], op=ALU.mult))
                    outproj(li, tb, sz, szB, w_out, w_outB, R1, R1B, tmpA, tmpAB)
                k.barrier()

    for li in layers:
        if li == 3:
            layer_sgu(li)
        elif li == 1:
            layer_swa(li)
        elif li == 2:
            layer_mla(li)
        elif li == 0:
            layer_s5(li)

    if not stored[0]:
        store_out()
    k._wait("sp", out_toks)
    k.barrier()


def host_inputs(inp, layers):
    f = lambda a: np.ascontiguousarray(np.asarray(a, dtype=np.float32))
    common = {}
    common["gpre"] = f(np.asarray(inp["pre_norm"]).reshape(4, 8, P).transpose(2, 0, 1).reshape(P, 32))
    common["gpost"] = f(np.asarray(inp["post_norm"]).reshape(4, 8, P).transpose(2, 0, 1).reshape(P, 32))
    common["ident"] = np.eye(P, dtype=np.float32)
    if 3 in layers:
        common["d_w_in"] = f(inp["d_w_in"][0])
        common["d_w_out"] = f(inp["d_w_out"][0])
        common["d_ws"] = f(np.asarray(inp["d_w_s"][0]).transpose(1, 0, 2))
        common["d_tril"] = np.tril(np.ones((P, P), dtype=np.float32))
        common["d_bs"] = f(inp["d_b_s"][0])
        common["d_lng"] = f(inp["d_ln_g"])
        common["d_lnb"] = f(inp["d_ln_b"])
    if 1 in layers:
        w = np.asarray(inp["b_w_in"][0], dtype=np.float32)
        q, kk, v, z = w[:, :1024], w[:, 1024:1152], w[:, 1152:1280], w[:, 1280:]
        common["b_w_in"] = f(np.concatenate([q, kk[:, :64], kk[:, :64], kk[:, 64:], kk[:, 64:], v, z], axis=1))
        common["b_w_out"] = f(inp["b_w_out"][0])
        common["b_sinks"] = f(inp["b_sinks"])
        def bucket(d):
            if d < 16:
                return d
            v_ = 16 + int(math.log(max(d, 1) / 16.0) / math.log(128 / 16.0) * 16)
            return min(v_, 31)
        rb = np.asarray(inp["rel_bias"], dtype=np.float32)
        bt = np.full((P, 16, 256), -1e30, dtype=np.float32)
        for kj in range(P):
            for qi in range(P):
                d_prev = qi + P - kj
                if d_prev < P:
                    bt[kj, :, qi] = rb[bucket(d_prev), :]
                d_cur = qi - kj
                if d_cur >= 0:
                    bt[kj, :, P + qi] = rb[bucket(d_cur), :]
        common["b_biasT"] = bt
    if 2 in layers:
        w = np.asarray(inp["c_w_in"][0], dtype=np.float32)
        kr = w[:, 1024:1056]
        common["c_w1"] = f(np.concatenate([w[:, :1024], kr, kr[:, 16:], kr[:, :16]], axis=1))
        common["c_wz"] = f(w[:, 1056:])
        uq = np.asarray(inp["c_w_uq"][0], dtype=np.float32).reshape(768, 16, 96)
        nope, rp = uq[:, :, :64], uq[:, :, 64:]
        common["c_wuq"] = f(np.concatenate([rp, rp[:, :, 16:], rp[:, :, :16], nope], axis=2).reshape(768, 2048))
        common["c_wukv"] = f(inp["c_w_ukv"][0])
        common["c_w_out"] = f(inp["c_w_out"][0])
        inv = (np.float32(10000.0) ** (-np.arange(0, 32, 2, dtype=np.float32) / np.float32(32))).astype(np.float32)
        ang = (np.arange(L, dtype=np.float32)[:, None] * inv[None, :]).astype(np.float32)
        cos, sin = np.cos(ang).astype(np.float32).T, np.sin(ang).astype(np.float32).T
        common["c_rope"] = f(np.concatenate([cos, cos, -sin, sin], axis=0))
        g = np.concatenate([np.asarray(inp["c_q_norm"][0]), np.asarray(inp["c_kv_norm"][0])]).astype(np.float32)
        common["c_gqkv"] = f(g.reshape(8, P).T)
        common["c_maskT"] = np.triu(np.ones((P, P), dtype=np.float32))
    if 0 in layers:
        w = np.asarray(inp["a_w_in"][0], dtype=np.float32)
        common["a_wu"] = f(w[:, :1024])
        common["a_wz"] = f(w[:, 1024:])
        common["a_wglu"] = f(inp["a_w_glu"][0])
        common["a_w_out"] = f(inp["a_w_out"][0])
        dg = np.concatenate([np.asarray(inp["a_d"][0]).reshape(8, P).T, np.asarray(inp["a_b_glu"][0]).reshape(8, P).T], axis=1)
        common["a_dg"] = f(dg)
        lam = np.stack([np.asarray(inp["a_lam_re"][0]).reshape(32, P).T, np.asarray(inp["a_lam_im"][0]).reshape(32, P).T,
                        np.repeat(np.asarray(inp["a_log_dt"][0]), 64).reshape(32, P).T], axis=1)
        common["a_lam"] = f(lam)
        brl = np.zeros((P, 32, P), np.float32); bil = np.zeros((P, 32, P), np.float32)
        crp = np.zeros((P, 32, P), np.float32); cip = np.zeros((P, 32, P), np.float32)
        b_re, b_im = np.asarray(inp["a_b_re"][0]), np.asarray(inp["a_b_im"][0])
        c_re, c_im = np.asarray(inp["a_c_re"][0]), np.asarray(inp["a_c_im"][0])
        for j in range(32):
            for gl in range(2):
                g = 2 * j + gl
                r0 = 32 * (j % 4) + gl * 16
                brl[r0:r0 + 16, j, gl * 64:(gl + 1) * 64] = b_re[g].T
                bil[r0:r0 + 16, j, gl * 64:(gl + 1) * 64] = b_im[g].T
                crp[gl * 64:(gl + 1) * 64, j, r0:r0 + 16] = c_re[g].T
                cip[gl * 64:(gl + 1) * 64, j, r0:r0 + 16] = c_im[g].T
        common["a_brl"], common["a_bil"], common["a_crp"], common["a_cip"] = brl, bil, crp, cip
        common["a_triu"] = np.triu(np.ones((P, P), dtype=np.float32))
        common["a_ntriu"] = -np.triu(np.ones((P, P), dtype=np.float32))
    return common


def run(inp, layers=(0, 1, 2, 3), cores=8, trace=False):
    nc = bass.Bass("TRN2", target_bir_lowering=False)
    build(nc, list(layers))
    common = host_inputs(inp, list(layers))
    x = np.asarray(inp["x"], dtype=np.float32)
    in_maps = []
    for b in range(cores):
        m = dict(common)
        m["xT"] = np.ascontiguousarray(x[b].T)
        in_maps.append(m)
    res = run_bass_kernel_spmd(nc, in_maps, core_ids=list(range(cores)), trace=trace)
    out = np.stack([np.ascontiguousarray(r["outT"].T) for r in res.results], axis=0)
    return out.astype(np.float32), res


def kernel(**inputs):
    out, _ = run(inputs)
    return out
```
